# Optimizing a Trainium2 kernel written in Bass

```python
import jax, jax.numpy as jnp
from jax import lax
import numpy as np

D_MODEL = 1024
BATCH = 32
SEQ = 256
DEPTH = 2
DEC_BATCH = 4
DEC_SEQ = 4096
PAST_LEN = 256

GRID_W = 64
HEAD_DIM = 64
N_Q_HEADS = 8
N_KV_HEADS = 2
N_GROUPS = N_Q_HEADS // N_KV_HEADS
ATTN_WIDTH = N_Q_HEADS * HEAD_DIM
KV_WIDTH = N_KV_HEADS * HEAD_DIM
N_DN_HEADS = 8
DK = 64
DV = 64
DN_K_WIDTH = N_DN_HEADS * DK
DN_V_WIDTH = N_DN_HEADS * DV
CONV_W = 3
CONV_CH = 2 * DN_K_WIDTH + DN_V_WIDTH
CHUNK = 64
Q_BLOCK = 128
D_FF = 4 * D_MODEL
ROPE_THETA = 10000.0
ROPE_FREQS = HEAD_DIM // 4
EPS = 1e-6
SPLIT_SIZES = (ATTN_WIDTH, KV_WIDTH, KV_WIDTH, DN_K_WIDTH, DN_K_WIDTH, DN_V_WIDTH, DN_V_WIDTH,
               2 * N_DN_HEADS, 2 * N_DN_HEADS, D_MODEL, D_MODEL)
IN_WIDTH = sum(SPLIT_SIZES)

kernel_name = 'hybrid_gqa_deltanet_diffusion_step'


def rmsnorm(x, g):
    xf = x.astype(jnp.float32)
    y = xf * lax.rsqrt(jnp.mean(xf * xf, axis=-1, keepdims=True) + EPS)
    return (y * g.astype(jnp.float32)).astype(x.dtype)


def l2norm(x):
    xf = x.astype(jnp.float32)
    return xf * lax.rsqrt(jnp.sum(xf * xf, axis=-1, keepdims=True) + EPS)


def adaln(c, w_mod, b_mod):
    return jax.nn.silu(c) @ w_mod + b_mod


def axial_rope(n_tokens):
    rows = n_tokens // GRID_W
    t = jnp.arange(rows * GRID_W)
    row = (t // GRID_W).astype(jnp.float32)
    col = (t % GRID_W).astype(jnp.float32)
    inv = ROPE_THETA ** (-jnp.arange(ROPE_FREQS, dtype=jnp.float32) / ROPE_FREQS)
    ang = jnp.concatenate([row[:, None] * inv, col[:, None] * inv], axis=-1)
    return jnp.cos(ang), jnp.sin(ang)


def apply_rope(x, cos, sin):
    half = x.shape[-1] // 2
    x1, x2 = x[..., :half], x[..., half:]
    cos = cos.astype(x.dtype)
    sin = sin.astype(x.dtype)
    return jnp.concatenate([x1 * cos - x2 * sin, x2 * cos + x1 * sin], axis=-1)


def short_conv(x, w):
    T = x.shape[1]
    pad = CONV_W // 2
    xp = jnp.pad(x, ((0, 0), (pad, CONV_W - 1 - pad), (0, 0)))
    y = xp[:, 0:T] * w[0]
    for j in range(1, CONV_W):
        y = y + xp[:, j:j + T] * w[j]
    return jax.nn.silu(y)


def block_attention(q, k, v):
    B, KV, G, T, hd = q.shape
    nb = T // Q_BLOCK
    qb = jnp.moveaxis(q.reshape(B, KV, G, nb, Q_BLOCK, hd), 3, 0)
    scale = HEAD_DIM ** -0.5

    def one_block(qblk):
        s = jnp.einsum('bkgqd,bkjd->bkgqj', qblk, k).astype(jnp.float32) * scale
        p = jax.nn.softmax(s, axis=-1).astype(v.dtype)
        return jnp.einsum('bkgqj,bkjd->bkgqd', p, v)

    ob = lax.map(one_block, qb)
    return jnp.moveaxis(ob, 0, 3).reshape(B, KV * G, T, hd)


def delta_chunked(q, k, v, log_a, beta, s0):
    B, H, T, dk = q.shape
    dv = v.shape[-1]
    n = T // CHUNK
    q = q.reshape(B, H, n, CHUNK, dk)
    k = k.reshape(B, H, n, CHUNK, dk)
    v = v.reshape(B, H, n, CHUNK, dv)
    g = jnp.cumsum(log_a.reshape(B, H, n, CHUNK), axis=-1)
    b = beta.reshape(B, H, n, CHUNK)
    idx = jnp.arange(CHUNK)
    incl = idx[:, None] >= idx[None, :]
    strict = idx[:, None] > idx[None, :]
    decay = jnp.exp(jnp.where(incl, g[..., :, None] - g[..., None, :], -jnp.inf))
    kk = jnp.einsum('bhnid,bhnjd->bhnij', k, k)
    lmat = jnp.where(strict, b[..., :, None] * kk * decay, 0.0) + jnp.eye(CHUNK, dtype=kk.dtype)
    w = lax.linalg.triangular_solve(lmat, (b * jnp.exp(g))[..., None] * k,
                                    left_side=True, lower=True, unit_diagonal=True)
    u = lax.linalg.triangular_solve(lmat, b[..., None] * v,
                                    left_side=True, lower=True, unit_diagonal=True)
    qk = jnp.einsum('bhnid,bhnjd->bhnij', q, k) * decay
    q_dec = q * jnp.exp(g)[..., None]
    g_last = g[..., -1]
    k_dec = k * jnp.exp(g_last[..., None] - g)[..., None]
    xs = (jnp.moveaxis(w, 2, 0), jnp.moveaxis(u, 2, 0), jnp.moveaxis(q_dec, 2, 0),
          jnp.moveaxis(qk, 2, 0), jnp.moveaxis(k_dec, 2, 0), jnp.moveaxis(jnp.exp(g_last), 2, 0))

    def step(S, inp):
        w_c, u_c, qd_c, qk_c, kd_c, eg_c = inp
        u_c = u_c - jnp.einsum('bhcd,bhde->bhce', w_c, S)
        o_c = jnp.einsum('bhcd,bhde->bhce', qd_c, S) + jnp.einsum('bhij,bhje->bhie', qk_c, u_c)
        S = eg_c[..., None, None] * S + jnp.einsum('bhcd,bhce->bhde', kd_c, u_c)
        return S, o_c

    S, o = lax.scan(step, s0, xs)
    return jnp.moveaxis(o, 0, 2).reshape(B, H, T, dv), S


def trunk_layer(x, mod, w_in, conv_w, q_gain, k_gain, a_log, dt_bias, dn_gain, w_pa, w_pd, w_out,
                norm1, norm2, w1, b1, w2, b2, ctx, rope):
    B, T, _ = x.shape
    shift1, scale1, gate1, shift2, scale2, gate2 = jnp.split(mod.astype(x.dtype), 6, axis=-1)
    h = rmsnorm(x, norm1) * (1 + scale1) + shift1
    split_idx = [int(i) for i in np.cumsum(SPLIT_SIZES)[:-1]]
    (q_a, k_a, v_a, q_d, k_d, v_d, g_out, a_in, beta_in, gate_a, gate_d) = jnp.split(
        h @ w_in, split_idx, axis=-1)
    q_a = rmsnorm(q_a.reshape(B, T, N_Q_HEADS, HEAD_DIM), q_gain).transpose(0, 2, 1, 3)
    k_a = rmsnorm(k_a.reshape(B, T, N_KV_HEADS, HEAD_DIM), k_gain).transpose(0, 2, 1, 3)
    v_a = v_a.reshape(B, T, N_KV_HEADS, HEAD_DIM).transpose(0, 2, 1, 3)
    qkv_d = short_conv(jnp.concatenate([q_d, k_d, v_d], axis=-1), conv_w)
    q_d, k_d, v_d = jnp.split(qkv_d, [DN_K_WIDTH, 2 * DN_K_WIDTH], axis=-1)
    q_d = (l2norm(q_d.reshape(B, T, N_DN_HEADS, DK)) * (DK ** -0.5)).transpose(0, 2, 1, 3)
    k_d = l2norm(k_d.reshape(B, T, N_DN_HEADS, DK)).transpose(0, 2, 1, 3)
    v_d = v_d.reshape(B, T, N_DN_HEADS, DV).astype(jnp.float32).transpose(0, 2, 1, 3)
    a_in = a_in.reshape(B, T, 2, N_DN_HEADS).astype(jnp.float32)
    log_a = (-jnp.exp(a_log.astype(jnp.float32)) *
             jax.nn.softplus(a_in + dt_bias.astype(jnp.float32))).transpose(0, 2, 3, 1)
    beta = jax.nn.sigmoid(beta_in.astype(jnp.float32)).reshape(B, T, 2, N_DN_HEADS).transpose(0, 2, 3, 1)

    if ctx is None:
        k_all, v_all = k_a, v_a
        s_f = jnp.zeros((B, N_DN_HEADS, DK, DV), jnp.float32)
        s_b = jnp.zeros((B, N_DN_HEADS, DK, DV), jnp.float32)
        q_use = q_a
    else:
        k_ctx, v_ctx, s_f, s_b = ctx
        cos, sin = rope
        q_use = apply_rope(q_a, cos, sin)
        k_all = jnp.concatenate([k_ctx.astype(x.dtype), apply_rope(k_a, cos, sin)], axis=2)
        v_all = jnp.concatenate([v_ctx.astype(x.dtype), v_a], axis=2)
        s_f = s_f.astype(jnp.float32)
        s_b = s_b.astype(jnp.float32)

    o_a = block_attention(q_use.reshape(B, N_KV_HEADS, N_GROUPS, T, HEAD_DIM), k_all, v_all)
    o_a = o_a.transpose(0, 2, 1, 3).reshape(B, T, ATTN_WIDTH)

    o_f, S_f = delta_chunked(q_d, k_d, v_d, log_a[:, 0], beta[:, 0], s_f)
    o_b, S_b = delta_chunked(jnp.flip(q_d, 2), jnp.flip(k_d, 2), jnp.flip(v_d, 2),
                             jnp.flip(log_a[:, 1], -1), jnp.flip(beta[:, 1], -1), s_b)
    o_d = (o_f + jnp.flip(o_b, 2)).transpose(0, 2, 1, 3)
    o_d = rmsnorm(o_d, dn_gain) * jax.nn.silu(g_out.reshape(B, T, N_DN_HEADS, DV).astype(jnp.float32))
    o_d = o_d.reshape(B, T, DN_V_WIDTH).astype(x.dtype)

    merged = jax.nn.sigmoid(gate_a) * (o_a @ w_pa) + jax.nn.sigmoid(gate_d) * (o_d @ w_pd)
    x = x + gate1 * (merged @ w_out)
    h2 = rmsnorm(x, norm2) * (1 + scale2) + shift2
    ff = jnp.square(jax.nn.relu(h2 @ w1 + b1)) @ w2 + b2
    x = x + gate2 * ff
    return x, k_a, v_a, S_f, S_b


def setup_inputs(seed: int = 0) -> dict:
    key = jax.random.key(seed)
    ks = jax.random.split(key, 32)

    def nrm(k, shape, scale):
        return jax.random.normal(k, shape, jnp.float32) * scale

    dt = jnp.exp(jax.random.uniform(ks[14], (DEPTH, 2, N_DN_HEADS), jnp.float32,
                                    float(np.log(1e-3)), float(np.log(1e-1))))
    return {
        'x_prompt': nrm(ks[0], (BATCH, SEQ, D_MODEL), 1.0),
        'x_sample': nrm(ks[1], (DEC_BATCH, DEC_SEQ, D_MODEL), 1.0),
        'c': nrm(ks[2], (DEC_BATCH, D_MODEL), 1.0),
        'cache_k': nrm(ks[3], (DEC_BATCH, DEPTH, N_KV_HEADS, PAST_LEN, HEAD_DIM), 1.0),
        'cache_v': nrm(ks[4], (DEC_BATCH, DEPTH, N_KV_HEADS, PAST_LEN, HEAD_DIM), 1.0),
        'state_delta': nrm(ks[5], (DEC_BATCH, DEPTH, 2, N_DN_HEADS, DK, DV), 0.1),
        'c_ctx': nrm(ks[6], (D_MODEL,), 1.0),
        'w_mod': nrm(ks[7], (DEPTH, D_MODEL, 6 * D_MODEL), 0.5 * D_MODEL ** -0.5),
        'b_mod': nrm(ks[8], (DEPTH, 6 * D_MODEL), 0.02),
        'norm1': 1.0 + nrm(ks[9], (DEPTH, D_MODEL), 0.02),
        'norm2': 1.0 + nrm(ks[10], (DEPTH, D_MODEL), 0.02),
        'w_in': nrm(ks[11], (DEPTH, D_MODEL, IN_WIDTH), D_MODEL ** -0.5),
        'conv_w': nrm(ks[12], (DEPTH, CONV_W, CONV_CH), CONV_W ** -0.5),
        'q_gain': 1.0 + nrm(ks[13], (DEPTH, HEAD_DIM), 0.02),
        'k_gain': 1.0 + nrm(ks[15], (DEPTH, HEAD_DIM), 0.02),
        'a_log': jnp.log(jax.random.uniform(ks[16], (DEPTH, 2, N_DN_HEADS), jnp.float32, 1.0, 16.0)),
        'dt_bias': dt + jnp.log(-jnp.expm1(-dt)),
        'dn_gain': 1.0 + nrm(ks[17], (DEPTH, DV), 0.02),
        'w_pa': nrm(ks[18], (DEPTH, ATTN_WIDTH, D_MODEL), ATTN_WIDTH ** -0.5),
        'w_pd': nrm(ks[19], (DEPTH, DN_V_WIDTH, D_MODEL), DN_V_WIDTH ** -0.5),
        'w_out': nrm(ks[20], (DEPTH, D_MODEL, D_MODEL), D_MODEL ** -0.5),
        'w1': nrm(ks[21], (DEPTH, D_MODEL, D_FF), D_MODEL ** -0.5),
        'b1': nrm(ks[22], (DEPTH, D_FF), 0.02),
        'w2': nrm(ks[23], (DEPTH, D_FF, D_MODEL), D_FF ** -0.5),
        'b2': nrm(ks[24], (DEPTH, D_MODEL), 0.02),
        'final_norm': 1.0 + nrm(ks[25], (D_MODEL,), 0.02),
    }


def reference(x_prompt, x_sample, c, cache_k, cache_v, state_delta, c_ctx, w_mod, b_mod, norm1, norm2,
              w_in, conv_w, q_gain, k_gain, a_log, dt_bias, dn_gain, w_pa, w_pd, w_out,
              w1, b1, w2, b2, final_norm):
    rope = axial_rope(x_sample.shape[1])
    xp, xs = x_prompt, x_sample
    ks_out, vs_out, st_out = [], [], []
    for l in range(DEPTH):
        lw = (w_in[l], conv_w[l], q_gain[l], k_gain[l], a_log[l], dt_bias[l], dn_gain[l],
              w_pa[l], w_pd[l], w_out[l], norm1[l], norm2[l], w1[l], b1[l], w2[l], b2[l])
        mod_ctx = adaln(c_ctx, w_mod[l], b_mod[l])[None, None, :]
        xp, k_c, v_c, s_fwd, s_bwd = trunk_layer(xp, mod_ctx, *lw, ctx=None, rope=None)
        ks_out.append(k_c)
        vs_out.append(v_c)
        st_out.append(jnp.stack([s_fwd, s_bwd], axis=1))
        mod_lat = adaln(c, w_mod[l], b_mod[l])[:, None, :]
        ctx = (cache_k[:, l], cache_v[:, l], state_delta[:, l, 0], state_delta[:, l, 1])
        xs, _, _, _, _ = trunk_layer(xs, mod_lat, *lw, ctx=ctx, rope=rope)
    y_prompt = rmsnorm(xp, final_norm)
    y_sample = rmsnorm(xs, final_norm)
    new_cache_k = jnp.stack(ks_out, axis=1)
    new_cache_v = jnp.stack(vs_out, axis=1)
    new_state_delta = jnp.stack(st_out, axis=1)
    return (y_prompt, y_sample, new_cache_k, new_cache_v, new_state_delta)
```

```python
import numpy as np
from contextlib import ExitStack
import concourse.bass as bass
import concourse.mybir as mybir
from concourse.bass_utils import run_bass_kernel_spmd

F32 = mybir.dt.float32
BF16 = mybir.dt.bfloat16
AF = mybir.ActivationFunctionType
ALU = mybir.AluOpType
AX = mybir.AxisListType

D = 1024
DEPTH = 2
TS = 4096
TP = 256
NPS = 4
NCTX = 256
INW = 4896
DFF = 4096
EPS = 1e-6
NEG = -30000.0

C_QA, C_KA, C_VA, C_QD, C_KD, C_VD, C_GO, C_AI, C_BI, C_GA, C_GD = 0, 512, 640, 768, 1280, 1792, 2304, 2816, 2832, 2848, 3872


ENGS = ["pe", "act", "dve", "pool", "sp"]
N_DMA_SEMS = {"sp": 40, "pool": 16, "act": 8}
EPOCH = 30000


class Ev:
    __slots__ = ("dma", "eng", "idx", "sem", "val")

    def __init__(self, dma, eng, idx, sem=None, val=None):
        self.dma, self.eng, self.idx, self.sem, self.val = dma, eng, idx, sem, val


class Rec:
    __slots__ = ("eng", "fn", "waits", "signal", "idx", "dma", "sig_sem", "sig_val")

    def __init__(self, eng, fn):
        self.eng, self.fn = eng, fn
        self.waits = []
        self.signal = False
        self.dma = None


class Buf:
    __slots__ = ("w", "r")

    def __init__(self):
        self.w = None
        self.r = []


class Prog:
    def __init__(self, nc):
        self.nc = nc
        self.ops = {e: [] for e in ENGS}
        self.waited = {e: {p: -1 for p in ENGS} for e in ENGS}
        self.waited_dma = {e: {} for e in ENGS}
        self.bufs = {}
        self.dma_count = {q: 0 for q in N_DMA_SEMS}

    def buf(self, k):
        b = self.bufs.get(k)
        if b is None:
            b = self.bufs[k] = Buf()
        return b

    def add(self, eng, fn, reads=(), writes=(), dma=False):
        rec = Rec(eng, fn)
        rec.idx = len(self.ops[eng])
        deps = []
        for k in reads:
            b = self.buf(k)
            if b.w is not None:
                deps.append((b.w, True))
        for k in writes:
            b = self.buf(k)
            if b.w is not None:
                deps.append((b.w, False))
            for r in b.r:
                deps.append((r, False))
        if dma:
            d = self.dma_count[eng]
            n = N_DMA_SEMS[eng]
            si, val = d % n, 16 * (d // n + 1)
            if d >= n:
                deps.append((Ev(True, eng, None, si, val - 16), True))
            rec.dma = (si, val)
            ev = Ev(True, eng, rec.idx, si, val)
            self.dma_count[eng] += 1
        else:
            ev = Ev(False, eng, rec.idx)
        for dep, raw in deps:
            if dep.dma:
                key = (dep.eng, dep.sem)
                if self.waited_dma[eng].get(key, 0) >= dep.val:
                    continue
                self.waited_dma[eng][key] = dep.val
                rec.waits.append(dep)
            else:
                if dep.eng == eng and eng == "pe":
                    continue
                if self.waited[eng][dep.eng] >= dep.idx:
                    continue
                self.waited[eng][dep.eng] = dep.idx
                rec.waits.append(dep)
                self.ops[dep.eng][dep.idx].signal = True
        for k in reads:
            self.buf(k).r.append(ev)
        for k in writes:
            b = self.buf(k)
            b.w = ev
            b.r = []
        self.ops[eng].append(rec)
        return rec

    def fence(self):
        last = {e: len(self.ops[e]) - 1 for e in ["pe", "act", "dve", "pool"]}
        dma_evs = []
        for q, n in N_DMA_SEMS.items():
            d = self.dma_count[q]
            for i in range(min(n, d)):
                uses = (d - i + n - 1) // n
                dma_evs.append(Ev(True, q, None, i, 16 * uses))
        for e in ENGS:
            rec = Rec(e, lambda eng: eng.nop())
            rec.idx = len(self.ops[e])
            for p, li in last.items():
                if p == e or li < 0:
                    continue
                j = li
                while j >= 0 and self.ops[p][j].dma is not None:
                    j -= 1
                if j < 0 or self.waited[e][p] >= j:
                    continue
                self.waited[e][p] = j
                rec.waits.append(Ev(False, p, j))
                self.ops[p][j].signal = True
            for dep in dma_evs:
                key = (dep.eng, dep.sem)
                if self.waited_dma[e].get(key, 0) >= dep.val:
                    continue
                self.waited_dma[e][key] = dep.val
                rec.waits.append(dep)
            self.ops[e].append(rec)

    def emit(self, es):
        nc = self.nc
        comp_sems = {}
        for e in ["pe", "act", "dve", "pool"]:
            cnt = 0
            for rec in self.ops[e]:
                if rec.signal and rec.dma is None:
                    ep = cnt // EPOCH
                    if (e, ep) not in comp_sems:
                        comp_sems[(e, ep)] = es.enter_context(nc.semaphore(f"c_{e}_{ep}"))
                    rec.sig_sem = comp_sems[(e, ep)]
                    rec.sig_val = cnt % EPOCH + 1
                    cnt += 1
        dma_sems = {}
        for q, n in N_DMA_SEMS.items():
            for i in range(min(n, self.dma_count[q])):
                dma_sems[(q, i)] = es.enter_context(nc.semaphore(f"d_{q}_{i}"))
        final_waits = []
        for q, n in N_DMA_SEMS.items():
            d = self.dma_count[q]
            for i in range(min(n, d)):
                uses = (d - i + n - 1) // n
                final_waits.append((dma_sems[(q, i)], 16 * uses))
        block = es.enter_context(nc.Block())
        ops = self.ops

        def run(engname, eng):
            for rec in ops[engname]:
                for dep in rec.waits:
                    if dep.dma:
                        eng.wait_ge(dma_sems[(dep.eng, dep.sem)], dep.val)
                    else:
                        prod = ops[dep.eng][dep.idx]
                        eng.wait_ge(prod.sig_sem, prod.sig_val)
                ins = rec.fn(eng)
                if rec.dma is not None:
                    ins.then_inc(dma_sems[(engname, rec.dma[0])], 16)
                elif rec.signal:
                    ins.then_inc(rec.sig_sem, 1)
            if engname == "sp":
                for s, v in final_waits:
                    eng.wait_ge(s, v)

        @block.tensor
        def _(t):
            run("pe", t)

        @block.scalar
        def _(a):
            run("act", a)

        @block.vector
        def _(v):
            run("dve", v)

        @block.gpsimd
        def _(g):
            run("pool", g)

        @block.sync
        def _(s):
            run("sp", s)


CST_LAYOUT = {}


def _build_consts():
    p = np.arange(128)
    cols = []
    off = 0

    def put(name, arr):
        nonlocal off
        arr = np.asarray(arr, np.float32).reshape(128, -1)
        CST_LAYOUT[name] = (off, arr.shape[1])
        cols.append(arr)
        off += arr.shape[1]

    put("ident", np.eye(128))
    put("ones", np.ones((128, 128)))
    put("negones", -np.ones((128, 128)))
    half = p // 64
    put("blk", (half[:, None] == half[None, :]).astype(np.float32))
    put("negblk", -(half[:, None] == half[None, :]).astype(np.float32))
    put("identst", (p[:, None] % 64 == np.arange(64)[None, :]).astype(np.float32))
    same = half[:, None] == half[None, :]
    put("tri_f", (same & (p[:, None] <= p[None, :])).astype(np.float32))
    put("tri_b", (same & (p[:, None] >= p[None, :])).astype(np.float32))
    put("half0", np.repeat((p < 64).astype(np.float32)[:, None], 128, 1))
    put("half1", np.repeat((p >= 64).astype(np.float32)[:, None], 128, 1))
    il = p % 64
    j = np.arange(64)

    def m(keep):
        return np.where(keep, 0.0, NEG).astype(np.float32)

    put("m1_f", m(il[:, None] > j[None, :]))
    put("m1_b", m(il[:, None] < j[None, :]))
    put("m2_f", m(j[None, :] > il[:, None]))
    put("m2_b", m(j[None, :] < il[:, None]))
    put("m3_f", m(j[None, :] >= il[:, None]))
    put("m3_b", m(j[None, :] <= il[:, None]))
    put("mask8", ((il[:, None] // 8) == (j[None, :] // 8)).astype(np.float32))
    for sz in (8, 16, 32):
        put("moff%d" % sz, (((il[:, None] // (2 * sz)) == (j[None, :] // (2 * sz)))
                            & ((il[:, None] // sz) != (j[None, :] // sz))).astype(np.float32))
    R = np.zeros((128, 128), np.float32)
    for q in range(128):
        if q % 64 < 32:
            R[q, q + 32] = -1.0
        else:
            R[q, q - 32] = 1.0
    put("rot", R.T)
    return np.concatenate(cols, 1)


CST = _build_consts()
NCST = CST.shape[1]


def _rope_tables():
    t = np.arange(TS)
    row = (t // 64).astype(np.float32)
    col = (t % 64).astype(np.float32)
    inv = (10000.0 ** (-np.arange(16, dtype=np.float32) / 16)).astype(np.float32)
    ang = np.concatenate([row[:, None] * inv, col[:, None] * inv], -1).astype(np.float32)
    cos = np.cos(ang).astype(np.float32).T
    sin = np.sin(ang).astype(np.float32).T
    return np.tile(cos, (4, 1)).copy(), np.tile(sin, (4, 1)).copy()


class Seq:
    def __init__(self, name, T, is_sample, idx, key0, tile0):
        self.name, self.T, self.is_sample, self.idx = name, T, is_sample, idx
        self.key0 = key0
        self.tile0 = tile0
        self.nctx = NCTX if is_sample else 0
        self.mj = 0 if is_sample else 1


def bcast(ap, axis, n):
    shp = list(ap.shape)
    shp.insert(axis, n)
    return ap.unsqueeze(axis).broadcast_to(shp)


def build_program(debug_outs=(), stop_after=None):
    nc = bass.Bass("TRN2", target_bir_lowering=False)
    es = ExitStack()
    P = Prog(nc)
    dbg = set(debug_outs)

    def din(name, shape, dt=F32):
        return nc.dram_tensor(name, list(shape), dt, kind="ExternalInput").ap()

    def dout(name, shape, dt=F32):
        return nc.dram_tensor(name, list(shape), dt, kind="ExternalOutput").ap()

    def dscr(name, shape, dt=F32):
        kind = "ExternalOutput" if name in dbg else "Internal"
        return nc.dram_tensor(name, list(shape), dt, kind=kind).ap()

    xs_in = din("xs", [TS, D])
    xp_in = din("xp", [NPS * TP, D])
    cvec = din("cvec", [2, D])
    ck_in = din("ck", [DEPTH, 2, NCTX, 64])
    cv_in = din("cv", [DEPTH, 2, NCTX, 64])
    st_in = din("st", [DEPTH, 2, 8, 64, 64])
    w_mod = din("w_mod", [DEPTH, D, 6 * D])
    b_mod = din("b_mod", [DEPTH, 6 * D])
    norm1 = din("norm1", [DEPTH, D])
    norm2 = din("norm2", [DEPTH, D])
    w_in = din("w_in", [DEPTH, D, INW])
    conv_w = din("conv_w", [DEPTH, 3, 1536])
    q_gain = din("q_gain", [DEPTH, 64])
    k_gain = din("k_gain", [DEPTH, 64])
    a_log = din("a_log", [DEPTH, 16])
    dt_bias = din("dt_bias", [DEPTH, 16])
    dn_gain = din("dn_gain", [DEPTH, 64])
    w_pa = din("w_pa", [DEPTH, 512, D])
    w_pd = din("w_pd", [DEPTH, 512, D])
    w_out = din("w_out", [DEPTH, D, D])
    w1 = din("w1", [DEPTH, D, DFF])
    b1 = din("b1", [DEPTH, DFF])
    w2 = din("w2", [DEPTH, DFF, D])
    b2 = din("b2", [DEPTH, D])
    final_norm = din("final_norm", [D])
    cst_in = din("cst", [128, NCST])
    cos_in = din("ropecos", [128, TS])
    sin_in = din("ropesin", [128, TS])
    ys_out = dout("ys", [TS, D])
    yp_out = dout("yp", [NPS * TP, D])
    nk_out = dout("nk", [NPS, DEPTH, 2, TP, 64])
    nv_out = dout("nv", [NPS, DEPTH, 2, TP, 64])
    nst_out = dout("nst", [NPS, DEPTH, 2, 8, 64, 64])

    seqs = [Seq("s", TS, True, 0, 0, 0)]
    for i in range(NPS):
        seqs.append(Seq(f"p{i}", TP, False, i, NCTX + TS + i * TP, TS // 128 + i * (TP // 128)))
    NKEY = NCTX + TS + NPS * TP
    NTILE = TS // 128 + NPS * TP // 128
    NVT = NKEY // 128

    XRES, PRE, QA, GATES, GS, QDT, KDT, QTM, KTM, VTM, OPART, OA, OD, X1, H2, ACTS = ({} for _ in range(16))
    for s in seqs:
        T = s.T
        XRES[s.name] = dscr(f"xres_{s.name}", [128, 8, T])
        PRE[s.name] = dscr(f"pre_{s.name}", [128, 12, T + 2])
        QA[s.name] = dscr(f"qa_{s.name}", [8, 64, T], BF16)
        GATES[s.name] = dscr(f"gates_{s.name}", [128, 16, T], BF16)
        GS[s.name] = dscr(f"gs_{s.name}", [T, 512], BF16)
        QDT[s.name] = dscr(f"qdt_{s.name}", [8, 64, T], BF16)
        KDT[s.name] = dscr(f"kdt_{s.name}", [8, 64, T], BF16)
        QTM[s.name] = dscr(f"qtm_{s.name}", [T, 512], BF16)
        KTM[s.name] = dscr(f"ktm_{s.name}", [T, 512], BF16)
        VTM[s.name] = dscr(f"vtm_{s.name}", [T, 512], BF16)
        OPART[s.name] = dscr(f"opart_{s.name}", [T, 512])
        OA[s.name] = dscr(f"oa_{s.name}", [8, 64, T], BF16)
        OD[s.name] = dscr(f"od_{s.name}", [128, 4, T], BF16)
        H2[s.name] = dscr(f"h2_{s.name}", [128, 8, T], BF16)

    def sb(name, shape, dt=F32):
        return es.enter_context(nc.sbuf_tensor("sb_" + name, list(shape), dt))

    cst = sb("cst", [128, NCST])
    cstb = sb("cstb", [128, NCST], BF16)

    def C(name, bf=False):
        o, n = CST_LAYOUT[name]
        return (cstb if bf else cst)[:, o:o + n]

    WAR = sb("warena", [128, 40960], BF16)
    KT_all = sb("kt_all", [128, NKEY], BF16)
    VA_all = sb("va_all", [128, NVT, 2, 65], BF16)
    LA = sb("la", [128, NTILE, 16])
    LB = sb("lb", [128, NTILE, 16])
    BETA = sb("beta", [128, NTILE, 16])
    MOD = sb("mod", [128, 48, 2])
    A1 = sb("a1", [128, 8, 2])
    A2 = sb("a2", [128, 8, 2])
    GB2 = sb("gb2", [128, 8, 2])
    n1f = sb("n1f", [128, 8])
    n2f = sb("n2f", [128, 8])
    fnf = sb("fnf", [128, 8])
    b2f = sb("b2f", [128, 8])
    b1f = sb("b1f", [128, 32])
    bmf = sb("bmf", [128, 48])
    cfm = sb("cfm", [128, 8, 2])
    scb = sb("scb", [128, 8, 2], BF16)
    qg = sb("qg", [128, 1])
    kg = sb("kg", [128, 1])
    cw = sb("cw", [128, 3, 12])
    dtb = sb("dtb", [128, 16])
    negA = sb("negA", [128, 16])
    dng = sb("dng", [128, 64])
    TB = 256
    BIGA = sb("bigA", [128, 12 * 258])
    BIGB = sb("bigB", [128, 12 * 256])
    HB = sb("hb16", [128, 16, 256], BF16)
    GATB = sb("gatb", [128, 16, 256], BF16)
    R1 = sb("r1", [128, 256])
    RSTD = sb("rstd", [128, 256])
    STG = [sb(f"stg{i}", [128, 512]) for i in range(4)]
    STB = [sb(f"stb{i}", [128, 512], BF16) for i in range(4)]
    COSB = sb("cosb", [128, 256])
    SINB = sb("sinb", [128, 256])
    SMALL = sb("small", [128, 64])
    SM = sb("sm", [128, 160])
    VAB = sb("vab", [128, 160])
    KTC = [sb(f"ktc{i}", [64, 8, 128], BF16) for i in range(2)]
    QTC = [sb(f"qtc{i}", [64, 8, 128], BF16) for i in range(2)]
    KTMC = [sb(f"ktmc{i}", [128, 512], BF16) for i in range(2)]
    QTMC = [sb(f"qtmc{i}", [128, 512], BF16) for i in range(2)]
    VTMC = [sb(f"vtmc{i}", [128, 512], BF16) for i in range(2)]
    def v512(big, i, p0=0, p1=128):
        return big[p0:p1, i * 512:(i + 1) * 512]

    E1, E2, E3, DG1, DG3 = (v512(BIGA, i) for i in range(5))
    U0 = [v512(BIGA, 5), v512(BIGB, 0)]
    OACC = [v512(BIGB, 1), v512(BIGB, 2)]
    RS = v512(BIGB, 3)
    S32 = [v512(BIGB, 4 + i, 0, 64).rearrange("p (h v) -> p h v", h=8) for i in range(2)]
    HBf = HB[:, :, :].rearrange("p a b -> p (a b)")
    GBf = GATB[:, :, :].rearrange("p a b -> p (a b)")
    PA = [v512(HBf, 0), v512(HBf, 1)]
    PT = [v512(HBf, 2), v512(HBf, 3)]
    RT = [v512(HBf, 4), v512(HBf, 5)]
    QKM = [v512(HBf, 6), v512(HBf, 7)]
    BEK = v512(GBf, 0)
    KDEC = [v512(GBf, 1), v512(GBf, 2)]
    BV = v512(GBf, 3)
    UB = [v512(GBf, 4), v512(GBf, 5)]
    DE = v512(GBf, 6)
    OB = v512(GBf, 7, 0, 64)
    XBW = [sb(f"xbw{i}", [128, 1024], BF16) for i in range(2)]
    XB = [XBW[0][:, 0:512], XBW[0][:, 512:1024], XBW[1][:, 0:512], XBW[1][:, 512:1024]]
    NWT = [sb(f"nwt{i}", [64, 16, 64], BF16) for i in range(2)]
    QDEC = [sb(f"qdec{i}", [64, 16, 64], BF16) for i in range(2)]
    SBF = [sb(f"sbf{i}", [64, 8, 64], BF16) for i in range(2)]
    EGL2 = [sb(f"egl2{i}", [128, 16]) for i in range(2)]
    QB = STB[3][:, :].rearrange("p (j t) -> p j t", j=4)
    PS = [es.enter_context(nc.psum_tensor(f"ps{i}", [128, 1024], F32)) for i in range(4)]
    dbg_mod = dscr("dbg_mod", [128, 48, 2])
    dbg_la = dscr("dbg_la", [128, NTILE, 16])
    dbg_lb = dscr("dbg_lb", [128, NTILE, 16])
    dbg_beta = dscr("dbg_beta", [128, NTILE, 16])

    XT = BIGA[:, 0:8 * 256].rearrange("p (c t) -> p c t", c=8)
    X1T = BIGB[:, 0:8 * 256].rearrange("p (c t) -> p c t", c=8)
    PRET = BIGA[:, :].rearrange("p (c t) -> p c t", c=12)
    CV = BIGB[:, :].rearrange("p (c t) -> p c t", c=12)
    SQ = HB[:, 0:8, :]
    HT = HB[:, 8:16, :]
    XTM = [BIGA[:, i * 1024:(i + 1) * 1024] for i in range(2)]
    XFM = [BIGB[:, i * 1024:(i + 1) * 1024].rearrange("p (c t) -> p c t", c=8) for i in range(2)]

    def psb(i):
        return PS[i // 2][:, (i % 2) * 512:(i % 2) * 512 + 512]

    def pk(i):
        return ("ps", i)

    def dma(q, out, in_, reads, writes, **kw):
        P.add(q, lambda e: e.dma_start(out=out, in_=in_, **kw), reads, writes, dma=True)

    def mm(out, lhsT, rhs, start, stop, reads, writes, **kw):
        P.add("pe", lambda e: e.matmul(out, lhsT, rhs, start=start, stop=stop, **kw), reads, writes)

    def tr(out, in_, ident, reads, writes):
        P.add("pe", lambda e: e.transpose(out, in_, ident), reads, writes)

    def act(out, in_, func, reads, writes, **kw):
        P.add("act", lambda e: e.activation(out, in_, func, **kw), reads, writes)

    def tt(eng, out, in0, in1, op, reads, writes):
        P.add(eng, lambda e: e.tensor_tensor(out, in0, in1, op), reads, writes)

    def tsc(eng, out, in0, s1, s2, op0, op1, reads, writes):
        if op1 is None:
            P.add(eng, lambda e: e.tensor_scalar(out, in0, s1, None, op0), reads, writes)
        else:
            P.add(eng, lambda e: e.tensor_scalar(out, in0, s1, s2, op0, op1), reads, writes)

    def stt(eng, out, in0, scalar, in1, op0, op1, reads, writes):
        P.add(eng, lambda e: e.scalar_tensor_tensor(out, in0, scalar, in1, op0, op1), reads, writes)

    def cp(eng, out, in_, reads, writes):
        if eng == "act":
            P.add("act", lambda e: e.copy(out, in_), reads, writes)
        else:
            P.add(eng, lambda e: e.tensor_copy(out, in_), reads, writes)

    def recip(out, in_, reads, writes):
        P.add("dve", lambda e: e.reciprocal(out, in_), reads, writes)

    def load_w(dst3, src2, K, wkey="war"):
        for k in range(K):
            dma("pool", dst3[:, k, :], src2[k * 128:(k + 1) * 128, :], [], [wkey])

    def rms_stats(src3, nt, srckey, sq=None, sqkey="hb"):
        if sq is None:
            sq = SQ
        act(sq[:, :, :nt], src3, AF.Square, [srckey], [sqkey])
        for c in range(8):
            mm(psb(0)[:, :nt], C("ones", True), sq[:, c, :nt], c == 0, c == 7, [sqkey, "cstb"], [pk(0)])
        act(R1[:, :nt], psb(0)[:, :nt], AF.Sqrt, [pk(0)], ["r1"], bias=EPS, scale=1.0 / D)
        recip(RSTD[:, :nt], R1[:, :nt], ["r1"], ["rstd"])

    dma("sp", cst[:], cst_in, [], ["cst"])
    cp("dve", cstb[:], cst[:], ["cst"], ["cstb"])
    P.add("pool", lambda e: e.memset(VA_all[:], 1.0), [], ["va"])
    dma("sp", fnf[:], final_norm.rearrange("(k p) -> p k", p=128), [], ["fnf"], allow_slow_non_contiguous=True)
    for j in range(2):
        dma("sp", cfm[:, :, j], cvec[j].rearrange("(k p) -> p k", p=128), [], ["cfm"], allow_slow_non_contiguous=True)
    act(scb[:], cfm[:], AF.Silu, ["cfm"], ["scb"])
    P.add("pool", lambda e: e.memset(SMALL[:], 0.0), [], ["small"])
    for s in seqs:
        for col in (0, s.T + 1):
            dma("sp", PRE[s.name][:, :, col:col + 1], SMALL[:, 0:12].unsqueeze(2), ["small"], [("pre", s.name, "pad", col)],
                allow_slow_non_contiguous=True)

    it = 0
    for s in seqs:
        src = xs_in if s.is_sample else xp_in[s.idx * TP:(s.idx + 1) * TP, :]
        for ti in range(s.T // 128):
            b = it % 2
            dma("sp", XTM[b], src[ti * 128:(ti + 1) * 128, :], [], ["bigA"])
            for c in range(8):
                tr(PS[b][:, c * 128:(c + 1) * 128], XTM[b][:, c * 128:(c + 1) * 128], C("ident"),
                   ["bigA", "cst"], [pk(2 * b), pk(2 * b + 1)])
            cp("act" if it % 2 == 0 else "dve", XFM[b], PS[b][:].rearrange("p (c t) -> p c t", c=8),
               [pk(2 * b), pk(2 * b + 1)], ["bigB"])
            dma("sp", XRES[s.name][:, :, ti * 128:(ti + 1) * 128], XFM[b], ["bigB"], [("xres", s.name, ti // 2)])
            it += 1

    def finish():
        P.emit(es)
        es.close()
        return nc

    if stop_after == "stage0":
        return finish()

    NLAYERS = DEPTH
    for l in range(NLAYERS):
        for (t_, src) in ((n1f, norm1[l]), (n2f, norm2[l]), (b2f, b2[l])):
            dma("sp", t_[:], src.rearrange("(k p) -> p k", p=128), [], ["lvec"], allow_slow_non_contiguous=True)
        dma("sp", b1f[:], b1[l].rearrange("(k p) -> p k", p=128), [], ["lvec"], allow_slow_non_contiguous=True)
        dma("sp", bmf[:], b_mod[l].rearrange("(k p) -> p k", p=128), [], ["lvec"], allow_slow_non_contiguous=True)
        for hh in range(2):
            dma("sp", qg[64 * hh:64 * hh + 64, :], q_gain[l].rearrange("(d o) -> d o", o=1), [], ["lvec"], allow_slow_non_contiguous=True)
            dma("sp", kg[64 * hh:64 * hh + 64, :], k_gain[l].rearrange("(d o) -> d o", o=1), [], ["lvec"], allow_slow_non_contiguous=True)
        for j in range(3):
            dma("sp", cw[:, j, :], conv_w[l, j].rearrange("(c p) -> p c", p=128), [], ["lvec"], allow_slow_non_contiguous=True)
        dma("sp", dtb[:], dt_bias[l:l + 1, :].broadcast_to([128, 16]), [], ["lvec"], allow_slow_non_contiguous=True)
        dma("sp", negA[:], a_log[l:l + 1, :].broadcast_to([128, 16]), [], ["lvec"], allow_slow_non_contiguous=True)
        dma("sp", dng[:], dn_gain[l:l + 1, :].broadcast_to([128, 64]), [], ["lvec"], allow_slow_non_contiguous=True)
        act(negA[:], negA[:], AF.Exp, ["lvec"], ["lvec2"])
        tsc("dve", negA[:], negA[:], -1.0, None, ALU.mult, None, ["lvec2"], ["lvec2"])

        Wm = WAR[:, 0:8 * 3072].rearrange("p (k n) -> p k n", k=8)
        for hh in range(2):
            load_w(Wm, w_mod[l][:, hh * 3072:(hh + 1) * 3072], 8)
            for n in range(24):
                nn = hh * 24 + n
                for k in range(8):
                    mm(psb(0)[:, nn * 2:nn * 2 + 2], Wm[:, k, n * 128:(n + 1) * 128], scb[:, k, :], k == 0, k == 7,
                       ["war", "scb"], [pk(0)])
        tt("dve", MOD[:], psb(0)[:, 0:96].rearrange("p (n j) -> p n j", j=2), bcast(bmf[:], 2, 2), ALU.add,
           [pk(0), "lvec"], ["mod"])
        stt("dve", A1[:], MOD[:, 8:16, :], 1.0, bcast(n1f[:], 2, 2), ALU.add, ALU.mult, ["mod", "lvec"], ["mod2"])
        stt("dve", A2[:], MOD[:, 32:40, :], 1.0, bcast(n2f[:], 2, 2), ALU.add, ALU.mult, ["mod", "lvec"], ["mod2"])
        tt("dve", GB2[:], MOD[:, 40:48, :], bcast(b2f[:], 2, 2), ALU.mult, ["mod", "lvec"], ["mod2"])
        MK = ["mod", "mod2", "lvec", "lvec2"]
        if stop_after == "adaln":
            dma("sp", dbg_mod, MOD[:], ["mod"], ["dbgmod"])
            return finish()

        Win = WAR[:, 0:8 * INW].rearrange("p (k n) -> p k n", k=8)
        load_w(Win, w_in[l], 8)
        bank_rr = [0]

        def next_bank():
            bank_rr[0] = (bank_rr[0] % 4) + 1
            return bank_rr[0]

        stg_rr = [0]

        def nstg():
            stg_rr[0] = (stg_rr[0] + 1) % 4
            return stg_rr[0]

        for s in seqs:
            mj = s.mj
            for blk in range(s.T // TB):
                t0 = blk * TB
                dma("sp", XT, XRES[s.name][:, :, t0:t0 + TB], [("xres", s.name, blk)], ["bigA"])
                if s.is_sample:
                    dma("sp", COSB[:], cos_in[:, t0:t0 + TB], [], ["cosb"])
                    dma("sp", SINB[:], sin_in[:, t0:t0 + TB], [], ["sinb"])
                rms_stats(XT, TB, "bigA")
                tt("dve", XT, XT, bcast(RSTD[:, :], 1, 8), ALU.mult, ["bigA", "rstd"], ["bigA"])
                for c in range(8):
                    if c % 2 == 0:
                        act(HT[:, c, :], XT[:, c, :], AF.Identity, ["bigA"] + MK, ["ht"],
                            scale=A1[:, c, mj:mj + 1], bias=MOD[:, c, mj:mj + 1])
                    else:
                        tsc("dve", HT[:, c, :], XT[:, c, :], A1[:, c, mj:mj + 1], MOD[:, c, mj:mj + 1], ALU.mult, ALU.add,
                            ["bigA"] + MK, ["ht"])

                def fm_chunk(col0):
                    bk = next_bank()
                    for k in range(8):
                        mm(psb(bk)[:, :TB], Win[:, k, col0:col0 + 128], HT[:, k, :], k == 0, k == 7, ["war", "ht"], [pk(bk)])
                    return bk

                for c in range(5):
                    is_k = (c == 4)
                    bk = fm_chunk(C_QA + c * 128)
                    gain = kg if is_k else qg
                    cp("act", STG[0][:, :TB], psb(bk)[:, :TB], [pk(bk)], ["stg0"])
                    act(STB[0][:, :TB], psb(bk)[:, :TB], AF.Square, [pk(bk)], ["stb0"])
                    mm(psb(6)[:, :TB], C("blk", True), STB[0][:, :TB], True, True, ["stb0", "cstb"], [pk(6)])
                    act(STG[1][:, :TB], psb(6)[:, :TB], AF.Sqrt, [pk(6)], ["stg1"], bias=EPS, scale=1.0 / 64)
                    recip(STG[1][:, :TB], STG[1][:, :TB], ["stg1"], ["stg1"])
                    stt("dve", STG[0][:, :TB], STG[0][:, :TB], gain[:, 0:1], STG[1][:, :TB], ALU.mult, ALU.mult,
                        ["stg0", "stg1", "lvec"], ["stg0"])
                    kcol = s.key0 + s.nctx + t0
                    dst = KT_all[:, kcol:kcol + TB] if is_k else STB[1][:, :TB]
                    dkey = "kt" if is_k else "stb1"
                    if s.is_sample:
                        mm(psb(7)[:, :TB], C("rot"), STG[0][:, :TB], True, True, ["stg0", "cst"], [pk(7)])
                        tt("dve", STG[2][:, :TB], STG[0][:, :TB], COSB[:], ALU.mult, ["stg0", "cosb"], ["stg2"])
                        tt("dve", STG[3][:, :TB], psb(7)[:, :TB], SINB[:], ALU.mult, [pk(7), "sinb"], ["stg3"])
                        tt("dve", dst, STG[2][:, :TB], STG[3][:, :TB], ALU.add, ["stg2", "stg3"], [dkey])
                    else:
                        cp("dve", dst, STG[0][:, :TB], ["stg0"], [dkey])
                    if not is_k:
                        dma("sp", QA[s.name][2 * c:2 * c + 2].rearrange("h d t -> (h d) t")[:, t0:t0 + TB], STB[1][:, :TB],
                            ["stb1"], [("qa", s.name)])
                    elif not s.is_sample:
                        for t2 in range(TB // 128):
                            tr(psb(7)[:, t2 * 128:(t2 + 1) * 128], STG[0][:, t2 * 128:(t2 + 1) * 128], C("ident"),
                               ["stg0", "cst"], [pk(7)])
                        cp("act", STG[2][:, :TB], psb(7)[:, :TB], [pk(7)], ["stg2"])
                        for t2 in range(TB // 128):
                            for g in range(2):
                                dma("sp", nk_out[s.idx, l, g, t0 + t2 * 128:t0 + (t2 + 1) * 128, :],
                                    STG[2][:, t2 * 128 + g * 64:t2 * 128 + g * 64 + 64], ["stg2"], [("nk", s.idx)])
                for c in range(12):
                    bk = fm_chunk(C_QD + c * 128)
                    i = nstg()
                    cp("act" if c % 2 == 0 else "dve", STG[i][:, :TB], psb(bk)[:, :TB], [pk(bk)], [f"stg{i}"])
                    dma("sp", PRE[s.name][:, c, 1 + t0:1 + t0 + TB], STG[i][:, :TB], [f"stg{i}"], [("pre", s.name, blk)])
                for c in range(16):
                    bk = fm_chunk(C_GA + c * 128)
                    i = nstg()
                    act(STB[i][:, :TB], psb(bk)[:, :TB], AF.Sigmoid, [pk(bk)], [f"stb{i}"])
                    dma("sp", GATES[s.name][:, c, t0:t0 + TB], STB[i][:, :TB], [f"stb{i}"], [("gates", s.name, blk)])
                for t2 in range(TB // 128):
                    tsl = slice(t2 * 128, (t2 + 1) * 128)
                    gti = s.tile0 + (t0 // 128) + t2
                    vt = (s.key0 + s.nctx + t0) // 128 + t2
                    for k in range(8):
                        mm(psb(5)[:, 0:512], HT[:, k, tsl], Win[:, k, C_GO:C_GO + 512], k == 0, k == 7, ["war", "ht"], [pk(5)])
                    for k in range(8):
                        mm(psb(6)[:, 0:128], HT[:, k, tsl], Win[:, k, C_VA:C_VA + 128], k == 0, k == 7, ["war", "ht"], [pk(6)])
                    for k in range(8):
                        mm(psb(6)[:, 128:160], HT[:, k, tsl], Win[:, k, C_AI:C_AI + 32], k == 0, k == 7, ["war", "ht"], [pk(6)])
                    i = nstg()
                    act(STB[i][:, :], psb(5)[:, :], AF.Silu, [pk(5)], [f"stb{i}"])
                    dma("sp", GS[s.name][t0 + t2 * 128:t0 + (t2 + 1) * 128, :], STB[i][:, :], [f"stb{i}"], [("gs", s.name)])
                    cp("dve", VAB[:, :], psb(6)[:, 0:160], [pk(6)], ["vab"])
                    cp("dve", VA_all[:, vt, :, 0:64], VAB[:, 0:128].rearrange("p (g d) -> p g d", g=2), ["vab"], ["va"])
                    if not s.is_sample:
                        for g in range(2):
                            dma("sp", nv_out[s.idx, l, g, t0 + t2 * 128:t0 + (t2 + 1) * 128, :], VAB[:, g * 64:g * 64 + 64],
                                ["vab"], [("nv", s.idx)])
                    tt("dve", SM[:, 0:16], VAB[:, 128:144], dtb[:], ALU.add, ["vab", "lvec"], ["sm"])
                    act(SM[:, 16:32], SM[:, 0:16], AF.Exp, ["sm"], ["sm1"])
                    act(SM[:, 32:48], SM[:, 16:32], AF.Ln, ["sm1"], ["sm2"], bias=1.0)
                    tt("dve", LA[:, gti, :], SM[:, 32:48], negA[:], ALU.mult, ["sm2", "lvec2"], ["la"])
                    act(BETA[:, gti, :], VAB[:, 144:160], AF.Sigmoid, ["vab"], ["beta"])
                    act(LB[:, gti, :], BETA[:, gti, :], AF.Ln, ["beta"], ["lb"])
        if "dbg_la" in dbg:
            dma("sp", dbg_la, LA[:], ["la"], ["dbgla"])
            dma("sp", dbg_lb, LB[:], ["lb"], ["dbglb"])
            dma("sp", dbg_beta, BETA[:], ["beta"], ["dbgbeta"])
        P.fence()
        if stop_after == "stageA":
            return finish()

        for s in seqs:
            for blk in range(s.T // TB):
                t0 = blk * TB
                dma("sp", PRET, PRE[s.name][:, :, t0:t0 + TB + 2],
                    [("pre", s.name, b_) for b_ in range(max(0, blk - 1), min(s.T // TB, blk + 2))]
                    + [("pre", s.name, "pad", 0), ("pre", s.name, "pad", s.T + 1)], ["bigA"])
                for c in range(12):
                    e_ = "dve"
                    tsc(e_, CV[:, c, :], PRET[:, c, 0:TB], cw[:, 0, c:c + 1], None, ALU.mult, None, ["bigA", "lvec"], [("cv", c)])
                    stt(e_, CV[:, c, :], PRET[:, c, 1:TB + 1], cw[:, 1, c:c + 1], CV[:, c, :], ALU.mult, ALU.add,
                        ["bigA", "lvec", ("cv", c)], [("cv", c)])
                    stt(e_, CV[:, c, :], PRET[:, c, 2:TB + 2], cw[:, 2, c:c + 1], CV[:, c, :], ALU.mult, ALU.add,
                        ["bigA", "lvec", ("cv", c)], [("cv", c)])
                for c in range(12):
                    act(CV[:, c, :], CV[:, c, :], AF.Silu, [("cv", c)], [("cv", c)])
                for c in range(8):
                    act(STB[0][:, :TB], CV[:, c, :], AF.Square, [("cv", c)], ["stb0"])
                    mm(psb(0)[:, :TB], C("blk", True), STB[0][:, :TB], True, True, ["stb0", "cstb"], [pk(0)])
                    act(STG[0][:, :TB], psb(0)[:, :TB], AF.Sqrt, [pk(0)], ["stg0"], bias=EPS, scale=1.0)
                    recip(STG[0][:, :TB], STG[0][:, :TB], ["stg0"], ["stg0"])
                    stt("dve", CV[:, c, :], CV[:, c, :], 0.125 if c < 4 else 1.0, STG[0][:, :TB], ALU.mult, ALU.mult,
                        [("cv", c), "stg0"], [("cv", c)])
                    i = nstg()
                    cp("act", STB[i][:, :TB], CV[:, c, :], [("cv", c)], [f"stb{i}"])
                    dstT = QDT if c < 4 else KDT
                    cc = c % 4
                    dma("sp", dstT[s.name][2 * cc:2 * cc + 2].rearrange("h d t -> (h d) t")[:, t0:t0 + TB], STB[i][:, :TB],
                        [f"stb{i}"], [("qkdt", s.name)])
                for t2 in range(TB // 128):
                    for grp, dstM in enumerate((QTM, KTM, VTM)):
                        bk = 1 + (grp % 2)
                        for cc in range(4):
                            tr(psb(bk)[:, cc * 128:(cc + 1) * 128], CV[:, grp * 4 + cc, t2 * 128:(t2 + 1) * 128], C("ident"),
                               [("cv", grp * 4 + cc), "cst"], [pk(bk)])
                        i = nstg()
                        cp("act" if grp % 2 == 0 else "dve", STB[i][:, :], psb(bk)[:, :], [pk(bk)], [f"stb{i}"])
                        dma("sp", dstM[s.name][t0 + t2 * 128:t0 + (t2 + 1) * 128, :], STB[i][:, :], [f"stb{i}"], [("tm", s.name)])
        P.fence()
        if stop_after == "stageB":
            return finish()

        for t2 in range(NCTX // 128):
            dma("sp", STG[0][:, 0:128].rearrange("p (g d) -> p g d", g=2),
                ck_in[l, :, t2 * 128:(t2 + 1) * 128, :].rearrange("g p d -> p g d"), [], ["stg0"])
            tr(psb(0)[:, 0:128], STG[0][:, 0:128], C("ident"), ["stg0", "cst"], [pk(0)])
            cp("dve", KT_all[:, t2 * 128:(t2 + 1) * 128], psb(0)[:, 0:128], [pk(0)], ["kt"])
            dma("sp", STG[1][:, 0:128].rearrange("p (g d) -> p g d", g=2),
                cv_in[l, :, t2 * 128:(t2 + 1) * 128, :].rearrange("g p d -> p g d"), [], ["stg1"])
            cp("dve", VA_all[:, t2, :, 0:64], STG[1][:, 0:128].rearrange("p (g d) -> p g d", g=2), ["stg1"], ["va"])
        QBs = [STB[3][:, :].rearrange("p (j t) -> p j t", j=4), STB[2][:, :].rearrange("p (j t) -> p j t", j=4)]
        items = []
        qcount = 0
        for s in seqs:
            ktiles = []
            if s.is_sample:
                ktiles += list(range(NCTX // 128))
            ktiles += [(s.key0 + s.nctx) // 128 + i for i in range(s.T // 128)]
            for qi in range(s.T // 128):
                for n_, kt in enumerate(ktiles):
                    items.append((s, qi, n_, kt, n_ == 0, n_ == len(ktiles) - 1, qcount % 2))
                qcount += 1
        LAG = 1
        for idx in range(len(items) + LAG):
            if idx < len(items):
                s, qi, n_, kt, first, last, qp = items[idx]
                q0 = qi * 128
                if first:
                    for g2 in range(2):
                        dma("sp", QBs[qp][64 * g2:64 * g2 + 64, :, :],
                            QA[s.name][4 * g2:4 * g2 + 4, :, q0:q0 + 128].rearrange("j d t -> d j t"),
                            [("qa", s.name)], [("qb", qp)])
                r_ = idx % 2
                for g in range(2):
                    mm(PS[r_][:, g * 512:(g + 1) * 512], KT_all[64 * g:64 * g + 64, kt * 128:(kt + 1) * 128],
                       QBs[qp][64 * g:64 * g + 64, :, :].rearrange("p j t -> p (j t)"), True, True, ["kt", ("qb", qp)],
                       [pk(2 * r_), pk(2 * r_ + 1)])
                act(XBW[r_][:, :], PS[r_][:, :], AF.Exp, [pk(2 * r_), pk(2 * r_ + 1)], [("ptt", r_)], scale=0.125)
            if idx >= LAG:
                s, qi, n_, kt, first, last, qp = items[idx - LAG]
                q0 = qi * 128
                r_ = (idx - LAG) % 2
                for g in range(2):
                    ob = 4 + g
                    mm(psb(ob)[0:65, :], VA_all[:, kt, g, :], XBW[r_][:, g * 512:(g + 1) * 512], first, last,
                       ["va", ("ptt", r_)], [pk(ob)])
                if last:
                    for g in range(2):
                        ob = 4 + g
                        cp("dve", RS[64:65, :], psb(ob)[64:65, :], [pk(ob)], ["rs"])
                        recip(RS[64:65, :], RS[64:65, :], ["rs"], ["rs"])
                        mm(psb(6)[0:64, :], C("ones")[64:65, 0:64], RS[64:65, :], True, True, ["rs", "cst"], [pk(6)])
                        cp("dve", STG[0][0:64, :], psb(ob)[0:64, :], [pk(ob)], ["stg0"])
                        tt("dve", OB[:, :], STG[0][0:64, :], psb(6)[0:64, :], ALU.mult, ["stg0", pk(6)], ["ob"])
                        dma("sp", OA[s.name][4 * g:4 * g + 4, :, q0:q0 + 128].rearrange("j d t -> d j t"),
                            OB[:, :].rearrange("p (j t) -> p j t", j=4), ["ob"], [("oa", s.name)])
        P.fence()
        if stop_after == "attn":
            return finish()

        H8 = 8
        CUT = 0

        def v3(t):
            return t.rearrange("p (h j) -> p h j", h=H8)

        woff = [0]

        def wtake(n, f32=False):
            ap = WAR[:, woff[0]:woff[0] + n]
            woff[0] += n
            return ap.bitcast(F32) if f32 else ap

        DS = [dict(SM=SM, DG1=DG1, DG3=DG3, DE=DE, E1=E1, E2=E2, E3=E3, PA=PA, PT=PT, RT=RT, XB=XB, BEK=BEK, BV=BV), None]
        DS[1] = dict(E1=wtake(1024, True), E2=wtake(1024, True), E3=wtake(1024, True), DG1=wtake(1024, True),
                     DG3=wtake(1024, True), SM=wtake(320, True),
                     PA=[wtake(512), wtake(512)], PT=[wtake(512), wtake(512)], RT=[wtake(512), wtake(512)],
                     XB=[wtake(512) for _ in range(4)], BEK=wtake(512), BV=wtake(512), DE=wtake(512))
        def w64(n):
            ap = WAR[0:64, woff[0]:woff[0] + n]
            woff[0] += n
            return ap.rearrange("p (x i) -> p x i", x=16)

        NWT2 = [[NWT[d][:, :, :], w64(1024)] for d in range(2)]
        QDEC2 = [[QDEC[d][:, :, :], w64(1024)] for d in range(2)]
        U02 = [[U0[d], wtake(1024, True)] for d in range(2)]
        QKM2 = [[QKM[d], wtake(512)] for d in range(2)]
        KDEC2 = [[KDEC[d], wtake(512)] for d in range(2)]
        EGL22 = [[EGL2[d][:, :], wtake(32, True)] for d in range(2)]
        ab = [0]
        fbk = [0]
        ppr = [0]

        def abank():
            ab[0] = (ab[0] + 1) % 6
            return ab[0]

        def fbank():
            fbk[0] ^= 1
            return 6 + fbk[0]

        def ppair():
            ppr[0] = (ppr[0] + 1) % 3
            return ppr[0]

        idb = bcast(C("identst", True), 1, H8)
        ist = bcast(C("identst"), 1, H8)

        def msk(name):
            return bcast(C(name, True), 1, H8)

        def prep(s, d, m, par):
            NWTp, QDECp, U0p, QKMp, KDECp, EGL2p = NWT2[d][par], QDEC2[d][par], U02[d][par], QKM2[d][par], KDEC2[d][par], EGL22[d][par]
            kq = (d, par)
            T_ = DS[d]
            SMd, DG1d, DG3d, DEd = T_["SM"], T_["DG1"], T_["DG3"], T_["DE"]
            E1d, E2d, E3d = T_["E1"], T_["E2"], T_["E3"]
            PAd, PTd, RTd, XBd, BEKd, BVd = T_["PA"], T_["PT"], T_["RT"], T_["XB"], T_["BEK"], T_["BV"]

            def K(name, *x):
                return (name, d) + tuple(x)

            gti = s.tile0 + m
            sfx = "f" if d == 0 else "b"
            rows = slice(m * 128, (m + 1) * 128)
            dma("sp", KTC[d][:, :, :], KDT[s.name][:, :, rows].rearrange("h d t -> d h t"), [("qkdt", s.name)], [("ktc", d)])
            dma("sp", QTC[d][:, :, :], QDT[s.name][:, :, rows].rearrange("h d t -> d h t"), [("qkdt", s.name)], [("qtc", d)])
            dma("sp", KTMC[d][:, :], KTM[s.name][rows, :], [("tm", s.name)], [("ktmc", d)])
            dma("sp", QTMC[d][:, :], QTM[s.name][rows, :], [("tm", s.name)], [("qtmc", d)])
            dma("sp", VTMC[d][:, :], VTM[s.name][rows, :], [("tm", s.name)], [("vtmc", d)])
            la = LA[:, gti, 8 * d:8 * d + 8]
            lb = LB[:, gti, 8 * d:8 * d + 8]
            be_ = BETA[:, gti, 8 * d:8 * d + 8]
            bg = fbank()
            mm(psb(bg)[:, 0:8], C("tri_" + sfx), la, True, True, ["la", "cst"], [pk(bg)])
            mm(psb(bg)[:, 8:16], C("half0"), la, True, True, ["la", "cst"], [pk(bg)])
            mm(psb(bg)[:, 16:24], C("half1"), la, True, True, ["la", "cst"], [pk(bg)])
            cp("dve", SMd[:, 0:24], psb(bg)[:, 0:24], [pk(bg)], [K("sm")])
            yield
            tt("dve", SMd[:, 24:32], SMd[:, 0:8], lb, ALU.add, [K("sm"), "lb"], [K("sm_glb")])
            act(SMd[:, 32:40], SMd[:, 0:8], AF.Exp, [K("sm")], [K("sm_eg")])
            cp("dve", SMd[0:64, 40:48], SMd[0:64, 8:16], [K("sm")], [K("sm_glo")])
            cp("dve", SMd[64:128, 40:48], SMd[64:128, 16:24], [K("sm")], [K("sm_glo")])
            tt("dve", SMd[:, 48:56], SMd[:, 40:48], SMd[:, 0:8], ALU.subtract, [K("sm"), K("sm_glo")], [K("sm_ek")])
            act(SMd[:, 48:56], SMd[:, 48:56], AF.Exp, [K("sm_ek")], [K("sm_ek")])
            act(EGL2p, SMd[:, 8:24], AF.Exp, [K("sm")], [("egl2",) + kq])
            tt("dve", SMd[:, 56:64], be_, SMd[:, 32:40], ALU.mult, ["beta", K("sm_eg")], [K("sm_be")])
            tsc("dve", SMd[:, 64:72], SMd[:, 0:8], -1.0, None, ALU.mult, None, [K("sm")], [K("sm_ng")])
            tt("pool", v3(DG1d), ist, bcast(SMd[:, 24:32], 2, 64), ALU.mult, ["cst", K("sm_glb")], [K("dg1")])
            tt("pool", v3(DG3d), ist, bcast(SMd[:, 0:8], 2, 64), ALU.mult, ["cst", K("sm")], [K("dg3")])
            tt("dve", v3(DEd), ist, bcast(SMd[:, 32:40], 2, 64), ALU.mult, ["cst", K("sm_eg")], [K("de")])
            yield
            b1 = fbank()
            mm(psb(b1)[:, :], C("negblk"), DG3d, True, False, [K("dg3"), "cst"], [pk(b1)])
            mm(psb(b1)[:, :], C("ident"), bcast(SMd[:, 24:32], 2, 64), False, False, [K("sm_glb"), "cst"], [pk(b1)])
            mm(psb(b1)[:, :], C("ident", True), msk("m1_" + sfx), False, True, ["cstb"], [pk(b1)])
            act(E1d, psb(b1)[:, :], AF.Exp, [pk(b1)], [K("e1")])
            yield
            b2 = fbank()
            mm(psb(b2)[:, :], C("blk"), DG1d, True, False, [K("dg1"), "cst"], [pk(b2)])
            mm(psb(b2)[:, :], C("ident"), bcast(SMd[:, 64:72], 2, 64), False, False, [K("sm_ng"), "cst"], [pk(b2)])
            mm(psb(b2)[:, :], C("ident", True), msk("m2_" + sfx), False, True, ["cstb"], [pk(b2)])
            act(E2d, psb(b2)[:, :], AF.Exp, [pk(b2)], [K("e2")])
            yield
            b3 = fbank()
            mm(psb(b3)[:, :], C("blk"), DG3d, True, False, [K("dg3"), "cst"], [pk(b3)])
            mm(psb(b3)[:, :], C("ident"), bcast(SMd[:, 64:72], 2, 64), False, False, [K("sm_ng"), "cst"], [pk(b3)])
            mm(psb(b3)[:, :], C("ident", True), msk("m3_" + sfx), False, True, ["cstb"], [pk(b3)])
            act(E3d, psb(b3)[:, :], AF.Exp, [pk(b3)], [K("e3")])
            yield
            bkk, bqk = abank(), abank()
            for h in range(H8):
                for a in range(2):
                    ts_ = slice(64 * a, 64 * a + 64)
                    mm(psb(bkk)[ts_, h * 64:(h + 1) * 64], KTC[d][:, h, ts_], KTC[d][:, h, ts_], True, True,
                       [("ktc", d)], [pk(bkk)], tile_position=(0, 64 * a))
                    mm(psb(bqk)[ts_, h * 64:(h + 1) * 64], KTC[d][:, h, ts_], QTC[d][:, h, ts_], True, True,
                       [("ktc", d), ("qtc", d)], [pk(bqk)], tile_position=(0, 64 * a))
            A_, AT_ = PAd[0], PTd[0]
            kA, kAT = K("pa", 0), K("pt", 0)
            tt("dve", A_, psb(bkk)[:, :], E1d, ALU.mult, [pk(bkk), K("e1")], [kA])
            tt("dve", AT_, psb(bkk)[:, :], E2d, ALU.mult, [pk(bkk), K("e2")], [kAT])
            tt("dve", QKMp, psb(bqk)[:, :], E3d, ALU.mult, [pk(bqk), K("e3")], [("qkm",) + kq])
            yield

            def grp(L, R, lkey, rkey):
                bank = abank()
                for h in range(H8):
                    for a in range(2):
                        ts_ = slice(64 * a, 64 * a + 64)
                        hs = slice(h * 64, (h + 1) * 64)
                        mm(psb(bank)[ts_, hs], L[ts_, hs], R[ts_, hs], True, True, [lkey, rkey], [pk(bank)],
                           tile_position=(64 * a, 64 * a))
                return bank

            D_, DT_ = PAd[1], PTd[1]
            kD, kDT = K("pa", 1), K("pt", 1)
            X = [XBd[0], XBd[1], BEKd, BVd, XBd[2], XBd[3]]
            kX = [K("xb", 0), K("xb", 1), K("bek"), K("bv"), K("xb", 2), K("xb", 3)]
            kR = [K("rt", 0), K("rt", 1)]
            tt("pool", v3(D_), v3(A_), msk("mask8"), ALU.mult, [kA, "cstb"], [kD])
            tt("pool", v3(DT_), v3(AT_), msk("mask8"), ALU.mult, [kAT, "cstb"], [kDT])
            tt("dve", v3(X[2]), idb, v3(DT_), ALU.subtract, ["cstb", kDT], [kX[2]])
            yield
            g1 = grp(DT_, D_, kDT, kD)
            g2 = grp(D_, DT_, kD, kDT)
            cp("act", X[0], psb(g1)[:, :], [pk(g1)], [kX[0]])
            tt("dve", v3(RTd[1]), v3(X[0]), idb, ALU.add, [kX[0], "cstb"], [kR[1]])
            cp("dve", RTd[0], psb(g2)[:, :], [pk(g2)], [kR[0]])
            yield
            g3 = grp(RTd[0], X[0], kR[0], kX[0])
            tt("dve", v3(X[1]), v3(psb(g3)[:, :]), idb, ALU.add, [pk(g3), "cstb"], [kX[1]])
            g1 = grp(RTd[1], X[2], kR[1], kX[2])
            cp("act", X[3], psb(g1)[:, :], [pk(g1)], [kX[3]])
            yield
            g2 = grp(X[3], X[1], kX[3], kX[1])
            g3 = grp(X[1], X[3], kX[1], kX[3])
            cp("act", X[4], psb(g2)[:, :], [pk(g2)], [kX[4]])
            cp("dve", X[5], psb(g3)[:, :], [pk(g3)], [kX[5]])
            yield
            Tb, kT = [X[4], X[0]], [kX[4], kX[0]]
            Mb, kM = [X[5], X[1]], [kX[5], kX[1]]
            cur = 0
            for li, mname in enumerate(("moff8", "moff16", "moff32")):
                last = (li == 2)
                nxt = 1 - cur
                tt("pool", v3(D_), v3(A_), msk(mname), ALU.mult, [kA, "cstb"], [kD])
                if not last:
                    tt("pool", v3(DT_), v3(AT_), msk(mname), ALU.mult, [kAT, "cstb"], [kDT])
                g1 = grp(D_, Mb[cur], kD, kM[cur])
                cp("act", RTd[1], psb(g1)[:, :], [pk(g1)], [kR[1]])
                if not last:
                    g2 = grp(DT_, Tb[cur], kDT, kT[cur])
                    cp("dve", RTd[0], psb(g2)[:, :], [pk(g2)], [kR[0]])
                yield
                g3 = grp(Tb[cur], RTd[1], kT[cur], kR[1])
                tt("dve", Mb[nxt], Mb[cur], psb(g3)[:, :], ALU.subtract, [kM[cur], pk(g3)], [kM[nxt]])
                if not last:
                    g1 = grp(Mb[cur], RTd[0], kM[cur], kR[0])
                    tt("dve", Tb[nxt], Tb[cur], psb(g1)[:, :], ALU.subtract, [kT[cur], pk(g1)], [kT[nxt]])
                cur = nxt
                yield
            TTm = Mb[cur]
            tkey = kM[cur]
            tt("dve", v3(BEKd), v3(KTMC[d][:, :]), bcast(SMd[:, 56:64], 2, 64), ALU.mult, [("ktmc", d), K("sm_be")], [K("bek")])
            tt("pool", v3(KDECp), v3(KTMC[d][:, :]), bcast(SMd[:, 48:56], 2, 64), ALU.mult, [("ktmc", d), K("sm_ek")], [("kdec",) + kq])
            tt("pool", v3(BVd), v3(VTMC[d][:, :]), bcast(be_, 2, 64), ALU.mult, [("vtmc", d), "beta"], [K("bv")])
            yield
            pp_ = ppair()
            pkeys = [pk(2 * pp_), pk(2 * pp_ + 1)]
            bu0 = abank()
            while bu0 in (2 * pp_, 2 * pp_ + 1):
                bu0 = abank()
            for h in range(H8):
                for a in range(2):
                    ts_ = slice(64 * a, 64 * a + 64)
                    hs = slice(h * 64, (h + 1) * 64)
                    cs = slice((a * 8 + h) * 64, (a * 8 + h) * 64 + 64)
                    mm(PS[pp_][0:64, cs], BEKd[ts_, hs], TTm[ts_, hs], True, True, [K("bek"), tkey], pkeys,
                       tile_position=(64 * a, 0))
                    mm(psb(bu0)[ts_, hs], TTm[ts_, hs], BVd[ts_, hs], True, True, [tkey, K("bv")], [pk(bu0)],
                       tile_position=(64 * a, 64 * a))
            tsc("dve", NWTp, PS[pp_][0:64, :].rearrange("p (x i) -> p x i", x=16), -1.0, None, ALU.mult, None,
                pkeys, [("nwt",) + kq])
            cp("act", U0p, psb(bu0)[:, :], [pk(bu0)], [("u0",) + kq])
            yield
            pp_ = ppair()
            pkeys = [pk(2 * pp_), pk(2 * pp_ + 1)]
            for a in range(2):
                mm(PS[pp_][0:64, a * 512:(a + 1) * 512], C("half%d" % a, True)[:, 0:64], DEd, True, True,
                   [K("de"), "cstb"], pkeys)
            for a in range(2):
                tt("dve", QDECp[:, a * 8:(a + 1) * 8, :],
                   QTC[d][:, :, a * 64:(a + 1) * 64],
                   PS[pp_][0:64, a * 512:(a + 1) * 512].rearrange("p (h i) -> p h i", h=8), ALU.mult,
                   [("qtc", d)] + pkeys, [("qdec",) + kq])
            yield

        def steps(s, d, m, par, first_visit):
            NWTp, QDECp, U0p, QKMp, KDECp, EGL2p = NWT2[d][par], QDEC2[d][par], U02[d][par], QKM2[d][par], KDEC2[d][par], EGL22[d][par]
            kq = (d, par)
            SMd = DS[d]["SM"]

            def K(name, *x):
                return (name, d) + tuple(x)

            rows = slice(m * 128, (m + 1) * 128)
            for a in ((0, 1) if d == 0 else (1, 0)):
                ts_ = slice(64 * a, 64 * a + 64)
                pu = abank()
                for h in range(H8):
                    hs = slice(h * 64, (h + 1) * 64)
                    mm(psb(pu)[ts_, hs], NWTp[:, a * 8 + h, :], SBF[d][:, h, :], True, True,
                       [("nwt",) + kq, ("sbf", d)], [pk(pu)], tile_position=(0, 64 * a))
                tt("dve", UB[d][ts_, :], U0p[ts_, :], psb(pu)[ts_, :], ALU.add, [("u0",) + kq, pk(pu)], [("ub", d)])
                yield
                po, pob, pS_ = abank(), abank(), abank()
                for h in range(H8):
                    hs = slice(h * 64, (h + 1) * 64)
                    mm(psb(po)[ts_, hs], QDECp[:, a * 8 + h, :], SBF[d][:, h, :], True, True,
                       [("qdec",) + kq, ("sbf", d)], [pk(po)], tile_position=(0, 64 * a))
                    mm(psb(pob)[ts_, hs], QKMp[ts_, hs], UB[d][ts_, hs], True, True,
                       [("qkm",) + kq, ("ub", d)], [pk(pob)], tile_position=(64 * a, 64 * a))
                    mm(psb(pS_)[0:64, hs], KDECp[ts_, hs], UB[d][ts_, hs], True, True,
                       [("kdec",) + kq, ("ub", d)], [pk(pS_)], tile_position=(64 * a, 0))
                tt("dve", S32[d][:, :, :], S32[d][:, :, :], bcast(EGL2p[0:64, a * 8:a * 8 + 8], 2, 64), ALU.mult,
                   [("s32", d), ("egl2",) + kq], [("s32", d)])
                tt("dve", S32[d][:, :, :], S32[d][:, :, :], psb(pS_)[0:64, :].rearrange("p (h v) -> p h v", h=H8), ALU.add,
                   [("s32", d), pk(pS_)], [("s32", d)])
                cp("act", SBF[d][:, :, :], S32[d][:, :, :], [("s32", d)], [("sbf", d)])
                cp("act", OACC[d][ts_, :], psb(po)[ts_, :], [pk(po)], [("oacc", d)])
                tt("dve", OACC[d][ts_, :], OACC[d][ts_, :], psb(pob)[ts_, :], ALU.add, [("oacc", d), pk(pob)], [("oacc", d)])
                yield
            if first_visit:
                dma("sp", OPART[s.name][rows, :], OACC[d][:, :], [("oacc", d)], [("opart", s.name, m)])
            else:
                dma("sp", STG[d][:, :], OPART[s.name][rows, :], [("opart", s.name, m)], [f"stg{d}"])
                tt("dve", OACC[d][:, :], OACC[d][:, :], STG[d][:, :], ALU.add, [("oacc", d), f"stg{d}"], [("oacc", d)])
                tt("pool", STG[2 + d][:, :], OACC[d][:, :], OACC[d][:, :], ALU.mult, [("oacc", d)], [f"stg{2 + d}"])
                P.add("dve", lambda e: e.tensor_reduce(SMd[:, 80:88], v3(STG[2 + d][:, :]), AX.X, ALU.add),
                      [f"stg{2 + d}"], [K("sm_rn")])
                act(SMd[:, 80:88], SMd[:, 80:88], AF.Sqrt, [K("sm_rn")], [K("sm_rn")], bias=EPS, scale=1.0 / 64)
                recip(SMd[:, 80:88], SMd[:, 80:88], [K("sm_rn")], [K("sm_rn")])
                yield
                tt("dve", v3(OACC[d][:, :]), v3(OACC[d][:, :]), bcast(SMd[:, 80:88], 2, 64), ALU.mult,
                   [("oacc", d), K("sm_rn")], [("oacc", d)])
                tt("dve", v3(OACC[d][:, :]), v3(OACC[d][:, :]), bcast(dng[:], 1, H8), ALU.mult,
                   [("oacc", d), "lvec"], [("oacc", d)])
                dma("sp", STB[d][:, :], GS[s.name][rows, :], [("gs", s.name)], [f"stb{d}"])
                tt("dve", OACC[d][:, :], OACC[d][:, :], STB[d][:, :], ALU.mult, [("oacc", d), f"stb{d}"], [("oacc", d)])
                bt = fbank()
                for cc in range(4):
                    tr(psb(bt)[:, cc * 128:(cc + 1) * 128], OACC[d][:, cc * 128:(cc + 1) * 128], C("ident"),
                       [("oacc", d), "cst"], [pk(bt)])
                cp("act", STB[2 + d][:, :], psb(bt)[:, :], [pk(bt)], [f"stb{2 + d}"])
                dma("sp", OD[s.name][:, :, rows], STB[2 + d][:, :].rearrange("p (c t) -> p c t", c=4), [f"stb{2 + d}"],
                    [("od", s.name)])

        for s in seqs:
            NP_ = s.T // 128
            for d in range(2):
                if s.is_sample:
                    dma("sp", S32[d][:, :, :], st_in[l, d].rearrange("h k v -> k h v"), [], [("s32", d)])
                else:
                    P.add("pool", lambda e, d=d: e.memset(S32[d][:, :, :], 0.0), [], [("s32", d)])
                cp("act", SBF[d][:, :, :], S32[d][:, :, :], [("s32", d)], [("sbf", d)])
            visited = set()

            def mof(d, step):
                return step if d == 0 else NP_ - 1 - step

            def rr(gens):
                active = list(gens)
                while active:
                    for g_ in list(active):
                        try:
                            next(g_)
                        except StopIteration:
                            active.remove(g_)

            rr([prep(s, d, mof(d, 0), 0) for d in range(2)])
            for step in range(NP_):
                gens = []
                for d in range(2):
                    m = mof(d, step)
                    gens.append(steps(s, d, m, step % 2, m not in visited))
                for d in range(2):
                    visited.add(mof(d, step))
                if step + 1 < NP_:
                    for d in range(2):
                        gens.append(prep(s, d, mof(d, step + 1), (step + 1) % 2))
                rr(gens)
            if not s.is_sample:
                for d in range(2):
                    dma("sp", nst_out[s.idx, l, d].rearrange("h k v -> k h v"), S32[d][:, :, :], [("s32", d)], [("nst", s.idx)])
        P.fence()
        if stop_after == "scan":
            return finish()

        Wpa = WAR[:, 0:4096].rearrange("p (k n) -> p k n", k=4)
        Wpd = WAR[:, 4096:8192].rearrange("p (k n) -> p k n", k=4)
        Wo = WAR[:, 8192:16384].rearrange("p (k n) -> p k n", k=8)
        load_w(Wpa, w_pa[l], 4)
        load_w(Wpd, w_pd[l], 4)
        load_w(Wo, w_out[l], 8)
        OAT = HB[:, 0:4, :]
        ODT = HB[:, 4:8, :]
        MG = HB[:, 8:16, :]
        for s in seqs:
            mj = s.mj
            for blk in range(s.T // TB):
                t0 = blk * TB
                for c in range(4):
                    dma("sp", OAT[:, c, :], OA[s.name][2 * c:2 * c + 2].rearrange("h d t -> (h d) t")[:, t0:t0 + TB],
                        [("oa", s.name)], ["hb"])
                dma("sp", ODT, OD[s.name][:, :, t0:t0 + TB], [("od", s.name)], ["hb"])
                dma("sp", GATB[:, :, :], GATES[s.name][:, :, t0:t0 + TB], [("gates", s.name, blk)], ["gatb"])
                dma("sp", XT, XRES[s.name][:, :, t0:t0 + TB], [("xres", s.name, blk)], ["bigA"])
                for n in range(8):
                    ns = slice(n * 128, (n + 1) * 128)
                    for k in range(4):
                        mm(psb(0)[:, :TB], Wpa[:, k, ns], OAT[:, k, :], k == 0, k == 3, ["war", "hb"], [pk(0)])
                    for k in range(4):
                        mm(psb(1)[:, :TB], Wpd[:, k, ns], ODT[:, k, :], k == 0, k == 3, ["war", "hb"], [pk(1)])
                    tt("dve", STG[0][:, :TB], psb(0)[:, :TB], GATB[:, n, :], ALU.mult, [pk(0), "gatb"], ["stg0"])
                    tt("dve", STG[1][:, :TB], psb(1)[:, :TB], GATB[:, 8 + n, :], ALU.mult, [pk(1), "gatb"], ["stg1"])
                    tt("pool", MG[:, n, :], STG[0][:, :TB], STG[1][:, :TB], ALU.add, ["stg0", "stg1"], ["ht"])
                for n in range(8):
                    ns = slice(n * 128, (n + 1) * 128)
                    bk = 2 + n % 2
                    for k in range(8):
                        mm(psb(bk)[:, :TB], Wo[:, k, ns], MG[:, k, :], k == 0, k == 7, ["war", "ht"], [pk(bk)])
                    stt("dve", X1T[:, n, :], psb(bk)[:, :TB], MOD[:, 16 + n, mj:mj + 1], XT[:, n, :], ALU.mult, ALU.add,
                        [pk(bk), "bigA"] + MK, ["bigB"])
                dma("sp", XRES[s.name][:, :, t0:t0 + TB], X1T, ["bigB"], [("xres", s.name, blk)])
                rms_stats(X1T, TB, "bigB")
                tt("dve", XT, X1T, bcast(RSTD[:, :], 1, 8), ALU.mult, ["bigB", "rstd"], ["bigA"])
                for c in range(8):
                    if c % 2 == 0:
                        act(HT[:, c, :], XT[:, c, :], AF.Identity, ["bigA"] + MK, ["ht"],
                            scale=A2[:, c, mj:mj + 1], bias=MOD[:, 24 + c, mj:mj + 1])
                    else:
                        tsc("dve", HT[:, c, :], XT[:, c, :], A2[:, c, mj:mj + 1], MOD[:, 24 + c, mj:mj + 1], ALU.mult, ALU.add,
                            ["bigA"] + MK, ["ht"])
                dma("sp", H2[s.name][:, :, t0:t0 + TB], HT, ["ht"], [("h2", s.name, blk)])
        P.fence()
        if stop_after == "stageD":
            return finish()

        W1h = WAR[:, 0:8 * 2048].rearrange("p (k n) -> p k n", k=8)
        W2h = WAR[:, 16384:16384 + 16 * 1024].rearrange("p (k n) -> p k n", k=16)
        for hf in range(2):
            load_w(W1h, w1[l][:, hf * 2048:(hf + 1) * 2048], 8)
            load_w(W2h, w2[l][hf * 2048:(hf + 1) * 2048, :], 16)
            jobs = [(s, blk) for s in seqs for blk in range(s.T // TB)]
            XTs = [(XT, "bigA"), (X1T, "bigB")]
            H2Ts = [(HB[:, 0:8, :], "hb"), (HB[:, 8:16, :], "ht")]

            def ef_loads(j):
                s, blk = jobs[j]
                t0 = blk * TB
                h2t, hkey = H2Ts[j % 2]
                xt, xkey = XTs[j % 2]
                dma("sp", h2t, H2[s.name][:, :, t0:t0 + TB], [("h2", s.name, blk)], [hkey])
                dma("sp", xt, XRES[s.name][:, :, t0:t0 + TB], [("xres", s.name, blk)], [xkey])

            ef_loads(0)
            for j, (s, blk) in enumerate(jobs):
                mj = s.mj
                t0 = blk * TB
                H2T, hkey = H2Ts[j % 2]
                XTj, xkey = XTs[j % 2]
                if j + 1 < len(jobs):
                    ef_loads(j + 1)
                for f in range(16):
                    bk = 1 + f % 3
                    for k in range(8):
                        mm(psb(bk)[:, :TB], W1h[:, k, f * 128:(f + 1) * 128], H2T[:, k, :], k == 0, k == 7, ["war", hkey], [pk(bk)])
                    i = nstg()
                    act(STG[i][:, :TB], psb(bk)[:, :TB], AF.Relu, [pk(bk), "lvec"], [f"stg{i}"],
                        bias=b1f[:, hf * 16 + f:hf * 16 + f + 1])
                    tt("dve" if f % 2 == 0 else "pool", GATB[:, f, :], STG[i][:, :TB], STG[i][:, :TB], ALU.mult,
                       [f"stg{i}"], ["gatb"])
                for n in range(8):
                    bk = 4 + n % 2
                    for f in range(16):
                        mm(psb(bk)[:, :TB], W2h[:, f, n * 128:(n + 1) * 128], GATB[:, f, :], f == 0, f == 15, ["war", "gatb"], [pk(bk)])
                    stt("dve", XTj[:, n, :], psb(bk)[:, :TB], MOD[:, 40 + n, mj:mj + 1], XTj[:, n, :], ALU.mult, ALU.add,
                        [pk(bk), xkey] + MK, [xkey])
                    if hf == 0:
                        tsc("dve", XTj[:, n, :], XTj[:, n, :], GB2[:, n, mj:mj + 1], None, ALU.add, None, [xkey] + MK, [xkey])
                if not (l == NLAYERS - 1 and hf == 1):
                    dma("sp", XRES[s.name][:, :, t0:t0 + TB], XTj, [xkey], [("xres", s.name, blk)])
                else:
                    rms_stats(XTj, TB, xkey, sq=GATB[:, 0:8, :], sqkey="gatb")
                    tt("dve", XTj, XTj, bcast(RSTD[:, :], 1, 8), ALU.mult, [xkey, "rstd"], [xkey])
                    tt("dve", XTj, XTj, bcast(fnf[:], 2, TB), ALU.mult, [xkey, "fnf"], [xkey])
                    dst = ys_out if s.is_sample else yp_out[s.idx * TP:(s.idx + 1) * TP, :]
                    for t2 in range(TB // 128):
                        for hh in range(2):
                            bk = 6 + hh
                            for c in range(4):
                                tr(psb(bk)[:, c * 128:(c + 1) * 128], XTj[:, hh * 4 + c, t2 * 128:(t2 + 1) * 128], C("ident"),
                                   [xkey, "cst"], [pk(bk)])
                            i = nstg()
                            cp("act" if hh == 0 else "dve", STG[i][:, :], psb(bk)[:, :], [pk(bk)], [f"stg{i}"])
                            dma("sp", dst[t0 + t2 * 128:t0 + (t2 + 1) * 128, hh * 512:(hh + 1) * 512], STG[i][:, :],
                                [f"stg{i}"], [("y", s.name)])
        P.fence()
        if stop_after == f"layer{l}":
            return finish()

    return finish()


def make_in_maps(inputs):
    cos, sin = _rope_tables()
    maps = []
    for core in range(8):
        b = core % 4
        m = {
            "xs": np.ascontiguousarray(inputs["x_sample"][b]),
            "xp": np.ascontiguousarray(inputs["x_prompt"][core * NPS:(core + 1) * NPS].reshape(NPS * TP, D)),
            "cvec": np.ascontiguousarray(np.stack([inputs["c"][b], inputs["c_ctx"]], 0)),
            "ck": np.ascontiguousarray(inputs["cache_k"][b]),
            "cv": np.ascontiguousarray(inputs["cache_v"][b]),
            "st": np.ascontiguousarray(inputs["state_delta"][b]),
            "a_log": np.ascontiguousarray(inputs["a_log"].reshape(DEPTH, 16)),
            "dt_bias": np.ascontiguousarray(inputs["dt_bias"].reshape(DEPTH, 16)),
            "cst": CST, "ropecos": cos, "ropesin": sin,
        }
        for k in ["w_mod", "b_mod", "norm1", "norm2", "w_in", "conv_w", "q_gain", "k_gain", "dn_gain",
                  "w_pa", "w_pd", "w_out", "w1", "b1", "w2", "b2", "final_norm"]:
            m[k] = np.ascontiguousarray(inputs[k])
        maps.append(m)
    return maps


_NC_CACHE = {}


def kernel(**inputs):
    inputs = {k: np.asarray(v) for k, v in inputs.items()}
    if "nc" not in _NC_CACHE:
        _NC_CACHE["nc"] = build_program()
    nc = _NC_CACHE["nc"]
    maps = make_in_maps(inputs)
    res = run_bass_kernel_spmd(nc, maps, core_ids=list(range(8)))
    r = res.results
    y_sample = np.stack([r[b]["ys"] for b in range(4)], 0)
    y_prompt = np.concatenate([r[c]["yp"].reshape(NPS, TP, D) for c in range(8)], 0)
    nk = np.concatenate([r[c]["nk"] for c in range(8)], 0)
    nv = np.concatenate([r[c]["nv"] for c in range(8)], 0)
    nst = np.concatenate([r[c]["nst"] for c in range(8)], 0)
    return (y_prompt.astype(np.float32), y_sample.astype(np.float32), nk.astype(np.float32),
            nv.astype(np.float32), nst.astype(np.float32))
```

```python
import numpy as np
from contextlib import ExitStack
import concourse.bass as bass
import concourse.mybir as mybir
from concourse.bass_utils import run_bass_kernel_spmd

F32 = mybir.dt.float32
BF16 = mybir.dt.bfloat16
AF = mybir.ActivationFunctionType
ALU = mybir.AluOpType
AX = mybir.AxisListType

D = 1024
DEPTH = 2
TS = 4096
TP = 256
NPS = 4
NCTX = 256
INW = 4896
DFF = 4096
EPS = 1e-6
NEG = -30000.0

C_QA, C_KA, C_VA, C_QD, C_KD, C_VD, C_GO, C_AI, C_BI, C_GA, C_GD = 0, 512, 640, 768, 1280, 1792, 2304, 2816, 2832, 2848, 3872


ENGS = ["pe", "act", "dve", "pool", "sp"]
N_DMA_SEMS = {"sp": 40, "pool": 16, "act": 8}
EPOCH = 30000


class Ev:
    __slots__ = ("dma", "eng", "idx", "sem", "val")

    def __init__(self, dma, eng, idx, sem=None, val=None):
        self.dma, self.eng, self.idx, self.sem, self.val = dma, eng, idx, sem, val


class Rec:
    __slots__ = ("eng", "fn", "waits", "signal", "idx", "dma", "sig_sem", "sig_val")

    def __init__(self, eng, fn):
        self.eng, self.fn = eng, fn
        self.waits = []
        self.signal = False
        self.dma = None


class Buf:
    __slots__ = ("w", "r")

    def __init__(self):
        self.w = None
        self.r = []


class Prog:
    def __init__(self, nc):
        self.nc = nc
        self.ops = {e: [] for e in ENGS}
        self.waited = {e: {p: -1 for p in ENGS} for e in ENGS}
        self.waited_dma = {e: {} for e in ENGS}
        self.bufs = {}
        self.dma_count = {q: 0 for q in N_DMA_SEMS}

    def buf(self, k):
        b = self.bufs.get(k)
        if b is None:
            b = self.bufs[k] = Buf()
        return b

    def add(self, eng, fn, reads=(), writes=(), dma=False):
        rec = Rec(eng, fn)
        rec.idx = len(self.ops[eng])
        deps = []
        for k in reads:
            b = self.buf(k)
            if b.w is not None:
                deps.append((b.w, True))
        for k in writes:
            b = self.buf(k)
            if b.w is not None:
                deps.append((b.w, False))
            for r in b.r:
                deps.append((r, False))
        if dma:
            d = self.dma_count[eng]
            n = N_DMA_SEMS[eng]
            si, val = d % n, 16 * (d // n + 1)
            if d >= n:
                deps.append((Ev(True, eng, None, si, val - 16), True))
            rec.dma = (si, val)
            ev = Ev(True, eng, rec.idx, si, val)
            self.dma_count[eng] += 1
        else:
            ev = Ev(False, eng, rec.idx)
        for dep, raw in deps:
            if dep.dma:
                key = (dep.eng, dep.sem)
                if self.waited_dma[eng].get(key, 0) >= dep.val:
                    continue
                self.waited_dma[eng][key] = dep.val
                rec.waits.append(dep)
            else:
                if dep.eng == eng and eng == "pe":
                    continue
                if self.waited[eng][dep.eng] >= dep.idx:
                    continue
                self.waited[eng][dep.eng] = dep.idx
                rec.waits.append(dep)
                self.ops[dep.eng][dep.idx].signal = True
        for k in reads:
            self.buf(k).r.append(ev)
        for k in writes:
            b = self.buf(k)
            b.w = ev
            b.r = []
        self.ops[eng].append(rec)
        return rec

    def fence(self):
        last = {e: len(self.ops[e]) - 1 for e in ["pe", "act", "dve", "pool"]}
        dma_evs = []
        for q, n in N_DMA_SEMS.items():
            d = self.dma_count[q]
            for i in range(min(n, d)):
                uses = (d - i + n - 1) // n
                dma_evs.append(Ev(True, q, None, i, 16 * uses))
        for e in ENGS:
            rec = Rec(e, lambda eng: eng.nop())
            rec.idx = len(self.ops[e])
            for p, li in last.items():
                if p == e or li < 0:
                    continue
                j = li
                while j >= 0 and self.ops[p][j].dma is not None:
                    j -= 1
                if j < 0 or self.waited[e][p] >= j:
                    continue
                self.waited[e][p] = j
                rec.waits.append(Ev(False, p, j))
                self.ops[p][j].signal = True
            for dep in dma_evs:
                key = (dep.eng, dep.sem)
                if self.waited_dma[e].get(key, 0) >= dep.val:
                    continue
                self.waited_dma[e][key] = dep.val
                rec.waits.append(dep)
            self.ops[e].append(rec)

    def emit(self, es):
        nc = self.nc
        comp_sems = {}
        for e in ["pe", "act", "dve", "pool"]:
            cnt = 0
            for rec in self.ops[e]:
                if rec.signal and rec.dma is None:
                    ep = cnt // EPOCH
                    if (e, ep) not in comp_sems:
                        comp_sems[(e, ep)] = es.enter_context(nc.semaphore(f"c_{e}_{ep}"))
                    rec.sig_sem = comp_sems[(e, ep)]
                    rec.sig_val = cnt % EPOCH + 1
                    cnt += 1
        dma_sems = {}
        for q, n in N_DMA_SEMS.items():
            for i in range(min(n, self.dma_count[q])):
                dma_sems[(q, i)] = es.enter_context(nc.semaphore(f"d_{q}_{i}"))
        final_waits = []
        for q, n in N_DMA_SEMS.items():
            d = self.dma_count[q]
            for i in range(min(n, d)):
                uses = (d - i + n - 1) // n
                final_waits.append((dma_sems[(q, i)], 16 * uses))
        block = es.enter_context(nc.Block())
        ops = self.ops

        def run(engname, eng):
            for rec in ops[engname]:
                for dep in rec.waits:
                    if dep.dma:
                        eng.wait_ge(dma_sems[(dep.eng, dep.sem)], dep.val)
                    else:
                        prod = ops[dep.eng][dep.idx]
                        eng.wait_ge(prod.sig_sem, prod.sig_val)
                ins = rec.fn(eng)
                if rec.dma is not None:
                    ins.then_inc(dma_sems[(engname, rec.dma[0])], 16)
                elif rec.signal:
                    ins.then_inc(rec.sig_sem, 1)
            if engname == "sp":
                for s, v in final_waits:
                    eng.wait_ge(s, v)

        @block.tensor
        def _(t):
            run("pe", t)

        @block.scalar
        def _(a):
            run("act", a)

        @block.vector
        def _(v):
            run("dve", v)

        @block.gpsimd
        def _(g):
            run("pool", g)

        @block.sync
        def _(s):
            run("sp", s)


CST_LAYOUT = {}


def _build_consts():
    p = np.arange(128)
    cols = []
    off = 0

    def put(name, arr):
        nonlocal off
        arr = np.asarray(arr, np.float32).reshape(128, -1)
        CST_LAYOUT[name] = (off, arr.shape[1])
        cols.append(arr)
        off += arr.shape[1]

    put("ident", np.eye(128))
    put("ones", np.ones((128, 128)))
    put("negones", -np.ones((128, 128)))
    half = p // 64
    put("blk", (half[:, None] == half[None, :]).astype(np.float32))
    put("negblk", -(half[:, None] == half[None, :]).astype(np.float32))
    put("identst", (p[:, None] % 64 == np.arange(64)[None, :]).astype(np.float32))
    same = half[:, None] == half[None, :]
    put("tri_f", (same & (p[:, None] <= p[None, :])).astype(np.float32))
    put("tri_b", (same & (p[:, None] >= p[None, :])).astype(np.float32))
    put("half0", np.repeat((p < 64).astype(np.float32)[:, None], 128, 1))
    put("half1", np.repeat((p >= 64).astype(np.float32)[:, None], 128, 1))
    il = p % 64
    j = np.arange(64)

    def m(keep):
        return np.where(keep, 0.0, NEG).astype(np.float32)

    put("m1_f", m(il[:, None] > j[None, :]))
    put("m1_b", m(il[:, None] < j[None, :]))
    put("m2_f", m(j[None, :] > il[:, None]))
    put("m2_b", m(j[None, :] < il[:, None]))
    put("m3_f", m(j[None, :] >= il[:, None]))
    put("m3_b", m(j[None, :] <= il[:, None]))
    put("mask8", ((il[:, None] // 8) == (j[None, :] // 8)).astype(np.float32))
    for sz in (8, 16, 32):
        put("moff%d" % sz, (((il[:, None] // (2 * sz)) == (j[None, :] // (2 * sz)))
                            & ((il[:, None] // sz) != (j[None, :] // sz))).astype(np.float32))
    R = np.zeros((128, 128), np.float32)
    for q in range(128):
        if q % 64 < 32:
            R[q, q + 32] = -1.0
        else:
            R[q, q - 32] = 1.0
    put("rot", R.T)
    return np.concatenate(cols, 1)


CST = _build_consts()
NCST = CST.shape[1]


def _rope_tables():
    t = np.arange(TS)
    row = (t // 64).astype(np.float32)
    col = (t % 64).astype(np.float32)
    inv = (10000.0 ** (-np.arange(16, dtype=np.float32) / 16)).astype(np.float32)
    ang = np.concatenate([row[:, None] * inv, col[:, None] * inv], -1).astype(np.float32)
    cos = np.cos(ang).astype(np.float32).T
    sin = np.sin(ang).astype(np.float32).T
    return np.tile(cos, (4, 1)).copy(), np.tile(sin, (4, 1)).copy()


class Seq:
    def __init__(self, name, T, is_sample, idx, key0, tile0):
        self.name, self.T, self.is_sample, self.idx = name, T, is_sample, idx
        self.key0 = key0
        self.tile0 = tile0
        self.nctx = NCTX if is_sample else 0
        self.mj = 0 if is_sample else 1


def bcast(ap, axis, n):
    shp = list(ap.shape)
    shp.insert(axis, n)
    return ap.unsqueeze(axis).broadcast_to(shp)


def build_program(debug_outs=(), stop_after=None):
    nc = bass.Bass("TRN2", target_bir_lowering=False)
    es = ExitStack()
    P = Prog(nc)
    dbg = set(debug_outs)

    def din(name, shape, dt=F32):
        return nc.dram_tensor(name, list(shape), dt, kind="ExternalInput").ap()

    def dout(name, shape, dt=F32):
        return nc.dram_tensor(name, list(shape), dt, kind="ExternalOutput").ap()

    def dscr(name, shape, dt=F32):
        kind = "ExternalOutput" if name in dbg else "Internal"
        return nc.dram_tensor(name, list(shape), dt, kind=kind).ap()

    xs_in = din("xs", [TS, D])
    xp_in = din("xp", [NPS * TP, D])
    cvec = din("cvec", [2, D])
    ck_in = din("ck", [DEPTH, 2, NCTX, 64])
    cv_in = din("cv", [DEPTH, 2, NCTX, 64])
    st_in = din("st", [DEPTH, 2, 8, 64, 64])
    w_mod = din("w_mod", [DEPTH, D, 6 * D])
    b_mod = din("b_mod", [DEPTH, 6 * D])
    norm1 = din("norm1", [DEPTH, D])
    norm2 = din("norm2", [DEPTH, D])
    w_in = din("w_in", [DEPTH, D, INW])
    conv_w = din("conv_w", [DEPTH, 3, 1536])
    q_gain = din("q_gain", [DEPTH, 64])
    k_gain = din("k_gain", [DEPTH, 64])
    a_log = din("a_log", [DEPTH, 16])
    dt_bias = din("dt_bias", [DEPTH, 16])
    dn_gain = din("dn_gain", [DEPTH, 64])
    w_pa = din("w_pa", [DEPTH, 512, D])
    w_pd = din("w_pd", [DEPTH, 512, D])
    w_out = din("w_out", [DEPTH, D, D])
    w1 = din("w1", [DEPTH, D, DFF])
    b1 = din("b1", [DEPTH, DFF])
    w2 = din("w2", [DEPTH, DFF, D])
    b2 = din("b2", [DEPTH, D])
    final_norm = din("final_norm", [D])
    cst_in = din("cst", [128, NCST])
    cos_in = din("ropecos", [128, TS])
    sin_in = din("ropesin", [128, TS])
    ys_out = dout("ys", [TS, D])
    yp_out = dout("yp", [NPS * TP, D])
    nk_out = dout("nk", [NPS, DEPTH, 2, TP, 64])
    nv_out = dout("nv", [NPS, DEPTH, 2, TP, 64])
    nst_out = dout("nst", [NPS, DEPTH, 2, 8, 64, 64])

    seqs = [Seq("s", TS, True, 0, 0, 0)]
    for i in range(NPS):
        seqs.append(Seq(f"p{i}", TP, False, i, NCTX + TS + i * TP, TS // 128 + i * (TP // 128)))
    NKEY = NCTX + TS + NPS * TP
    NTILE = TS // 128 + NPS * TP // 128
    NVT = NKEY // 128

    XRES, PRE, QA, GATES, GS, QDT, KDT, QTM, KTM, VTM, OPART, OA, OD, X1, H2, ACTS = ({} for _ in range(16))
    for s in seqs:
        T = s.T
        XRES[s.name] = dscr(f"xres_{s.name}", [128, 8, T])
        PRE[s.name] = dscr(f"pre_{s.name}", [128, 12, T + 2])
        QA[s.name] = dscr(f"qa_{s.name}", [8, 64, T], BF16)
        GATES[s.name] = dscr(f"gates_{s.name}", [128, 16, T], BF16)
        GS[s.name] = dscr(f"gs_{s.name}", [T, 512], BF16)
        QDT[s.name] = dscr(f"qdt_{s.name}", [8, 64, T], BF16)
        KDT[s.name] = dscr(f"kdt_{s.name}", [8, 64, T], BF16)
        QTM[s.name] = dscr(f"qtm_{s.name}", [T, 512], BF16)
        KTM[s.name] = dscr(f"ktm_{s.name}", [T, 512], BF16)
        VTM[s.name] = dscr(f"vtm_{s.name}", [T, 512], BF16)
        OPART[s.name] = dscr(f"opart_{s.name}", [T, 512])
        OA[s.name] = dscr(f"oa_{s.name}", [8, 64, T], BF16)
        OD[s.name] = dscr(f"od_{s.name}", [128, 4, T], BF16)
        H2[s.name] = dscr(f"h2_{s.name}", [128, 8, T], BF16)

    def sb(name, shape, dt=F32):
        return es.enter_context(nc.sbuf_tensor("sb_" + name, list(shape), dt))

    cst = sb("cst", [128, NCST])
    cstb = sb("cstb", [128, NCST], BF16)

    def C(name, bf=False):
        o, n = CST_LAYOUT[name]
        return (cstb if bf else cst)[:, o:o + n]

    WAR = sb("warena", [128, 40960], BF16)
    KT_all = sb("kt_all", [128, NKEY], BF16)
    VA_all = sb("va_all", [128, NVT, 2, 65], BF16)
    LA = sb("la", [128, NTILE, 16])
    LB = sb("lb", [128, NTILE, 16])
    BETA = sb("beta", [128, NTILE, 16])
    MOD = sb("mod", [128, 48, 2])
    A1 = sb("a1", [128, 8, 2])
    A2 = sb("a2", [128, 8, 2])
    GB2 = sb("gb2", [128, 8, 2])
    n1f = sb("n1f", [128, 8])
    n2f = sb("n2f", [128, 8])
    fnf = sb("fnf", [128, 8])
    b2f = sb("b2f", [128, 8])
    b1f = sb("b1f", [128, 32])
    bmf = sb("bmf", [128, 48])
    cfm = sb("cfm", [128, 8, 2])
    scb = sb("scb", [128, 8, 2], BF16)
    qg = sb("qg", [128, 1])
    kg = sb("kg", [128, 1])
    cw = sb("cw", [128, 3, 12])
    dtb = sb("dtb", [128, 16])
    negA = sb("negA", [128, 16])
    dng = sb("dng", [128, 64])
    TB = 256
    BIGA = sb("bigA", [128, 12 * 258])
    BIGB = sb("bigB", [128, 12 * 256])
    HB = sb("hb16", [128, 16, 256], BF16)
    GATB = sb("gatb", [128, 16, 256], BF16)
    R1 = sb("r1", [128, 256])
    RSTD = sb("rstd", [128, 256])
    STG = [sb(f"stg{i}", [128, 512]) for i in range(4)]
    STB = [sb(f"stb{i}", [128, 512], BF16) for i in range(4)]
    COSB = sb("cosb", [128, 256])
    SINB = sb("sinb", [128, 256])
    SMALL = sb("small", [128, 64])
    SM = sb("sm", [128, 160])
    VAB = sb("vab", [128, 160])
    KTC = [sb(f"ktc{i}", [64, 8, 128], BF16) for i in range(2)]
    QTC = [sb(f"qtc{i}", [64, 8, 128], BF16) for i in range(2)]
    KTMC = [sb(f"ktmc{i}", [128, 512], BF16) for i in range(2)]
    QTMC = [sb(f"qtmc{i}", [128, 512], BF16) for i in range(2)]
    VTMC = [sb(f"vtmc{i}", [128, 512], BF16) for i in range(2)]
    def v512(big, i, p0=0, p1=128):
        return big[p0:p1, i * 512:(i + 1) * 512]

    E1, E2, E3, DG1, DG3 = (v512(BIGA, i) for i in range(5))
    U0 = [v512(BIGA, 5), v512(BIGB, 0)]
    OACC = [v512(BIGB, 1), v512(BIGB, 2)]
    RS = v512(BIGB, 3)
    S32 = [v512(BIGB, 4 + i, 0, 64).rearrange("p (h v) -> p h v", h=8) for i in range(2)]
    HBf = HB[:, :, :].rearrange("p a b -> p (a b)")
    GBf = GATB[:, :, :].rearrange("p a b -> p (a b)")
    PA = [v512(HBf, 0), v512(HBf, 1)]
    PT = [v512(HBf, 2), v512(HBf, 3)]
    RT = [v512(HBf, 4), v512(HBf, 5)]
    QKM = [v512(HBf, 6), v512(HBf, 7)]
    BEK = v512(GBf, 0)
    KDEC = [v512(GBf, 1), v512(GBf, 2)]
    BV = v512(GBf, 3)
    UB = [v512(GBf, 4), v512(GBf, 5)]
    DE = v512(GBf, 6)
    OB = v512(GBf, 7, 0, 64)
    XBW = [sb(f"xbw{i}", [128, 1024], BF16) for i in range(2)]
    XB = [XBW[0][:, 0:512], XBW[0][:, 512:1024], XBW[1][:, 0:512], XBW[1][:, 512:1024]]
    NWT = [sb(f"nwt{i}", [64, 16, 64], BF16) for i in range(2)]
    QDEC = [sb(f"qdec{i}", [64, 16, 64], BF16) for i in range(2)]
    SBF = [sb(f"sbf{i}", [64, 8, 64], BF16) for i in range(2)]
    EGL2 = [sb(f"egl2{i}", [128, 16]) for i in range(2)]
    QB = STB[3][:, :].rearrange("p (j t) -> p j t", j=4)
    PS = [es.enter_context(nc.psum_tensor(f"ps{i}", [128, 1024], F32)) for i in range(4)]
    dbg_mod = dscr("dbg_mod", [128, 48, 2])
    dbg_la = dscr("dbg_la", [128, NTILE, 16])
    dbg_lb = dscr("dbg_lb", [128, NTILE, 16])
    dbg_beta = dscr("dbg_beta", [128, NTILE, 16])

    XT = BIGA[:, 0:8 * 256].rearrange("p (c t) -> p c t", c=8)
    X1T = BIGB[:, 0:8 * 256].rearrange("p (c t) -> p c t", c=8)
    PRET = BIGA[:, :].rearrange("p (c t) -> p c t", c=12)
    CV = BIGB[:, :].rearrange("p (c t) -> p c t", c=12)
    SQ = HB[:, 0:8, :]
    HT = HB[:, 8:16, :]
    XTM = [BIGA[:, i * 1024:(i + 1) * 1024] for i in range(2)]
    XFM = [BIGB[:, i * 1024:(i + 1) * 1024].rearrange("p (c t) -> p c t", c=8) for i in range(2)]

    def psb(i):
        return PS[i // 2][:, (i % 2) * 512:(i % 2) * 512 + 512]

    def pk(i):
        return ("ps", i)

    def dma(q, out, in_, reads, writes, **kw):
        P.add(q, lambda e: e.dma_start(out=out, in_=in_, **kw), reads, writes, dma=True)

    def mm(out, lhsT, rhs, start, stop, reads, writes, **kw):
        P.add("pe", lambda e: e.matmul(out, lhsT, rhs, start=start, stop=stop, **kw), reads, writes)

    def tr(out, in_, ident, reads, writes):
        P.add("pe", lambda e: e.transpose(out, in_, ident), reads, writes)

    def act(out, in_, func, reads, writes, **kw):
        P.add("act", lambda e: e.activation(out, in_, func, **kw), reads, writes)

    def tt(eng, out, in0, in1, op, reads, writes):
        P.add(eng, lambda e: e.tensor_tensor(out, in0, in1, op), reads, writes)

    def tsc(eng, out, in0, s1, s2, op0, op1, reads, writes):
        if op1 is None:
            P.add(eng, lambda e: e.tensor_scalar(out, in0, s1, None, op0), reads, writes)
        else:
            P.add(eng, lambda e: e.tensor_scalar(out, in0, s1, s2, op0, op1), reads, writes)

    def stt(eng, out, in0, scalar, in1, op0, op1, reads, writes):
        P.add(eng, lambda e: e.scalar_tensor_tensor(out, in0, scalar, in1, op0, op1), reads, writes)

    def cp(eng, out, in_, reads, writes):
        if eng == "act":
            P.add("act", lambda e: e.copy(out, in_), reads, writes)
        else:
            P.add(eng, lambda e: e.tensor_copy(out, in_), reads, writes)

    def recip(out, in_, reads, writes):
        P.add("dve", lambda e: e.reciprocal(out, in_), reads, writes)

    def load_w(dst3, src2, K, wkey="war"):
        for k in range(K):
            dma("pool", dst3[:, k, :], src2[k * 128:(k + 1) * 128, :], [], [wkey])

    def rms_stats(src3, nt, srckey, sq=None, sqkey="hb"):
        if sq is None:
            sq = SQ
        act(sq[:, :, :nt], src3, AF.Square, [srckey], [sqkey])
        for c in range(8):
            mm(psb(0)[:, :nt], C("ones", True), sq[:, c, :nt], c == 0, c == 7, [sqkey, "cstb"], [pk(0)])
        act(R1[:, :nt], psb(0)[:, :nt], AF.Sqrt, [pk(0)], ["r1"], bias=EPS, scale=1.0 / D)
        recip(RSTD[:, :nt], R1[:, :nt], ["r1"], ["rstd"])

    dma("sp", cst[:], cst_in, [], ["cst"])
    cp("dve", cstb[:], cst[:], ["cst"], ["cstb"])
    P.add("pool", lambda e: e.memset(VA_all[:], 1.0), [], ["va"])
    dma("sp", fnf[:], final_norm.rearrange("(k p) -> p k", p=128), [], ["fnf"], allow_slow_non_contiguous=True)
    for j in range(2):
        dma("sp", cfm[:, :, j], cvec[j].rearrange("(k p) -> p k", p=128), [], ["cfm"], allow_slow_non_contiguous=True)
    act(scb[:], cfm[:], AF.Silu, ["cfm"], ["scb"])
    P.add("pool", lambda e: e.memset(SMALL[:], 0.0), [], ["small"])
    for s in seqs:
        for col in (0, s.T + 1):
            dma("sp", PRE[s.name][:, :, col:col + 1], SMALL[:, 0:12].unsqueeze(2), ["small"], [("pre", s.name, "pad", col)],
                allow_slow_non_contiguous=True)

    it = 0
    for s in seqs:
        src = xs_in if s.is_sample else xp_in[s.idx * TP:(s.idx + 1) * TP, :]
        for ti in range(s.T // 128):
            b = it % 2
            dma("sp", XTM[b], src[ti * 128:(ti + 1) * 128, :], [], ["bigA"])
            for c in range(8):
                tr(PS[b][:, c * 128:(c + 1) * 128], XTM[b][:, c * 128:(c + 1) * 128], C("ident"),
                   ["bigA", "cst"], [pk(2 * b), pk(2 * b + 1)])
            cp("act" if it % 2 == 0 else "dve", XFM[b], PS[b][:].rearrange("p (c t) -> p c t", c=8),
               [pk(2 * b), pk(2 * b + 1)], ["bigB"])
            dma("pool", XRES[s.name][:, :, ti * 128:(ti + 1) * 128], XFM[b], ["bigB"], [("xres", s.name, ti // 2)])
            it += 1

    def finish():
        P.emit(es)
        es.close()
        return nc

    if stop_after == "stage0":
        return finish()

    NLAYERS = DEPTH
    for l in range(NLAYERS):
        for (t_, src) in ((n1f, norm1[l]), (n2f, norm2[l]), (b2f, b2[l])):
            dma("sp", t_[:], src.rearrange("(k p) -> p k", p=128), [], ["lvec"], allow_slow_non_contiguous=True)
        dma("sp", b1f[:], b1[l].rearrange("(k p) -> p k", p=128), [], ["lvec"], allow_slow_non_contiguous=True)
        dma("sp", bmf[:], b_mod[l].rearrange("(k p) -> p k", p=128), [], ["lvec"], allow_slow_non_contiguous=True)
        for hh in range(2):
            dma("sp", qg[64 * hh:64 * hh + 64, :], q_gain[l].rearrange("(d o) -> d o", o=1), [], ["lvec"], allow_slow_non_contiguous=True)
            dma("sp", kg[64 * hh:64 * hh + 64, :], k_gain[l].rearrange("(d o) -> d o", o=1), [], ["lvec"], allow_slow_non_contiguous=True)
        for j in range(3):
            dma("sp", cw[:, j, :], conv_w[l, j].rearrange("(c p) -> p c", p=128), [], ["lvec"], allow_slow_non_contiguous=True)
        dma("sp", dtb[:], dt_bias[l:l + 1, :].broadcast_to([128, 16]), [], ["lvec"], allow_slow_non_contiguous=True)
        dma("sp", negA[:], a_log[l:l + 1, :].broadcast_to([128, 16]), [], ["lvec"], allow_slow_non_contiguous=True)
        dma("sp", dng[:], dn_gain[l:l + 1, :].broadcast_to([128, 64]), [], ["lvec"], allow_slow_non_contiguous=True)
        act(negA[:], negA[:], AF.Exp, ["lvec"], ["lvec2"])
        tsc("dve", negA[:], negA[:], -1.0, None, ALU.mult, None, ["lvec2"], ["lvec2"])

        Wm = WAR[:, 0:8 * 3072].rearrange("p (k n) -> p k n", k=8)
        for hh in range(2):
            load_w(Wm, w_mod[l][:, hh * 3072:(hh + 1) * 3072], 8)
            for n in range(24):
                nn = hh * 24 + n
                for k in range(8):
                    mm(psb(0)[:, nn * 2:nn * 2 + 2], Wm[:, k, n * 128:(n + 1) * 128], scb[:, k, :], k == 0, k == 7,
                       ["war", "scb"], [pk(0)])
        tt("dve", MOD[:], psb(0)[:, 0:96].rearrange("p (n j) -> p n j", j=2), bcast(bmf[:], 2, 2), ALU.add,
           [pk(0), "lvec"], ["mod"])
        stt("dve", A1[:], MOD[:, 8:16, :], 1.0, bcast(n1f[:], 2, 2), ALU.add, ALU.mult, ["mod", "lvec"], ["mod2"])
        stt("dve", A2[:], MOD[:, 32:40, :], 1.0, bcast(n2f[:], 2, 2), ALU.add, ALU.mult, ["mod", "lvec"], ["mod2"])
        tt("dve", GB2[:], MOD[:, 40:48, :], bcast(b2f[:], 2, 2), ALU.mult, ["mod", "lvec"], ["mod2"])
        MK = ["mod", "mod2", "lvec", "lvec2"]
        if stop_after == "adaln":
            dma("sp", dbg_mod, MOD[:], ["mod"], ["dbgmod"])
            return finish()

        Win = WAR[:, 0:8 * INW].rearrange("p (k n) -> p k n", k=8)
        load_w(Win, w_in[l], 8)
        bank_rr = [0]

        def next_bank():
            bank_rr[0] = (bank_rr[0] % 4) + 1
            return bank_rr[0]

        stg_rr = [0]

        def nstg():
            stg_rr[0] = (stg_rr[0] + 1) % 4
            return stg_rr[0]

        jobsA = [(s, blk) for s in seqs for blk in range(s.T // TB)]
        XTsA = [(XT, "bigA"), (X1T, "bigB")]
        HTsA = [(HB[:, 8:16, :], "ht"), (GATB[:, 0:8, :], "gatb")]

        def blockA(j):
            s, blk = jobsA[j]
            mj = s.mj
            t0 = blk * TB
            XTj, xkey = XTsA[j % 2]
            HTj, hkey = HTsA[j % 2]
            dma("sp", XTj, XRES[s.name][:, :, t0:t0 + TB], [("xres", s.name, blk)], [xkey])
            if s.is_sample:
                dma("sp", COSB[:], cos_in[:, t0:t0 + TB], [], ["cosb"])
                dma("sp", SINB[:], sin_in[:, t0:t0 + TB], [], ["sinb"])
            rms_stats(XTj, TB, xkey)
            tt("dve", XTj, XTj, bcast(RSTD[:, :], 1, 8), ALU.mult, [xkey, "rstd"], [xkey])
            for c in range(8):
                if c % 2 == 0:
                    act(HTj[:, c, :], XTj[:, c, :], AF.Identity, [xkey] + MK, [hkey],
                        scale=A1[:, c, mj:mj + 1], bias=MOD[:, c, mj:mj + 1])
                else:
                    tsc("dve", HTj[:, c, :], XTj[:, c, :], A1[:, c, mj:mj + 1], MOD[:, c, mj:mj + 1], ALU.mult, ALU.add,
                        [xkey] + MK, [hkey])

            yield

            def fm_chunk(col0):
                bk = next_bank()
                for k in range(8):
                    mm(psb(bk)[:, :TB], Win[:, k, col0:col0 + 128], HTj[:, k, :], k == 0, k == 7, ["war", hkey], [pk(bk)])
                return bk

            def qk_epi(c, bk):
                is_k = (c == 4)
                gain = kg if is_k else qg
                cp("act", STG[0][:, :TB], psb(bk)[:, :TB], [pk(bk)], ["stg0"])
                act(STB[0][:, :TB], psb(bk)[:, :TB], AF.Square, [pk(bk)], ["stb0"])
                yield
                mm(psb(6)[:, :TB], C("blk", True), STB[0][:, :TB], True, True, ["stb0", "cstb"], [pk(6)])
                act(STG[1][:, :TB], psb(6)[:, :TB], AF.Sqrt, [pk(6)], ["stg1"], bias=EPS, scale=1.0 / 64)
                recip(STG[1][:, :TB], STG[1][:, :TB], ["stg1"], ["stg1"])
                stt("dve", STG[0][:, :TB], STG[0][:, :TB], gain[:, 0:1], STG[1][:, :TB], ALU.mult, ALU.mult,
                    ["stg0", "stg1", "lvec"], ["stg0"])
                yield
                kcol = s.key0 + s.nctx + t0
                dst = KT_all[:, kcol:kcol + TB] if is_k else STB[1][:, :TB]
                dkey = "kt" if is_k else "stb1"
                if s.is_sample:
                    mm(psb(7)[:, :TB], C("rot"), STG[0][:, :TB], True, True, ["stg0", "cst"], [pk(7)])
                    tt("dve", STG[2][:, :TB], STG[0][:, :TB], COSB[:], ALU.mult, ["stg0", "cosb"], ["stg2"])
                    yield
                    tt("dve", STG[3][:, :TB], psb(7)[:, :TB], SINB[:], ALU.mult, [pk(7), "sinb"], ["stg3"])
                    tt("dve", dst, STG[2][:, :TB], STG[3][:, :TB], ALU.add, ["stg2", "stg3"], [dkey])
                else:
                    cp("dve", dst, STG[0][:, :TB], ["stg0"], [dkey])
                if not is_k:
                    dma("pool", QA[s.name][2 * c:2 * c + 2].rearrange("h d t -> (h d) t")[:, t0:t0 + TB], STB[1][:, :TB],
                        ["stb1"], [("qa", s.name)])
                elif not s.is_sample:
                    for t2 in range(TB // 128):
                        tr(psb(7)[:, t2 * 128:(t2 + 1) * 128], STG[0][:, t2 * 128:(t2 + 1) * 128], C("ident"),
                           ["stg0", "cst"], [pk(7)])
                    cp("act", STG[2][:, :TB], psb(7)[:, :TB], [pk(7)], ["stg2"])
                    for t2 in range(TB // 128):
                        for g in range(2):
                            dma("pool", nk_out[s.idx, l, g, t0 + t2 * 128:t0 + (t2 + 1) * 128, :],
                                STG[2][:, t2 * 128 + g * 64:t2 * 128 + g * 64 + 64], ["stg2"], [("nk", s.idx)])
                yield

            hrr = [0]

            def nh():
                hrr[0] = (hrr[0] + 1) % 4
                return hrr[0]

            def filler():
                for c in range(12):
                    bk = fm_chunk(C_QD + c * 128)
                    i = nh()
                    cp("act" if c % 2 == 0 else "dve", STG[i][:, 256:512], psb(bk)[:, :TB], [pk(bk)], [f"stgh{i}"])
                    dma("pool", PRE[s.name][:, c, 1 + t0:1 + t0 + TB], STG[i][:, 256:512], [f"stgh{i}"], [("pre", s.name, blk)])
                    yield
                for c in range(16):
                    bk = fm_chunk(C_GA + c * 128)
                    i = nh()
                    act(STB[i][:, 256:512], psb(bk)[:, :TB], AF.Sigmoid, [pk(bk)], [f"stbh{i}"])
                    dma("pool", GATES[s.name][:, c, t0:t0 + TB], STB[i][:, 256:512], [f"stbh{i}"], [("gates", s.name, blk)])
                    yield

            fg = filler()
            for c in range(5):
                bk = fm_chunk(C_QA + c * 128)
                for _ in qk_epi(c, bk):
                    next(fg, None)
            for _ in fg:
                pass
            yield
            for t2 in range(TB // 128):
                tsl = slice(t2 * 128, (t2 + 1) * 128)
                gti = s.tile0 + (t0 // 128) + t2
                vt = (s.key0 + s.nctx + t0) // 128 + t2
                for k in range(8):
                    mm(psb(5)[:, 0:512], HTj[:, k, tsl], Win[:, k, C_GO:C_GO + 512], k == 0, k == 7, ["war", hkey], [pk(5)])
                for k in range(8):
                    mm(psb(6)[:, 0:128], HTj[:, k, tsl], Win[:, k, C_VA:C_VA + 128], k == 0, k == 7, ["war", hkey], [pk(6)])
                for k in range(8):
                    mm(psb(6)[:, 128:160], HTj[:, k, tsl], Win[:, k, C_AI:C_AI + 32], k == 0, k == 7, ["war", hkey], [pk(6)])
                i = nstg()
                act(STB[i][:, :], psb(5)[:, :], AF.Silu, [pk(5)], [f"stb{i}", f"stbh{i}"])
                dma("pool", GS[s.name][t0 + t2 * 128:t0 + (t2 + 1) * 128, :], STB[i][:, :], [f"stb{i}"], [("gs", s.name)])
                cp("dve", VAB[:, :], psb(6)[:, 0:160], [pk(6)], ["vab"])
                cp("dve", VA_all[:, vt, :, 0:64], VAB[:, 0:128].rearrange("p (g d) -> p g d", g=2), ["vab"], ["va"])
                if not s.is_sample:
                    for g in range(2):
                        dma("pool", nv_out[s.idx, l, g, t0 + t2 * 128:t0 + (t2 + 1) * 128, :], VAB[:, g * 64:g * 64 + 64],
                            ["vab"], [("nv", s.idx)])
                tt("dve", SM[:, 0:16], VAB[:, 128:144], dtb[:], ALU.add, ["vab", "lvec"], ["sm"])
                act(SM[:, 16:32], SM[:, 0:16], AF.Exp, ["sm"], ["sm1"])
                act(SM[:, 32:48], SM[:, 16:32], AF.Ln, ["sm1"], ["sm2"], bias=1.0)
                tt("dve", LA[:, gti, :], SM[:, 32:48], negA[:], ALU.mult, ["sm2", "lvec2"], ["la"])
                act(BETA[:, gti, :], VAB[:, 144:160], AF.Sigmoid, ["vab"], ["beta"])
                act(LB[:, gti, :], BETA[:, gti, :], AF.Ln, ["beta"], ["lb"])

        gA = [blockA(j) for j in range(len(jobsA))]
        next(gA[0])
        for j in range(len(jobsA)):
            next(gA[j])
            if j + 1 < len(jobsA):
                next(gA[j + 1])
            for _ in gA[j]:
                pass
        if "dbg_la" in dbg:
            dma("sp", dbg_la, LA[:], ["la"], ["dbgla"])
            dma("sp", dbg_lb, LB[:], ["lb"], ["dbglb"])
            dma("sp", dbg_beta, BETA[:], ["beta"], ["dbgbeta"])
        P.fence()
        if stop_after == "stageA":
            return finish()

        for s in seqs:
            for blk in range(s.T // TB):
                t0 = blk * TB
                dma("sp", PRET, PRE[s.name][:, :, t0:t0 + TB + 2],
                    [("pre", s.name, b_) for b_ in range(max(0, blk - 1), min(s.T // TB, blk + 2))]
                    + [("pre", s.name, "pad", 0), ("pre", s.name, "pad", s.T + 1)], ["bigA"])
                for c in range(12):
                    e_ = "dve"
                    tsc(e_, CV[:, c, :], PRET[:, c, 0:TB], cw[:, 0, c:c + 1], None, ALU.mult, None, ["bigA", "lvec"], [("cv", c)])
                    stt(e_, CV[:, c, :], PRET[:, c, 1:TB + 1], cw[:, 1, c:c + 1], CV[:, c, :], ALU.mult, ALU.add,
                        ["bigA", "lvec", ("cv", c)], [("cv", c)])
                    stt(e_, CV[:, c, :], PRET[:, c, 2:TB + 2], cw[:, 2, c:c + 1], CV[:, c, :], ALU.mult, ALU.add,
                        ["bigA", "lvec", ("cv", c)], [("cv", c)])
                for c in range(12):
                    act(CV[:, c, :], CV[:, c, :], AF.Silu, [("cv", c)], [("cv", c)])
                for c in range(8):
                    act(STB[0][:, :TB], CV[:, c, :], AF.Square, [("cv", c)], ["stb0"])
                    mm(psb(0)[:, :TB], C("blk", True), STB[0][:, :TB], True, True, ["stb0", "cstb"], [pk(0)])
                    act(STG[0][:, :TB], psb(0)[:, :TB], AF.Sqrt, [pk(0)], ["stg0"], bias=EPS, scale=1.0)
                    recip(STG[0][:, :TB], STG[0][:, :TB], ["stg0"], ["stg0"])
                    stt("dve", CV[:, c, :], CV[:, c, :], 0.125 if c < 4 else 1.0, STG[0][:, :TB], ALU.mult, ALU.mult,
                        [("cv", c), "stg0"], [("cv", c)])
                    i = nstg()
                    cp("act", STB[i][:, :TB], CV[:, c, :], [("cv", c)], [f"stb{i}"])
                    dstT = QDT if c < 4 else KDT
                    cc = c % 4
                    dma("pool", dstT[s.name][2 * cc:2 * cc + 2].rearrange("h d t -> (h d) t")[:, t0:t0 + TB], STB[i][:, :TB],
                        [f"stb{i}"], [("qkdt", s.name)])
                for t2 in range(TB // 128):
                    for grp, dstM in enumerate((QTM, KTM, VTM)):
                        bk = 1 + (grp % 2)
                        for cc in range(4):
                            tr(psb(bk)[:, cc * 128:(cc + 1) * 128], CV[:, grp * 4 + cc, t2 * 128:(t2 + 1) * 128], C("ident"),
                               [("cv", grp * 4 + cc), "cst"], [pk(bk)])
                        i = nstg()
                        cp("act" if grp % 2 == 0 else "dve", STB[i][:, :], psb(bk)[:, :], [pk(bk)], [f"stb{i}"])
                        dma("pool", dstM[s.name][t0 + t2 * 128:t0 + (t2 + 1) * 128, :], STB[i][:, :], [f"stb{i}"], [("tm", s.name)])
        P.fence()
        if stop_after == "stageB":
            return finish()

        for t2 in range(NCTX // 128):
            dma("sp", STG[0][:, 0:128].rearrange("p (g d) -> p g d", g=2),
                ck_in[l, :, t2 * 128:(t2 + 1) * 128, :].rearrange("g p d -> p g d"), [], ["stg0"])
            tr(psb(0)[:, 0:128], STG[0][:, 0:128], C("ident"), ["stg0", "cst"], [pk(0)])
            cp("dve", KT_all[:, t2 * 128:(t2 + 1) * 128], psb(0)[:, 0:128], [pk(0)], ["kt"])
            dma("sp", STG[1][:, 0:128].rearrange("p (g d) -> p g d", g=2),
                cv_in[l, :, t2 * 128:(t2 + 1) * 128, :].rearrange("g p d -> p g d"), [], ["stg1"])
            cp("dve", VA_all[:, t2, :, 0:64], STG[1][:, 0:128].rearrange("p (g d) -> p g d", g=2), ["stg1"], ["va"])
        QBs = [STB[3][:, :].rearrange("p (j t) -> p j t", j=4), STB[2][:, :].rearrange("p (j t) -> p j t", j=4)]
        items = []
        qcount = 0
        for s in seqs:
            ktiles = []
            if s.is_sample:
                ktiles += list(range(NCTX // 128))
            ktiles += [(s.key0 + s.nctx) // 128 + i for i in range(s.T // 128)]
            for qi in range(s.T // 128):
                for n_, kt in enumerate(ktiles):
                    items.append((s, qi, n_, kt, n_ == 0, n_ == len(ktiles) - 1, qcount % 2))
                qcount += 1
        LAG = 1
        for idx in range(len(items) + LAG):
            if idx < len(items):
                s, qi, n_, kt, first, last, qp = items[idx]
                q0 = qi * 128
                if first:
                    for g2 in range(2):
                        dma("sp", QBs[qp][64 * g2:64 * g2 + 64, :, :],
                            QA[s.name][4 * g2:4 * g2 + 4, :, q0:q0 + 128].rearrange("j d t -> d j t"),
                            [("qa", s.name)], [("qb", qp)])
                r_ = idx % 2
                for g in range(2):
                    mm(PS[r_][:, g * 512:(g + 1) * 512], KT_all[64 * g:64 * g + 64, kt * 128:(kt + 1) * 128],
                       QBs[qp][64 * g:64 * g + 64, :, :].rearrange("p j t -> p (j t)"), True, True, ["kt", ("qb", qp)],
                       [pk(2 * r_), pk(2 * r_ + 1)])
                act(XBW[r_][:, :], PS[r_][:, :], AF.Exp, [pk(2 * r_), pk(2 * r_ + 1)], [("ptt", r_)], scale=0.125)
            if idx >= LAG:
                s, qi, n_, kt, first, last, qp = items[idx - LAG]
                q0 = qi * 128
                r_ = (idx - LAG) % 2
                for g in range(2):
                    ob = 4 + g
                    mm(psb(ob)[0:65, :], VA_all[:, kt, g, :], XBW[r_][:, g * 512:(g + 1) * 512], first, last,
                       ["va", ("ptt", r_)], [pk(ob)])
                if last:
                    for g in range(2):
                        ob = 4 + g
                        cp("dve", RS[64:65, :], psb(ob)[64:65, :], [pk(ob)], ["rs"])
                        recip(RS[64:65, :], RS[64:65, :], ["rs"], ["rs"])
                        mm(psb(6)[0:64, :], C("ones")[64:65, 0:64], RS[64:65, :], True, True, ["rs", "cst"], [pk(6)])
                        cp("dve", STG[0][0:64, :], psb(ob)[0:64, :], [pk(ob)], ["stg0"])
                        tt("dve", OB[:, :], STG[0][0:64, :], psb(6)[0:64, :], ALU.mult, ["stg0", pk(6)], ["ob"])
                        dma("sp", OA[s.name][4 * g:4 * g + 4, :, q0:q0 + 128].rearrange("j d t -> d j t"),
                            OB[:, :].rearrange("p (j t) -> p j t", j=4), ["ob"], [("oa", s.name)])
        P.fence()
        if stop_after == "attn":
            return finish()

        H8 = 8
        CUT = 0

        def v3(t):
            return t.rearrange("p (h j) -> p h j", h=H8)

        woff = [0]

        def wtake(n, f32=False):
            ap = WAR[:, woff[0]:woff[0] + n]
            woff[0] += n
            return ap.bitcast(F32) if f32 else ap

        DS = [dict(SM=SM, DG1=DG1, DG3=DG3, DE=DE, E1=E1, E2=E2, E3=E3, PA=PA, PT=PT, RT=RT, XB=XB, BEK=BEK, BV=BV), None]
        DS[1] = dict(E1=wtake(1024, True), E2=wtake(1024, True), E3=wtake(1024, True), DG1=wtake(1024, True),
                     DG3=wtake(1024, True), SM=wtake(320, True),
                     PA=[wtake(512), wtake(512)], PT=[wtake(512), wtake(512)], RT=[wtake(512), wtake(512)],
                     XB=[wtake(512) for _ in range(4)], BEK=wtake(512), BV=wtake(512), DE=wtake(512))
        def w64(n):
            ap = WAR[0:64, woff[0]:woff[0] + n]
            woff[0] += n
            return ap.rearrange("p (x i) -> p x i", x=16)

        NWT2 = [[NWT[d][:, :, :], w64(1024)] for d in range(2)]
        QDEC2 = [[QDEC[d][:, :, :], w64(1024)] for d in range(2)]
        U02 = [[U0[d], wtake(1024, True)] for d in range(2)]
        QKM2 = [[QKM[d], wtake(512)] for d in range(2)]
        KDEC2 = [[KDEC[d], wtake(512)] for d in range(2)]
        EGL22 = [[EGL2[d][:, :], wtake(32, True)] for d in range(2)]
        ab = [0]
        fbk = [0]
        ppr = [0]

        def abank():
            ab[0] = (ab[0] + 1) % 6
            return ab[0]

        def fbank():
            fbk[0] ^= 1
            return 6 + fbk[0]

        def ppair():
            ppr[0] = (ppr[0] + 1) % 3
            return ppr[0]

        idb = bcast(C("identst", True), 1, H8)
        ist = bcast(C("identst"), 1, H8)

        def msk(name):
            return bcast(C(name, True), 1, H8)

        def prep(s, d, m, par):
            NWTp, QDECp, U0p, QKMp, KDECp, EGL2p = NWT2[d][par], QDEC2[d][par], U02[d][par], QKM2[d][par], KDEC2[d][par], EGL22[d][par]
            kq = (d, par)
            T_ = DS[d]
            SMd, DG1d, DG3d, DEd = T_["SM"], T_["DG1"], T_["DG3"], T_["DE"]
            E1d, E2d, E3d = T_["E1"], T_["E2"], T_["E3"]
            PAd, PTd, RTd, XBd, BEKd, BVd = T_["PA"], T_["PT"], T_["RT"], T_["XB"], T_["BEK"], T_["BV"]

            def K(name, *x):
                return (name, d) + tuple(x)

            gti = s.tile0 + m
            sfx = "f" if d == 0 else "b"
            rows = slice(m * 128, (m + 1) * 128)
            dma("sp", KTC[d][:, :, :], KDT[s.name][:, :, rows].rearrange("h d t -> d h t"), [("qkdt", s.name)], [("ktc", d)])
            dma("sp", QTC[d][:, :, :], QDT[s.name][:, :, rows].rearrange("h d t -> d h t"), [("qkdt", s.name)], [("qtc", d)])
            dma("sp", KTMC[d][:, :], KTM[s.name][rows, :], [("tm", s.name)], [("ktmc", d)])
            dma("sp", QTMC[d][:, :], QTM[s.name][rows, :], [("tm", s.name)], [("qtmc", d)])
            dma("sp", VTMC[d][:, :], VTM[s.name][rows, :], [("tm", s.name)], [("vtmc", d)])
            la = LA[:, gti, 8 * d:8 * d + 8]
            lb = LB[:, gti, 8 * d:8 * d + 8]
            be_ = BETA[:, gti, 8 * d:8 * d + 8]
            bg = fbank()
            mm(psb(bg)[:, 0:8], C("tri_" + sfx), la, True, True, ["la", "cst"], [pk(bg)])
            mm(psb(bg)[:, 8:16], C("half0"), la, True, True, ["la", "cst"], [pk(bg)])
            mm(psb(bg)[:, 16:24], C("half1"), la, True, True, ["la", "cst"], [pk(bg)])
            cp("dve", SMd[:, 0:24], psb(bg)[:, 0:24], [pk(bg)], [K("sm")])
            yield
            tt("dve", SMd[:, 24:32], SMd[:, 0:8], lb, ALU.add, [K("sm"), "lb"], [K("sm_glb")])
            act(SMd[:, 32:40], SMd[:, 0:8], AF.Exp, [K("sm")], [K("sm_eg")])
            cp("dve", SMd[0:64, 40:48], SMd[0:64, 8:16], [K("sm")], [K("sm_glo")])
            cp("dve", SMd[64:128, 40:48], SMd[64:128, 16:24], [K("sm")], [K("sm_glo")])
            tt("dve", SMd[:, 48:56], SMd[:, 40:48], SMd[:, 0:8], ALU.subtract, [K("sm"), K("sm_glo")], [K("sm_ek")])
            act(SMd[:, 48:56], SMd[:, 48:56], AF.Exp, [K("sm_ek")], [K("sm_ek")])
            act(EGL2p, SMd[:, 8:24], AF.Exp, [K("sm")], [("egl2",) + kq])
            tt("dve", SMd[:, 56:64], be_, SMd[:, 32:40], ALU.mult, ["beta", K("sm_eg")], [K("sm_be")])
            tsc("dve", SMd[:, 64:72], SMd[:, 0:8], -1.0, None, ALU.mult, None, [K("sm")], [K("sm_ng")])
            tt("pool", v3(DG1d), ist, bcast(SMd[:, 24:32], 2, 64), ALU.mult, ["cst", K("sm_glb")], [K("dg1")])
            tt("pool", v3(DG3d), ist, bcast(SMd[:, 0:8], 2, 64), ALU.mult, ["cst", K("sm")], [K("dg3")])
            tt("dve", v3(DEd), ist, bcast(SMd[:, 32:40], 2, 64), ALU.mult, ["cst", K("sm_eg")], [K("de")])
            yield
            b1 = fbank()
            mm(psb(b1)[:, :], C("negblk"), DG3d, True, False, [K("dg3"), "cst"], [pk(b1)])
            mm(psb(b1)[:, :], C("ident"), bcast(SMd[:, 24:32], 2, 64), False, False, [K("sm_glb"), "cst"], [pk(b1)])
            mm(psb(b1)[:, :], C("ident", True), msk("m1_" + sfx), False, True, ["cstb"], [pk(b1)])
            act(E1d, psb(b1)[:, :], AF.Exp, [pk(b1)], [K("e1")])
            yield
            b2 = fbank()
            mm(psb(b2)[:, :], C("blk"), DG1d, True, False, [K("dg1"), "cst"], [pk(b2)])
            mm(psb(b2)[:, :], C("ident"), bcast(SMd[:, 64:72], 2, 64), False, False, [K("sm_ng"), "cst"], [pk(b2)])
            mm(psb(b2)[:, :], C("ident", True), msk("m2_" + sfx), False, True, ["cstb"], [pk(b2)])
            act(E2d, psb(b2)[:, :], AF.Exp, [pk(b2)], [K("e2")])
            yield
            b3 = fbank()
            mm(psb(b3)[:, :], C("blk"), DG3d, True, False, [K("dg3"), "cst"], [pk(b3)])
            mm(psb(b3)[:, :], C("ident"), bcast(SMd[:, 64:72], 2, 64), False, False, [K("sm_ng"), "cst"], [pk(b3)])
            mm(psb(b3)[:, :], C("ident", True), msk("m3_" + sfx), False, True, ["cstb"], [pk(b3)])
            act(E3d, psb(b3)[:, :], AF.Exp, [pk(b3)], [K("e3")])
            yield
            bkk, bqk = abank(), abank()
            for h in range(H8):
                for a in range(2):
                    ts_ = slice(64 * a, 64 * a + 64)
                    mm(psb(bkk)[ts_, h * 64:(h + 1) * 64], KTC[d][:, h, ts_], KTC[d][:, h, ts_], True, True,
                       [("ktc", d)], [pk(bkk)], tile_position=(0, 64 * a))
                    mm(psb(bqk)[ts_, h * 64:(h + 1) * 64], KTC[d][:, h, ts_], QTC[d][:, h, ts_], True, True,
                       [("ktc", d), ("qtc", d)], [pk(bqk)], tile_position=(0, 64 * a))
            A_, AT_ = PAd[0], PTd[0]
            kA, kAT = K("pa", 0), K("pt", 0)
            tt("dve", A_, psb(bkk)[:, :], E1d, ALU.mult, [pk(bkk), K("e1")], [kA])
            tt("dve", AT_, psb(bkk)[:, :], E2d, ALU.mult, [pk(bkk), K("e2")], [kAT])
            tt("dve", QKMp, psb(bqk)[:, :], E3d, ALU.mult, [pk(bqk), K("e3")], [("qkm",) + kq])
            yield

            def grp(L, R, lkey, rkey):
                bank = abank()
                for h in range(H8):
                    for a in range(2):
                        ts_ = slice(64 * a, 64 * a + 64)
                        hs = slice(h * 64, (h + 1) * 64)
                        mm(psb(bank)[ts_, hs], L[ts_, hs], R[ts_, hs], True, True, [lkey, rkey], [pk(bank)],
                           tile_position=(64 * a, 64 * a))
                return bank

            D_, DT_ = PAd[1], PTd[1]
            kD, kDT = K("pa", 1), K("pt", 1)
            X = [XBd[0], XBd[1], BEKd, BVd, XBd[2], XBd[3]]
            kX = [K("xb", 0), K("xb", 1), K("bek"), K("bv"), K("xb", 2), K("xb", 3)]
            kR = [K("rt", 0), K("rt", 1)]
            tt("pool", v3(D_), v3(A_), msk("mask8"), ALU.mult, [kA, "cstb"], [kD])
            tt("pool", v3(DT_), v3(AT_), msk("mask8"), ALU.mult, [kAT, "cstb"], [kDT])
            tt("dve", v3(X[2]), idb, v3(DT_), ALU.subtract, ["cstb", kDT], [kX[2]])
            yield
            g1 = grp(DT_, D_, kDT, kD)
            g2 = grp(D_, DT_, kD, kDT)
            cp("act", X[0], psb(g1)[:, :], [pk(g1)], [kX[0]])
            tt("dve", v3(RTd[1]), v3(X[0]), idb, ALU.add, [kX[0], "cstb"], [kR[1]])
            cp("dve", RTd[0], psb(g2)[:, :], [pk(g2)], [kR[0]])
            yield
            g3 = grp(RTd[0], X[0], kR[0], kX[0])
            tt("dve", v3(X[1]), v3(psb(g3)[:, :]), idb, ALU.add, [pk(g3), "cstb"], [kX[1]])
            g1 = grp(RTd[1], X[2], kR[1], kX[2])
            cp("act", X[3], psb(g1)[:, :], [pk(g1)], [kX[3]])
            yield
            g2 = grp(X[3], X[1], kX[3], kX[1])
            g3 = grp(X[1], X[3], kX[1], kX[3])
            cp("act", X[4], psb(g2)[:, :], [pk(g2)], [kX[4]])
            cp("dve", X[5], psb(g3)[:, :], [pk(g3)], [kX[5]])
            yield
            Tb, kT = [X[4], X[0]], [kX[4], kX[0]]
            Mb, kM = [X[5], X[1]], [kX[5], kX[1]]
            cur = 0
            for li, mname in enumerate(("moff8", "moff16", "moff32")):
                last = (li == 2)
                nxt = 1 - cur
                tt("pool", v3(D_), v3(A_), msk(mname), ALU.mult, [kA, "cstb"], [kD])
                if not last:
                    tt("pool", v3(DT_), v3(AT_), msk(mname), ALU.mult, [kAT, "cstb"], [kDT])
                g1 = grp(D_, Mb[cur], kD, kM[cur])
                cp("act", RTd[1], psb(g1)[:, :], [pk(g1)], [kR[1]])
                if not last:
                    g2 = grp(DT_, Tb[cur], kDT, kT[cur])
                    cp("dve", RTd[0], psb(g2)[:, :], [pk(g2)], [kR[0]])
                yield
                g3 = grp(Tb[cur], RTd[1], kT[cur], kR[1])
                tt("dve", Mb[nxt], Mb[cur], psb(g3)[:, :], ALU.subtract, [kM[cur], pk(g3)], [kM[nxt]])
                if not last:
                    g1 = grp(Mb[cur], RTd[0], kM[cur], kR[0])
                    tt("dve", Tb[nxt], Tb[cur], psb(g1)[:, :], ALU.subtract, [kT[cur], pk(g1)], [kT[nxt]])
                cur = nxt
                yield
            TTm = Mb[cur]
            tkey = kM[cur]
            tt("dve", v3(BEKd), v3(KTMC[d][:, :]), bcast(SMd[:, 56:64], 2, 64), ALU.mult, [("ktmc", d), K("sm_be")], [K("bek")])
            tt("pool", v3(KDECp), v3(KTMC[d][:, :]), bcast(SMd[:, 48:56], 2, 64), ALU.mult, [("ktmc", d), K("sm_ek")], [("kdec",) + kq])
            tt("pool", v3(BVd), v3(VTMC[d][:, :]), bcast(be_, 2, 64), ALU.mult, [("vtmc", d), "beta"], [K("bv")])
            yield
            pp_ = ppair()
            pkeys = [pk(2 * pp_), pk(2 * pp_ + 1)]
            bu0 = abank()
            while bu0 in (2 * pp_, 2 * pp_ + 1):
                bu0 = abank()
            for h in range(H8):
                for a in range(2):
                    ts_ = slice(64 * a, 64 * a + 64)
                    hs = slice(h * 64, (h + 1) * 64)
                    cs = slice((a * 8 + h) * 64, (a * 8 + h) * 64 + 64)
                    mm(PS[pp_][0:64, cs], BEKd[ts_, hs], TTm[ts_, hs], True, True, [K("bek"), tkey], pkeys,
                       tile_position=(64 * a, 0))
                    mm(psb(bu0)[ts_, hs], TTm[ts_, hs], BVd[ts_, hs], True, True, [tkey, K("bv")], [pk(bu0)],
                       tile_position=(64 * a, 64 * a))
            tsc("dve", NWTp, PS[pp_][0:64, :].rearrange("p (x i) -> p x i", x=16), -1.0, None, ALU.mult, None,
                pkeys, [("nwt",) + kq])
            cp("act", U0p, psb(bu0)[:, :], [pk(bu0)], [("u0",) + kq])
            yield
            pp_ = ppair()
            pkeys = [pk(2 * pp_), pk(2 * pp_ + 1)]
            for a in range(2):
                mm(PS[pp_][0:64, a * 512:(a + 1) * 512], C("half%d" % a, True)[:, 0:64], DEd, True, True,
                   [K("de"), "cstb"], pkeys)
            for a in range(2):
                tt("dve", QDECp[:, a * 8:(a + 1) * 8, :],
                   QTC[d][:, :, a * 64:(a + 1) * 64],
                   PS[pp_][0:64, a * 512:(a + 1) * 512].rearrange("p (h i) -> p h i", h=8), ALU.mult,
                   [("qtc", d)] + pkeys, [("qdec",) + kq])
            yield

        def steps(s, d, m, par, first_visit):
            NWTp, QDECp, U0p, QKMp, KDECp, EGL2p = NWT2[d][par], QDEC2[d][par], U02[d][par], QKM2[d][par], KDEC2[d][par], EGL22[d][par]
            kq = (d, par)
            SMd = DS[d]["SM"]

            def K(name, *x):
                return (name, d) + tuple(x)

            rows = slice(m * 128, (m + 1) * 128)
            for a in ((0, 1) if d == 0 else (1, 0)):
                ts_ = slice(64 * a, 64 * a + 64)
                pu = abank()
                for h in range(H8):
                    hs = slice(h * 64, (h + 1) * 64)
                    mm(psb(pu)[ts_, hs], NWTp[:, a * 8 + h, :], SBF[d][:, h, :], True, True,
                       [("nwt",) + kq, ("sbf", d)], [pk(pu)], tile_position=(0, 64 * a))
                tt("dve", UB[d][ts_, :], U0p[ts_, :], psb(pu)[ts_, :], ALU.add, [("u0",) + kq, pk(pu)], [("ub", d)])
                yield
                po, pob, pS_ = abank(), abank(), abank()
                for h in range(H8):
                    hs = slice(h * 64, (h + 1) * 64)
                    mm(psb(po)[ts_, hs], QDECp[:, a * 8 + h, :], SBF[d][:, h, :], True, True,
                       [("qdec",) + kq, ("sbf", d)], [pk(po)], tile_position=(0, 64 * a))
                    mm(psb(pob)[ts_, hs], QKMp[ts_, hs], UB[d][ts_, hs], True, True,
                       [("qkm",) + kq, ("ub", d)], [pk(pob)], tile_position=(64 * a, 64 * a))
                    mm(psb(pS_)[0:64, hs], KDECp[ts_, hs], UB[d][ts_, hs], True, True,
                       [("kdec",) + kq, ("ub", d)], [pk(pS_)], tile_position=(64 * a, 0))
                tt("dve", S32[d][:, :, :], S32[d][:, :, :], bcast(EGL2p[0:64, a * 8:a * 8 + 8], 2, 64), ALU.mult,
                   [("s32", d), ("egl2",) + kq], [("s32", d)])
                tt("dve", S32[d][:, :, :], S32[d][:, :, :], psb(pS_)[0:64, :].rearrange("p (h v) -> p h v", h=H8), ALU.add,
                   [("s32", d), pk(pS_)], [("s32", d)])
                cp("act", SBF[d][:, :, :], S32[d][:, :, :], [("s32", d)], [("sbf", d)])
                cp("act", OACC[d][ts_, :], psb(po)[ts_, :], [pk(po)], [("oacc", d)])
                tt("dve", OACC[d][ts_, :], OACC[d][ts_, :], psb(pob)[ts_, :], ALU.add, [("oacc", d), pk(pob)], [("oacc", d)])
                yield
            if first_visit:
                dma("sp", OPART[s.name][rows, :], OACC[d][:, :], [("oacc", d)], [("opart", s.name, m)])
            else:
                dma("sp", STG[d][:, :], OPART[s.name][rows, :], [("opart", s.name, m)], [f"stg{d}"])
                tt("dve", OACC[d][:, :], OACC[d][:, :], STG[d][:, :], ALU.add, [("oacc", d), f"stg{d}"], [("oacc", d)])
                tt("pool", STG[2 + d][:, :], OACC[d][:, :], OACC[d][:, :], ALU.mult, [("oacc", d)], [f"stg{2 + d}"])
                P.add("dve", lambda e: e.tensor_reduce(SMd[:, 80:88], v3(STG[2 + d][:, :]), AX.X, ALU.add),
                      [f"stg{2 + d}"], [K("sm_rn")])
                act(SMd[:, 80:88], SMd[:, 80:88], AF.Sqrt, [K("sm_rn")], [K("sm_rn")], bias=EPS, scale=1.0 / 64)
                recip(SMd[:, 80:88], SMd[:, 80:88], [K("sm_rn")], [K("sm_rn")])
                yield
                tt("dve", v3(OACC[d][:, :]), v3(OACC[d][:, :]), bcast(SMd[:, 80:88], 2, 64), ALU.mult,
                   [("oacc", d), K("sm_rn")], [("oacc", d)])
                tt("dve", v3(OACC[d][:, :]), v3(OACC[d][:, :]), bcast(dng[:], 1, H8), ALU.mult,
                   [("oacc", d), "lvec"], [("oacc", d)])
                dma("sp", STB[d][:, :], GS[s.name][rows, :], [("gs", s.name)], [f"stb{d}"])
                tt("dve", OACC[d][:, :], OACC[d][:, :], STB[d][:, :], ALU.mult, [("oacc", d), f"stb{d}"], [("oacc", d)])
                bt = fbank()
                for cc in range(4):
                    tr(psb(bt)[:, cc * 128:(cc + 1) * 128], OACC[d][:, cc * 128:(cc + 1) * 128], C("ident"),
                       [("oacc", d), "cst"], [pk(bt)])
                cp("act", STB[2 + d][:, :], psb(bt)[:, :], [pk(bt)], [f"stb{2 + d}"])
                dma("sp", OD[s.name][:, :, rows], STB[2 + d][:, :].rearrange("p (c t) -> p c t", c=4), [f"stb{2 + d}"],
                    [("od", s.name)])

        for s in seqs:
            NP_ = s.T // 128
            for d in range(2):
                if s.is_sample:
                    dma("sp", S32[d][:, :, :], st_in[l, d].rearrange("h k v -> k h v"), [], [("s32", d)])
                else:
                    P.add("pool", lambda e, d=d: e.memset(S32[d][:, :, :], 0.0), [], [("s32", d)])
                cp("act", SBF[d][:, :, :], S32[d][:, :, :], [("s32", d)], [("sbf", d)])
            visited = set()

            def mof(d, step):
                return step if d == 0 else NP_ - 1 - step

            def rr(gens):
                active = list(gens)
                while active:
                    for g_ in list(active):
                        try:
                            next(g_)
                        except StopIteration:
                            active.remove(g_)

            rr([prep(s, d, mof(d, 0), 0) for d in range(2)])
            for step in range(NP_):
                gens = []
                for d in range(2):
                    m = mof(d, step)
                    gens.append(steps(s, d, m, step % 2, m not in visited))
                for d in range(2):
                    visited.add(mof(d, step))
                if step + 1 < NP_:
                    for d in range(2):
                        gens.append(prep(s, d, mof(d, step + 1), (step + 1) % 2))
                rr(gens)
            if not s.is_sample:
                for d in range(2):
                    dma("sp", nst_out[s.idx, l, d].rearrange("h k v -> k h v"), S32[d][:, :, :], [("s32", d)], [("nst", s.idx)])
        P.fence()
        if stop_after == "scan":
            return finish()

        Wpa = WAR[:, 0:4096].rearrange("p (k n) -> p k n", k=4)
        Wpd = WAR[:, 4096:8192].rearrange("p (k n) -> p k n", k=4)
        Wo = WAR[:, 8192:16384].rearrange("p (k n) -> p k n", k=8)
        load_w(Wpa, w_pa[l], 4)
        load_w(Wpd, w_pd[l], 4)
        load_w(Wo, w_out[l], 8)
        OAT = HB[:, 0:4, :]
        ODT = HB[:, 4:8, :]
        MG = HB[:, 8:16, :]
        for s in seqs:
            mj = s.mj
            for blk in range(s.T // TB):
                t0 = blk * TB
                for c in range(4):
                    dma("sp", OAT[:, c, :], OA[s.name][2 * c:2 * c + 2].rearrange("h d t -> (h d) t")[:, t0:t0 + TB],
                        [("oa", s.name)], ["hb"])
                dma("sp", ODT, OD[s.name][:, :, t0:t0 + TB], [("od", s.name)], ["hb"])
                dma("sp", GATB[:, :, :], GATES[s.name][:, :, t0:t0 + TB], [("gates", s.name, blk)], ["gatb"])
                dma("sp", XT, XRES[s.name][:, :, t0:t0 + TB], [("xres", s.name, blk)], ["bigA"])
                for n in range(8):
                    ns = slice(n * 128, (n + 1) * 128)
                    for k in range(4):
                        mm(psb(0)[:, :TB], Wpa[:, k, ns], OAT[:, k, :], k == 0, k == 3, ["war", "hb"], [pk(0)])
                    for k in range(4):
                        mm(psb(1)[:, :TB], Wpd[:, k, ns], ODT[:, k, :], k == 0, k == 3, ["war", "hb"], [pk(1)])
                    tt("dve", STG[0][:, :TB], psb(0)[:, :TB], GATB[:, n, :], ALU.mult, [pk(0), "gatb"], ["stg0"])
                    tt("dve", STG[1][:, :TB], psb(1)[:, :TB], GATB[:, 8 + n, :], ALU.mult, [pk(1), "gatb"], ["stg1"])
                    tt("pool", MG[:, n, :], STG[0][:, :TB], STG[1][:, :TB], ALU.add, ["stg0", "stg1"], ["ht"])
                for n in range(8):
                    ns = slice(n * 128, (n + 1) * 128)
                    bk = 2 + n % 2
                    for k in range(8):
                        mm(psb(bk)[:, :TB], Wo[:, k, ns], MG[:, k, :], k == 0, k == 7, ["war", "ht"], [pk(bk)])
                    stt("dve", X1T[:, n, :], psb(bk)[:, :TB], MOD[:, 16 + n, mj:mj + 1], XT[:, n, :], ALU.mult, ALU.add,
                        [pk(bk), "bigA"] + MK, ["bigB"])
                dma("pool", XRES[s.name][:, :, t0:t0 + TB], X1T, ["bigB"], [("xres", s.name, blk)])
                rms_stats(X1T, TB, "bigB")
                tt("dve", XT, X1T, bcast(RSTD[:, :], 1, 8), ALU.mult, ["bigB", "rstd"], ["bigA"])
                for c in range(8):
                    if c % 2 == 0:
                        act(HT[:, c, :], XT[:, c, :], AF.Identity, ["bigA"] + MK, ["ht"],
                            scale=A2[:, c, mj:mj + 1], bias=MOD[:, 24 + c, mj:mj + 1])
                    else:
                        tsc("dve", HT[:, c, :], XT[:, c, :], A2[:, c, mj:mj + 1], MOD[:, 24 + c, mj:mj + 1], ALU.mult, ALU.add,
                            ["bigA"] + MK, ["ht"])
                dma("pool", H2[s.name][:, :, t0:t0 + TB], HT, ["ht"], [("h2", s.name, blk)])
        P.fence()
        if stop_after == "stageD":
            return finish()

        W1h = WAR[:, 0:8 * 2048].rearrange("p (k n) -> p k n", k=8)
        W2h = WAR[:, 16384:16384 + 16 * 1024].rearrange("p (k n) -> p k n", k=16)
        for hf in range(2):
            load_w(W1h, w1[l][:, hf * 2048:(hf + 1) * 2048], 8)
            load_w(W2h, w2[l][hf * 2048:(hf + 1) * 2048, :], 16)
            jobs = [(s, blk) for s in seqs for blk in range(s.T // TB)]
            XTs = [(XT, "bigA"), (X1T, "bigB")]
            H2Ts = [(HB[:, 0:8, :], "hb"), (HB[:, 8:16, :], "ht")]

            def ef_loads(j):
                s, blk = jobs[j]
                t0 = blk * TB
                h2t, hkey = H2Ts[j % 2]
                xt, xkey = XTs[j % 2]
                dma("sp", h2t, H2[s.name][:, :, t0:t0 + TB], [("h2", s.name, blk)], [hkey])
                dma("sp", xt, XRES[s.name][:, :, t0:t0 + TB], [("xres", s.name, blk)], [xkey])

            ef_loads(0)
            for j, (s, blk) in enumerate(jobs):
                mj = s.mj
                t0 = blk * TB
                H2T, hkey = H2Ts[j % 2]
                XTj, xkey = XTs[j % 2]
                if j + 1 < len(jobs):
                    ef_loads(j + 1)
                for f in range(16):
                    bk = 1 + f % 3
                    for k in range(8):
                        mm(psb(bk)[:, :TB], W1h[:, k, f * 128:(f + 1) * 128], H2T[:, k, :], k == 0, k == 7, ["war", hkey], [pk(bk)])
                    i = nstg()
                    act(STG[i][:, :TB], psb(bk)[:, :TB], AF.Relu, [pk(bk), "lvec"], [f"stg{i}"],
                        bias=b1f[:, hf * 16 + f:hf * 16 + f + 1])
                    tt("dve" if f % 2 == 0 else "pool", GATB[:, f, :], STG[i][:, :TB], STG[i][:, :TB], ALU.mult,
                       [f"stg{i}"], ["gatb"])
                for n in range(8):
                    bk = 4 + n % 2
                    for f in range(16):
                        mm(psb(bk)[:, :TB], W2h[:, f, n * 128:(n + 1) * 128], GATB[:, f, :], f == 0, f == 15, ["war", "gatb"], [pk(bk)])
                    stt("dve", XTj[:, n, :], psb(bk)[:, :TB], MOD[:, 40 + n, mj:mj + 1], XTj[:, n, :], ALU.mult, ALU.add,
                        [pk(bk), xkey] + MK, [xkey])
                    if hf == 0:
                        tsc("dve", XTj[:, n, :], XTj[:, n, :], GB2[:, n, mj:mj + 1], None, ALU.add, None, [xkey] + MK, [xkey])
                if not (l == NLAYERS - 1 and hf == 1):
                    dma("pool", XRES[s.name][:, :, t0:t0 + TB], XTj, [xkey], [("xres", s.name, blk)])
                else:
                    rms_stats(XTj, TB, xkey, sq=GATB[:, 0:8, :], sqkey="gatb")
                    tt("dve", XTj, XTj, bcast(RSTD[:, :], 1, 8), ALU.mult, [xkey, "rstd"], [xkey])
                    tt("dve", XTj, XTj, bcast(fnf[:], 2, TB), ALU.mult, [xkey, "fnf"], [xkey])
                    dst = ys_out if s.is_sample else yp_out[s.idx * TP:(s.idx + 1) * TP, :]
                    for t2 in range(TB // 128):
                        for hh in range(2):
                            bk = 6 + hh
                            for c in range(4):
                                tr(psb(bk)[:, c * 128:(c + 1) * 128], XTj[:, hh * 4 + c, t2 * 128:(t2 + 1) * 128], C("ident"),
                                   [xkey, "cst"], [pk(bk)])
                            i = nstg()
                            cp("act" if hh == 0 else "dve", STG[i][:, :], psb(bk)[:, :], [pk(bk)], [f"stg{i}"])
                            dma("pool", dst[t0 + t2 * 128:t0 + (t2 + 1) * 128, hh * 512:(hh + 1) * 512], STG[i][:, :],
                                [f"stg{i}"], [("y", s.name)])
        P.fence()
        if stop_after == f"layer{l}":
            return finish()

    return finish()


def make_in_maps(inputs):
    cos, sin = _rope_tables()
    maps = []
    for core in range(8):
        b = core % 4
        m = {
            "xs": np.ascontiguousarray(inputs["x_sample"][b]),
            "xp": np.ascontiguousarray(inputs["x_prompt"][core * NPS:(core + 1) * NPS].reshape(NPS * TP, D)),
            "cvec": np.ascontiguousarray(np.stack([inputs["c"][b], inputs["c_ctx"]], 0)),
            "ck": np.ascontiguousarray(inputs["cache_k"][b]),
            "cv": np.ascontiguousarray(inputs["cache_v"][b]),
            "st": np.ascontiguousarray(inputs["state_delta"][b]),
            "a_log": np.ascontiguousarray(inputs["a_log"].reshape(DEPTH, 16)),
            "dt_bias": np.ascontiguousarray(inputs["dt_bias"].reshape(DEPTH, 16)),
            "cst": CST, "ropecos": cos, "ropesin": sin,
        }
        for k in ["w_mod", "b_mod", "norm1", "norm2", "w_in", "conv_w", "q_gain", "k_gain", "dn_gain",
                  "w_pa", "w_pd", "w_out", "w1", "b1", "w2", "b2", "final_norm"]:
            m[k] = np.ascontiguousarray(inputs[k])
        maps.append(m)
    return maps


_NC_CACHE = {}


def kernel(**inputs):
    inputs = {k: np.asarray(v) for k, v in inputs.items()}
    if "nc" not in _NC_CACHE:
        _NC_CACHE["nc"] = build_program()
    nc = _NC_CACHE["nc"]
    maps = make_in_maps(inputs)
    res = run_bass_kernel_spmd(nc, maps, core_ids=list(range(8)))
    r = res.results
    y_sample = np.stack([r[b]["ys"] for b in range(4)], 0)
    y_prompt = np.concatenate([r[c]["yp"].reshape(NPS, TP, D) for c in range(8)], 0)
    nk = np.concatenate([r[c]["nk"] for c in range(8)], 0)
    nv = np.concatenate([r[c]["nv"] for c in range(8)], 0)
    nst = np.concatenate([r[c]["nst"] for c in range(8)], 0)
    return (y_prompt.astype(np.float32), y_sample.astype(np.float32), nk.astype(np.float32),
            nv.astype(np.float32), nst.astype(np.float32))
```

```python
import numpy as np
from contextlib import ExitStack
import concourse.bass as bass
import concourse.mybir as mybir
from concourse.bass_utils import run_bass_kernel_spmd

F32 = mybir.dt.float32
BF16 = mybir.dt.bfloat16
AF = mybir.ActivationFunctionType
ALU = mybir.AluOpType
AX = mybir.AxisListType

D = 1024
DEPTH = 2
TS = 4096
TP = 256
NPS = 4
NCTX = 256
INW = 4896
DFF = 4096
EPS = 1e-6
NEG = -30000.0

C_QA, C_KA, C_VA, C_QD, C_KD, C_VD, C_GO, C_AI, C_BI, C_GA, C_GD = 0, 512, 640, 768, 1280, 1792, 2304, 2816, 2832, 2848, 3872


ENGS = ["pe", "act", "dve", "pool", "sp"]
N_DMA_SEMS = {"sp": 40, "pool": 16, "act": 8}
EPOCH = 30000


class Ev:
    __slots__ = ("dma", "eng", "idx", "sem", "val")

    def __init__(self, dma, eng, idx, sem=None, val=None):
        self.dma, self.eng, self.idx, self.sem, self.val = dma, eng, idx, sem, val


class Rec:
    __slots__ = ("eng", "fn", "waits", "signal", "idx", "dma", "sig_sem", "sig_val")

    def __init__(self, eng, fn):
        self.eng, self.fn = eng, fn
        self.waits = []
        self.signal = False
        self.dma = None


class Buf:
    __slots__ = ("w", "r")

    def __init__(self):
        self.w = None
        self.r = []


class Prog:
    def __init__(self, nc):
        self.nc = nc
        self.ops = {e: [] for e in ENGS}
        self.waited = {e: {p: -1 for p in ENGS} for e in ENGS}
        self.waited_dma = {e: {} for e in ENGS}
        self.bufs = {}
        self.dma_count = {q: 0 for q in N_DMA_SEMS}

    def buf(self, k):
        b = self.bufs.get(k)
        if b is None:
            b = self.bufs[k] = Buf()
        return b

    def add(self, eng, fn, reads=(), writes=(), dma=False):
        rec = Rec(eng, fn)
        rec.idx = len(self.ops[eng])
        deps = []
        for k in reads:
            b = self.buf(k)
            if b.w is not None:
                deps.append((b.w, True))
        for k in writes:
            b = self.buf(k)
            if b.w is not None:
                deps.append((b.w, False))
            for r in b.r:
                deps.append((r, False))
        if dma:
            d = self.dma_count[eng]
            n = N_DMA_SEMS[eng]
            si, val = d % n, 16 * (d // n + 1)
            if d >= n:
                deps.append((Ev(True, eng, None, si, val - 16), True))
            rec.dma = (si, val)
            ev = Ev(True, eng, rec.idx, si, val)
            self.dma_count[eng] += 1
        else:
            ev = Ev(False, eng, rec.idx)
        for dep, raw in deps:
            if dep.dma:
                key = (dep.eng, dep.sem)
                if self.waited_dma[eng].get(key, 0) >= dep.val:
                    continue
                self.waited_dma[eng][key] = dep.val
                rec.waits.append(dep)
            else:
                if dep.eng == eng and eng == "pe":
                    continue
                if self.waited[eng][dep.eng] >= dep.idx:
                    continue
                self.waited[eng][dep.eng] = dep.idx
                rec.waits.append(dep)
                self.ops[dep.eng][dep.idx].signal = True
        for k in reads:
            self.buf(k).r.append(ev)
        for k in writes:
            b = self.buf(k)
            b.w = ev
            b.r = []
        self.ops[eng].append(rec)
        return rec

    def fence(self):
        last = {e: len(self.ops[e]) - 1 for e in ["pe", "act", "dve", "pool"]}
        dma_evs = []
        for q, n in N_DMA_SEMS.items():
            d = self.dma_count[q]
            for i in range(min(n, d)):
                uses = (d - i + n - 1) // n
                dma_evs.append(Ev(True, q, None, i, 16 * uses))
        for e in ENGS:
            rec = Rec(e, lambda eng: eng.nop())
            rec.idx = len(self.ops[e])
            for p, li in last.items():
                if p == e or li < 0:
                    continue
                j = li
                while j >= 0 and self.ops[p][j].dma is not None:
                    j -= 1
                if j < 0 or self.waited[e][p] >= j:
                    continue
                self.waited[e][p] = j
                rec.waits.append(Ev(False, p, j))
                self.ops[p][j].signal = True
            for dep in dma_evs:
                key = (dep.eng, dep.sem)
                if self.waited_dma[e].get(key, 0) >= dep.val:
                    continue
                self.waited_dma[e][key] = dep.val
                rec.waits.append(dep)
            self.ops[e].append(rec)

    def emit(self, es):
        nc = self.nc
        comp_sems = {}
        for e in ["pe", "act", "dve", "pool"]:
            cnt = 0
            for rec in self.ops[e]:
                if rec.signal and rec.dma is None:
                    ep = cnt // EPOCH
                    if (e, ep) not in comp_sems:
                        comp_sems[(e, ep)] = es.enter_context(nc.semaphore(f"c_{e}_{ep}"))
                    rec.sig_sem = comp_sems[(e, ep)]
                    rec.sig_val = cnt % EPOCH + 1
                    cnt += 1
        dma_sems = {}
        for q, n in N_DMA_SEMS.items():
            for i in range(min(n, self.dma_count[q])):
                dma_sems[(q, i)] = es.enter_context(nc.semaphore(f"d_{q}_{i}"))
        final_waits = []
        for q, n in N_DMA_SEMS.items():
            d = self.dma_count[q]
            for i in range(min(n, d)):
                uses = (d - i + n - 1) // n
                final_waits.append((dma_sems[(q, i)], 16 * uses))
        block = es.enter_context(nc.Block())
        ops = self.ops

        def run(engname, eng):
            for rec in ops[engname]:
                for dep in rec.waits:
                    if dep.dma:
                        eng.wait_ge(dma_sems[(dep.eng, dep.sem)], dep.val)
                    else:
                        prod = ops[dep.eng][dep.idx]
                        eng.wait_ge(prod.sig_sem, prod.sig_val)
                ins = rec.fn(eng)
                if rec.dma is not None:
                    ins.then_inc(dma_sems[(engname, rec.dma[0])], 16)
                elif rec.signal:
                    ins.then_inc(rec.sig_sem, 1)
            if engname == "sp":
                for s, v in final_waits:
                    eng.wait_ge(s, v)

        @block.tensor
        def _(t):
            run("pe", t)

        @block.scalar
        def _(a):
            run("act", a)

        @block.vector
        def _(v):
            run("dve", v)

        @block.gpsimd
        def _(g):
            run("pool", g)

        @block.sync
        def _(s):
            run("sp", s)


CST_LAYOUT = {}


def _build_consts():
    p = np.arange(128)
    cols = []
    off = 0

    def put(name, arr):
        nonlocal off
        arr = np.asarray(arr, np.float32).reshape(128, -1)
        CST_LAYOUT[name] = (off, arr.shape[1])
        cols.append(arr)
        off += arr.shape[1]

    put("ident", np.eye(128))
    put("ones", np.ones((128, 128)))
    put("negones", -np.ones((128, 128)))
    half = p // 64
    put("blk", (half[:, None] == half[None, :]).astype(np.float32))
    put("negblk", -(half[:, None] == half[None, :]).astype(np.float32))
    put("identst", (p[:, None] % 64 == np.arange(64)[None, :]).astype(np.float32))
    same = half[:, None] == half[None, :]
    put("tri_f", (same & (p[:, None] <= p[None, :])).astype(np.float32))
    put("tri_b", (same & (p[:, None] >= p[None, :])).astype(np.float32))
    put("half0", np.repeat((p < 64).astype(np.float32)[:, None], 128, 1))
    put("half1", np.repeat((p >= 64).astype(np.float32)[:, None], 128, 1))
    il = p % 64
    j = np.arange(64)

    def m(keep):
        return np.where(keep, 0.0, NEG).astype(np.float32)

    put("m1_f", m(il[:, None] > j[None, :]))
    put("m1_b", m(il[:, None] < j[None, :]))
    put("m2_f", m(j[None, :] > il[:, None]))
    put("m2_b", m(j[None, :] < il[:, None]))
    put("m3_f", m(j[None, :] >= il[:, None]))
    put("m3_b", m(j[None, :] <= il[:, None]))
    put("mask8", ((il[:, None] // 8) == (j[None, :] // 8)).astype(np.float32))
    for sz in (8, 16, 32):
        put("moff%d" % sz, (((il[:, None] // (2 * sz)) == (j[None, :] // (2 * sz)))
                            & ((il[:, None] // sz) != (j[None, :] // sz))).astype(np.float32))
    R = np.zeros((128, 128), np.float32)
    for q in range(128):
        if q % 64 < 32:
            R[q, q + 32] = -1.0
        else:
            R[q, q - 32] = 1.0
    put("rot", R.T)
    return np.concatenate(cols, 1)


CST = _build_consts()
NCST = CST.shape[1]


def _rope_tables():
    t = np.arange(TS)
    row = (t // 64).astype(np.float32)
    col = (t % 64).astype(np.float32)
    inv = (10000.0 ** (-np.arange(16, dtype=np.float32) / 16)).astype(np.float32)
    ang = np.concatenate([row[:, None] * inv, col[:, None] * inv], -1).astype(np.float32)
    cos = np.cos(ang).astype(np.float32).T
    sin = np.sin(ang).astype(np.float32).T
    return np.tile(cos, (4, 1)).copy(), np.tile(sin, (4, 1)).copy()


class Seq:
    def __init__(self, name, T, is_sample, idx, key0, tile0):
        self.name, self.T, self.is_sample, self.idx = name, T, is_sample, idx
        self.key0 = key0
        self.tile0 = tile0
        self.nctx = NCTX if is_sample else 0
        self.mj = 0 if is_sample else 1


def bcast(ap, axis, n):
    shp = list(ap.shape)
    shp.insert(axis, n)
    return ap.unsqueeze(axis).broadcast_to(shp)


def build_program(debug_outs=(), stop_after=None):
    nc = bass.Bass("TRN2", target_bir_lowering=False)
    es = ExitStack()
    P = Prog(nc)
    dbg = set(debug_outs)

    def din(name, shape, dt=F32):
        return nc.dram_tensor(name, list(shape), dt, kind="ExternalInput").ap()

    def dout(name, shape, dt=F32):
        return nc.dram_tensor(name, list(shape), dt, kind="ExternalOutput").ap()

    def dscr(name, shape, dt=F32):
        kind = "ExternalOutput" if name in dbg else "Internal"
        return nc.dram_tensor(name, list(shape), dt, kind=kind).ap()

    xs_in = din("xs", [TS, D])
    xp_in = din("xp", [NPS * TP, D])
    cvec = din("cvec", [2, D])
    ck_in = din("ck", [DEPTH, 2, NCTX, 64])
    cv_in = din("cv", [DEPTH, 2, NCTX, 64])
    st_in = din("st", [DEPTH, 2, 8, 64, 64])
    w_mod = din("w_mod", [DEPTH, D, 6 * D])
    b_mod = din("b_mod", [DEPTH, 6 * D])
    norm1 = din("norm1", [DEPTH, D])
    norm2 = din("norm2", [DEPTH, D])
    w_in = din("w_in", [DEPTH, D, INW])
    conv_w = din("conv_w", [DEPTH, 3, 1536])
    q_gain = din("q_gain", [DEPTH, 64])
    k_gain = din("k_gain", [DEPTH, 64])
    a_log = din("a_log", [DEPTH, 16])
    dt_bias = din("dt_bias", [DEPTH, 16])
    dn_gain = din("dn_gain", [DEPTH, 64])
    w_pa = din("w_pa", [DEPTH, 512, D])
    w_pd = din("w_pd", [DEPTH, 512, D])
    w_out = din("w_out", [DEPTH, D, D])
    w1 = din("w1", [DEPTH, D, DFF])
    b1 = din("b1", [DEPTH, DFF])
    w2 = din("w2", [DEPTH, DFF, D])
    b2 = din("b2", [DEPTH, D])
    final_norm = din("final_norm", [D])
    cst_in = din("cst", [128, NCST])
    cos_in = din("ropecos", [128, TS])
    sin_in = din("ropesin", [128, TS])
    ys_out = dout("ys", [TS, D])
    yp_out = dout("yp", [NPS * TP, D])
    nk_out = dout("nk", [NPS, DEPTH, 2, TP, 64])
    nv_out = dout("nv", [NPS, DEPTH, 2, TP, 64])
    nst_out = dout("nst", [NPS, DEPTH, 2, 8, 64, 64])

    seqs = [Seq("s", TS, True, 0, 0, 0)]
    for i in range(NPS):
        seqs.append(Seq(f"p{i}", TP, False, i, NCTX + TS + i * TP, TS // 128 + i * (TP // 128)))
    NKEY = NCTX + TS + NPS * TP
    NTILE = TS // 128 + NPS * TP // 128
    NVT = NKEY // 128

    XRES, PRE, QA, GATES, GS, QDT, KDT, QTM, KTM, VTM, OPART, OA, OD, X1, H2, ACTS = ({} for _ in range(16))
    for s in seqs:
        T = s.T
        XRES[s.name] = dscr(f"xres_{s.name}", [128, 8, T])
        PRE[s.name] = dscr(f"pre_{s.name}", [128, 12, T + 2])
        QA[s.name] = dscr(f"qa_{s.name}", [8, 64, T], BF16)
        GATES[s.name] = dscr(f"gates_{s.name}", [128, 16, T], BF16)
        GS[s.name] = dscr(f"gs_{s.name}", [T, 512], BF16)
        QDT[s.name] = dscr(f"qdt_{s.name}", [8, 64, T], BF16)
        KDT[s.name] = dscr(f"kdt_{s.name}", [8, 64, T], BF16)
        QTM[s.name] = dscr(f"qtm_{s.name}", [T, 512], BF16)
        KTM[s.name] = dscr(f"ktm_{s.name}", [T, 512], BF16)
        VTM[s.name] = dscr(f"vtm_{s.name}", [T, 512], BF16)
        OPART[s.name] = dscr(f"opart_{s.name}", [T, 512])
        OA[s.name] = dscr(f"oa_{s.name}", [8, 64, T], BF16)
        OD[s.name] = dscr(f"od_{s.name}", [128, 4, T], BF16)
        H2[s.name] = dscr(f"h2_{s.name}", [128, 8, T], BF16)

    def sb(name, shape, dt=F32):
        return es.enter_context(nc.sbuf_tensor("sb_" + name, list(shape), dt))

    cst = sb("cst", [128, NCST])
    cstb = sb("cstb", [128, NCST], BF16)

    def C(name, bf=False):
        o, n = CST_LAYOUT[name]
        return (cstb if bf else cst)[:, o:o + n]

    WAR = sb("warena", [128, 40960], BF16)
    KT_all = sb("kt_all", [128, NKEY], BF16)
    VA_all = sb("va_all", [128, NVT, 2, 65], BF16)
    LA = sb("la", [128, NTILE, 16])
    LB = sb("lb", [128, NTILE, 16])
    BETA = sb("beta", [128, NTILE, 16])
    MOD = sb("mod", [128, 48, 2])
    A1 = sb("a1", [128, 8, 2])
    A2 = sb("a2", [128, 8, 2])
    GB2 = sb("gb2", [128, 8, 2])
    n1f = sb("n1f", [128, 8])
    n2f = sb("n2f", [128, 8])
    fnf = sb("fnf", [128, 8])
    b2f = sb("b2f", [128, 8])
    b1f = sb("b1f", [128, 32])
    bmf = sb("bmf", [128, 48])
    cfm = sb("cfm", [128, 8, 2])
    scb = sb("scb", [128, 8, 2], BF16)
    qg = sb("qg", [128, 1])
    kg = sb("kg", [128, 1])
    cw = sb("cw", [128, 3, 12])
    dtb = sb("dtb", [128, 16])
    negA = sb("negA", [128, 16])
    dng = sb("dng", [128, 64])
    TB = 256
    BIGA = sb("bigA", [128, 12 * 258])
    BIGB = sb("bigB", [128, 12 * 256])
    HB = sb("hb16", [128, 16, 256], BF16)
    GATB = sb("gatb", [128, 16, 256], BF16)
    R1 = sb("r1", [128, 256])
    RSTD = sb("rstd", [128, 256])
    STG = [sb(f"stg{i}", [128, 512]) for i in range(4)]
    STB = [sb(f"stb{i}", [128, 512], BF16) for i in range(4)]
    COSB = sb("cosb", [128, 256])
    SINB = sb("sinb", [128, 256])
    SMALL = sb("small", [128, 64])
    SM = sb("sm", [128, 160])
    VAB = sb("vab", [128, 160])
    KTC = [sb(f"ktc{i}", [64, 8, 128], BF16) for i in range(2)]
    QTC = [sb(f"qtc{i}", [64, 8, 128], BF16) for i in range(2)]
    KTMC = [sb(f"ktmc{i}", [128, 512], BF16) for i in range(2)]
    QTMC = [sb(f"qtmc{i}", [128, 512], BF16) for i in range(2)]
    VTMC = [sb(f"vtmc{i}", [128, 512], BF16) for i in range(2)]
    def v512(big, i, p0=0, p1=128):
        return big[p0:p1, i * 512:(i + 1) * 512]

    E1, E2, E3, DG1, DG3 = (v512(BIGA, i) for i in range(5))
    U0 = [v512(BIGA, 5), v512(BIGB, 0)]
    OACC = [v512(BIGB, 1), v512(BIGB, 2)]
    RS = v512(BIGB, 3)
    S32 = [v512(BIGB, 4 + i, 0, 64).rearrange("p (h v) -> p h v", h=8) for i in range(2)]
    HBf = HB[:, :, :].rearrange("p a b -> p (a b)")
    GBf = GATB[:, :, :].rearrange("p a b -> p (a b)")
    PA = [v512(HBf, 0), v512(HBf, 1)]
    PT = [v512(HBf, 2), v512(HBf, 3)]
    RT = [v512(HBf, 4), v512(HBf, 5)]
    QKM = [v512(HBf, 6), v512(HBf, 7)]
    BEK = v512(GBf, 0)
    KDEC = [v512(GBf, 1), v512(GBf, 2)]
    BV = v512(GBf, 3)
    UB = [v512(GBf, 4), v512(GBf, 5)]
    DE = v512(GBf, 6)
    OB = v512(GBf, 7, 0, 64)
    XBW = [sb(f"xbw{i}", [128, 1024], BF16) for i in range(2)]
    XB = [XBW[0][:, 0:512], XBW[0][:, 512:1024], XBW[1][:, 0:512], XBW[1][:, 512:1024]]
    NWT = [sb(f"nwt{i}", [64, 16, 64], BF16) for i in range(2)]
    QDEC = [sb(f"qdec{i}", [64, 16, 64], BF16) for i in range(2)]
    SBF = [sb(f"sbf{i}", [64, 8, 64], BF16) for i in range(2)]
    EGL2 = [sb(f"egl2{i}", [128, 16]) for i in range(2)]
    QB = STB[3][:, :].rearrange("p (j t) -> p j t", j=4)
    PS = [es.enter_context(nc.psum_tensor(f"ps{i}", [128, 1024], F32)) for i in range(4)]
    dbg_mod = dscr("dbg_mod", [128, 48, 2])
    dbg_la = dscr("dbg_la", [128, NTILE, 16])
    dbg_lb = dscr("dbg_lb", [128, NTILE, 16])
    dbg_beta = dscr("dbg_beta", [128, NTILE, 16])

    XT = BIGA[:, 0:8 * 256].rearrange("p (c t) -> p c t", c=8)
    X1T = BIGB[:, 0:8 * 256].rearrange("p (c t) -> p c t", c=8)
    PRET = BIGA[:, :].rearrange("p (c t) -> p c t", c=12)
    CV = BIGB[:, :].rearrange("p (c t) -> p c t", c=12)
    SQ = HB[:, 0:8, :]
    HT = HB[:, 8:16, :]
    XTM = [BIGA[:, i * 1024:(i + 1) * 1024] for i in range(2)]
    XFM = [BIGB[:, i * 1024:(i + 1) * 1024].rearrange("p (c t) -> p c t", c=8) for i in range(2)]

    def psb(i):
        return PS[i // 2][:, (i % 2) * 512:(i % 2) * 512 + 512]

    def pk(i):
        return ("ps", i)

    def dma(q, out, in_, reads, writes, **kw):
        P.add(q, lambda e: e.dma_start(out=out, in_=in_, **kw), reads, writes, dma=True)

    def mm(out, lhsT, rhs, start, stop, reads, writes, **kw):
        P.add("pe", lambda e: e.matmul(out, lhsT, rhs, start=start, stop=stop, **kw), reads, writes)

    def tr(out, in_, ident, reads, writes):
        P.add("pe", lambda e: e.transpose(out, in_, ident), reads, writes)

    def act(out, in_, func, reads, writes, **kw):
        P.add("act", lambda e: e.activation(out, in_, func, **kw), reads, writes)

    def tt(eng, out, in0, in1, op, reads, writes):
        P.add(eng, lambda e: e.tensor_tensor(out, in0, in1, op), reads, writes)

    def tsc(eng, out, in0, s1, s2, op0, op1, reads, writes):
        if op1 is None:
            P.add(eng, lambda e: e.tensor_scalar(out, in0, s1, None, op0), reads, writes)
        else:
            P.add(eng, lambda e: e.tensor_scalar(out, in0, s1, s2, op0, op1), reads, writes)

    def stt(eng, out, in0, scalar, in1, op0, op1, reads, writes):
        P.add(eng, lambda e: e.scalar_tensor_tensor(out, in0, scalar, in1, op0, op1), reads, writes)

    def cp(eng, out, in_, reads, writes):
        if eng == "act":
            P.add("act", lambda e: e.copy(out, in_), reads, writes)
        else:
            P.add(eng, lambda e: e.tensor_copy(out, in_), reads, writes)

    def recip(out, in_, reads, writes):
        P.add("dve", lambda e: e.reciprocal(out, in_), reads, writes)

    def load_w(dst3, src2, K, wkey="war"):
        for k in range(K):
            dma("pool", dst3[:, k, :], src2[k * 128:(k + 1) * 128, :], [], [wkey])

    def rms_stats(src3, nt, srckey, sq=None, sqkey="hb"):
        if sq is None:
            sq = SQ
        act(sq[:, :, :nt], src3, AF.Square, [srckey], [sqkey])
        for c in range(8):
            mm(psb(0)[:, :nt], C("ones", True), sq[:, c, :nt], c == 0, c == 7, [sqkey, "cstb"], [pk(0)])
        act(R1[:, :nt], psb(0)[:, :nt], AF.Sqrt, [pk(0)], ["r1"], bias=EPS, scale=1.0 / D)
        recip(RSTD[:, :nt], R1[:, :nt], ["r1"], ["rstd"])

    dma("sp", cst[:], cst_in, [], ["cst"])
    cp("dve", cstb[:], cst[:], ["cst"], ["cstb"])
    P.add("pool", lambda e: e.memset(VA_all[:], 1.0), [], ["va"])
    dma("sp", fnf[:], final_norm.rearrange("(k p) -> p k", p=128), [], ["fnf"], allow_slow_non_contiguous=True)
    for j in range(2):
        dma("sp", cfm[:, :, j], cvec[j].rearrange("(k p) -> p k", p=128), [], ["cfm"], allow_slow_non_contiguous=True)
    act(scb[:], cfm[:], AF.Silu, ["cfm"], ["scb"])
    P.add("pool", lambda e: e.memset(SMALL[:], 0.0), [], ["small"])
    for s in seqs:
        for col in (0, s.T + 1):
            dma("sp", PRE[s.name][:, :, col:col + 1], SMALL[:, 0:12].unsqueeze(2), ["small"], [("pre", s.name, "pad", col)],
                allow_slow_non_contiguous=True)

    it = 0
    for s in seqs:
        src = xs_in if s.is_sample else xp_in[s.idx * TP:(s.idx + 1) * TP, :]
        for ti in range(s.T // 128):
            b = it % 2
            dma("sp", XTM[b], src[ti * 128:(ti + 1) * 128, :], [], ["bigA"])
            for c in range(8):
                tr(PS[b][:, c * 128:(c + 1) * 128], XTM[b][:, c * 128:(c + 1) * 128], C("ident"),
                   ["bigA", "cst"], [pk(2 * b), pk(2 * b + 1)])
            cp("act" if it % 2 == 0 else "dve", XFM[b], PS[b][:].rearrange("p (c t) -> p c t", c=8),
               [pk(2 * b), pk(2 * b + 1)], ["bigB"])
            dma("pool", XRES[s.name][:, :, ti * 128:(ti + 1) * 128], XFM[b], ["bigB"], [("xres", s.name, ti // 2)])
            it += 1

    def finish():
        P.emit(es)
        es.close()
        return nc

    if stop_after == "stage0":
        return finish()

    NLAYERS = DEPTH
    for l in range(NLAYERS):
        for (t_, src) in ((n1f, norm1[l]), (n2f, norm2[l]), (b2f, b2[l])):
            dma("sp", t_[:], src.rearrange("(k p) -> p k", p=128), [], ["lvec"], allow_slow_non_contiguous=True)
        dma("sp", b1f[:], b1[l].rearrange("(k p) -> p k", p=128), [], ["lvec"], allow_slow_non_contiguous=True)
        dma("sp", bmf[:], b_mod[l].rearrange("(k p) -> p k", p=128), [], ["lvec"], allow_slow_non_contiguous=True)
        for hh in range(2):
            dma("sp", qg[64 * hh:64 * hh + 64, :], q_gain[l].rearrange("(d o) -> d o", o=1), [], ["lvec"], allow_slow_non_contiguous=True)
            dma("sp", kg[64 * hh:64 * hh + 64, :], k_gain[l].rearrange("(d o) -> d o", o=1), [], ["lvec"], allow_slow_non_contiguous=True)
        for j in range(3):
            dma("sp", cw[:, j, :], conv_w[l, j].rearrange("(c p) -> p c", p=128), [], ["lvec"], allow_slow_non_contiguous=True)
        dma("sp", dtb[:], dt_bias[l:l + 1, :].broadcast_to([128, 16]), [], ["lvec"], allow_slow_non_contiguous=True)
        dma("sp", negA[:], a_log[l:l + 1, :].broadcast_to([128, 16]), [], ["lvec"], allow_slow_non_contiguous=True)
        dma("sp", dng[:], dn_gain[l:l + 1, :].broadcast_to([128, 64]), [], ["lvec"], allow_slow_non_contiguous=True)
        act(negA[:], negA[:], AF.Exp, ["lvec"], ["lvec2"])
        tsc("dve", negA[:], negA[:], -1.0, None, ALU.mult, None, ["lvec2"], ["lvec2"])

        Wm = WAR[:, 0:8 * 3072].rearrange("p (k n) -> p k n", k=8)
        for hh in range(2):
            load_w(Wm, w_mod[l][:, hh * 3072:(hh + 1) * 3072], 8)
            for n in range(24):
                nn = hh * 24 + n
                for k in range(8):
                    mm(psb(0)[:, nn * 2:nn * 2 + 2], Wm[:, k, n * 128:(n + 1) * 128], scb[:, k, :], k == 0, k == 7,
                       ["war", "scb"], [pk(0)])
        tt("dve", MOD[:], psb(0)[:, 0:96].rearrange("p (n j) -> p n j", j=2), bcast(bmf[:], 2, 2), ALU.add,
           [pk(0), "lvec"], ["mod"])
        stt("dve", A1[:], MOD[:, 8:16, :], 1.0, bcast(n1f[:], 2, 2), ALU.add, ALU.mult, ["mod", "lvec"], ["mod2"])
        stt("dve", A2[:], MOD[:, 32:40, :], 1.0, bcast(n2f[:], 2, 2), ALU.add, ALU.mult, ["mod", "lvec"], ["mod2"])
        tt("dve", GB2[:], MOD[:, 40:48, :], bcast(b2f[:], 2, 2), ALU.mult, ["mod", "lvec"], ["mod2"])
        MK = ["mod", "mod2", "lvec", "lvec2"]
        if stop_after == "adaln":
            dma("sp", dbg_mod, MOD[:], ["mod"], ["dbgmod"])
            return finish()

        Win = WAR[:, 0:8 * INW].rearrange("p (k n) -> p k n", k=8)
        load_w(Win, w_in[l], 8)
        bank_rr = [0]

        def next_bank():
            bank_rr[0] = (bank_rr[0] % 4) + 1
            return bank_rr[0]

        stg_rr = [0]

        def nstg():
            stg_rr[0] = (stg_rr[0] + 1) % 4
            return stg_rr[0]

        jobsA = [(s, blk) for s in seqs for blk in range(s.T // TB)]
        XTsA = [(XT, "bigA"), (X1T, "bigB")]
        HTsA = [(HB[:, 8:16, :], "ht"), (GATB[:, 0:8, :], "gatb")]

        def blockA(j):
            s, blk = jobsA[j]
            mj = s.mj
            t0 = blk * TB
            XTj, xkey = XTsA[j % 2]
            HTj, hkey = HTsA[j % 2]
            dma("sp", XTj, XRES[s.name][:, :, t0:t0 + TB], [("xres", s.name, blk)], [xkey])
            if s.is_sample:
                dma("sp", COSB[:], cos_in[:, t0:t0 + TB], [], ["cosb"])
                dma("sp", SINB[:], sin_in[:, t0:t0 + TB], [], ["sinb"])
            rms_stats(XTj, TB, xkey)
            tt("dve", XTj, XTj, bcast(RSTD[:, :], 1, 8), ALU.mult, [xkey, "rstd"], [xkey])
            for c in range(8):
                if c % 2 == 0:
                    act(HTj[:, c, :], XTj[:, c, :], AF.Identity, [xkey] + MK, [hkey],
                        scale=A1[:, c, mj:mj + 1], bias=MOD[:, c, mj:mj + 1])
                else:
                    tsc("dve", HTj[:, c, :], XTj[:, c, :], A1[:, c, mj:mj + 1], MOD[:, c, mj:mj + 1], ALU.mult, ALU.add,
                        [xkey] + MK, [hkey])

            yield

            def fm_chunk(col0):
                bk = next_bank()
                for k in range(8):
                    mm(psb(bk)[:, :TB], Win[:, k, col0:col0 + 128], HTj[:, k, :], k == 0, k == 7, ["war", hkey], [pk(bk)])
                return bk

            def qk_epi(c, bk):
                is_k = (c == 4)
                gain = kg if is_k else qg
                cp("act", STG[0][:, :TB], psb(bk)[:, :TB], [pk(bk)], ["stg0"])
                act(STB[0][:, :TB], psb(bk)[:, :TB], AF.Square, [pk(bk)], ["stb0"])
                yield
                mm(psb(6)[:, :TB], C("blk", True), STB[0][:, :TB], True, True, ["stb0", "cstb"], [pk(6)])
                act(STG[1][:, :TB], psb(6)[:, :TB], AF.Sqrt, [pk(6)], ["stg1"], bias=EPS, scale=1.0 / 64)
                recip(STG[1][:, :TB], STG[1][:, :TB], ["stg1"], ["stg1"])
                stt("dve", STG[0][:, :TB], STG[0][:, :TB], gain[:, 0:1], STG[1][:, :TB], ALU.mult, ALU.mult,
                    ["stg0", "stg1", "lvec"], ["stg0"])
                yield
                kcol = s.key0 + s.nctx + t0
                dst = KT_all[:, kcol:kcol + TB] if is_k else STB[1][:, :TB]
                dkey = "kt" if is_k else "stb1"
                if s.is_sample:
                    mm(psb(7)[:, :TB], C("rot"), STG[0][:, :TB], True, True, ["stg0", "cst"], [pk(7)])
                    tt("dve", STG[2][:, :TB], STG[0][:, :TB], COSB[:], ALU.mult, ["stg0", "cosb"], ["stg2"])
                    yield
                    tt("dve", STG[3][:, :TB], psb(7)[:, :TB], SINB[:], ALU.mult, [pk(7), "sinb"], ["stg3"])
                    tt("dve", dst, STG[2][:, :TB], STG[3][:, :TB], ALU.add, ["stg2", "stg3"], [dkey])
                else:
                    cp("dve", dst, STG[0][:, :TB], ["stg0"], [dkey])
                if not is_k:
                    dma("pool", QA[s.name][2 * c:2 * c + 2].rearrange("h d t -> (h d) t")[:, t0:t0 + TB], STB[1][:, :TB],
                        ["stb1"], [("qa", s.name)])
                elif not s.is_sample:
                    for t2 in range(TB // 128):
                        tr(psb(7)[:, t2 * 128:(t2 + 1) * 128], STG[0][:, t2 * 128:(t2 + 1) * 128], C("ident"),
                           ["stg0", "cst"], [pk(7)])
                    cp("act", STG[2][:, :TB], psb(7)[:, :TB], [pk(7)], ["stg2"])
                    for t2 in range(TB // 128):
                        for g in range(2):
                            dma("pool", nk_out[s.idx, l, g, t0 + t2 * 128:t0 + (t2 + 1) * 128, :],
                                STG[2][:, t2 * 128 + g * 64:t2 * 128 + g * 64 + 64], ["stg2"], [("nk", s.idx)])
                yield

            hrr = [0]

            def nh():
                hrr[0] = (hrr[0] + 1) % 4
                return hrr[0]

            def filler():
                for c in range(12):
                    bk = fm_chunk(C_QD + c * 128)
                    i = nh()
                    cp("act" if c % 2 == 0 else "dve", STG[i][:, 256:512], psb(bk)[:, :TB], [pk(bk)], [f"stgh{i}"])
                    dma("pool", PRE[s.name][:, c, 1 + t0:1 + t0 + TB], STG[i][:, 256:512], [f"stgh{i}"], [("pre", s.name, blk)])
                    yield
                for c in range(16):
                    bk = fm_chunk(C_GA + c * 128)
                    i = nh()
                    act(STB[i][:, 256:512], psb(bk)[:, :TB], AF.Sigmoid, [pk(bk)], [f"stbh{i}"])
                    dma("pool", GATES[s.name][:, c, t0:t0 + TB], STB[i][:, 256:512], [f"stbh{i}"], [("gates", s.name, blk)])
                    yield

            fg = filler()
            for c in range(5):
                bk = fm_chunk(C_QA + c * 128)
                for _ in qk_epi(c, bk):
                    next(fg, None)
            for _ in fg:
                pass
            yield
            for t2 in range(TB // 128):
                tsl = slice(t2 * 128, (t2 + 1) * 128)
                gti = s.tile0 + (t0 // 128) + t2
                vt = (s.key0 + s.nctx + t0) // 128 + t2
                for k in range(8):
                    mm(psb(5)[:, 0:512], HTj[:, k, tsl], Win[:, k, C_GO:C_GO + 512], k == 0, k == 7, ["war", hkey], [pk(5)])
                for k in range(8):
                    mm(psb(6)[:, 0:128], HTj[:, k, tsl], Win[:, k, C_VA:C_VA + 128], k == 0, k == 7, ["war", hkey], [pk(6)])
                for k in range(8):
                    mm(psb(6)[:, 128:160], HTj[:, k, tsl], Win[:, k, C_AI:C_AI + 32], k == 0, k == 7, ["war", hkey], [pk(6)])
                i = nstg()
                act(STB[i][:, :], psb(5)[:, :], AF.Silu, [pk(5)], [f"stb{i}", f"stbh{i}"])
                dma("pool", GS[s.name][t0 + t2 * 128:t0 + (t2 + 1) * 128, :], STB[i][:, :], [f"stb{i}"], [("gs", s.name)])
                cp("dve", VAB[:, :], psb(6)[:, 0:160], [pk(6)], ["vab"])
                cp("dve", VA_all[:, vt, :, 0:64], VAB[:, 0:128].rearrange("p (g d) -> p g d", g=2), ["vab"], ["va"])
                if not s.is_sample:
                    for g in range(2):
                        dma("pool", nv_out[s.idx, l, g, t0 + t2 * 128:t0 + (t2 + 1) * 128, :], VAB[:, g * 64:g * 64 + 64],
                            ["vab"], [("nv", s.idx)])
                tt("dve", SM[:, 0:16], VAB[:, 128:144], dtb[:], ALU.add, ["vab", "lvec"], ["sm"])
                act(SM[:, 16:32], SM[:, 0:16], AF.Exp, ["sm"], ["sm1"])
                act(SM[:, 32:48], SM[:, 16:32], AF.Ln, ["sm1"], ["sm2"], bias=1.0)
                tt("dve", LA[:, gti, :], SM[:, 32:48], negA[:], ALU.mult, ["sm2", "lvec2"], ["la"])
                act(BETA[:, gti, :], VAB[:, 144:160], AF.Sigmoid, ["vab"], ["beta"])
                act(LB[:, gti, :], BETA[:, gti, :], AF.Ln, ["beta"], ["lb"])

        gA = [blockA(j) for j in range(len(jobsA))]
        next(gA[0])
        for j in range(len(jobsA)):
            next(gA[j])
            if j + 1 < len(jobsA):
                next(gA[j + 1])
            for _ in gA[j]:
                pass
        if "dbg_la" in dbg:
            dma("sp", dbg_la, LA[:], ["la"], ["dbgla"])
            dma("sp", dbg_lb, LB[:], ["lb"], ["dbglb"])
            dma("sp", dbg_beta, BETA[:], ["beta"], ["dbgbeta"])
        P.fence()
        if stop_after == "stageA":
            return finish()

        for s in seqs:
            for blk in range(s.T // TB):
                t0 = blk * TB
                dma("sp", PRET, PRE[s.name][:, :, t0:t0 + TB + 2],
                    [("pre", s.name, b_) for b_ in range(max(0, blk - 1), min(s.T // TB, blk + 2))]
                    + [("pre", s.name, "pad", 0), ("pre", s.name, "pad", s.T + 1)], ["bigA"])
                for c in range(12):
                    e_ = "dve"
                    tsc(e_, CV[:, c, :], PRET[:, c, 0:TB], cw[:, 0, c:c + 1], None, ALU.mult, None, ["bigA", "lvec"], [("cv", c)])
                    stt(e_, CV[:, c, :], PRET[:, c, 1:TB + 1], cw[:, 1, c:c + 1], CV[:, c, :], ALU.mult, ALU.add,
                        ["bigA", "lvec", ("cv", c)], [("cv", c)])
                    stt(e_, CV[:, c, :], PRET[:, c, 2:TB + 2], cw[:, 2, c:c + 1], CV[:, c, :], ALU.mult, ALU.add,
                        ["bigA", "lvec", ("cv", c)], [("cv", c)])
                for c in range(12):
                    act(CV[:, c, :], CV[:, c, :], AF.Silu, [("cv", c)], [("cv", c)])
                for c in range(8):
                    act(STB[0][:, :TB], CV[:, c, :], AF.Square, [("cv", c)], ["stb0"])
                    mm(psb(0)[:, :TB], C("blk", True), STB[0][:, :TB], True, True, ["stb0", "cstb"], [pk(0)])
                    act(STG[0][:, :TB], psb(0)[:, :TB], AF.Sqrt, [pk(0)], ["stg0"], bias=EPS, scale=1.0)
                    recip(STG[0][:, :TB], STG[0][:, :TB], ["stg0"], ["stg0"])
                    stt("dve", CV[:, c, :], CV[:, c, :], 0.125 if c < 4 else 1.0, STG[0][:, :TB], ALU.mult, ALU.mult,
                        [("cv", c), "stg0"], [("cv", c)])
                    i = nstg()
                    cp("act", STB[i][:, :TB], CV[:, c, :], [("cv", c)], [f"stb{i}"])
                    dstT = QDT if c < 4 else KDT
                    cc = c % 4
                    dma("pool", dstT[s.name][2 * cc:2 * cc + 2].rearrange("h d t -> (h d) t")[:, t0:t0 + TB], STB[i][:, :TB],
                        [f"stb{i}"], [("qkdt", s.name)])
                for t2 in range(TB // 128):
                    for grp, dstM in enumerate((QTM, KTM, VTM)):
                        bk = 1 + (grp % 2)
                        for cc in range(4):
                            tr(psb(bk)[:, cc * 128:(cc + 1) * 128], CV[:, grp * 4 + cc, t2 * 128:(t2 + 1) * 128], C("ident"),
                               [("cv", grp * 4 + cc), "cst"], [pk(bk)])
                        i = nstg()
                        cp("act" if grp % 2 == 0 else "dve", STB[i][:, :], psb(bk)[:, :], [pk(bk)], [f"stb{i}"])
                        dma("pool", dstM[s.name][t0 + t2 * 128:t0 + (t2 + 1) * 128, :], STB[i][:, :], [f"stb{i}"], [("tm", s.name)])
        P.fence()
        if stop_after == "stageB":
            return finish()

        for t2 in range(NCTX // 128):
            dma("sp", STG[0][:, 0:128].rearrange("p (g d) -> p g d", g=2),
                ck_in[l, :, t2 * 128:(t2 + 1) * 128, :].rearrange("g p d -> p g d"), [], ["stg0"])
            tr(psb(0)[:, 0:128], STG[0][:, 0:128], C("ident"), ["stg0", "cst"], [pk(0)])
            cp("dve", KT_all[:, t2 * 128:(t2 + 1) * 128], psb(0)[:, 0:128], [pk(0)], ["kt"])
            dma("sp", STG[1][:, 0:128].rearrange("p (g d) -> p g d", g=2),
                cv_in[l, :, t2 * 128:(t2 + 1) * 128, :].rearrange("g p d -> p g d"), [], ["stg1"])
            cp("dve", VA_all[:, t2, :, 0:64], STG[1][:, 0:128].rearrange("p (g d) -> p g d", g=2), ["stg1"], ["va"])
        QBz = [[WAR[:, (qp * 2 + g) * 512:(qp * 2 + g + 1) * 512] for g in range(2)] for qp in range(2)]
        VAp = WAR[:, 2048:2048 + NVT * 256].rearrange("p (t g d) -> p t g d", t=NVT, g=2)
        P.add("pool", lambda e: e.memset(WAR[:, 0:2048 + NVT * 256], 0.0), [], ["qbz", "vap"])
        cp("pool", VAp[:, :, :, 0:65], VA_all[:, :, :, :], ["va", "vap"], ["vap"])
        items = []
        qcount = 0
        for s in seqs:
            ktiles = []
            if s.is_sample:
                ktiles += list(range(NCTX // 128))
            ktiles += [(s.key0 + s.nctx) // 128 + i for i in range(s.T // 128)]
            for qi in range(s.T // 128):
                for n_, kt in enumerate(ktiles):
                    items.append((s, qi, n_, kt, n_ == 0, n_ == len(ktiles) - 1, qcount % 2))
                qcount += 1
        LAG = 1
        for idx in range(len(items) + LAG):
            if idx < len(items):
                s, qi, n_, kt, first, last, qp = items[idx]
                q0 = qi * 128
                if first:
                    for g2 in range(2):
                        dma("sp", QBz[qp][g2][64 * g2:64 * g2 + 64, :].rearrange("p (j t) -> p j t", j=4),
                            QA[s.name][4 * g2:4 * g2 + 4, :, q0:q0 + 128].rearrange("j d t -> d j t"),
                            [("qa", s.name), "qbz"], [("qb", qp)])
                r_ = idx % 2
                for g in range(2):
                    mm(PS[r_][:, g * 512:(g + 1) * 512], KT_all[:, kt * 128:(kt + 1) * 128],
                       QBz[qp][g], True, True, ["kt", ("qb", qp)],
                       [pk(2 * r_), pk(2 * r_ + 1)])
                act(XBW[r_][:, :], PS[r_][:, :], AF.Exp, [pk(2 * r_), pk(2 * r_ + 1)], [("ptt", r_)], scale=0.125)
            if idx >= LAG:
                s, qi, n_, kt, first, last, qp = items[idx - LAG]
                q0 = qi * 128
                r_ = (idx - LAG) % 2
                for g in range(2):
                    ob = 4 + g
                    mm(psb(ob)[:, :], VAp[:, kt, g, :], XBW[r_][:, g * 512:(g + 1) * 512], first, last,
                       ["vap", ("ptt", r_)], [pk(ob)])
                if last:
                    for g in range(2):
                        ob = 4 + g
                        cp("dve", RS[64:65, :], psb(ob)[64:65, :], [pk(ob)], ["rs"])
                        recip(RS[64:65, :], RS[64:65, :], ["rs"], ["rs"])
                        mm(psb(6)[0:64, :], C("ones")[64:65, 0:64], RS[64:65, :], True, True, ["rs", "cst"], [pk(6)])
                        cp("dve", STG[0][0:64, :], psb(ob)[0:64, :], [pk(ob)], ["stg0"])
                        tt("dve", OB[:, :], STG[0][0:64, :], psb(6)[0:64, :], ALU.mult, ["stg0", pk(6)], ["ob"])
                        dma("sp", OA[s.name][4 * g:4 * g + 4, :, q0:q0 + 128].rearrange("j d t -> d j t"),
                            OB[:, :].rearrange("p (j t) -> p j t", j=4), ["ob"], [("oa", s.name)])
        P.fence()
        if stop_after == "attn":
            return finish()

        H8 = 8
        CUT = 0

        def v3(t):
            return t.rearrange("p (h j) -> p h j", h=H8)

        woff = [0]

        def wtake(n, f32=False):
            ap = WAR[:, woff[0]:woff[0] + n]
            woff[0] += n
            return ap.bitcast(F32) if f32 else ap

        DS = [dict(SM=SM, DG1=DG1, DG3=DG3, DE=DE, E1=E1, E2=E2, E3=E3, PA=PA, PT=PT, RT=RT, XB=XB, BEK=BEK, BV=BV), None]
        DS[1] = dict(E1=wtake(1024, True), E2=wtake(1024, True), E3=wtake(1024, True), DG1=wtake(1024, True),
                     DG3=wtake(1024, True), SM=wtake(320, True),
                     PA=[wtake(512), wtake(512)], PT=[wtake(512), wtake(512)], RT=[wtake(512), wtake(512)],
                     XB=[wtake(512) for _ in range(4)], BEK=wtake(512), BV=wtake(512), DE=wtake(512))
        def w64(n):
            ap = WAR[0:64, woff[0]:woff[0] + n]
            woff[0] += n
            return ap.rearrange("p (x i) -> p x i", x=16)

        NWT2 = [[NWT[d][:, :, :], w64(1024)] for d in range(2)]
        QDEC2 = [[QDEC[d][:, :, :], w64(1024)] for d in range(2)]
        U02 = [[U0[d], wtake(1024, True)] for d in range(2)]
        QKM2 = [[QKM[d], wtake(512)] for d in range(2)]
        KDEC2 = [[KDEC[d], wtake(512)] for d in range(2)]
        EGL22 = [[EGL2[d][:, :], wtake(32, True)] for d in range(2)]
        ab = [0]
        fbk = [0]
        ppr = [0]

        def abank():
            ab[0] = (ab[0] + 1) % 6
            return ab[0]

        def fbank():
            fbk[0] ^= 1
            return 6 + fbk[0]

        def ppair():
            ppr[0] = (ppr[0] + 1) % 3
            return ppr[0]

        idb = bcast(C("identst", True), 1, H8)
        ist = bcast(C("identst"), 1, H8)

        def msk(name):
            return bcast(C(name, True), 1, H8)

        def prep(s, d, m, par):
            NWTp, QDECp, U0p, QKMp, KDECp, EGL2p = NWT2[d][par], QDEC2[d][par], U02[d][par], QKM2[d][par], KDEC2[d][par], EGL22[d][par]
            kq = (d, par)
            T_ = DS[d]
            SMd, DG1d, DG3d, DEd = T_["SM"], T_["DG1"], T_["DG3"], T_["DE"]
            E1d, E2d, E3d = T_["E1"], T_["E2"], T_["E3"]
            PAd, PTd, RTd, XBd, BEKd, BVd = T_["PA"], T_["PT"], T_["RT"], T_["XB"], T_["BEK"], T_["BV"]

            def K(name, *x):
                return (name, d) + tuple(x)

            gti = s.tile0 + m
            sfx = "f" if d == 0 else "b"
            rows = slice(m * 128, (m + 1) * 128)
            dma("sp", KTC[d][:, :, :], KDT[s.name][:, :, rows].rearrange("h d t -> d h t"), [("qkdt", s.name)], [("ktc", d)])
            dma("sp", QTC[d][:, :, :], QDT[s.name][:, :, rows].rearrange("h d t -> d h t"), [("qkdt", s.name)], [("qtc", d)])
            dma("sp", KTMC[d][:, :], KTM[s.name][rows, :], [("tm", s.name)], [("ktmc", d)])
            dma("sp", QTMC[d][:, :], QTM[s.name][rows, :], [("tm", s.name)], [("qtmc", d)])
            dma("sp", VTMC[d][:, :], VTM[s.name][rows, :], [("tm", s.name)], [("vtmc", d)])
            la = LA[:, gti, 8 * d:8 * d + 8]
            lb = LB[:, gti, 8 * d:8 * d + 8]
            be_ = BETA[:, gti, 8 * d:8 * d + 8]
            bg = fbank()
            mm(psb(bg)[:, 0:8], C("tri_" + sfx), la, True, True, ["la", "cst"], [pk(bg)])
            mm(psb(bg)[:, 8:16], C("half0"), la, True, True, ["la", "cst"], [pk(bg)])
            mm(psb(bg)[:, 16:24], C("half1"), la, True, True, ["la", "cst"], [pk(bg)])
            cp("dve", SMd[:, 0:24], psb(bg)[:, 0:24], [pk(bg)], [K("sm")])
            yield
            tt("dve", SMd[:, 24:32], SMd[:, 0:8], lb, ALU.add, [K("sm"), "lb"], [K("sm_glb")])
            act(SMd[:, 32:40], SMd[:, 0:8], AF.Exp, [K("sm")], [K("sm_eg")])
            cp("dve", SMd[0:64, 40:48], SMd[0:64, 8:16], [K("sm")], [K("sm_glo")])
            cp("dve", SMd[64:128, 40:48], SMd[64:128, 16:24], [K("sm")], [K("sm_glo")])
            tt("dve", SMd[:, 48:56], SMd[:, 40:48], SMd[:, 0:8], ALU.subtract, [K("sm"), K("sm_glo")], [K("sm_ek")])
            act(SMd[:, 48:56], SMd[:, 48:56], AF.Exp, [K("sm_ek")], [K("sm_ek")])
            act(EGL2p, SMd[:, 8:24], AF.Exp, [K("sm")], [("egl2",) + kq])
            tt("dve", SMd[:, 56:64], be_, SMd[:, 32:40], ALU.mult, ["beta", K("sm_eg")], [K("sm_be")])
            tsc("dve", SMd[:, 64:72], SMd[:, 0:8], -1.0, None, ALU.mult, None, [K("sm")], [K("sm_ng")])
            tt("pool", v3(DG1d), ist, bcast(SMd[:, 24:32], 2, 64), ALU.mult, ["cst", K("sm_glb")], [K("dg1")])
            tt("pool", v3(DG3d), ist, bcast(SMd[:, 0:8], 2, 64), ALU.mult, ["cst", K("sm")], [K("dg3")])
            tt("dve", v3(DEd), ist, bcast(SMd[:, 32:40], 2, 64), ALU.mult, ["cst", K("sm_eg")], [K("de")])
            yield
            b1 = fbank()
            mm(psb(b1)[:, :], C("negblk"), DG3d, True, False, [K("dg3"), "cst"], [pk(b1)])
            mm(psb(b1)[:, :], C("ident"), bcast(SMd[:, 24:32], 2, 64), False, False, [K("sm_glb"), "cst"], [pk(b1)])
            mm(psb(b1)[:, :], C("ident", True), msk("m1_" + sfx), False, True, ["cstb"], [pk(b1)])
            act(E1d, psb(b1)[:, :], AF.Exp, [pk(b1)], [K("e1")])
            yield
            b2 = fbank()
            mm(psb(b2)[:, :], C("blk"), DG1d, True, False, [K("dg1"), "cst"], [pk(b2)])
            mm(psb(b2)[:, :], C("ident"), bcast(SMd[:, 64:72], 2, 64), False, False, [K("sm_ng"), "cst"], [pk(b2)])
            mm(psb(b2)[:, :], C("ident", True), msk("m2_" + sfx), False, True, ["cstb"], [pk(b2)])
            act(E2d, psb(b2)[:, :], AF.Exp, [pk(b2)], [K("e2")])
            yield
            b3 = fbank()
            mm(psb(b3)[:, :], C("blk"), DG3d, True, False, [K("dg3"), "cst"], [pk(b3)])
            mm(psb(b3)[:, :], C("ident"), bcast(SMd[:, 64:72], 2, 64), False, False, [K("sm_ng"), "cst"], [pk(b3)])
            mm(psb(b3)[:, :], C("ident", True), msk("m3_" + sfx), False, True, ["cstb"], [pk(b3)])
            act(E3d, psb(b3)[:, :], AF.Exp, [pk(b3)], [K("e3")])
            yield
            bkk, bqk = abank(), abank()
            for h in range(H8):
                for a in range(2):
                    ts_ = slice(64 * a, 64 * a + 64)
                    mm(psb(bkk)[ts_, h * 64:(h + 1) * 64], KTC[d][:, h, ts_], KTC[d][:, h, ts_], True, True,
                       [("ktc", d)], [pk(bkk)], tile_position=(0, 64 * a))
                    mm(psb(bqk)[ts_, h * 64:(h + 1) * 64], KTC[d][:, h, ts_], QTC[d][:, h, ts_], True, True,
                       [("ktc", d), ("qtc", d)], [pk(bqk)], tile_position=(0, 64 * a))
            A_, AT_ = PAd[0], PTd[0]
            kA, kAT = K("pa", 0), K("pt", 0)
            tt("dve", A_, psb(bkk)[:, :], E1d, ALU.mult, [pk(bkk), K("e1")], [kA])
            tt("dve", AT_, psb(bkk)[:, :], E2d, ALU.mult, [pk(bkk), K("e2")], [kAT])
            tt("dve", QKMp, psb(bqk)[:, :], E3d, ALU.mult, [pk(bqk), K("e3")], [("qkm",) + kq])
            yield

            def grp(L, R, lkey, rkey):
                bank = abank()
                for h in range(H8):
                    for a in range(2):
                        ts_ = slice(64 * a, 64 * a + 64)
                        hs = slice(h * 64, (h + 1) * 64)
                        mm(psb(bank)[ts_, hs], L[ts_, hs], R[ts_, hs], True, True, [lkey, rkey], [pk(bank)],
                           tile_position=(64 * a, 64 * a))
                return bank

            D_, DT_ = PAd[1], PTd[1]
            kD, kDT = K("pa", 1), K("pt", 1)
            X = [XBd[0], XBd[1], BEKd, BVd, XBd[2], XBd[3]]
            kX = [K("xb", 0), K("xb", 1), K("bek"), K("bv"), K("xb", 2), K("xb", 3)]
            kR = [K("rt", 0), K("rt", 1)]
            tt("pool", v3(D_), v3(A_), msk("mask8"), ALU.mult, [kA, "cstb"], [kD])
            tt("pool", v3(DT_), v3(AT_), msk("mask8"), ALU.mult, [kAT, "cstb"], [kDT])
            tt("dve", v3(X[2]), idb, v3(DT_), ALU.subtract, ["cstb", kDT], [kX[2]])
            yield
            g1 = grp(DT_, D_, kDT, kD)
            g2 = grp(D_, DT_, kD, kDT)
            cp("act", X[0], psb(g1)[:, :], [pk(g1)], [kX[0]])
            tt("dve", v3(RTd[1]), v3(X[0]), idb, ALU.add, [kX[0], "cstb"], [kR[1]])
            cp("dve", RTd[0], psb(g2)[:, :], [pk(g2)], [kR[0]])
            yield
            g3 = grp(RTd[0], X[0], kR[0], kX[0])
            tt("dve", v3(X[1]), v3(psb(g3)[:, :]), idb, ALU.add, [pk(g3), "cstb"], [kX[1]])
            g1 = grp(RTd[1], X[2], kR[1], kX[2])
            cp("act", X[3], psb(g1)[:, :], [pk(g1)], [kX[3]])
            yield
            g2 = grp(X[3], X[1], kX[3], kX[1])
            g3 = grp(X[1], X[3], kX[1], kX[3])
            cp("act", X[4], psb(g2)[:, :], [pk(g2)], [kX[4]])
            cp("dve", X[5], psb(g3)[:, :], [pk(g3)], [kX[5]])
            yield
            Tb, kT = [X[4], X[0]], [kX[4], kX[0]]
            Mb, kM = [X[5], X[1]], [kX[5], kX[1]]
            cur = 0
            for li, mname in enumerate(("moff8", "moff16", "moff32")):
                last = (li == 2)
                nxt = 1 - cur
                tt("pool", v3(D_), v3(A_), msk(mname), ALU.mult, [kA, "cstb"], [kD])
                if not last:
                    tt("pool", v3(DT_), v3(AT_), msk(mname), ALU.mult, [kAT, "cstb"], [kDT])
                g1 = grp(D_, Mb[cur], kD, kM[cur])
                cp("act", RTd[1], psb(g1)[:, :], [pk(g1)], [kR[1]])
                if not last:
                    g2 = grp(DT_, Tb[cur], kDT, kT[cur])
                    cp("dve", RTd[0], psb(g2)[:, :], [pk(g2)], [kR[0]])
                yield
                g3 = grp(Tb[cur], RTd[1], kT[cur], kR[1])
                tt("dve", Mb[nxt], Mb[cur], psb(g3)[:, :], ALU.subtract, [kM[cur], pk(g3)], [kM[nxt]])
                if not last:
                    g1 = grp(Mb[cur], RTd[0], kM[cur], kR[0])
                    tt("dve", Tb[nxt], Tb[cur], psb(g1)[:, :], ALU.subtract, [kT[cur], pk(g1)], [kT[nxt]])
                cur = nxt
                yield
            TTm = Mb[cur]
            tkey = kM[cur]
            tt("dve", v3(BEKd), v3(KTMC[d][:, :]), bcast(SMd[:, 56:64], 2, 64), ALU.mult, [("ktmc", d), K("sm_be")], [K("bek")])
            tt("pool", v3(KDECp), v3(KTMC[d][:, :]), bcast(SMd[:, 48:56], 2, 64), ALU.mult, [("ktmc", d), K("sm_ek")], [("kdec",) + kq])
            tt("pool", v3(BVd), v3(VTMC[d][:, :]), bcast(be_, 2, 64), ALU.mult, [("vtmc", d), "beta"], [K("bv")])
            yield
            pp_ = ppair()
            pkeys = [pk(2 * pp_), pk(2 * pp_ + 1)]
            bu0 = abank()
            while bu0 in (2 * pp_, 2 * pp_ + 1):
                bu0 = abank()
            for h in range(H8):
                for a in range(2):
                    ts_ = slice(64 * a, 64 * a + 64)
                    hs = slice(h * 64, (h + 1) * 64)
                    cs = slice((a * 8 + h) * 64, (a * 8 + h) * 64 + 64)
                    mm(PS[pp_][0:64, cs], BEKd[ts_, hs], TTm[ts_, hs], True, True, [K("bek"), tkey], pkeys,
                       tile_position=(64 * a, 0))
                    mm(psb(bu0)[ts_, hs], TTm[ts_, hs], BVd[ts_, hs], True, True, [tkey, K("bv")], [pk(bu0)],
                       tile_position=(64 * a, 64 * a))
            tsc("dve", NWTp, PS[pp_][0:64, :].rearrange("p (x i) -> p x i", x=16), -1.0, None, ALU.mult, None,
                pkeys, [("nwt",) + kq])
            cp("act", U0p, psb(bu0)[:, :], [pk(bu0)], [("u0",) + kq])
            yield
            pp_ = ppair()
            pkeys = [pk(2 * pp_), pk(2 * pp_ + 1)]
            for a in range(2):
                mm(PS[pp_][0:64, a * 512:(a + 1) * 512], C("half%d" % a, True)[:, 0:64], DEd, True, True,
                   [K("de"), "cstb"], pkeys)
            for a in range(2):
                tt("dve", QDECp[:, a * 8:(a + 1) * 8, :],
                   QTC[d][:, :, a * 64:(a + 1) * 64],
                   PS[pp_][0:64, a * 512:(a + 1) * 512].rearrange("p (h i) -> p h i", h=8), ALU.mult,
                   [("qtc", d)] + pkeys, [("qdec",) + kq])
            yield

        def steps(s, d, m, par, first_visit):
            NWTp, QDECp, U0p, QKMp, KDECp, EGL2p = NWT2[d][par], QDEC2[d][par], U02[d][par], QKM2[d][par], KDEC2[d][par], EGL22[d][par]
            kq = (d, par)
            SMd = DS[d]["SM"]

            def K(name, *x):
                return (name, d) + tuple(x)

            rows = slice(m * 128, (m + 1) * 128)
            for a in ((0, 1) if d == 0 else (1, 0)):
                ts_ = slice(64 * a, 64 * a + 64)
                pu = abank()
                for h in range(H8):
                    hs = slice(h * 64, (h + 1) * 64)
                    mm(psb(pu)[ts_, hs], NWTp[:, a * 8 + h, :], SBF[d][:, h, :], True, True,
                       [("nwt",) + kq, ("sbf", d)], [pk(pu)], tile_position=(0, 64 * a))
                tt("dve", UB[d][ts_, :], U0p[ts_, :], psb(pu)[ts_, :], ALU.add, [("u0",) + kq, pk(pu)], [("ub", d)])
                yield
                po, pob, pS_ = abank(), abank(), abank()
                for h in range(H8):
                    hs = slice(h * 64, (h + 1) * 64)
                    mm(psb(po)[ts_, hs], QDECp[:, a * 8 + h, :], SBF[d][:, h, :], True, True,
                       [("qdec",) + kq, ("sbf", d)], [pk(po)], tile_position=(0, 64 * a))
                    mm(psb(pob)[ts_, hs], QKMp[ts_, hs], UB[d][ts_, hs], True, True,
                       [("qkm",) + kq, ("ub", d)], [pk(pob)], tile_position=(64 * a, 64 * a))
                    mm(psb(pS_)[0:64, hs], KDECp[ts_, hs], UB[d][ts_, hs], True, True,
                       [("kdec",) + kq, ("ub", d)], [pk(pS_)], tile_position=(64 * a, 0))
                tt("dve", S32[d][:, :, :], S32[d][:, :, :], bcast(EGL2p[0:64, a * 8:a * 8 + 8], 2, 64), ALU.mult,
                   [("s32", d), ("egl2",) + kq], [("s32", d)])
                tt("dve", S32[d][:, :, :], S32[d][:, :, :], psb(pS_)[0:64, :].rearrange("p (h v) -> p h v", h=H8), ALU.add,
                   [("s32", d), pk(pS_)], [("s32", d)])
                cp("act", SBF[d][:, :, :], S32[d][:, :, :], [("s32", d)], [("sbf", d)])
                cp("act", OACC[d][ts_, :], psb(po)[ts_, :], [pk(po)], [("oacc", d)])
                tt("dve", OACC[d][ts_, :], OACC[d][ts_, :], psb(pob)[ts_, :], ALU.add, [("oacc", d), pk(pob)], [("oacc", d)])
                yield
            if first_visit:
                dma("sp", OPART[s.name][rows, :], OACC[d][:, :], [("oacc", d)], [("opart", s.name, m)])
            else:
                dma("sp", STG[d][:, :], OPART[s.name][rows, :], [("opart", s.name, m)], [f"stg{d}"])
                tt("dve", OACC[d][:, :], OACC[d][:, :], STG[d][:, :], ALU.add, [("oacc", d), f"stg{d}"], [("oacc", d)])
                tt("pool", STG[2 + d][:, :], OACC[d][:, :], OACC[d][:, :], ALU.mult, [("oacc", d)], [f"stg{2 + d}"])
                P.add("dve", lambda e: e.tensor_reduce(SMd[:, 80:88], v3(STG[2 + d][:, :]), AX.X, ALU.add),
                      [f"stg{2 + d}"], [K("sm_rn")])
                act(SMd[:, 80:88], SMd[:, 80:88], AF.Sqrt, [K("sm_rn")], [K("sm_rn")], bias=EPS, scale=1.0 / 64)
                recip(SMd[:, 80:88], SMd[:, 80:88], [K("sm_rn")], [K("sm_rn")])
                yield
                tt("dve", v3(OACC[d][:, :]), v3(OACC[d][:, :]), bcast(SMd[:, 80:88], 2, 64), ALU.mult,
                   [("oacc", d), K("sm_rn")], [("oacc", d)])
                tt("dve", v3(OACC[d][:, :]), v3(OACC[d][:, :]), bcast(dng[:], 1, H8), ALU.mult,
                   [("oacc", d), "lvec"], [("oacc", d)])
                dma("sp", STB[d][:, :], GS[s.name][rows, :], [("gs", s.name)], [f"stb{d}"])
                tt("dve", OACC[d][:, :], OACC[d][:, :], STB[d][:, :], ALU.mult, [("oacc", d), f"stb{d}"], [("oacc", d)])
                bt = fbank()
                for cc in range(4):
                    tr(psb(bt)[:, cc * 128:(cc + 1) * 128], OACC[d][:, cc * 128:(cc + 1) * 128], C("ident"),
                       [("oacc", d), "cst"], [pk(bt)])
                cp("act", STB[2 + d][:, :], psb(bt)[:, :], [pk(bt)], [f"stb{2 + d}"])
                dma("sp", OD[s.name][:, :, rows], STB[2 + d][:, :].rearrange("p (c t) -> p c t", c=4), [f"stb{2 + d}"],
                    [("od", s.name)])

        for s in seqs:
            NP_ = s.T // 128
            for d in range(2):
                if s.is_sample:
                    dma("sp", S32[d][:, :, :], st_in[l, d].rearrange("h k v -> k h v"), [], [("s32", d)])
                else:
                    P.add("pool", lambda e, d=d: e.memset(S32[d][:, :, :], 0.0), [], [("s32", d)])
                cp("act", SBF[d][:, :, :], S32[d][:, :, :], [("s32", d)], [("sbf", d)])
            visited = set()

            def mof(d, step):
                return step if d == 0 else NP_ - 1 - step

            def rr(gens):
                active = list(gens)
                while active:
                    for g_ in list(active):
                        try:
                            next(g_)
                        except StopIteration:
                            active.remove(g_)

            rr([prep(s, d, mof(d, 0), 0) for d in range(2)])
            for step in range(NP_):
                gens = []
                for d in range(2):
                    m = mof(d, step)
                    gens.append(steps(s, d, m, step % 2, m not in visited))
                for d in range(2):
                    visited.add(mof(d, step))
                if step + 1 < NP_:
                    for d in range(2):
                        gens.append(prep(s, d, mof(d, step + 1), (step + 1) % 2))
                rr(gens)
            if not s.is_sample:
                for d in range(2):
                    dma("sp", nst_out[s.idx, l, d].rearrange("h k v -> k h v"), S32[d][:, :, :], [("s32", d)], [("nst", s.idx)])
        P.fence()
        if stop_after == "scan":
            return finish()

        Wpa = WAR[:, 0:4096].rearrange("p (k n) -> p k n", k=4)
        Wpd = WAR[:, 4096:8192].rearrange("p (k n) -> p k n", k=4)
        Wo = WAR[:, 8192:16384].rearrange("p (k n) -> p k n", k=8)
        load_w(Wpa, w_pa[l], 4)
        load_w(Wpd, w_pd[l], 4)
        load_w(Wo, w_out[l], 8)
        OAT = HB[:, 0:4, :]
        ODT = HB[:, 4:8, :]
        MG = HB[:, 8:16, :]
        for s in seqs:
            mj = s.mj
            for blk in range(s.T // TB):
                t0 = blk * TB
                for c in range(4):
                    dma("sp", OAT[:, c, :], OA[s.name][2 * c:2 * c + 2].rearrange("h d t -> (h d) t")[:, t0:t0 + TB],
                        [("oa", s.name)], ["hb"])
                dma("sp", ODT, OD[s.name][:, :, t0:t0 + TB], [("od", s.name)], ["hb"])
                dma("sp", GATB[:, :, :], GATES[s.name][:, :, t0:t0 + TB], [("gates", s.name, blk)], ["gatb"])
                dma("sp", XT, XRES[s.name][:, :, t0:t0 + TB], [("xres", s.name, blk)], ["bigA"])
                for n in range(8):
                    ns = slice(n * 128, (n + 1) * 128)
                    for k in range(4):
                        mm(psb(0)[:, :TB], Wpa[:, k, ns], OAT[:, k, :], k == 0, k == 3, ["war", "hb"], [pk(0)])
                    for k in range(4):
                        mm(psb(1)[:, :TB], Wpd[:, k, ns], ODT[:, k, :], k == 0, k == 3, ["war", "hb"], [pk(1)])
                    tt("dve", STG[0][:, :TB], psb(0)[:, :TB], GATB[:, n, :], ALU.mult, [pk(0), "gatb"], ["stg0"])
                    tt("dve", STG[1][:, :TB], psb(1)[:, :TB], GATB[:, 8 + n, :], ALU.mult, [pk(1), "gatb"], ["stg1"])
                    tt("pool", MG[:, n, :], STG[0][:, :TB], STG[1][:, :TB], ALU.add, ["stg0", "stg1"], ["ht"])
                for n in range(8):
                    ns = slice(n * 128, (n + 1) * 128)
                    bk = 2 + n % 2
                    for k in range(8):
                        mm(psb(bk)[:, :TB], Wo[:, k, ns], MG[:, k, :], k == 0, k == 7, ["war", "ht"], [pk(bk)])
                    stt("dve", X1T[:, n, :], psb(bk)[:, :TB], MOD[:, 16 + n, mj:mj + 1], XT[:, n, :], ALU.mult, ALU.add,
                        [pk(bk), "bigA"] + MK, ["bigB"])
                dma("pool", XRES[s.name][:, :, t0:t0 + TB], X1T, ["bigB"], [("xres", s.name, blk)])
                rms_stats(X1T, TB, "bigB")
                tt("dve", XT, X1T, bcast(RSTD[:, :], 1, 8), ALU.mult, ["bigB", "rstd"], ["bigA"])
                for c in range(8):
                    if c % 2 == 0:
                        act(HT[:, c, :], XT[:, c, :], AF.Identity, ["bigA"] + MK, ["ht"],
                            scale=A2[:, c, mj:mj + 1], bias=MOD[:, 24 + c, mj:mj + 1])
                    else:
                        tsc("dve", HT[:, c, :], XT[:, c, :], A2[:, c, mj:mj + 1], MOD[:, 24 + c, mj:mj + 1], ALU.mult, ALU.add,
                            ["bigA"] + MK, ["ht"])
                dma("pool", H2[s.name][:, :, t0:t0 + TB], HT, ["ht"], [("h2", s.name, blk)])
        P.fence()
        if stop_after == "stageD":
            return finish()

        W1h = WAR[:, 0:8 * 2048].rearrange("p (k n) -> p k n", k=8)
        W2h = WAR[:, 16384:16384 + 16 * 1024].rearrange("p (k n) -> p k n", k=16)
        for hf in range(2):
            load_w(W1h, w1[l][:, hf * 2048:(hf + 1) * 2048], 8)
            load_w(W2h, w2[l][hf * 2048:(hf + 1) * 2048, :], 16)
            jobs = [(s, blk) for s in seqs for blk in range(s.T // TB)]
            XTs = [(XT, "bigA"), (X1T, "bigB")]
            H2Ts = [(HB[:, 0:8, :], "hb"), (HB[:, 8:16, :], "ht")]

            def ef_loads(j):
                s, blk = jobs[j]
                t0 = blk * TB
                h2t, hkey = H2Ts[j % 2]
                xt, xkey = XTs[j % 2]
                dma("sp", h2t, H2[s.name][:, :, t0:t0 + TB], [("h2", s.name, blk)], [hkey])
                dma("sp", xt, XRES[s.name][:, :, t0:t0 + TB], [("xres", s.name, blk)], [xkey])

            ef_loads(0)
            for j, (s, blk) in enumerate(jobs):
                mj = s.mj
                t0 = blk * TB
                H2T, hkey = H2Ts[j % 2]
                XTj, xkey = XTs[j % 2]
                if j + 1 < len(jobs):
                    ef_loads(j + 1)
                for f in range(16):
                    bk = 1 + f % 3
                    for k in range(8):
                        mm(psb(bk)[:, :TB], W1h[:, k, f * 128:(f + 1) * 128], H2T[:, k, :], k == 0, k == 7, ["war", hkey], [pk(bk)])
                    i = nstg()
                    act(STG[i][:, :TB], psb(bk)[:, :TB], AF.Relu, [pk(bk), "lvec"], [f"stg{i}"],
                        bias=b1f[:, hf * 16 + f:hf * 16 + f + 1])
                    tt("dve" if f % 2 == 0 else "pool", GATB[:, f, :], STG[i][:, :TB], STG[i][:, :TB], ALU.mult,
                       [f"stg{i}"], ["gatb"])
                for n in range(8):
                    bk = 4 + n % 2
                    for f in range(16):
                        mm(psb(bk)[:, :TB], W2h[:, f, n * 128:(n + 1) * 128], GATB[:, f, :], f == 0, f == 15, ["war", "gatb"], [pk(bk)])
                    stt("dve", XTj[:, n, :], psb(bk)[:, :TB], MOD[:, 40 + n, mj:mj + 1], XTj[:, n, :], ALU.mult, ALU.add,
                        [pk(bk), xkey] + MK, [xkey])
                    if hf == 0:
                        tsc("dve", XTj[:, n, :], XTj[:, n, :], GB2[:, n, mj:mj + 1], None, ALU.add, None, [xkey] + MK, [xkey])
                if not (l == NLAYERS - 1 and hf == 1):
                    dma("pool", XRES[s.name][:, :, t0:t0 + TB], XTj, [xkey], [("xres", s.name, blk)])
                else:
                    rms_stats(XTj, TB, xkey, sq=GATB[:, 0:8, :], sqkey="gatb")
                    tt("dve", XTj, XTj, bcast(RSTD[:, :], 1, 8), ALU.mult, [xkey, "rstd"], [xkey])
                    tt("dve", XTj, XTj, bcast(fnf[:], 2, TB), ALU.mult, [xkey, "fnf"], [xkey])
                    dst = ys_out if s.is_sample else yp_out[s.idx * TP:(s.idx + 1) * TP, :]
                    for t2 in range(TB // 128):
                        for hh in range(2):
                            bk = 6 + hh
                            for c in range(4):
                                tr(psb(bk)[:, c * 128:(c + 1) * 128], XTj[:, hh * 4 + c, t2 * 128:(t2 + 1) * 128], C("ident"),
                                   [xkey, "cst"], [pk(bk)])
                            i = nstg()
                            cp("act" if hh == 0 else "dve", STG[i][:, :], psb(bk)[:, :], [pk(bk)], [f"stg{i}"])
                            dma("pool", dst[t0 + t2 * 128:t0 + (t2 + 1) * 128, hh * 512:(hh + 1) * 512], STG[i][:, :],
                                [f"stg{i}"], [("y", s.name)])
        P.fence()
        if stop_after == f"layer{l}":
            return finish()

    return finish()


def make_in_maps(inputs):
    cos, sin = _rope_tables()
    maps = []
    for core in range(8):
        b = core % 4
        m = {
            "xs": np.ascontiguousarray(inputs["x_sample"][b]),
            "xp": np.ascontiguousarray(inputs["x_prompt"][core * NPS:(core + 1) * NPS].reshape(NPS * TP, D)),
            "cvec": np.ascontiguousarray(np.stack([inputs["c"][b], inputs["c_ctx"]], 0)),
            "ck": np.ascontiguousarray(inputs["cache_k"][b]),
            "cv": np.ascontiguousarray(inputs["cache_v"][b]),
            "st": np.ascontiguousarray(inputs["state_delta"][b]),
            "a_log": np.ascontiguousarray(inputs["a_log"].reshape(DEPTH, 16)),
            "dt_bias": np.ascontiguousarray(inputs["dt_bias"].reshape(DEPTH, 16)),
            "cst": CST, "ropecos": cos, "ropesin": sin,
        }
        for k in ["w_mod", "b_mod", "norm1", "norm2", "w_in", "conv_w", "q_gain", "k_gain", "dn_gain",
                  "w_pa", "w_pd", "w_out", "w1", "b1", "w2", "b2", "final_norm"]:
            m[k] = np.ascontiguousarray(inputs[k])
        maps.append(m)
    return maps


_NC_CACHE = {}


def kernel(**inputs):
    inputs = {k: np.asarray(v) for k, v in inputs.items()}
    if "nc" not in _NC_CACHE:
        _NC_CACHE["nc"] = build_program()
    nc = _NC_CACHE["nc"]
    maps = make_in_maps(inputs)
    res = run_bass_kernel_spmd(nc, maps, core_ids=list(range(8)))
    r = res.results
    y_sample = np.stack([r[b]["ys"] for b in range(4)], 0)
    y_prompt = np.concatenate([r[c]["yp"].reshape(NPS, TP, D) for c in range(8)], 0)
    nk = np.concatenate([r[c]["nk"] for c in range(8)], 0)
    nv = np.concatenate([r[c]["nv"] for c in range(8)], 0)
    nst = np.concatenate([r[c]["nst"] for c in range(8)], 0)
    return (y_prompt.astype(np.float32), y_sample.astype(np.float32), nk.astype(np.float32),
            nv.astype(np.float32), nst.astype(np.float32))
```

```python
import numpy as np
from contextlib import ExitStack
import concourse.bass as bass
import concourse.mybir as mybir
from concourse.bass_utils import run_bass_kernel_spmd

F32 = mybir.dt.float32
BF16 = mybir.dt.bfloat16
AF = mybir.ActivationFunctionType
ALU = mybir.AluOpType
AX = mybir.AxisListType

D = 1024
DEPTH = 2
TS = 4096
TP = 256
NPS = 4
NCTX = 256
INW = 4896
DFF = 4096
EPS = 1e-6
NEG = -30000.0

C_QA, C_KA, C_VA, C_QD, C_KD, C_VD, C_GO, C_AI, C_BI, C_GA, C_GD = 0, 512, 640, 768, 1280, 1792, 2304, 2816, 2832, 2848, 3872


ENGS = ["pe", "act", "dve", "pool", "sp"]
N_DMA_SEMS = {"sp": 40, "pool": 16, "act": 8}
EPOCH = 30000


class Ev:
    __slots__ = ("dma", "eng", "idx", "sem", "val")

    def __init__(self, dma, eng, idx, sem=None, val=None):
        self.dma, self.eng, self.idx, self.sem, self.val = dma, eng, idx, sem, val


class Rec:
    __slots__ = ("eng", "fn", "waits", "signal", "idx", "dma", "sig_sem", "sig_val")

    def __init__(self, eng, fn):
        self.eng, self.fn = eng, fn
        self.waits = []
        self.signal = False
        self.dma = None


class Buf:
    __slots__ = ("w", "r")

    def __init__(self):
        self.w = None
        self.r = []


class Prog:
    def __init__(self, nc):
        self.nc = nc
        self.ops = {e: [] for e in ENGS}
        self.waited = {e: {p: -1 for p in ENGS} for e in ENGS}
        self.waited_dma = {e: {} for e in ENGS}
        self.bufs = {}
        self.dma_count = {q: 0 for q in N_DMA_SEMS}

    def buf(self, k):
        b = self.bufs.get(k)
        if b is None:
            b = self.bufs[k] = Buf()
        return b

    def add(self, eng, fn, reads=(), writes=(), dma=False):
        rec = Rec(eng, fn)
        rec.idx = len(self.ops[eng])
        deps = []
        for k in reads:
            b = self.buf(k)
            if b.w is not None:
                deps.append((b.w, True))
        for k in writes:
            b = self.buf(k)
            if b.w is not None:
                deps.append((b.w, False))
            for r in b.r:
                deps.append((r, False))
        if dma:
            d = self.dma_count[eng]
            n = N_DMA_SEMS[eng]
            si, val = d % n, 16 * (d // n + 1)
            if d >= n:
                deps.append((Ev(True, eng, None, si, val - 16), True))
            rec.dma = (si, val)
            ev = Ev(True, eng, rec.idx, si, val)
            self.dma_count[eng] += 1
        else:
            ev = Ev(False, eng, rec.idx)
        for dep, raw in deps:
            if dep.dma:
                key = (dep.eng, dep.sem)
                if self.waited_dma[eng].get(key, 0) >= dep.val:
                    continue
                self.waited_dma[eng][key] = dep.val
                rec.waits.append(dep)
            else:
                if dep.eng == eng and eng == "pe":
                    continue
                if self.waited[eng][dep.eng] >= dep.idx:
                    continue
                self.waited[eng][dep.eng] = dep.idx
                rec.waits.append(dep)
                self.ops[dep.eng][dep.idx].signal = True
        for k in reads:
            self.buf(k).r.append(ev)
        for k in writes:
            b = self.buf(k)
            b.w = ev
            b.r = []
        self.ops[eng].append(rec)
        return rec

    def fence(self):
        last = {e: len(self.ops[e]) - 1 for e in ["pe", "act", "dve", "pool"]}
        dma_evs = []
        for q, n in N_DMA_SEMS.items():
            d = self.dma_count[q]
            for i in range(min(n, d)):
                uses = (d - i + n - 1) // n
                dma_evs.append(Ev(True, q, None, i, 16 * uses))
        for e in ENGS:
            rec = Rec(e, lambda eng: eng.nop())
            rec.idx = len(self.ops[e])
            for p, li in last.items():
                if p == e or li < 0:
                    continue
                j = li
                while j >= 0 and self.ops[p][j].dma is not None:
                    j -= 1
                if j < 0 or self.waited[e][p] >= j:
                    continue
                self.waited[e][p] = j
                rec.waits.append(Ev(False, p, j))
                self.ops[p][j].signal = True
            for dep in dma_evs:
                key = (dep.eng, dep.sem)
                if self.waited_dma[e].get(key, 0) >= dep.val:
                    continue
                self.waited_dma[e][key] = dep.val
                rec.waits.append(dep)
            self.ops[e].append(rec)

    def emit(self, es):
        nc = self.nc
        comp_sems = {}
        for e in ["pe", "act", "dve", "pool"]:
            cnt = 0
            for rec in self.ops[e]:
                if rec.signal and rec.dma is None:
                    ep = cnt // EPOCH
                    if (e, ep) not in comp_sems:
                        comp_sems[(e, ep)] = es.enter_context(nc.semaphore(f"c_{e}_{ep}"))
                    rec.sig_sem = comp_sems[(e, ep)]
                    rec.sig_val = cnt % EPOCH + 1
                    cnt += 1
        dma_sems = {}
        for q, n in N_DMA_SEMS.items():
            for i in range(min(n, self.dma_count[q])):
                dma_sems[(q, i)] = es.enter_context(nc.semaphore(f"d_{q}_{i}"))
        final_waits = []
        for q, n in N_DMA_SEMS.items():
            d = self.dma_count[q]
            for i in range(min(n, d)):
                uses = (d - i + n - 1) // n
                final_waits.append((dma_sems[(q, i)], 16 * uses))
        block = es.enter_context(nc.Block())
        ops = self.ops

        def run(engname, eng):
            for rec in ops[engname]:
                for dep in rec.waits:
                    if dep.dma:
                        eng.wait_ge(dma_sems[(dep.eng, dep.sem)], dep.val)
                    else:
                        prod = ops[dep.eng][dep.idx]
                        eng.wait_ge(prod.sig_sem, prod.sig_val)
                ins = rec.fn(eng)
                if rec.dma is not None:
                    ins.then_inc(dma_sems[(engname, rec.dma[0])], 16)
                elif rec.signal:
                    ins.then_inc(rec.sig_sem, 1)
            if engname == "sp":
                for s, v in final_waits:
                    eng.wait_ge(s, v)

        @block.tensor
        def _(t):
            run("pe", t)

        @block.scalar
        def _(a):
            run("act", a)

        @block.vector
        def _(v):
            run("dve", v)

        @block.gpsimd
        def _(g):
            run("pool", g)

        @block.sync
        def _(s):
            run("sp", s)


CST_LAYOUT = {}


def _build_consts():
    p = np.arange(128)
    cols = []
    off = 0

    def put(name, arr):
        nonlocal off
        arr = np.asarray(arr, np.float32).reshape(128, -1)
        CST_LAYOUT[name] = (off, arr.shape[1])
        cols.append(arr)
        off += arr.shape[1]

    put("ident", np.eye(128))
    put("ones", np.ones((128, 128)))
    put("negones", -np.ones((128, 128)))
    half = p // 64
    put("blk", (half[:, None] == half[None, :]).astype(np.float32))
    put("negblk", -(half[:, None] == half[None, :]).astype(np.float32))
    put("identst", (p[:, None] % 64 == np.arange(64)[None, :]).astype(np.float32))
    same = half[:, None] == half[None, :]
    put("tri_f", (same & (p[:, None] <= p[None, :])).astype(np.float32))
    put("tri_b", (same & (p[:, None] >= p[None, :])).astype(np.float32))
    put("half0", np.repeat((p < 64).astype(np.float32)[:, None], 128, 1))
    put("half1", np.repeat((p >= 64).astype(np.float32)[:, None], 128, 1))
    il = p % 64
    j = np.arange(64)

    def m(keep):
        return np.where(keep, 0.0, NEG).astype(np.float32)

    put("m1_f", m(il[:, None] > j[None, :]))
    put("m1_b", m(il[:, None] < j[None, :]))
    put("m2_f", m(j[None, :] > il[:, None]))
    put("m2_b", m(j[None, :] < il[:, None]))
    put("m3_f", m(j[None, :] >= il[:, None]))
    put("m3_b", m(j[None, :] <= il[:, None]))
    put("mask8", ((il[:, None] // 8) == (j[None, :] // 8)).astype(np.float32))
    for sz in (8, 16, 32):
        put("moff%d" % sz, (((il[:, None] // (2 * sz)) == (j[None, :] // (2 * sz)))
                            & ((il[:, None] // sz) != (j[None, :] // sz))).astype(np.float32))
    R = np.zeros((128, 128), np.float32)
    for q in range(128):
        if q % 64 < 32:
            R[q, q + 32] = -1.0
        else:
            R[q, q - 32] = 1.0
    put("rot", R.T)
    return np.concatenate(cols, 1)


CST = _build_consts()
NCST = CST.shape[1]


def _rope_tables():
    t = np.arange(TS)
    row = (t // 64).astype(np.float32)
    col = (t % 64).astype(np.float32)
    inv = (10000.0 ** (-np.arange(16, dtype=np.float32) / 16)).astype(np.float32)
    ang = np.concatenate([row[:, None] * inv, col[:, None] * inv], -1).astype(np.float32)
    cos = np.cos(ang).astype(np.float32).T
    sin = np.sin(ang).astype(np.float32).T
    return np.tile(cos, (4, 1)).copy(), np.tile(sin, (4, 1)).copy()


class Seq:
    def __init__(self, name, T, is_sample, idx, key0, tile0):
        self.name, self.T, self.is_sample, self.idx = name, T, is_sample, idx
        self.key0 = key0
        self.tile0 = tile0
        self.nctx = NCTX if is_sample else 0
        self.mj = 0 if is_sample else 1


def bcast(ap, axis, n):
    shp = list(ap.shape)
    shp.insert(axis, n)
    return ap.unsqueeze(axis).broadcast_to(shp)


def build_program(debug_outs=(), stop_after=None):
    nc = bass.Bass("TRN2", target_bir_lowering=False)
    es = ExitStack()
    P = Prog(nc)
    dbg = set(debug_outs)

    def din(name, shape, dt=F32):
        return nc.dram_tensor(name, list(shape), dt, kind="ExternalInput").ap()

    def dout(name, shape, dt=F32):
        return nc.dram_tensor(name, list(shape), dt, kind="ExternalOutput").ap()

    def dscr(name, shape, dt=F32):
        kind = "ExternalOutput" if name in dbg else "Internal"
        return nc.dram_tensor(name, list(shape), dt, kind=kind).ap()

    xs_in = din("xs", [TS, D])
    xp_in = din("xp", [NPS * TP, D])
    cvec = din("cvec", [2, D])
    ck_in = din("ck", [DEPTH, 2, NCTX, 64])
    cv_in = din("cv", [DEPTH, 2, NCTX, 64])
    st_in = din("st", [DEPTH, 2, 8, 64, 64])
    w_mod = din("w_mod", [DEPTH, D, 6 * D])
    b_mod = din("b_mod", [DEPTH, 6 * D])
    norm1 = din("norm1", [DEPTH, D])
    norm2 = din("norm2", [DEPTH, D])
    w_in = din("w_in", [DEPTH, D, INW])
    conv_w = din("conv_w", [DEPTH, 3, 1536])
    q_gain = din("q_gain", [DEPTH, 64])
    k_gain = din("k_gain", [DEPTH, 64])
    a_log = din("a_log", [DEPTH, 16])
    dt_bias = din("dt_bias", [DEPTH, 16])
    dn_gain = din("dn_gain", [DEPTH, 64])
    w_pa = din("w_pa", [DEPTH, 512, D])
    w_pd = din("w_pd", [DEPTH, 512, D])
    w_out = din("w_out", [DEPTH, D, D])
    w1 = din("w1", [DEPTH, D, DFF])
    b1 = din("b1", [DEPTH, DFF])
    w2 = din("w2", [DEPTH, DFF, D])
    b2 = din("b2", [DEPTH, D])
    final_norm = din("final_norm", [D])
    cst_in = din("cst", [128, NCST])
    cos_in = din("ropecos", [128, TS])
    sin_in = din("ropesin", [128, TS])
    ys_out = dout("ys", [TS, D])
    yp_out = dout("yp", [NPS * TP, D])
    nk_out = dout("nk", [NPS, DEPTH, 2, TP, 64])
    nv_out = dout("nv", [NPS, DEPTH, 2, TP, 64])
    nst_out = dout("nst", [NPS, DEPTH, 2, 8, 64, 64])

    seqs = [Seq("s", TS, True, 0, 0, 0)]
    for i in range(NPS):
        seqs.append(Seq(f"p{i}", TP, False, i, NCTX + TS + i * TP, TS // 128 + i * (TP // 128)))
    NKEY = NCTX + TS + NPS * TP
    NTILE = TS // 128 + NPS * TP // 128
    NVT = NKEY // 128

    XRES, PRE, QA, GATES, GS, QDT, KDT, QTM, KTM, VTM, OPART, OA, OD, X1, H2, ACTS = ({} for _ in range(16))
    for s in seqs:
        T = s.T
        XRES[s.name] = dscr(f"xres_{s.name}", [128, 8, T])
        PRE[s.name] = dscr(f"pre_{s.name}", [128, 12, T + 2])
        QA[s.name] = dscr(f"qa_{s.name}", [8, 64, T], BF16)
        GATES[s.name] = dscr(f"gates_{s.name}", [128, 16, T], BF16)
        GS[s.name] = dscr(f"gs_{s.name}", [T, 512], BF16)
        QDT[s.name] = dscr(f"qdt_{s.name}", [8, 64, T], BF16)
        KDT[s.name] = dscr(f"kdt_{s.name}", [8, 64, T], BF16)
        QTM[s.name] = dscr(f"qtm_{s.name}", [T, 512], BF16)
        KTM[s.name] = dscr(f"ktm_{s.name}", [T, 512], BF16)
        VTM[s.name] = dscr(f"vtm_{s.name}", [T, 512], BF16)
        OPART[s.name] = dscr(f"opart_{s.name}", [T, 512])
        OA[s.name] = dscr(f"oa_{s.name}", [8, 64, T], BF16)
        OD[s.name] = dscr(f"od_{s.name}", [128, 4, T], BF16)
        H2[s.name] = dscr(f"h2_{s.name}", [128, 8, T], BF16)

    def sb(name, shape, dt=F32):
        return es.enter_context(nc.sbuf_tensor("sb_" + name, list(shape), dt))

    cst = sb("cst", [128, NCST])
    cstb = sb("cstb", [128, NCST], BF16)

    def C(name, bf=False):
        o, n = CST_LAYOUT[name]
        return (cstb if bf else cst)[:, o:o + n]

    WAR = sb("warena", [128, 40960], BF16)
    KT_all = sb("kt_all", [128, NKEY], BF16)
    VA_all = sb("va_all", [128, NVT, 2, 65], BF16)
    LA = sb("la", [128, NTILE, 16])
    LB = sb("lb", [128, NTILE, 16])
    BETA = sb("beta", [128, NTILE, 16])
    MOD = sb("mod", [128, 48, 2])
    A1 = sb("a1", [128, 8, 2])
    A2 = sb("a2", [128, 8, 2])
    GB2 = sb("gb2", [128, 8, 2])
    n1f = sb("n1f", [128, 8])
    n2f = sb("n2f", [128, 8])
    fnf = sb("fnf", [128, 8])
    b2f = sb("b2f", [128, 8])
    b1f = sb("b1f", [128, 32])
    bmf = sb("bmf", [128, 48])
    cfm = sb("cfm", [128, 8, 2])
    scb = sb("scb", [128, 8, 2], BF16)
    qg = sb("qg", [128, 1])
    kg = sb("kg", [128, 1])
    cw = sb("cw", [128, 3, 12])
    dtb = sb("dtb", [128, 16])
    negA = sb("negA", [128, 16])
    dng = sb("dng", [128, 64])
    TB = 256
    BIGA = sb("bigA", [128, 12 * 258])
    BIGB = sb("bigB", [128, 12 * 256])
    HB = sb("hb16", [128, 16, 256], BF16)
    GATB = sb("gatb", [128, 16, 256], BF16)
    R1 = sb("r1", [128, 256])
    RSTD = sb("rstd", [128, 256])
    STG = [sb(f"stg{i}", [128, 512]) for i in range(4)]
    STB = [sb(f"stb{i}", [128, 512], BF16) for i in range(4)]
    COSB = sb("cosb", [128, 256])
    SINB = sb("sinb", [128, 256])
    SMALL = sb("small", [128, 64])
    SM = sb("sm", [128, 160])
    VAB = sb("vab", [128, 160])
    KTC = [sb(f"ktc{i}", [64, 8, 128], BF16) for i in range(2)]
    QTC = [sb(f"qtc{i}", [64, 8, 128], BF16) for i in range(2)]
    KTMC = [sb(f"ktmc{i}", [128, 512], BF16) for i in range(2)]
    QTMC = [sb(f"qtmc{i}", [128, 512], BF16) for i in range(2)]
    VTMC = [sb(f"vtmc{i}", [128, 512], BF16) for i in range(2)]
    def v512(big, i, p0=0, p1=128):
        return big[p0:p1, i * 512:(i + 1) * 512]

    E1, E2, E3, DG1, DG3 = (v512(BIGA, i) for i in range(5))
    U0 = [v512(BIGA, 5), v512(BIGB, 0)]
    OACC = [v512(BIGB, 1), v512(BIGB, 2)]
    RS = v512(BIGB, 3)
    S32 = [v512(BIGB, 4 + i, 0, 64).rearrange("p (h v) -> p h v", h=8) for i in range(2)]
    HBf = HB[:, :, :].rearrange("p a b -> p (a b)")
    GBf = GATB[:, :, :].rearrange("p a b -> p (a b)")
    PA = [v512(HBf, 0), v512(HBf, 1)]
    PT = [v512(HBf, 2), v512(HBf, 3)]
    RT = [v512(HBf, 4), v512(HBf, 5)]
    QKM = [v512(HBf, 6), v512(HBf, 7)]
    BEK = v512(GBf, 0)
    KDEC = [v512(GBf, 1), v512(GBf, 2)]
    BV = v512(GBf, 3)
    UB = [v512(GBf, 4), v512(GBf, 5)]
    DE = v512(GBf, 6)
    OB = v512(GBf, 7, 0, 64)
    XBW = [sb(f"xbw{i}", [128, 1024], BF16) for i in range(2)]
    XB = [XBW[0][:, 0:512], XBW[0][:, 512:1024], XBW[1][:, 0:512], XBW[1][:, 512:1024]]
    NWT = [sb(f"nwt{i}", [64, 16, 64], BF16) for i in range(2)]
    QDEC = [sb(f"qdec{i}", [64, 16, 64], BF16) for i in range(2)]
    SBF = [sb(f"sbf{i}", [64, 8, 64], BF16) for i in range(2)]
    EGL2 = [sb(f"egl2{i}", [128, 16]) for i in range(2)]
    QB = STB[3][:, :].rearrange("p (j t) -> p j t", j=4)
    PS = [es.enter_context(nc.psum_tensor(f"ps{i}", [128, 1024], F32)) for i in range(4)]
    dbg_mod = dscr("dbg_mod", [128, 48, 2])
    dbg_la = dscr("dbg_la", [128, NTILE, 16])
    dbg_lb = dscr("dbg_lb", [128, NTILE, 16])
    dbg_beta = dscr("dbg_beta", [128, NTILE, 16])

    XT = BIGA[:, 0:8 * 256].rearrange("p (c t) -> p c t", c=8)
    X1T = BIGB[:, 0:8 * 256].rearrange("p (c t) -> p c t", c=8)
    PRET = BIGA[:, :].rearrange("p (c t) -> p c t", c=12)
    CV = BIGB[:, :].rearrange("p (c t) -> p c t", c=12)
    SQ = HB[:, 0:8, :]
    HT = HB[:, 8:16, :]
    XTM = [BIGA[:, i * 1024:(i + 1) * 1024] for i in range(2)]
    XFM = [BIGB[:, i * 1024:(i + 1) * 1024].rearrange("p (c t) -> p c t", c=8) for i in range(2)]

    def psb(i):
        return PS[i // 2][:, (i % 2) * 512:(i % 2) * 512 + 512]

    def pk(i):
        return ("ps", i)

    def dma(q, out, in_, reads, writes, **kw):
        P.add(q, lambda e: e.dma_start(out=out, in_=in_, **kw), reads, writes, dma=True)

    def mm(out, lhsT, rhs, start, stop, reads, writes, **kw):
        P.add("pe", lambda e: e.matmul(out, lhsT, rhs, start=start, stop=stop, **kw), reads, writes)

    def tr(out, in_, ident, reads, writes):
        P.add("pe", lambda e: e.transpose(out, in_, ident), reads, writes)

    def act(out, in_, func, reads, writes, **kw):
        P.add("act", lambda e: e.activation(out, in_, func, **kw), reads, writes)

    def tt(eng, out, in0, in1, op, reads, writes):
        P.add(eng, lambda e: e.tensor_tensor(out, in0, in1, op), reads, writes)

    def tsc(eng, out, in0, s1, s2, op0, op1, reads, writes):
        if op1 is None:
            P.add(eng, lambda e: e.tensor_scalar(out, in0, s1, None, op0), reads, writes)
        else:
            P.add(eng, lambda e: e.tensor_scalar(out, in0, s1, s2, op0, op1), reads, writes)

    def stt(eng, out, in0, scalar, in1, op0, op1, reads, writes):
        P.add(eng, lambda e: e.scalar_tensor_tensor(out, in0, scalar, in1, op0, op1), reads, writes)

    def cp(eng, out, in_, reads, writes):
        if eng == "act":
            P.add("act", lambda e: e.copy(out, in_), reads, writes)
        else:
            P.add(eng, lambda e: e.tensor_copy(out, in_), reads, writes)

    def recip(out, in_, reads, writes):
        P.add("dve", lambda e: e.reciprocal(out, in_), reads, writes)

    def load_w(dst3, src2, K, wkey="war"):
        for k in range(K):
            dma("pool", dst3[:, k, :], src2[k * 128:(k + 1) * 128, :], [], [wkey])

    def rms_stats(src3, nt, srckey, sq=None, sqkey="hb", bank=0, r1=None, rstd=None, rkey=""):
        if sq is None:
            sq = SQ
        if r1 is None:
            r1, rstd = R1[:, :], RSTD[:, :]
        act(sq[:, :, :nt], src3, AF.Square, [srckey], [sqkey])
        for c in range(8):
            mm(psb(bank)[:, :nt], C("ones", True), sq[:, c, :nt], c == 0, c == 7, [sqkey, "cstb"], [pk(bank)])
        act(r1[:, :nt], psb(bank)[:, :nt], AF.Sqrt, [pk(bank)], ["r1" + rkey], bias=EPS, scale=1.0 / D)
        recip(rstd[:, :nt], r1[:, :nt], ["r1" + rkey], ["rstd" + rkey])

    dma("sp", cst[:], cst_in, [], ["cst"])
    cp("dve", cstb[:], cst[:], ["cst"], ["cstb"])
    P.add("pool", lambda e: e.memset(VA_all[:], 1.0), [], ["va"])
    dma("sp", fnf[:], final_norm.rearrange("(k p) -> p k", p=128), [], ["fnf"], allow_slow_non_contiguous=True)
    for j in range(2):
        dma("sp", cfm[:, :, j], cvec[j].rearrange("(k p) -> p k", p=128), [], ["cfm"], allow_slow_non_contiguous=True)
    act(scb[:], cfm[:], AF.Silu, ["cfm"], ["scb"])
    P.add("pool", lambda e: e.memset(SMALL[:], 0.0), [], ["small"])
    for s in seqs:
        for col in (0, s.T + 1):
            dma("sp", PRE[s.name][:, :, col:col + 1], SMALL[:, 0:12].unsqueeze(2), ["small"], [("pre", s.name, "pad", col)],
                allow_slow_non_contiguous=True)

    it = 0
    for s in seqs:
        src = xs_in if s.is_sample else xp_in[s.idx * TP:(s.idx + 1) * TP, :]
        for ti in range(s.T // 128):
            b = it % 2
            dma("sp", XTM[b], src[ti * 128:(ti + 1) * 128, :], [], ["bigA"])
            for c in range(8):
                tr(PS[b][:, c * 128:(c + 1) * 128], XTM[b][:, c * 128:(c + 1) * 128], C("ident"),
                   ["bigA", "cst"], [pk(2 * b), pk(2 * b + 1)])
            cp("act" if it % 2 == 0 else "dve", XFM[b], PS[b][:].rearrange("p (c t) -> p c t", c=8),
               [pk(2 * b), pk(2 * b + 1)], ["bigB"])
            dma("pool", XRES[s.name][:, :, ti * 128:(ti + 1) * 128], XFM[b], ["bigB"], [("xres", s.name, ti // 2)])
            it += 1

    def finish():
        P.emit(es)
        es.close()
        return nc

    if stop_after == "stage0":
        return finish()

    NLAYERS = DEPTH
    for l in range(NLAYERS):
        for (t_, src) in ((n1f, norm1[l]), (n2f, norm2[l]), (b2f, b2[l])):
            dma("sp", t_[:], src.rearrange("(k p) -> p k", p=128), [], ["lvec"], allow_slow_non_contiguous=True)
        dma("sp", b1f[:], b1[l].rearrange("(k p) -> p k", p=128), [], ["lvec"], allow_slow_non_contiguous=True)
        dma("sp", bmf[:], b_mod[l].rearrange("(k p) -> p k", p=128), [], ["lvec"], allow_slow_non_contiguous=True)
        for hh in range(2):
            dma("sp", qg[64 * hh:64 * hh + 64, :], q_gain[l].rearrange("(d o) -> d o", o=1), [], ["lvec"], allow_slow_non_contiguous=True)
            dma("sp", kg[64 * hh:64 * hh + 64, :], k_gain[l].rearrange("(d o) -> d o", o=1), [], ["lvec"], allow_slow_non_contiguous=True)
        for j in range(3):
            dma("sp", cw[:, j, :], conv_w[l, j].rearrange("(c p) -> p c", p=128), [], ["lvec"], allow_slow_non_contiguous=True)
        dma("sp", dtb[:], dt_bias[l:l + 1, :].broadcast_to([128, 16]), [], ["lvec"], allow_slow_non_contiguous=True)
        dma("sp", negA[:], a_log[l:l + 1, :].broadcast_to([128, 16]), [], ["lvec"], allow_slow_non_contiguous=True)
        dma("sp", dng[:], dn_gain[l:l + 1, :].broadcast_to([128, 64]), [], ["lvec"], allow_slow_non_contiguous=True)
        act(negA[:], negA[:], AF.Exp, ["lvec"], ["lvec2"])
        tsc("dve", negA[:], negA[:], -1.0, None, ALU.mult, None, ["lvec2"], ["lvec2"])

        Wm = WAR[:, 0:8 * 3072].rearrange("p (k n) -> p k n", k=8)
        for hh in range(2):
            load_w(Wm, w_mod[l][:, hh * 3072:(hh + 1) * 3072], 8)
            for n in range(24):
                nn = hh * 24 + n
                for k in range(8):
                    mm(psb(0)[:, nn * 2:nn * 2 + 2], Wm[:, k, n * 128:(n + 1) * 128], scb[:, k, :], k == 0, k == 7,
                       ["war", "scb"], [pk(0)])
        tt("dve", MOD[:], psb(0)[:, 0:96].rearrange("p (n j) -> p n j", j=2), bcast(bmf[:], 2, 2), ALU.add,
           [pk(0), "lvec"], ["mod"])
        stt("dve", A1[:], MOD[:, 8:16, :], 1.0, bcast(n1f[:], 2, 2), ALU.add, ALU.mult, ["mod", "lvec"], ["mod2"])
        stt("dve", A2[:], MOD[:, 32:40, :], 1.0, bcast(n2f[:], 2, 2), ALU.add, ALU.mult, ["mod", "lvec"], ["mod2"])
        tt("dve", GB2[:], MOD[:, 40:48, :], bcast(b2f[:], 2, 2), ALU.mult, ["mod", "lvec"], ["mod2"])
        MK = ["mod", "mod2", "lvec", "lvec2"]
        if stop_after == "adaln":
            dma("sp", dbg_mod, MOD[:], ["mod"], ["dbgmod"])
            return finish()

        Win = WAR[:, 0:8 * INW].rearrange("p (k n) -> p k n", k=8)
        load_w(Win, w_in[l], 8)
        bank_rr = [0]

        def next_bank():
            bank_rr[0] = (bank_rr[0] % 4) + 1
            return bank_rr[0]

        stg_rr = [0]

        def nstg():
            stg_rr[0] = (stg_rr[0] + 1) % 4
            return stg_rr[0]

        jobsA = [(s, blk) for s in seqs for blk in range(s.T // TB)]
        XTsA = [(XT, "bigA"), (X1T, "bigB")]
        HTsA = [(HB[:, 8:16, :], "ht"), (GATB[:, 0:8, :], "gatb")]

        def blockA(j):
            s, blk = jobsA[j]
            mj = s.mj
            t0 = blk * TB
            XTj, xkey = XTsA[j % 2]
            HTj, hkey = HTsA[j % 2]
            dma("sp", XTj, XRES[s.name][:, :, t0:t0 + TB], [("xres", s.name, blk)], [xkey])
            if s.is_sample:
                dma("sp", COSB[:], cos_in[:, t0:t0 + TB], [], ["cosb"])
                dma("sp", SINB[:], sin_in[:, t0:t0 + TB], [], ["sinb"])
            rms_stats(XTj, TB, xkey)
            tt("dve", XTj, XTj, bcast(RSTD[:, :], 1, 8), ALU.mult, [xkey, "rstd"], [xkey])
            for c in range(8):
                if c % 2 == 0:
                    act(HTj[:, c, :], XTj[:, c, :], AF.Identity, [xkey] + MK, [hkey],
                        scale=A1[:, c, mj:mj + 1], bias=MOD[:, c, mj:mj + 1])
                else:
                    tsc("dve", HTj[:, c, :], XTj[:, c, :], A1[:, c, mj:mj + 1], MOD[:, c, mj:mj + 1], ALU.mult, ALU.add,
                        [xkey] + MK, [hkey])

            yield

            def fm_chunk(col0):
                bk = next_bank()
                for k in range(8):
                    mm(psb(bk)[:, :TB], Win[:, k, col0:col0 + 128], HTj[:, k, :], k == 0, k == 7, ["war", hkey], [pk(bk)])
                return bk

            def qk_epi(c, bk):
                is_k = (c == 4)
                gain = kg if is_k else qg
                cp("act", STG[0][:, :TB], psb(bk)[:, :TB], [pk(bk)], ["stg0"])
                act(STB[0][:, :TB], psb(bk)[:, :TB], AF.Square, [pk(bk)], ["stb0"])
                yield
                mm(psb(6)[:, :TB], C("blk", True), STB[0][:, :TB], True, True, ["stb0", "cstb"], [pk(6)])
                act(STG[1][:, :TB], psb(6)[:, :TB], AF.Sqrt, [pk(6)], ["stg1"], bias=EPS, scale=1.0 / 64)
                recip(STG[1][:, :TB], STG[1][:, :TB], ["stg1"], ["stg1"])
                stt("dve", STG[0][:, :TB], STG[0][:, :TB], gain[:, 0:1], STG[1][:, :TB], ALU.mult, ALU.mult,
                    ["stg0", "stg1", "lvec"], ["stg0"])
                yield
                kcol = s.key0 + s.nctx + t0
                dst = KT_all[:, kcol:kcol + TB] if is_k else STB[1][:, :TB]
                dkey = "kt" if is_k else "stb1"
                if s.is_sample:
                    mm(psb(7)[:, :TB], C("rot"), STG[0][:, :TB], True, True, ["stg0", "cst"], [pk(7)])
                    tt("dve", STG[2][:, :TB], STG[0][:, :TB], COSB[:], ALU.mult, ["stg0", "cosb"], ["stg2"])
                    yield
                    tt("dve", STG[3][:, :TB], psb(7)[:, :TB], SINB[:], ALU.mult, [pk(7), "sinb"], ["stg3"])
                    tt("dve", dst, STG[2][:, :TB], STG[3][:, :TB], ALU.add, ["stg2", "stg3"], [dkey])
                else:
                    cp("dve", dst, STG[0][:, :TB], ["stg0"], [dkey])
                if not is_k:
                    dma("pool", QA[s.name][2 * c:2 * c + 2].rearrange("h d t -> (h d) t")[:, t0:t0 + TB], STB[1][:, :TB],
                        ["stb1"], [("qa", s.name)])
                elif not s.is_sample:
                    for t2 in range(TB // 128):
                        tr(psb(7)[:, t2 * 128:(t2 + 1) * 128], STG[0][:, t2 * 128:(t2 + 1) * 128], C("ident"),
                           ["stg0", "cst"], [pk(7)])
                    cp("act", STG[2][:, :TB], psb(7)[:, :TB], [pk(7)], ["stg2"])
                    for t2 in range(TB // 128):
                        for g in range(2):
                            dma("pool", nk_out[s.idx, l, g, t0 + t2 * 128:t0 + (t2 + 1) * 128, :],
                                STG[2][:, t2 * 128 + g * 64:t2 * 128 + g * 64 + 64], ["stg2"], [("nk", s.idx)])
                yield

            hrr = [0]

            def nh():
                hrr[0] = (hrr[0] + 1) % 4
                return hrr[0]

            def filler():
                for c in range(12):
                    bk = fm_chunk(C_QD + c * 128)
                    i = nh()
                    cp("act" if c % 2 == 0 else "dve", STG[i][:, 256:512], psb(bk)[:, :TB], [pk(bk)], [f"stgh{i}"])
                    dma("pool", PRE[s.name][:, c, 1 + t0:1 + t0 + TB], STG[i][:, 256:512], [f"stgh{i}"], [("pre", s.name, blk)])
                    yield
                for c in range(16):
                    bk = fm_chunk(C_GA + c * 128)
                    i = nh()
                    act(STB[i][:, 256:512], psb(bk)[:, :TB], AF.Sigmoid, [pk(bk)], [f"stbh{i}"])
                    dma("pool", GATES[s.name][:, c, t0:t0 + TB], STB[i][:, 256:512], [f"stbh{i}"], [("gates", s.name, blk)])
                    yield

            fg = filler()
            for c in range(5):
                bk = fm_chunk(C_QA + c * 128)
                for _ in qk_epi(c, bk):
                    next(fg, None)
            for _ in fg:
                pass
            yield
            for t2 in range(TB // 128):
                tsl = slice(t2 * 128, (t2 + 1) * 128)
                gti = s.tile0 + (t0 // 128) + t2
                vt = (s.key0 + s.nctx + t0) // 128 + t2
                for k in range(8):
                    mm(psb(5)[:, 0:512], HTj[:, k, tsl], Win[:, k, C_GO:C_GO + 512], k == 0, k == 7, ["war", hkey], [pk(5)])
                for k in range(8):
                    mm(psb(6)[:, 0:128], HTj[:, k, tsl], Win[:, k, C_VA:C_VA + 128], k == 0, k == 7, ["war", hkey], [pk(6)])
                for k in range(8):
                    mm(psb(6)[:, 128:160], HTj[:, k, tsl], Win[:, k, C_AI:C_AI + 32], k == 0, k == 7, ["war", hkey], [pk(6)])
                i = nstg()
                act(STB[i][:, :], psb(5)[:, :], AF.Silu, [pk(5)], [f"stb{i}", f"stbh{i}"])
                dma("pool", GS[s.name][t0 + t2 * 128:t0 + (t2 + 1) * 128, :], STB[i][:, :], [f"stb{i}"], [("gs", s.name)])
                cp("dve", VAB[:, :], psb(6)[:, 0:160], [pk(6)], ["vab"])
                cp("dve", VA_all[:, vt, :, 0:64], VAB[:, 0:128].rearrange("p (g d) -> p g d", g=2), ["vab"], ["va"])
                if not s.is_sample:
                    for g in range(2):
                        dma("pool", nv_out[s.idx, l, g, t0 + t2 * 128:t0 + (t2 + 1) * 128, :], VAB[:, g * 64:g * 64 + 64],
                            ["vab"], [("nv", s.idx)])
                tt("dve", SM[:, 0:16], VAB[:, 128:144], dtb[:], ALU.add, ["vab", "lvec"], ["sm"])
                act(SM[:, 16:32], SM[:, 0:16], AF.Exp, ["sm"], ["sm1"])
                act(SM[:, 32:48], SM[:, 16:32], AF.Ln, ["sm1"], ["sm2"], bias=1.0)
                tt("dve", LA[:, gti, :], SM[:, 32:48], negA[:], ALU.mult, ["sm2", "lvec2"], ["la"])
                act(BETA[:, gti, :], VAB[:, 144:160], AF.Sigmoid, ["vab"], ["beta"])
                act(LB[:, gti, :], BETA[:, gti, :], AF.Ln, ["beta"], ["lb"])

        gA = [blockA(j) for j in range(len(jobsA))]
        next(gA[0])
        for j in range(len(jobsA)):
            next(gA[j])
            if j + 1 < len(jobsA):
                next(gA[j + 1])
            for _ in gA[j]:
                pass
        if "dbg_la" in dbg:
            dma("sp", dbg_la, LA[:], ["la"], ["dbgla"])
            dma("sp", dbg_lb, LB[:], ["lb"], ["dbglb"])
            dma("sp", dbg_beta, BETA[:], ["beta"], ["dbgbeta"])
        P.fence()
        if stop_after == "stageA":
            return finish()

        for s in seqs:
            for blk in range(s.T // TB):
                t0 = blk * TB
                dma("sp", PRET, PRE[s.name][:, :, t0:t0 + TB + 2],
                    [("pre", s.name, b_) for b_ in range(max(0, blk - 1), min(s.T // TB, blk + 2))]
                    + [("pre", s.name, "pad", 0), ("pre", s.name, "pad", s.T + 1)], ["bigA"])
                for c in range(12):
                    e_ = "dve"
                    tsc(e_, CV[:, c, :], PRET[:, c, 0:TB], cw[:, 0, c:c + 1], None, ALU.mult, None, ["bigA", "lvec"], [("cv", c)])
                    stt(e_, CV[:, c, :], PRET[:, c, 1:TB + 1], cw[:, 1, c:c + 1], CV[:, c, :], ALU.mult, ALU.add,
                        ["bigA", "lvec", ("cv", c)], [("cv", c)])
                    stt(e_, CV[:, c, :], PRET[:, c, 2:TB + 2], cw[:, 2, c:c + 1], CV[:, c, :], ALU.mult, ALU.add,
                        ["bigA", "lvec", ("cv", c)], [("cv", c)])
                for c in range(12):
                    act(CV[:, c, :], CV[:, c, :], AF.Silu, [("cv", c)], [("cv", c)])
                for c in range(8):
                    act(STB[0][:, :TB], CV[:, c, :], AF.Square, [("cv", c)], ["stb0"])
                    mm(psb(0)[:, :TB], C("blk", True), STB[0][:, :TB], True, True, ["stb0", "cstb"], [pk(0)])
                    act(STG[0][:, :TB], psb(0)[:, :TB], AF.Sqrt, [pk(0)], ["stg0"], bias=EPS, scale=1.0)
                    recip(STG[0][:, :TB], STG[0][:, :TB], ["stg0"], ["stg0"])
                    stt("dve", CV[:, c, :], CV[:, c, :], 0.125 if c < 4 else 1.0, STG[0][:, :TB], ALU.mult, ALU.mult,
                        [("cv", c), "stg0"], [("cv", c)])
                    i = nstg()
                    cp("act", STB[i][:, :TB], CV[:, c, :], [("cv", c)], [f"stb{i}"])
                    dstT = QDT if c < 4 else KDT
                    cc = c % 4
                    dma("pool", dstT[s.name][2 * cc:2 * cc + 2].rearrange("h d t -> (h d) t")[:, t0:t0 + TB], STB[i][:, :TB],
                        [f"stb{i}"], [("qkdt", s.name)])
                for t2 in range(TB // 128):
                    for grp, dstM in enumerate((QTM, KTM, VTM)):
                        bk = 1 + (grp % 2)
                        for cc in range(4):
                            tr(psb(bk)[:, cc * 128:(cc + 1) * 128], CV[:, grp * 4 + cc, t2 * 128:(t2 + 1) * 128], C("ident"),
                               [("cv", grp * 4 + cc), "cst"], [pk(bk)])
                        i = nstg()
                        cp("act" if grp % 2 == 0 else "dve", STB[i][:, :], psb(bk)[:, :], [pk(bk)], [f"stb{i}"])
                        dma("pool", dstM[s.name][t0 + t2 * 128:t0 + (t2 + 1) * 128, :], STB[i][:, :], [f"stb{i}"], [("tm", s.name)])
        P.fence()
        if stop_after == "stageB":
            return finish()

        for t2 in range(NCTX // 128):
            dma("sp", STG[0][:, 0:128].rearrange("p (g d) -> p g d", g=2),
                ck_in[l, :, t2 * 128:(t2 + 1) * 128, :].rearrange("g p d -> p g d"), [], ["stg0"])
            tr(psb(0)[:, 0:128], STG[0][:, 0:128], C("ident"), ["stg0", "cst"], [pk(0)])
            cp("dve", KT_all[:, t2 * 128:(t2 + 1) * 128], psb(0)[:, 0:128], [pk(0)], ["kt"])
            dma("sp", STG[1][:, 0:128].rearrange("p (g d) -> p g d", g=2),
                cv_in[l, :, t2 * 128:(t2 + 1) * 128, :].rearrange("g p d -> p g d"), [], ["stg1"])
            cp("dve", VA_all[:, t2, :, 0:64], STG[1][:, 0:128].rearrange("p (g d) -> p g d", g=2), ["stg1"], ["va"])
        QBz = [[WAR[:, (qp * 2 + g) * 512:(qp * 2 + g + 1) * 512] for g in range(2)] for qp in range(2)]
        VAp = WAR[:, 2048:2048 + NVT * 256].rearrange("p (t g d) -> p t g d", t=NVT, g=2)
        P.add("pool", lambda e: e.memset(WAR[:, 0:2048 + NVT * 256], 0.0), [], ["qbz", "vap"])
        cp("pool", VAp[:, :, :, 0:65], VA_all[:, :, :, :], ["va", "vap"], ["vap"])
        items = []
        qcount = 0
        for s in seqs:
            ktiles = []
            if s.is_sample:
                ktiles += list(range(NCTX // 128))
            ktiles += [(s.key0 + s.nctx) // 128 + i for i in range(s.T // 128)]
            for qi in range(s.T // 128):
                for n_, kt in enumerate(ktiles):
                    items.append((s, qi, n_, kt, n_ == 0, n_ == len(ktiles) - 1, qcount % 2))
                qcount += 1
        LAG = 1
        for idx in range(len(items) + LAG):
            if idx < len(items):
                s, qi, n_, kt, first, last, qp = items[idx]
                q0 = qi * 128
                if first:
                    for g2 in range(2):
                        dma("sp", QBz[qp][g2][64 * g2:64 * g2 + 64, :].rearrange("p (j t) -> p j t", j=4),
                            QA[s.name][4 * g2:4 * g2 + 4, :, q0:q0 + 128].rearrange("j d t -> d j t"),
                            [("qa", s.name), "qbz"], [("qb", qp)])
                r_ = idx % 2
                for g in range(2):
                    mm(PS[r_][:, g * 512:(g + 1) * 512], KT_all[:, kt * 128:(kt + 1) * 128],
                       QBz[qp][g], True, True, ["kt", ("qb", qp)],
                       [pk(2 * r_), pk(2 * r_ + 1)])
                act(XBW[r_][:, :], PS[r_][:, :], AF.Exp, [pk(2 * r_), pk(2 * r_ + 1)], [("ptt", r_)], scale=0.125)
            if idx >= LAG:
                s, qi, n_, kt, first, last, qp = items[idx - LAG]
                q0 = qi * 128
                r_ = (idx - LAG) % 2
                for g in range(2):
                    ob = 4 + g
                    mm(psb(ob)[:, :], VAp[:, kt, g, :], XBW[r_][:, g * 512:(g + 1) * 512], first, last,
                       ["vap", ("ptt", r_)], [pk(ob)])
                if last:
                    for g in range(2):
                        ob = 4 + g
                        cp("dve", RS[64:65, :], psb(ob)[64:65, :], [pk(ob)], ["rs"])
                        recip(RS[64:65, :], RS[64:65, :], ["rs"], ["rs"])
                        mm(psb(6)[0:64, :], C("ones")[64:65, 0:64], RS[64:65, :], True, True, ["rs", "cst"], [pk(6)])
                        cp("dve", STG[0][0:64, :], psb(ob)[0:64, :], [pk(ob)], ["stg0"])
                        tt("dve", OB[:, :], STG[0][0:64, :], psb(6)[0:64, :], ALU.mult, ["stg0", pk(6)], ["ob"])
                        dma("sp", OA[s.name][4 * g:4 * g + 4, :, q0:q0 + 128].rearrange("j d t -> d j t"),
                            OB[:, :].rearrange("p (j t) -> p j t", j=4), ["ob"], [("oa", s.name)])
        P.fence()
        if stop_after == "attn":
            return finish()

        H8 = 8
        CUT = 0

        def v3(t):
            return t.rearrange("p (h j) -> p h j", h=H8)

        woff = [0]

        def wtake(n, f32=False):
            ap = WAR[:, woff[0]:woff[0] + n]
            woff[0] += n
            return ap.bitcast(F32) if f32 else ap

        DS = [dict(SM=SM, DG1=DG1, DG3=DG3, DE=DE, E1=E1, E2=E2, E3=E3, PA=PA, PT=PT, RT=RT, XB=XB, BEK=BEK, BV=BV), None]
        DS[1] = dict(E1=wtake(1024, True), E2=wtake(1024, True), E3=wtake(1024, True), DG1=wtake(1024, True),
                     DG3=wtake(1024, True), SM=wtake(320, True),
                     PA=[wtake(512), wtake(512)], PT=[wtake(512), wtake(512)], RT=[wtake(512), wtake(512)],
                     XB=[wtake(512) for _ in range(4)], BEK=wtake(512), BV=wtake(512), DE=wtake(512))
        def w64(n):
            ap = WAR[0:64, woff[0]:woff[0] + n]
            woff[0] += n
            return ap.rearrange("p (x i) -> p x i", x=16)

        NWT2 = [[NWT[d][:, :, :], w64(1024)] for d in range(2)]
        QDEC2 = [[QDEC[d][:, :, :], w64(1024)] for d in range(2)]
        U02 = [[U0[d], wtake(1024, True)] for d in range(2)]
        QKM2 = [[QKM[d], wtake(512)] for d in range(2)]
        KDEC2 = [[KDEC[d], wtake(512)] for d in range(2)]
        EGL22 = [[EGL2[d][:, :], wtake(32, True)] for d in range(2)]
        ab = [0]
        fbk = [0]
        ppr = [0]

        def abank():
            ab[0] = (ab[0] + 1) % 6
            return ab[0]

        def fbank():
            fbk[0] ^= 1
            return 6 + fbk[0]

        def ppair():
            ppr[0] = (ppr[0] + 1) % 3
            return ppr[0]

        idb = bcast(C("identst", True), 1, H8)
        ist = bcast(C("identst"), 1, H8)

        def msk(name):
            return bcast(C(name, True), 1, H8)

        def prep(s, d, m, par):
            NWTp, QDECp, U0p, QKMp, KDECp, EGL2p = NWT2[d][par], QDEC2[d][par], U02[d][par], QKM2[d][par], KDEC2[d][par], EGL22[d][par]
            kq = (d, par)
            T_ = DS[d]
            SMd, DG1d, DG3d, DEd = T_["SM"], T_["DG1"], T_["DG3"], T_["DE"]
            E1d, E2d, E3d = T_["E1"], T_["E2"], T_["E3"]
            PAd, PTd, RTd, XBd, BEKd, BVd = T_["PA"], T_["PT"], T_["RT"], T_["XB"], T_["BEK"], T_["BV"]

            def K(name, *x):
                return (name, d) + tuple(x)

            gti = s.tile0 + m
            sfx = "f" if d == 0 else "b"
            rows = slice(m * 128, (m + 1) * 128)
            dma("sp", KTC[d][:, :, :], KDT[s.name][:, :, rows].rearrange("h d t -> d h t"), [("qkdt", s.name)], [("ktc", d)])
            dma("sp", QTC[d][:, :, :], QDT[s.name][:, :, rows].rearrange("h d t -> d h t"), [("qkdt", s.name)], [("qtc", d)])
            dma("sp", KTMC[d][:, :], KTM[s.name][rows, :], [("tm", s.name)], [("ktmc", d)])
            dma("sp", QTMC[d][:, :], QTM[s.name][rows, :], [("tm", s.name)], [("qtmc", d)])
            dma("sp", VTMC[d][:, :], VTM[s.name][rows, :], [("tm", s.name)], [("vtmc", d)])
            la = LA[:, gti, 8 * d:8 * d + 8]
            lb = LB[:, gti, 8 * d:8 * d + 8]
            be_ = BETA[:, gti, 8 * d:8 * d + 8]
            bg = fbank()
            mm(psb(bg)[:, 0:8], C("tri_" + sfx), la, True, True, ["la", "cst"], [pk(bg)])
            mm(psb(bg)[:, 8:16], C("half0"), la, True, True, ["la", "cst"], [pk(bg)])
            mm(psb(bg)[:, 16:24], C("half1"), la, True, True, ["la", "cst"], [pk(bg)])
            cp("dve", SMd[:, 0:24], psb(bg)[:, 0:24], [pk(bg)], [K("sm")])
            yield
            tt("dve", SMd[:, 24:32], SMd[:, 0:8], lb, ALU.add, [K("sm"), "lb"], [K("sm_glb")])
            act(SMd[:, 32:40], SMd[:, 0:8], AF.Exp, [K("sm")], [K("sm_eg")])
            cp("dve", SMd[0:64, 40:48], SMd[0:64, 8:16], [K("sm")], [K("sm_glo")])
            cp("dve", SMd[64:128, 40:48], SMd[64:128, 16:24], [K("sm")], [K("sm_glo")])
            tt("dve", SMd[:, 48:56], SMd[:, 40:48], SMd[:, 0:8], ALU.subtract, [K("sm"), K("sm_glo")], [K("sm_ek")])
            act(SMd[:, 48:56], SMd[:, 48:56], AF.Exp, [K("sm_ek")], [K("sm_ek")])
            act(EGL2p, SMd[:, 8:24], AF.Exp, [K("sm")], [("egl2",) + kq])
            tt("dve", SMd[:, 56:64], be_, SMd[:, 32:40], ALU.mult, ["beta", K("sm_eg")], [K("sm_be")])
            tsc("dve", SMd[:, 64:72], SMd[:, 0:8], -1.0, None, ALU.mult, None, [K("sm")], [K("sm_ng")])
            tt("pool", v3(DG1d), ist, bcast(SMd[:, 24:32], 2, 64), ALU.mult, ["cst", K("sm_glb")], [K("dg1")])
            tt("pool", v3(DG3d), ist, bcast(SMd[:, 0:8], 2, 64), ALU.mult, ["cst", K("sm")], [K("dg3")])
            tt("dve", v3(DEd), ist, bcast(SMd[:, 32:40], 2, 64), ALU.mult, ["cst", K("sm_eg")], [K("de")])
            yield
            b1 = fbank()
            mm(psb(b1)[:, :], C("negblk"), DG3d, True, False, [K("dg3"), "cst"], [pk(b1)])
            mm(psb(b1)[:, :], C("ident"), bcast(SMd[:, 24:32], 2, 64), False, False, [K("sm_glb"), "cst"], [pk(b1)])
            mm(psb(b1)[:, :], C("ident", True), msk("m1_" + sfx), False, True, ["cstb"], [pk(b1)])
            act(E1d, psb(b1)[:, :], AF.Exp, [pk(b1)], [K("e1")])
            yield
            b2 = fbank()
            mm(psb(b2)[:, :], C("blk"), DG1d, True, False, [K("dg1"), "cst"], [pk(b2)])
            mm(psb(b2)[:, :], C("ident"), bcast(SMd[:, 64:72], 2, 64), False, False, [K("sm_ng"), "cst"], [pk(b2)])
            mm(psb(b2)[:, :], C("ident", True), msk("m2_" + sfx), False, True, ["cstb"], [pk(b2)])
            act(E2d, psb(b2)[:, :], AF.Exp, [pk(b2)], [K("e2")])
            yield
            b3 = fbank()
            mm(psb(b3)[:, :], C("blk"), DG3d, True, False, [K("dg3"), "cst"], [pk(b3)])
            mm(psb(b3)[:, :], C("ident"), bcast(SMd[:, 64:72], 2, 64), False, False, [K("sm_ng"), "cst"], [pk(b3)])
            mm(psb(b3)[:, :], C("ident", True), msk("m3_" + sfx), False, True, ["cstb"], [pk(b3)])
            act(E3d, psb(b3)[:, :], AF.Exp, [pk(b3)], [K("e3")])
            yield
            bkk, bqk = abank(), abank()
            for h in range(H8):
                for a in range(2):
                    ts_ = slice(64 * a, 64 * a + 64)
                    mm(psb(bkk)[ts_, h * 64:(h + 1) * 64], KTC[d][:, h, ts_], KTC[d][:, h, ts_], True, True,
                       [("ktc", d)], [pk(bkk)], tile_position=(0, 64 * a))
                    mm(psb(bqk)[ts_, h * 64:(h + 1) * 64], KTC[d][:, h, ts_], QTC[d][:, h, ts_], True, True,
                       [("ktc", d), ("qtc", d)], [pk(bqk)], tile_position=(0, 64 * a))
            A_, AT_ = PAd[0], PTd[0]
            kA, kAT = K("pa", 0), K("pt", 0)
            tt("dve", A_, psb(bkk)[:, :], E1d, ALU.mult, [pk(bkk), K("e1")], [kA])
            tt("dve", AT_, psb(bkk)[:, :], E2d, ALU.mult, [pk(bkk), K("e2")], [kAT])
            tt("dve", QKMp, psb(bqk)[:, :], E3d, ALU.mult, [pk(bqk), K("e3")], [("qkm",) + kq])
            yield

            def grp(L, R, lkey, rkey):
                bank = abank()
                for h in range(H8):
                    for a in range(2):
                        ts_ = slice(64 * a, 64 * a + 64)
                        hs = slice(h * 64, (h + 1) * 64)
                        mm(psb(bank)[ts_, hs], L[ts_, hs], R[ts_, hs], True, True, [lkey, rkey], [pk(bank)],
                           tile_position=(64 * a, 64 * a))
                return bank

            D_, DT_ = PAd[1], PTd[1]
            kD, kDT = K("pa", 1), K("pt", 1)
            X = [XBd[0], XBd[1], BEKd, BVd, XBd[2], XBd[3]]
            kX = [K("xb", 0), K("xb", 1), K("bek"), K("bv"), K("xb", 2), K("xb", 3)]
            kR = [K("rt", 0), K("rt", 1)]
            tt("pool", v3(D_), v3(A_), msk("mask8"), ALU.mult, [kA, "cstb"], [kD])
            tt("pool", v3(DT_), v3(AT_), msk("mask8"), ALU.mult, [kAT, "cstb"], [kDT])
            tt("dve", v3(X[2]), idb, v3(DT_), ALU.subtract, ["cstb", kDT], [kX[2]])
            yield
            g1 = grp(DT_, D_, kDT, kD)
            g2 = grp(D_, DT_, kD, kDT)
            cp("act", X[0], psb(g1)[:, :], [pk(g1)], [kX[0]])
            tt("dve", v3(RTd[1]), v3(X[0]), idb, ALU.add, [kX[0], "cstb"], [kR[1]])
            cp("dve", RTd[0], psb(g2)[:, :], [pk(g2)], [kR[0]])
            yield
            g3 = grp(RTd[0], X[0], kR[0], kX[0])
            tt("dve", v3(X[1]), v3(psb(g3)[:, :]), idb, ALU.add, [pk(g3), "cstb"], [kX[1]])
            g1 = grp(RTd[1], X[2], kR[1], kX[2])
            cp("act", X[3], psb(g1)[:, :], [pk(g1)], [kX[3]])
            yield
            g2 = grp(X[3], X[1], kX[3], kX[1])
            g3 = grp(X[1], X[3], kX[1], kX[3])
            cp("act", X[4], psb(g2)[:, :], [pk(g2)], [kX[4]])
            cp("dve", X[5], psb(g3)[:, :], [pk(g3)], [kX[5]])
            yield
            Tb, kT = [X[4], X[0]], [kX[4], kX[0]]
            Mb, kM = [X[5], X[1]], [kX[5], kX[1]]
            cur = 0
            for li, mname in enumerate(("moff8", "moff16", "moff32")):
                last = (li == 2)
                nxt = 1 - cur
                tt("pool", v3(D_), v3(A_), msk(mname), ALU.mult, [kA, "cstb"], [kD])
                if not last:
                    tt("pool", v3(DT_), v3(AT_), msk(mname), ALU.mult, [kAT, "cstb"], [kDT])
                g1 = grp(D_, Mb[cur], kD, kM[cur])
                cp("act", RTd[1], psb(g1)[:, :], [pk(g1)], [kR[1]])
                if not last:
                    g2 = grp(DT_, Tb[cur], kDT, kT[cur])
                    cp("dve", RTd[0], psb(g2)[:, :], [pk(g2)], [kR[0]])
                yield
                g3 = grp(Tb[cur], RTd[1], kT[cur], kR[1])
                tt("dve", Mb[nxt], Mb[cur], psb(g3)[:, :], ALU.subtract, [kM[cur], pk(g3)], [kM[nxt]])
                if not last:
                    g1 = grp(Mb[cur], RTd[0], kM[cur], kR[0])
                    tt("dve", Tb[nxt], Tb[cur], psb(g1)[:, :], ALU.subtract, [kT[cur], pk(g1)], [kT[nxt]])
                cur = nxt
                yield
            TTm = Mb[cur]
            tkey = kM[cur]
            tt("dve", v3(BEKd), v3(KTMC[d][:, :]), bcast(SMd[:, 56:64], 2, 64), ALU.mult, [("ktmc", d), K("sm_be")], [K("bek")])
            tt("pool", v3(KDECp), v3(KTMC[d][:, :]), bcast(SMd[:, 48:56], 2, 64), ALU.mult, [("ktmc", d), K("sm_ek")], [("kdec",) + kq])
            tt("pool", v3(BVd), v3(VTMC[d][:, :]), bcast(be_, 2, 64), ALU.mult, [("vtmc", d), "beta"], [K("bv")])
            yield
            pp_ = ppair()
            pkeys = [pk(2 * pp_), pk(2 * pp_ + 1)]
            bu0 = abank()
            while bu0 in (2 * pp_, 2 * pp_ + 1):
                bu0 = abank()
            for h in range(H8):
                for a in range(2):
                    ts_ = slice(64 * a, 64 * a + 64)
                    hs = slice(h * 64, (h + 1) * 64)
                    cs = slice((a * 8 + h) * 64, (a * 8 + h) * 64 + 64)
                    mm(PS[pp_][0:64, cs], BEKd[ts_, hs], TTm[ts_, hs], True, True, [K("bek"), tkey], pkeys,
                       tile_position=(64 * a, 0))
                    mm(psb(bu0)[ts_, hs], TTm[ts_, hs], BVd[ts_, hs], True, True, [tkey, K("bv")], [pk(bu0)],
                       tile_position=(64 * a, 64 * a))
            tsc("dve", NWTp, PS[pp_][0:64, :].rearrange("p (x i) -> p x i", x=16), -1.0, None, ALU.mult, None,
                pkeys, [("nwt",) + kq])
            cp("act", U0p, psb(bu0)[:, :], [pk(bu0)], [("u0",) + kq])
            yield
            pp_ = ppair()
            pkeys = [pk(2 * pp_), pk(2 * pp_ + 1)]
            for a in range(2):
                mm(PS[pp_][0:64, a * 512:(a + 1) * 512], C("half%d" % a, True)[:, 0:64], DEd, True, True,
                   [K("de"), "cstb"], pkeys)
            for a in range(2):
                tt("dve", QDECp[:, a * 8:(a + 1) * 8, :],
                   QTC[d][:, :, a * 64:(a + 1) * 64],
                   PS[pp_][0:64, a * 512:(a + 1) * 512].rearrange("p (h i) -> p h i", h=8), ALU.mult,
                   [("qtc", d)] + pkeys, [("qdec",) + kq])
            yield

        def steps(s, d, m, par, first_visit):
            NWTp, QDECp, U0p, QKMp, KDECp, EGL2p = NWT2[d][par], QDEC2[d][par], U02[d][par], QKM2[d][par], KDEC2[d][par], EGL22[d][par]
            kq = (d, par)
            SMd = DS[d]["SM"]

            def K(name, *x):
                return (name, d) + tuple(x)

            rows = slice(m * 128, (m + 1) * 128)
            for a in ((0, 1) if d == 0 else (1, 0)):
                ts_ = slice(64 * a, 64 * a + 64)
                pu = abank()
                for h in range(H8):
                    hs = slice(h * 64, (h + 1) * 64)
                    mm(psb(pu)[ts_, hs], NWTp[:, a * 8 + h, :], SBF[d][:, h, :], True, True,
                       [("nwt",) + kq, ("sbf", d)], [pk(pu)], tile_position=(0, 64 * a))
                tt("dve", UB[d][ts_, :], U0p[ts_, :], psb(pu)[ts_, :], ALU.add, [("u0",) + kq, pk(pu)], [("ub", d)])
                yield
                po, pob, pS_ = abank(), abank(), abank()
                for h in range(H8):
                    hs = slice(h * 64, (h + 1) * 64)
                    mm(psb(po)[ts_, hs], QDECp[:, a * 8 + h, :], SBF[d][:, h, :], True, True,
                       [("qdec",) + kq, ("sbf", d)], [pk(po)], tile_position=(0, 64 * a))
                    mm(psb(pob)[ts_, hs], QKMp[ts_, hs], UB[d][ts_, hs], True, True,
                       [("qkm",) + kq, ("ub", d)], [pk(pob)], tile_position=(64 * a, 64 * a))
                    mm(psb(pS_)[0:64, hs], KDECp[ts_, hs], UB[d][ts_, hs], True, True,
                       [("kdec",) + kq, ("ub", d)], [pk(pS_)], tile_position=(64 * a, 0))
                tt("dve", S32[d][:, :, :], S32[d][:, :, :], bcast(EGL2p[0:64, a * 8:a * 8 + 8], 2, 64), ALU.mult,
                   [("s32", d), ("egl2",) + kq], [("s32", d)])
                tt("dve", S32[d][:, :, :], S32[d][:, :, :], psb(pS_)[0:64, :].rearrange("p (h v) -> p h v", h=H8), ALU.add,
                   [("s32", d), pk(pS_)], [("s32", d)])
                cp("act", SBF[d][:, :, :], S32[d][:, :, :], [("s32", d)], [("sbf", d)])
                cp("act", OACC[d][ts_, :], psb(po)[ts_, :], [pk(po)], [("oacc", d)])
                tt("dve", OACC[d][ts_, :], OACC[d][ts_, :], psb(pob)[ts_, :], ALU.add, [("oacc", d), pk(pob)], [("oacc", d)])
                yield
            if first_visit:
                dma("sp", OPART[s.name][rows, :], OACC[d][:, :], [("oacc", d)], [("opart", s.name, m)])
            else:
                dma("sp", STG[d][:, :], OPART[s.name][rows, :], [("opart", s.name, m)], [f"stg{d}"])
                tt("dve", OACC[d][:, :], OACC[d][:, :], STG[d][:, :], ALU.add, [("oacc", d), f"stg{d}"], [("oacc", d)])
                tt("pool", STG[2 + d][:, :], OACC[d][:, :], OACC[d][:, :], ALU.mult, [("oacc", d)], [f"stg{2 + d}"])
                P.add("dve", lambda e: e.tensor_reduce(SMd[:, 80:88], v3(STG[2 + d][:, :]), AX.X, ALU.add),
                      [f"stg{2 + d}"], [K("sm_rn")])
                act(SMd[:, 80:88], SMd[:, 80:88], AF.Sqrt, [K("sm_rn")], [K("sm_rn")], bias=EPS, scale=1.0 / 64)
                recip(SMd[:, 80:88], SMd[:, 80:88], [K("sm_rn")], [K("sm_rn")])
                yield
                tt("dve", v3(OACC[d][:, :]), v3(OACC[d][:, :]), bcast(SMd[:, 80:88], 2, 64), ALU.mult,
                   [("oacc", d), K("sm_rn")], [("oacc", d)])
                tt("dve", v3(OACC[d][:, :]), v3(OACC[d][:, :]), bcast(dng[:], 1, H8), ALU.mult,
                   [("oacc", d), "lvec"], [("oacc", d)])
                dma("sp", STB[d][:, :], GS[s.name][rows, :], [("gs", s.name)], [f"stb{d}"])
                tt("dve", OACC[d][:, :], OACC[d][:, :], STB[d][:, :], ALU.mult, [("oacc", d), f"stb{d}"], [("oacc", d)])
                bt = fbank()
                for cc in range(4):
                    tr(psb(bt)[:, cc * 128:(cc + 1) * 128], OACC[d][:, cc * 128:(cc + 1) * 128], C("ident"),
                       [("oacc", d), "cst"], [pk(bt)])
                cp("act", STB[2 + d][:, :], psb(bt)[:, :], [pk(bt)], [f"stb{2 + d}"])
                dma("sp", OD[s.name][:, :, rows], STB[2 + d][:, :].rearrange("p (c t) -> p c t", c=4), [f"stb{2 + d}"],
                    [("od", s.name)])

        for s in seqs:
            NP_ = s.T // 128
            for d in range(2):
                if s.is_sample:
                    dma("sp", S32[d][:, :, :], st_in[l, d].rearrange("h k v -> k h v"), [], [("s32", d)])
                else:
                    P.add("pool", lambda e, d=d: e.memset(S32[d][:, :, :], 0.0), [], [("s32", d)])
                cp("act", SBF[d][:, :, :], S32[d][:, :, :], [("s32", d)], [("sbf", d)])
            visited = set()

            def mof(d, step):
                return step if d == 0 else NP_ - 1 - step

            def rr(gens):
                active = list(gens)
                while active:
                    for g_ in list(active):
                        try:
                            next(g_)
                        except StopIteration:
                            active.remove(g_)

            rr([prep(s, d, mof(d, 0), 0) for d in range(2)])
            for step in range(NP_):
                gens = []
                for d in range(2):
                    m = mof(d, step)
                    gens.append(steps(s, d, m, step % 2, m not in visited))
                for d in range(2):
                    visited.add(mof(d, step))
                if step + 1 < NP_:
                    for d in range(2):
                        gens.append(prep(s, d, mof(d, step + 1), (step + 1) % 2))
                rr(gens)
            if not s.is_sample:
                for d in range(2):
                    dma("sp", nst_out[s.idx, l, d].rearrange("h k v -> k h v"), S32[d][:, :, :], [("s32", d)], [("nst", s.idx)])
        P.fence()
        if stop_after == "scan":
            return finish()

        Wpa = WAR[:, 0:4096].rearrange("p (k n) -> p k n", k=4)
        Wpd = WAR[:, 4096:8192].rearrange("p (k n) -> p k n", k=4)
        Wo = WAR[:, 8192:16384].rearrange("p (k n) -> p k n", k=8)
        load_w(Wpa, w_pa[l], 4)
        load_w(Wpd, w_pd[l], 4)
        load_w(Wo, w_out[l], 8)
        def w3(off, n, f32, shape3):
            ap = WAR[:, off:off + n]
            if f32:
                ap = ap.bitcast(F32)
            return ap.rearrange("p (c t) -> p c t", c=shape3)

        DSET = [
            dict(HB=HB[:, :, :], GATB=GATB[:, :, :], XT=XT, X1T=X1T, S0=STG[0][:, :], S1=STG[1][:, :], R1=R1[:, :], RSTD=RSTD[:, :],
                 bpa=0, bpd=1, bop=(2, 3), bst=0, k="0"),
            dict(HB=w3(16384, 4096, False, 16), GATB=w3(20480, 4096, False, 16), XT=w3(24576, 4096, True, 8),
                 X1T=w3(28672, 4096, True, 8), S0=WAR[:, 32768:33792].bitcast(F32), S1=WAR[:, 33792:34816].bitcast(F32),
                 R1=WAR[:, 34816:35328].bitcast(F32), RSTD=WAR[:, 35328:35840].bitcast(F32),
                 bpa=4, bpd=5, bop=(6, 7), bst=4, k="1"),
        ]
        jobsD = [(s, blk) for s in seqs for blk in range(s.T // TB)]

        def blockD(j, slot):
            s, blk = jobsD[j]
            mj = s.mj
            t0 = blk * TB
            B_ = DSET[slot]
            kk_ = B_["k"]
            HBd, GATd, XTd, X1Td, S0, S1, R1d, RSTDd = B_["HB"], B_["GATB"], B_["XT"], B_["X1T"], B_["S0"], B_["S1"], B_["R1"], B_["RSTD"]
            OATd, ODTd, MGd = HBd[:, 0:4, :], HBd[:, 4:8, :], HBd[:, 8:16, :]
            khb, kht, kg, kx, kx1, ks0, ks1 = "dhb" + kk_, "dht" + kk_, "dg" + kk_, "dx" + kk_, "dx1" + kk_, "ds0" + kk_, "ds1" + kk_
            for c in range(4):
                dma("sp", OATd[:, c, :], OA[s.name][2 * c:2 * c + 2].rearrange("h d t -> (h d) t")[:, t0:t0 + TB],
                    [("oa", s.name)], [khb])
            dma("sp", ODTd, OD[s.name][:, :, t0:t0 + TB], [("od", s.name)], [khb])
            dma("sp", GATd, GATES[s.name][:, :, t0:t0 + TB], [("gates", s.name, blk)], [kg])
            dma("sp", XTd, XRES[s.name][:, :, t0:t0 + TB], [("xres", s.name, blk)], [kx])
            yield
            for n in range(8):
                ns = slice(n * 128, (n + 1) * 128)
                for k in range(4):
                    mm(psb(B_["bpa"])[:, :TB], Wpa[:, k, ns], OATd[:, k, :], k == 0, k == 3, ["war", khb], [pk(B_["bpa"])])
                for k in range(4):
                    mm(psb(B_["bpd"])[:, :TB], Wpd[:, k, ns], ODTd[:, k, :], k == 0, k == 3, ["war", khb], [pk(B_["bpd"])])
                tt("dve", S0[:, :TB], psb(B_["bpa"])[:, :TB], GATd[:, n, :], ALU.mult, [pk(B_["bpa"]), kg], [ks0])
                tt("dve", S1[:, :TB], psb(B_["bpd"])[:, :TB], GATd[:, 8 + n, :], ALU.mult, [pk(B_["bpd"]), kg], [ks1])
                tt("pool", MGd[:, n, :], S0[:, :TB], S1[:, :TB], ALU.add, [ks0, ks1], [kht])
                yield
            for n in range(8):
                ns = slice(n * 128, (n + 1) * 128)
                bk = B_["bop"][n % 2]
                for k in range(8):
                    mm(psb(bk)[:, :TB], Wo[:, k, ns], MGd[:, k, :], k == 0, k == 7, ["war", kht], [pk(bk)])
                stt("dve", X1Td[:, n, :], psb(bk)[:, :TB], MOD[:, 16 + n, mj:mj + 1], XTd[:, n, :], ALU.mult, ALU.add,
                    [pk(bk), kx] + MK, [kx1])
                yield
            dma("sp", XRES[s.name][:, :, t0:t0 + TB], X1Td, [kx1], [("xres", s.name, blk)])
            rms_stats(X1Td, TB, kx1, sq=HBd[:, 0:8, :], sqkey=khb, bank=B_["bst"], r1=R1d, rstd=RSTDd, rkey=kk_)
            yield
            tt("dve", XTd, X1Td, bcast(RSTDd, 1, 8), ALU.mult, [kx1, "rstd" + kk_], [kx])
            for c in range(8):
                if c % 2 == 0:
                    act(MGd[:, c, :], XTd[:, c, :], AF.Identity, [kx] + MK, [kht],
                        scale=A2[:, c, mj:mj + 1], bias=MOD[:, 24 + c, mj:mj + 1])
                else:
                    tsc("dve", MGd[:, c, :], XTd[:, c, :], A2[:, c, mj:mj + 1], MOD[:, 24 + c, mj:mj + 1], ALU.mult, ALU.add,
                        [kx] + MK, [kht])
            dma("sp", H2[s.name][:, :, t0:t0 + TB], MGd, [kht], [("h2", s.name, blk)])

        def two_way(make, n, stagger):
            gens = [None, None]
            nxt = 0
            started = 0
            while True:
                progressed = False
                for slot in range(2):
                    if gens[slot] is None and nxt < n and (slot == 0 or started >= stagger or nxt > 1):
                        gens[slot] = make(nxt, slot)
                        nxt += 1
                    if gens[slot] is not None:
                        try:
                            next(gens[slot])
                            progressed = True
                            if slot == 0:
                                started += 1
                        except StopIteration:
                            gens[slot] = None
                            progressed = True
                if not progressed and nxt >= n and gens[0] is None and gens[1] is None:
                    break

        two_way(blockD, len(jobsD), 9)
        P.fence()
        if stop_after == "stageD":
            return finish()

        W1h = WAR[:, 0:8 * 2048].rearrange("p (k n) -> p k n", k=8)
        W2h = WAR[:, 16384:16384 + 16 * 1024].rearrange("p (k n) -> p k n", k=16)
        for hf in range(2):
            load_w(W1h, w1[l][:, hf * 2048:(hf + 1) * 2048], 8)
            load_w(W2h, w2[l][hf * 2048:(hf + 1) * 2048, :], 16)
            jobs = [(s, blk) for s in seqs for blk in range(s.T // TB)]
            XTs = [(XT, "bigA"), (X1T, "bigB")]
            H2Ts = [(HB[:, 0:8, :], "hb"), (HB[:, 8:16, :], "ht")]

            def ef_loads(j):
                s, blk = jobs[j]
                t0 = blk * TB
                h2t, hkey = H2Ts[j % 2]
                xt, xkey = XTs[j % 2]
                dma("sp", h2t, H2[s.name][:, :, t0:t0 + TB], [("h2", s.name, blk)], [hkey])
                dma("sp", xt, XRES[s.name][:, :, t0:t0 + TB], [("xres", s.name, blk)], [xkey])

            ef_loads(0)
            for j, (s, blk) in enumerate(jobs):
                mj = s.mj
                t0 = blk * TB
                H2T, hkey = H2Ts[j % 2]
                XTj, xkey = XTs[j % 2]
                if j + 1 < len(jobs):
                    ef_loads(j + 1)
                for f in range(16):
                    bk = 1 + f % 3
                    for k in range(8):
                        mm(psb(bk)[:, :TB], W1h[:, k, f * 128:(f + 1) * 128], H2T[:, k, :], k == 0, k == 7, ["war", hkey], [pk(bk)])
                    i = nstg()
                    act(STG[i][:, :TB], psb(bk)[:, :TB], AF.Relu, [pk(bk), "lvec"], [f"stg{i}"],
                        bias=b1f[:, hf * 16 + f:hf * 16 + f + 1])
                    tt("dve" if f % 2 == 0 else "pool", GATB[:, f, :], STG[i][:, :TB], STG[i][:, :TB], ALU.mult,
                       [f"stg{i}"], ["gatb"])
                for n in range(8):
                    bk = 4 + n % 2
                    for f in range(16):
                        mm(psb(bk)[:, :TB], W2h[:, f, n * 128:(n + 1) * 128], GATB[:, f, :], f == 0, f == 15, ["war", "gatb"], [pk(bk)])
                    stt("dve", XTj[:, n, :], psb(bk)[:, :TB], MOD[:, 40 + n, mj:mj + 1], XTj[:, n, :], ALU.mult, ALU.add,
                        [pk(bk), xkey] + MK, [xkey])
                    if hf == 0:
                        tsc("dve", XTj[:, n, :], XTj[:, n, :], GB2[:, n, mj:mj + 1], None, ALU.add, None, [xkey] + MK, [xkey])
                if not (l == NLAYERS - 1 and hf == 1):
                    dma("pool", XRES[s.name][:, :, t0:t0 + TB], XTj, [xkey], [("xres", s.name, blk)])
                else:
                    rms_stats(XTj, TB, xkey, sq=GATB[:, 0:8, :], sqkey="gatb")
                    tt("dve", XTj, XTj, bcast(RSTD[:, :], 1, 8), ALU.mult, [xkey, "rstd"], [xkey])
                    tt("dve", XTj, XTj, bcast(fnf[:], 2, TB), ALU.mult, [xkey, "fnf"], [xkey])
                    dst = ys_out if s.is_sample else yp_out[s.idx * TP:(s.idx + 1) * TP, :]
                    for t2 in range(TB // 128):
                        for hh in range(2):
                            bk = 6 + hh
                            for c in range(4):
                                tr(psb(bk)[:, c * 128:(c + 1) * 128], XTj[:, hh * 4 + c, t2 * 128:(t2 + 1) * 128], C("ident"),
                                   [xkey, "cst"], [pk(bk)])
                            i = nstg()
                            cp("act" if hh == 0 else "dve", STG[i][:, :], psb(bk)[:, :], [pk(bk)], [f"stg{i}"])
                            dma("pool", dst[t0 + t2 * 128:t0 + (t2 + 1) * 128, hh * 512:(hh + 1) * 512], STG[i][:, :],
                                [f"stg{i}"], [("y", s.name)])
        P.fence()
        if stop_after == f"layer{l}":
            return finish()

    return finish()


def make_in_maps(inputs):
    cos, sin = _rope_tables()
    maps = []
    for core in range(8):
        b = core % 4
        m = {
            "xs": np.ascontiguousarray(inputs["x_sample"][b]),
            "xp": np.ascontiguousarray(inputs["x_prompt"][core * NPS:(core + 1) * NPS].reshape(NPS * TP, D)),
            "cvec": np.ascontiguousarray(np.stack([inputs["c"][b], inputs["c_ctx"]], 0)),
            "ck": np.ascontiguousarray(inputs["cache_k"][b]),
            "cv": np.ascontiguousarray(inputs["cache_v"][b]),
            "st": np.ascontiguousarray(inputs["state_delta"][b]),
            "a_log": np.ascontiguousarray(inputs["a_log"].reshape(DEPTH, 16)),
            "dt_bias": np.ascontiguousarray(inputs["dt_bias"].reshape(DEPTH, 16)),
            "cst": CST, "ropecos": cos, "ropesin": sin,
        }
        for k in ["w_mod", "b_mod", "norm1", "norm2", "w_in", "conv_w", "q_gain", "k_gain", "dn_gain",
                  "w_pa", "w_pd", "w_out", "w1", "b1", "w2", "b2", "final_norm"]:
            m[k] = np.ascontiguousarray(inputs[k])
        maps.append(m)
    return maps


_NC_CACHE = {}


def kernel(**inputs):
    inputs = {k: np.asarray(v) for k, v in inputs.items()}
    if "nc" not in _NC_CACHE:
        _NC_CACHE["nc"] = build_program()
    nc = _NC_CACHE["nc"]
    maps = make_in_maps(inputs)
    res = run_bass_kernel_spmd(nc, maps, core_ids=list(range(8)))
    r = res.results
    y_sample = np.stack([r[b]["ys"] for b in range(4)], 0)
    y_prompt = np.concatenate([r[c]["yp"].reshape(NPS, TP, D) for c in range(8)], 0)
    nk = np.concatenate([r[c]["nk"] for c in range(8)], 0)
    nv = np.concatenate([r[c]["nv"] for c in range(8)], 0)
    nst = np.concatenate([r[c]["nst"] for c in range(8)], 0)
    return (y_prompt.astype(np.float32), y_sample.astype(np.float32), nk.astype(np.float32),
            nv.astype(np.float32), nst.astype(np.float32))
```

```python
import numpy as np
from contextlib import ExitStack
import concourse.bass as bass
import concourse.mybir as mybir
from concourse.bass_utils import run_bass_kernel_spmd

F32 = mybir.dt.float32
BF16 = mybir.dt.bfloat16
AF = mybir.ActivationFunctionType
ALU = mybir.AluOpType
AX = mybir.AxisListType

D = 1024
DEPTH = 2
TS = 4096
TP = 256
NPS = 4
NCTX = 256
INW = 4896
DFF = 4096
EPS = 1e-6
NEG = -30000.0

C_QA, C_KA, C_VA, C_QD, C_KD, C_VD, C_GO, C_AI, C_BI, C_GA, C_GD = 0, 512, 640, 768, 1280, 1792, 2304, 2816, 2832, 2848, 3872


ENGS = ["pe", "act", "dve", "pool", "sp"]
N_DMA_SEMS = {"sp": 40, "pool": 16, "act": 8}
EPOCH = 30000


class Ev:
    __slots__ = ("dma", "eng", "idx", "sem", "val")

    def __init__(self, dma, eng, idx, sem=None, val=None):
        self.dma, self.eng, self.idx, self.sem, self.val = dma, eng, idx, sem, val


class Rec:
    __slots__ = ("eng", "fn", "waits", "signal", "idx", "dma", "sig_sem", "sig_val")

    def __init__(self, eng, fn):
        self.eng, self.fn = eng, fn
        self.waits = []
        self.signal = False
        self.dma = None


class Buf:
    __slots__ = ("w", "r")

    def __init__(self):
        self.w = None
        self.r = []


class Prog:
    def __init__(self, nc):
        self.nc = nc
        self.ops = {e: [] for e in ENGS}
        self.waited = {e: {p: -1 for p in ENGS} for e in ENGS}
        self.waited_dma = {e: {} for e in ENGS}
        self.bufs = {}
        self.dma_count = {q: 0 for q in N_DMA_SEMS}

    def buf(self, k):
        b = self.bufs.get(k)
        if b is None:
            b = self.bufs[k] = Buf()
        return b

    def add(self, eng, fn, reads=(), writes=(), dma=False):
        rec = Rec(eng, fn)
        rec.idx = len(self.ops[eng])
        deps = []
        for k in reads:
            b = self.buf(k)
            if b.w is not None:
                deps.append((b.w, True))
        for k in writes:
            b = self.buf(k)
            if b.w is not None:
                deps.append((b.w, False))
            for r in b.r:
                deps.append((r, False))
        if dma:
            d = self.dma_count[eng]
            n = N_DMA_SEMS[eng]
            si, val = d % n, 16 * (d // n + 1)
            if d >= n:
                deps.append((Ev(True, eng, None, si, val - 16), True))
            rec.dma = (si, val)
            ev = Ev(True, eng, rec.idx, si, val)
            self.dma_count[eng] += 1
        else:
            ev = Ev(False, eng, rec.idx)
        for dep, raw in deps:
            if dep.dma:
                key = (dep.eng, dep.sem)
                if self.waited_dma[eng].get(key, 0) >= dep.val:
                    continue
                self.waited_dma[eng][key] = dep.val
                rec.waits.append(dep)
            else:
                if dep.eng == eng and eng == "pe":
                    continue
                if self.waited[eng][dep.eng] >= dep.idx:
                    continue
                self.waited[eng][dep.eng] = dep.idx
                rec.waits.append(dep)
                self.ops[dep.eng][dep.idx].signal = True
        for k in reads:
            self.buf(k).r.append(ev)
        for k in writes:
            b = self.buf(k)
            b.w = ev
            b.r = []
        self.ops[eng].append(rec)
        return rec

    def fence(self):
        last = {e: len(self.ops[e]) - 1 for e in ["pe", "act", "dve", "pool"]}
        dma_evs = []
        for q, n in N_DMA_SEMS.items():
            d = self.dma_count[q]
            for i in range(min(n, d)):
                uses = (d - i + n - 1) // n
                dma_evs.append(Ev(True, q, None, i, 16 * uses))
        for e in ENGS:
            rec = Rec(e, lambda eng: eng.nop())
            rec.idx = len(self.ops[e])
            for p, li in last.items():
                if p == e or li < 0:
                    continue
                j = li
                while j >= 0 and self.ops[p][j].dma is not None:
                    j -= 1
                if j < 0 or self.waited[e][p] >= j:
                    continue
                self.waited[e][p] = j
                rec.waits.append(Ev(False, p, j))
                self.ops[p][j].signal = True
            for dep in dma_evs:
                key = (dep.eng, dep.sem)
                if self.waited_dma[e].get(key, 0) >= dep.val:
                    continue
                self.waited_dma[e][key] = dep.val
                rec.waits.append(dep)
            self.ops[e].append(rec)

    def emit(self, es):
        nc = self.nc
        comp_sems = {}
        for e in ["pe", "act", "dve", "pool"]:
            cnt = 0
            for rec in self.ops[e]:
                if rec.signal and rec.dma is None:
                    ep = cnt // EPOCH
                    if (e, ep) not in comp_sems:
                        comp_sems[(e, ep)] = es.enter_context(nc.semaphore(f"c_{e}_{ep}"))
                    rec.sig_sem = comp_sems[(e, ep)]
                    rec.sig_val = cnt % EPOCH + 1
                    cnt += 1
        dma_sems = {}
        for q, n in N_DMA_SEMS.items():
            for i in range(min(n, self.dma_count[q])):
                dma_sems[(q, i)] = es.enter_context(nc.semaphore(f"d_{q}_{i}"))
        final_waits = []
        for q, n in N_DMA_SEMS.items():
            d = self.dma_count[q]
            for i in range(min(n, d)):
                uses = (d - i + n - 1) // n
                final_waits.append((dma_sems[(q, i)], 16 * uses))
        block = es.enter_context(nc.Block())
        ops = self.ops

        def run(engname, eng):
            for rec in ops[engname]:
                for dep in rec.waits:
                    if dep.dma:
                        eng.wait_ge(dma_sems[(dep.eng, dep.sem)], dep.val)
                    else:
                        prod = ops[dep.eng][dep.idx]
                        eng.wait_ge(prod.sig_sem, prod.sig_val)
                ins = rec.fn(eng)
                if rec.dma is not None:
                    ins.then_inc(dma_sems[(engname, rec.dma[0])], 16)
                elif rec.signal:
                    ins.then_inc(rec.sig_sem, 1)
            if engname == "sp":
                for s, v in final_waits:
                    eng.wait_ge(s, v)

        @block.tensor
        def _(t):
            run("pe", t)

        @block.scalar
        def _(a):
            run("act", a)

        @block.vector
        def _(v):
            run("dve", v)

        @block.gpsimd
        def _(g):
            run("pool", g)

        @block.sync
        def _(s):
            run("sp", s)


CST_LAYOUT = {}


def _build_consts():
    p = np.arange(128)
    cols = []
    off = 0

    def put(name, arr):
        nonlocal off
        arr = np.asarray(arr, np.float32).reshape(128, -1)
        CST_LAYOUT[name] = (off, arr.shape[1])
        cols.append(arr)
        off += arr.shape[1]

    put("ident", np.eye(128))
    put("ones", np.ones((128, 128)))
    put("negones", -np.ones((128, 128)))
    half = p // 64
    put("blk", (half[:, None] == half[None, :]).astype(np.float32))
    put("negblk", -(half[:, None] == half[None, :]).astype(np.float32))
    put("identst", (p[:, None] % 64 == np.arange(64)[None, :]).astype(np.float32))
    same = half[:, None] == half[None, :]
    put("tri_f", (same & (p[:, None] <= p[None, :])).astype(np.float32))
    put("tri_b", (same & (p[:, None] >= p[None, :])).astype(np.float32))
    put("half0", np.repeat((p < 64).astype(np.float32)[:, None], 128, 1))
    put("half1", np.repeat((p >= 64).astype(np.float32)[:, None], 128, 1))
    il = p % 64
    j = np.arange(64)

    def m(keep):
        return np.where(keep, 0.0, NEG).astype(np.float32)

    put("m1_f", m(il[:, None] > j[None, :]))
    put("m1_b", m(il[:, None] < j[None, :]))
    put("m2_f", m(j[None, :] > il[:, None]))
    put("m2_b", m(j[None, :] < il[:, None]))
    put("m3_f", m(j[None, :] >= il[:, None]))
    put("m3_b", m(j[None, :] <= il[:, None]))
    put("mask8", ((il[:, None] // 8) == (j[None, :] // 8)).astype(np.float32))
    for sz in (8, 16, 32):
        put("moff%d" % sz, (((il[:, None] // (2 * sz)) == (j[None, :] // (2 * sz)))
                            & ((il[:, None] // sz) != (j[None, :] // sz))).astype(np.float32))
    R = np.zeros((128, 128), np.float32)
    for q in range(128):
        if q % 64 < 32:
            R[q, q + 32] = -1.0
        else:
            R[q, q - 32] = 1.0
    put("rot", R.T)
    return np.concatenate(cols, 1)


CST = _build_consts()
NCST = CST.shape[1]


def _rope_tables():
    t = np.arange(TS)
    row = (t // 64).astype(np.float32)
    col = (t % 64).astype(np.float32)
    inv = (10000.0 ** (-np.arange(16, dtype=np.float32) / 16)).astype(np.float32)
    ang = np.concatenate([row[:, None] * inv, col[:, None] * inv], -1).astype(np.float32)
    cos = np.cos(ang).astype(np.float32).T
    sin = np.sin(ang).astype(np.float32).T
    return np.tile(cos, (4, 1)).copy(), np.tile(sin, (4, 1)).copy()


class Seq:
    def __init__(self, name, T, is_sample, idx, key0, tile0):
        self.name, self.T, self.is_sample, self.idx = name, T, is_sample, idx
        self.key0 = key0
        self.tile0 = tile0
        self.nctx = NCTX if is_sample else 0
        self.mj = 0 if is_sample else 1


def bcast(ap, axis, n):
    shp = list(ap.shape)
    shp.insert(axis, n)
    return ap.unsqueeze(axis).broadcast_to(shp)


def build_program(debug_outs=(), stop_after=None):
    nc = bass.Bass("TRN2", target_bir_lowering=False)
    es = ExitStack()
    P = Prog(nc)
    dbg = set(debug_outs)

    def din(name, shape, dt=F32):
        return nc.dram_tensor(name, list(shape), dt, kind="ExternalInput").ap()

    def dout(name, shape, dt=F32):
        return nc.dram_tensor(name, list(shape), dt, kind="ExternalOutput").ap()

    def dscr(name, shape, dt=F32):
        kind = "ExternalOutput" if name in dbg else "Internal"
        return nc.dram_tensor(name, list(shape), dt, kind=kind).ap()

    xs_in = din("xs", [TS, D])
    xp_in = din("xp", [NPS * TP, D])
    cvec = din("cvec", [2, D])
    ck_in = din("ck", [DEPTH, 2, NCTX, 64])
    cv_in = din("cv", [DEPTH, 2, NCTX, 64])
    st_in = din("st", [DEPTH, 2, 8, 64, 64])
    w_mod = din("w_mod", [DEPTH, D, 6 * D])
    b_mod = din("b_mod", [DEPTH, 6 * D])
    norm1 = din("norm1", [DEPTH, D])
    norm2 = din("norm2", [DEPTH, D])
    w_in = din("w_in", [DEPTH, D, INW])
    conv_w = din("conv_w", [DEPTH, 3, 1536])
    q_gain = din("q_gain", [DEPTH, 64])
    k_gain = din("k_gain", [DEPTH, 64])
    a_log = din("a_log", [DEPTH, 16])
    dt_bias = din("dt_bias", [DEPTH, 16])
    dn_gain = din("dn_gain", [DEPTH, 64])
    w_pa = din("w_pa", [DEPTH, 512, D])
    w_pd = din("w_pd", [DEPTH, 512, D])
    w_out = din("w_out", [DEPTH, D, D])
    w1 = din("w1", [DEPTH, D, DFF])
    b1 = din("b1", [DEPTH, DFF])
    w2 = din("w2", [DEPTH, DFF, D])
    b2 = din("b2", [DEPTH, D])
    final_norm = din("final_norm", [D])
    cst_in = din("cst", [128, NCST])
    cos_in = din("ropecos", [128, TS])
    sin_in = din("ropesin", [128, TS])
    ys_out = dout("ys", [TS, D])
    yp_out = dout("yp", [NPS * TP, D])
    nk_out = dout("nk", [NPS, DEPTH, 2, TP, 64])
    nv_out = dout("nv", [NPS, DEPTH, 2, TP, 64])
    nst_out = dout("nst", [NPS, DEPTH, 2, 8, 64, 64])

    seqs = [Seq("s", TS, True, 0, 0, 0)]
    for i in range(NPS):
        seqs.append(Seq(f"p{i}", TP, False, i, NCTX + TS + i * TP, TS // 128 + i * (TP // 128)))
    NKEY = NCTX + TS + NPS * TP
    NTILE = TS // 128 + NPS * TP // 128
    NVT = NKEY // 128

    XRES, PRE, QA, GATES, GS, QDT, KDT, QTM, KTM, VTM, OPART, OA, OD, X1, H2, ACTS = ({} for _ in range(16))
    for s in seqs:
        T = s.T
        XRES[s.name] = dscr(f"xres_{s.name}", [128, 8, T])
        PRE[s.name] = dscr(f"pre_{s.name}", [128, 12, T + 2])
        QA[s.name] = dscr(f"qa_{s.name}", [8, 64, T], BF16)
        GATES[s.name] = dscr(f"gates_{s.name}", [128, 16, T], BF16)
        GS[s.name] = dscr(f"gs_{s.name}", [T, 512], BF16)
        QDT[s.name] = dscr(f"qdt_{s.name}", [8, 64, T], BF16)
        KDT[s.name] = dscr(f"kdt_{s.name}", [8, 64, T], BF16)
        QTM[s.name] = dscr(f"qtm_{s.name}", [T, 512], BF16)
        KTM[s.name] = dscr(f"ktm_{s.name}", [T, 512], BF16)
        VTM[s.name] = dscr(f"vtm_{s.name}", [T, 512], BF16)
        OPART[s.name] = dscr(f"opart_{s.name}", [T, 512])
        OA[s.name] = dscr(f"oa_{s.name}", [8, 64, T], BF16)
        OD[s.name] = dscr(f"od_{s.name}", [128, 4, T], BF16)
        H2[s.name] = dscr(f"h2_{s.name}", [128, 8, T], BF16)

    def sb(name, shape, dt=F32):
        return es.enter_context(nc.sbuf_tensor("sb_" + name, list(shape), dt))

    cst = sb("cst", [128, NCST])
    cstb = sb("cstb", [128, NCST], BF16)

    def C(name, bf=False):
        o, n = CST_LAYOUT[name]
        return (cstb if bf else cst)[:, o:o + n]

    WAR = sb("warena", [128, 40960], BF16)
    KT_all = sb("kt_all", [128, NKEY], BF16)
    VA_all = sb("va_all", [128, NVT, 2, 65], BF16)
    LA = sb("la", [128, NTILE, 16])
    LB = sb("lb", [128, NTILE, 16])
    BETA = sb("beta", [128, NTILE, 16])
    MOD = sb("mod", [128, 48, 2])
    A1 = sb("a1", [128, 8, 2])
    A2 = sb("a2", [128, 8, 2])
    GB2 = sb("gb2", [128, 8, 2])
    n1f = sb("n1f", [128, 8])
    n2f = sb("n2f", [128, 8])
    fnf = sb("fnf", [128, 8])
    b2f = sb("b2f", [128, 8])
    b1f = sb("b1f", [128, 32])
    bmf = sb("bmf", [128, 48])
    cfm = sb("cfm", [128, 8, 2])
    scb = sb("scb", [128, 8, 2], BF16)
    qg = sb("qg", [128, 1])
    kg = sb("kg", [128, 1])
    cw = sb("cw", [128, 3, 12])
    dtb = sb("dtb", [128, 16])
    negA = sb("negA", [128, 16])
    dng = sb("dng", [128, 64])
    TB = 256
    BIGA = sb("bigA", [128, 12 * 258])
    BIGB = sb("bigB", [128, 12 * 256])
    HB = sb("hb16", [128, 16, 256], BF16)
    GATB = sb("gatb", [128, 16, 256], BF16)
    R1 = sb("r1", [128, 256])
    RSTD = sb("rstd", [128, 256])
    STG = [sb(f"stg{i}", [128, 512]) for i in range(4)]
    STB = [sb(f"stb{i}", [128, 512], BF16) for i in range(4)]
    COSB = sb("cosb", [128, 256])
    SINB = sb("sinb", [128, 256])
    SMALL = sb("small", [128, 64])
    SM = sb("sm", [128, 160])
    VAB = sb("vab", [128, 160])
    KTC = [sb(f"ktc{i}", [64, 8, 128], BF16) for i in range(2)]
    QTC = [sb(f"qtc{i}", [64, 8, 128], BF16) for i in range(2)]
    KTMC = [sb(f"ktmc{i}", [128, 512], BF16) for i in range(2)]
    QTMC = [sb(f"qtmc{i}", [128, 512], BF16) for i in range(2)]
    VTMC = [sb(f"vtmc{i}", [128, 512], BF16) for i in range(2)]
    def v512(big, i, p0=0, p1=128):
        return big[p0:p1, i * 512:(i + 1) * 512]

    E1, E2, E3, DG1, DG3 = (v512(BIGA, i) for i in range(5))
    U0 = [v512(BIGA, 5), v512(BIGB, 0)]
    OACC = [v512(BIGB, 1), v512(BIGB, 2)]
    RS = v512(BIGB, 3)
    S32 = [v512(BIGB, 4 + i, 0, 64).rearrange("p (h v) -> p h v", h=8) for i in range(2)]
    HBf = HB[:, :, :].rearrange("p a b -> p (a b)")
    GBf = GATB[:, :, :].rearrange("p a b -> p (a b)")
    PA = [v512(HBf, 0), v512(HBf, 1)]
    PT = [v512(HBf, 2), v512(HBf, 3)]
    RT = [v512(HBf, 4), v512(HBf, 5)]
    QKM = [v512(HBf, 6), v512(HBf, 7)]
    BEK = v512(GBf, 0)
    KDEC = [v512(GBf, 1), v512(GBf, 2)]
    BV = v512(GBf, 3)
    UB = [v512(GBf, 4), v512(GBf, 5)]
    DE = v512(GBf, 6)
    OB = v512(GBf, 7, 0, 64)
    XBW = [sb(f"xbw{i}", [128, 1024], BF16) for i in range(2)]
    XB = [XBW[0][:, 0:512], XBW[0][:, 512:1024], XBW[1][:, 0:512], XBW[1][:, 512:1024]]
    NWT = [sb(f"nwt{i}", [64, 16, 64], BF16) for i in range(2)]
    QDEC = [sb(f"qdec{i}", [64, 16, 64], BF16) for i in range(2)]
    SBF = [sb(f"sbf{i}", [64, 8, 64], BF16) for i in range(2)]
    EGL2 = [sb(f"egl2{i}", [128, 16]) for i in range(2)]
    QB = STB[3][:, :].rearrange("p (j t) -> p j t", j=4)
    PS = [es.enter_context(nc.psum_tensor(f"ps{i}", [128, 1024], F32)) for i in range(4)]
    dbg_mod = dscr("dbg_mod", [128, 48, 2])
    dbg_la = dscr("dbg_la", [128, NTILE, 16])
    dbg_lb = dscr("dbg_lb", [128, NTILE, 16])
    dbg_beta = dscr("dbg_beta", [128, NTILE, 16])

    XT = BIGA[:, 0:8 * 256].rearrange("p (c t) -> p c t", c=8)
    X1T = BIGB[:, 0:8 * 256].rearrange("p (c t) -> p c t", c=8)
    PRET = BIGA[:, :].rearrange("p (c t) -> p c t", c=12)
    CV = BIGB[:, :].rearrange("p (c t) -> p c t", c=12)
    SQ = HB[:, 0:8, :]
    HT = HB[:, 8:16, :]
    XTM = [BIGA[:, i * 1024:(i + 1) * 1024] for i in range(2)]
    XFM = [BIGB[:, i * 1024:(i + 1) * 1024].rearrange("p (c t) -> p c t", c=8) for i in range(2)]

    def psb(i):
        return PS[i // 2][:, (i % 2) * 512:(i % 2) * 512 + 512]

    def pk(i):
        return ("ps", i)

    def dma(q, out, in_, reads, writes, **kw):
        P.add(q, lambda e: e.dma_start(out=out, in_=in_, **kw), reads, writes, dma=True)

    def mm(out, lhsT, rhs, start, stop, reads, writes, **kw):
        P.add("pe", lambda e: e.matmul(out, lhsT, rhs, start=start, stop=stop, **kw), reads, writes)

    def tr(out, in_, ident, reads, writes):
        P.add("pe", lambda e: e.transpose(out, in_, ident), reads, writes)

    def act(out, in_, func, reads, writes, **kw):
        P.add("act", lambda e: e.activation(out, in_, func, **kw), reads, writes)

    def tt(eng, out, in0, in1, op, reads, writes):
        P.add(eng, lambda e: e.tensor_tensor(out, in0, in1, op), reads, writes)

    def tsc(eng, out, in0, s1, s2, op0, op1, reads, writes):
        if op1 is None:
            P.add(eng, lambda e: e.tensor_scalar(out, in0, s1, None, op0), reads, writes)
        else:
            P.add(eng, lambda e: e.tensor_scalar(out, in0, s1, s2, op0, op1), reads, writes)

    def stt(eng, out, in0, scalar, in1, op0, op1, reads, writes):
        P.add(eng, lambda e: e.scalar_tensor_tensor(out, in0, scalar, in1, op0, op1), reads, writes)

    def cp(eng, out, in_, reads, writes):
        if eng == "act":
            P.add("act", lambda e: e.copy(out, in_), reads, writes)
        else:
            P.add(eng, lambda e: e.tensor_copy(out, in_), reads, writes)

    def recip(out, in_, reads, writes):
        P.add("dve", lambda e: e.reciprocal(out, in_), reads, writes)

    def load_w(dst3, src2, K, wkey="war"):
        for k in range(K):
            dma("pool", dst3[:, k, :], src2[k * 128:(k + 1) * 128, :], [], [wkey])

    def rms_stats(src3, nt, srckey, sq=None, sqkey="hb", bank=0, r1=None, rstd=None, rkey=""):
        if sq is None:
            sq = SQ
        if r1 is None:
            r1, rstd = R1[:, :], RSTD[:, :]
        act(sq[:, :, :nt], src3, AF.Square, [srckey], [sqkey])
        for c in range(8):
            mm(psb(bank)[:, :nt], C("ones", True), sq[:, c, :nt], c == 0, c == 7, [sqkey, "cstb"], [pk(bank)])
        act(r1[:, :nt], psb(bank)[:, :nt], AF.Sqrt, [pk(bank)], ["r1" + rkey], bias=EPS, scale=1.0 / D)
        recip(rstd[:, :nt], r1[:, :nt], ["r1" + rkey], ["rstd" + rkey])

    dma("sp", cst[:], cst_in, [], ["cst"])
    cp("dve", cstb[:], cst[:], ["cst"], ["cstb"])
    P.add("pool", lambda e: e.memset(VA_all[:], 1.0), [], ["va"])
    dma("sp", fnf[:], final_norm.rearrange("(k p) -> p k", p=128), [], ["fnf"], allow_slow_non_contiguous=True)
    for j in range(2):
        dma("sp", cfm[:, :, j], cvec[j].rearrange("(k p) -> p k", p=128), [], ["cfm"], allow_slow_non_contiguous=True)
    act(scb[:], cfm[:], AF.Silu, ["cfm"], ["scb"])
    P.add("pool", lambda e: e.memset(SMALL[:], 0.0), [], ["small"])
    for s in seqs:
        for col in (0, s.T + 1):
            dma("sp", PRE[s.name][:, :, col:col + 1], SMALL[:, 0:12].unsqueeze(2), ["small"], [("pre", s.name, "pad", col)],
                allow_slow_non_contiguous=True)

    it = 0
    for s in seqs:
        src = xs_in if s.is_sample else xp_in[s.idx * TP:(s.idx + 1) * TP, :]
        for ti in range(s.T // 128):
            b = it % 2
            dma("sp", XTM[b], src[ti * 128:(ti + 1) * 128, :], [], ["bigA"])
            for c in range(8):
                tr(PS[b][:, c * 128:(c + 1) * 128], XTM[b][:, c * 128:(c + 1) * 128], C("ident"),
                   ["bigA", "cst"], [pk(2 * b), pk(2 * b + 1)])
            cp("act" if it % 2 == 0 else "dve", XFM[b], PS[b][:].rearrange("p (c t) -> p c t", c=8),
               [pk(2 * b), pk(2 * b + 1)], ["bigB"])
            dma("pool", XRES[s.name][:, :, ti * 128:(ti + 1) * 128], XFM[b], ["bigB"], [("xres", s.name, ti // 2)])
            it += 1

    def finish():
        P.emit(es)
        es.close()
        return nc

    if stop_after == "stage0":
        return finish()

    NLAYERS = DEPTH
    for l in range(NLAYERS):
        for (t_, src) in ((n1f, norm1[l]), (n2f, norm2[l]), (b2f, b2[l])):
            dma("sp", t_[:], src.rearrange("(k p) -> p k", p=128), [], ["lvec"], allow_slow_non_contiguous=True)
        dma("sp", b1f[:], b1[l].rearrange("(k p) -> p k", p=128), [], ["lvec"], allow_slow_non_contiguous=True)
        dma("sp", bmf[:], b_mod[l].rearrange("(k p) -> p k", p=128), [], ["lvec"], allow_slow_non_contiguous=True)
        for hh in range(2):
            dma("sp", qg[64 * hh:64 * hh + 64, :], q_gain[l].rearrange("(d o) -> d o", o=1), [], ["lvec"], allow_slow_non_contiguous=True)
            dma("sp", kg[64 * hh:64 * hh + 64, :], k_gain[l].rearrange("(d o) -> d o", o=1), [], ["lvec"], allow_slow_non_contiguous=True)
        for j in range(3):
            dma("sp", cw[:, j, :], conv_w[l, j].rearrange("(c p) -> p c", p=128), [], ["lvec"], allow_slow_non_contiguous=True)
        dma("sp", dtb[:], dt_bias[l:l + 1, :].broadcast_to([128, 16]), [], ["lvec"], allow_slow_non_contiguous=True)
        dma("sp", negA[:], a_log[l:l + 1, :].broadcast_to([128, 16]), [], ["lvec"], allow_slow_non_contiguous=True)
        dma("sp", dng[:], dn_gain[l:l + 1, :].broadcast_to([128, 64]), [], ["lvec"], allow_slow_non_contiguous=True)
        act(negA[:], negA[:], AF.Exp, ["lvec"], ["lvec2"])
        tsc("dve", negA[:], negA[:], -1.0, None, ALU.mult, None, ["lvec2"], ["lvec2"])

        Wm = WAR[:, 0:8 * 3072].rearrange("p (k n) -> p k n", k=8)
        for hh in range(2):
            load_w(Wm, w_mod[l][:, hh * 3072:(hh + 1) * 3072], 8)
            for n in range(24):
                nn = hh * 24 + n
                for k in range(8):
                    mm(psb(0)[:, nn * 2:nn * 2 + 2], Wm[:, k, n * 128:(n + 1) * 128], scb[:, k, :], k == 0, k == 7,
                       ["war", "scb"], [pk(0)])
        tt("dve", MOD[:], psb(0)[:, 0:96].rearrange("p (n j) -> p n j", j=2), bcast(bmf[:], 2, 2), ALU.add,
           [pk(0), "lvec"], ["mod"])
        stt("dve", A1[:], MOD[:, 8:16, :], 1.0, bcast(n1f[:], 2, 2), ALU.add, ALU.mult, ["mod", "lvec"], ["mod2"])
        stt("dve", A2[:], MOD[:, 32:40, :], 1.0, bcast(n2f[:], 2, 2), ALU.add, ALU.mult, ["mod", "lvec"], ["mod2"])
        tt("dve", GB2[:], MOD[:, 40:48, :], bcast(b2f[:], 2, 2), ALU.mult, ["mod", "lvec"], ["mod2"])
        MK = ["mod", "mod2", "lvec", "lvec2"]
        if stop_after == "adaln":
            dma("sp", dbg_mod, MOD[:], ["mod"], ["dbgmod"])
            return finish()

        Win = WAR[:, 0:8 * INW].rearrange("p (k n) -> p k n", k=8)
        load_w(Win, w_in[l], 8)
        bank_rr = [0]

        def next_bank():
            bank_rr[0] = (bank_rr[0] % 4) + 1
            return bank_rr[0]

        stg_rr = [0]

        def nstg():
            stg_rr[0] = (stg_rr[0] + 1) % 4
            return stg_rr[0]

        jobsA = [(s, blk) for s in seqs for blk in range(s.T // TB)]
        XTsA = [(XT, "bigA"), (X1T, "bigB")]
        HTsA = [(HB[:, 8:16, :], "ht"), (GATB[:, 0:8, :], "gatb")]

        def blockA(j):
            s, blk = jobsA[j]
            mj = s.mj
            t0 = blk * TB
            XTj, xkey = XTsA[j % 2]
            HTj, hkey = HTsA[j % 2]
            dma("sp", XTj, XRES[s.name][:, :, t0:t0 + TB], [("xres", s.name, blk)], [xkey])
            if s.is_sample:
                dma("sp", COSB[:], cos_in[:, t0:t0 + TB], [], ["cosb"])
                dma("sp", SINB[:], sin_in[:, t0:t0 + TB], [], ["sinb"])
            rms_stats(XTj, TB, xkey)
            tt("dve", XTj, XTj, bcast(RSTD[:, :], 1, 8), ALU.mult, [xkey, "rstd"], [xkey])
            for c in range(8):
                if c % 2 == 0:
                    act(HTj[:, c, :], XTj[:, c, :], AF.Identity, [xkey] + MK, [hkey],
                        scale=A1[:, c, mj:mj + 1], bias=MOD[:, c, mj:mj + 1])
                else:
                    tsc("dve", HTj[:, c, :], XTj[:, c, :], A1[:, c, mj:mj + 1], MOD[:, c, mj:mj + 1], ALU.mult, ALU.add,
                        [xkey] + MK, [hkey])

            yield

            def fm_chunk(col0):
                bk = next_bank()
                for k in range(8):
                    mm(psb(bk)[:, :TB], Win[:, k, col0:col0 + 128], HTj[:, k, :], k == 0, k == 7, ["war", hkey], [pk(bk)])
                return bk

            def qk_epi(c, bk):
                is_k = (c == 4)
                gain = kg if is_k else qg
                cp("act", STG[0][:, :TB], psb(bk)[:, :TB], [pk(bk)], ["stg0"])
                act(STB[0][:, :TB], psb(bk)[:, :TB], AF.Square, [pk(bk)], ["stb0"])
                yield
                mm(psb(6)[:, :TB], C("blk", True), STB[0][:, :TB], True, True, ["stb0", "cstb"], [pk(6)])
                act(STG[1][:, :TB], psb(6)[:, :TB], AF.Sqrt, [pk(6)], ["stg1"], bias=EPS, scale=1.0 / 64)
                recip(STG[1][:, :TB], STG[1][:, :TB], ["stg1"], ["stg1"])
                stt("dve", STG[0][:, :TB], STG[0][:, :TB], gain[:, 0:1], STG[1][:, :TB], ALU.mult, ALU.mult,
                    ["stg0", "stg1", "lvec"], ["stg0"])
                yield
                kcol = s.key0 + s.nctx + t0
                dst = KT_all[:, kcol:kcol + TB] if is_k else STB[1][:, :TB]
                dkey = "kt" if is_k else "stb1"
                if s.is_sample:
                    mm(psb(7)[:, :TB], C("rot"), STG[0][:, :TB], True, True, ["stg0", "cst"], [pk(7)])
                    tt("dve", STG[2][:, :TB], STG[0][:, :TB], COSB[:], ALU.mult, ["stg0", "cosb"], ["stg2"])
                    yield
                    tt("dve", STG[3][:, :TB], psb(7)[:, :TB], SINB[:], ALU.mult, [pk(7), "sinb"], ["stg3"])
                    tt("dve", dst, STG[2][:, :TB], STG[3][:, :TB], ALU.add, ["stg2", "stg3"], [dkey])
                else:
                    cp("dve", dst, STG[0][:, :TB], ["stg0"], [dkey])
                if not is_k:
                    dma("pool", QA[s.name][2 * c:2 * c + 2].rearrange("h d t -> (h d) t")[:, t0:t0 + TB], STB[1][:, :TB],
                        ["stb1"], [("qa", s.name)])
                elif not s.is_sample:
                    for t2 in range(TB // 128):
                        tr(psb(7)[:, t2 * 128:(t2 + 1) * 128], STG[0][:, t2 * 128:(t2 + 1) * 128], C("ident"),
                           ["stg0", "cst"], [pk(7)])
                    cp("act", STG[2][:, :TB], psb(7)[:, :TB], [pk(7)], ["stg2"])
                    for t2 in range(TB // 128):
                        for g in range(2):
                            dma("pool", nk_out[s.idx, l, g, t0 + t2 * 128:t0 + (t2 + 1) * 128, :],
                                STG[2][:, t2 * 128 + g * 64:t2 * 128 + g * 64 + 64], ["stg2"], [("nk", s.idx)])
                yield

            hrr = [0]

            def nh():
                hrr[0] = (hrr[0] + 1) % 4
                return hrr[0]

            def filler():
                for c in range(12):
                    bk = fm_chunk(C_QD + c * 128)
                    i = nh()
                    cp("act" if c % 2 == 0 else "dve", STG[i][:, 256:512], psb(bk)[:, :TB], [pk(bk)], [f"stgh{i}"])
                    dma("pool", PRE[s.name][:, c, 1 + t0:1 + t0 + TB], STG[i][:, 256:512], [f"stgh{i}"], [("pre", s.name, blk)])
                    yield
                for c in range(16):
                    bk = fm_chunk(C_GA + c * 128)
                    i = nh()
                    act(STB[i][:, 256:512], psb(bk)[:, :TB], AF.Sigmoid, [pk(bk)], [f"stbh{i}"])
                    dma("pool", GATES[s.name][:, c, t0:t0 + TB], STB[i][:, 256:512], [f"stbh{i}"], [("gates", s.name, blk)])
                    yield

            fg = filler()
            for c in range(5):
                bk = fm_chunk(C_QA + c * 128)
                for _ in qk_epi(c, bk):
                    next(fg, None)
            yield
            for _ in fg:
                pass
            for t2 in range(TB // 128):
                tsl = slice(t2 * 128, (t2 + 1) * 128)
                gti = s.tile0 + (t0 // 128) + t2
                vt = (s.key0 + s.nctx + t0) // 128 + t2
                for k in range(8):
                    mm(psb(5)[:, 0:512], HTj[:, k, tsl], Win[:, k, C_GO:C_GO + 512], k == 0, k == 7, ["war", hkey], [pk(5)])
                for k in range(8):
                    mm(psb(6)[:, 0:128], HTj[:, k, tsl], Win[:, k, C_VA:C_VA + 128], k == 0, k == 7, ["war", hkey], [pk(6)])
                for k in range(8):
                    mm(psb(6)[:, 128:160], HTj[:, k, tsl], Win[:, k, C_AI:C_AI + 32], k == 0, k == 7, ["war", hkey], [pk(6)])
                i = nstg()
                act(STB[i][:, :], psb(5)[:, :], AF.Silu, [pk(5)], [f"stb{i}", f"stbh{i}"])
                dma("pool", GS[s.name][t0 + t2 * 128:t0 + (t2 + 1) * 128, :], STB[i][:, :], [f"stb{i}"], [("gs", s.name)])
                cp("dve", VAB[:, :], psb(6)[:, 0:160], [pk(6)], ["vab"])
                cp("dve", VA_all[:, vt, :, 0:64], VAB[:, 0:128].rearrange("p (g d) -> p g d", g=2), ["vab"], ["va"])
                if not s.is_sample:
                    for g in range(2):
                        dma("pool", nv_out[s.idx, l, g, t0 + t2 * 128:t0 + (t2 + 1) * 128, :], VAB[:, g * 64:g * 64 + 64],
                            ["vab"], [("nv", s.idx)])
                tt("dve", SM[:, 0:16], VAB[:, 128:144], dtb[:], ALU.add, ["vab", "lvec"], ["sm"])
                act(SM[:, 16:32], SM[:, 0:16], AF.Exp, ["sm"], ["sm1"])
                act(SM[:, 32:48], SM[:, 16:32], AF.Ln, ["sm1"], ["sm2"], bias=1.0)
                tt("dve", LA[:, gti, :], SM[:, 32:48], negA[:], ALU.mult, ["sm2", "lvec2"], ["la"])
                act(BETA[:, gti, :], VAB[:, 144:160], AF.Sigmoid, ["vab"], ["beta"])
                act(LB[:, gti, :], BETA[:, gti, :], AF.Ln, ["beta"], ["lb"])

        gA = [blockA(j) for j in range(len(jobsA))]
        next(gA[0])
        for j in range(len(jobsA)):
            next(gA[j])
            if j + 1 < len(jobsA):
                next(gA[j + 1])
            for _ in gA[j]:
                pass
        if "dbg_la" in dbg:
            dma("sp", dbg_la, LA[:], ["la"], ["dbgla"])
            dma("sp", dbg_lb, LB[:], ["lb"], ["dbglb"])
            dma("sp", dbg_beta, BETA[:], ["beta"], ["dbgbeta"])
        P.fence()
        if stop_after == "stageA":
            return finish()

        brr = [0]

        def nb():
            brr[0] = brr[0] % 3 + 1
            return brr[0]

        def genB():
            for s in seqs:
                for blk in range(s.T // TB):
                    t0 = blk * TB
                    dma("sp", PRET, PRE[s.name][:, :, t0:t0 + TB + 2],
                        [("pre", s.name, b_) for b_ in range(max(0, blk - 1), min(s.T // TB, blk + 2))]
                        + [("pre", s.name, "pad", 0), ("pre", s.name, "pad", s.T + 1)], ["bigA"])
                    for c in range(12):
                        tsc("dve", CV[:, c, :], PRET[:, c, 0:TB], cw[:, 0, c:c + 1], None, ALU.mult, None, ["bigA", "lvec"], [("cv", c)])
                        stt("dve", CV[:, c, :], PRET[:, c, 1:TB + 1], cw[:, 1, c:c + 1], CV[:, c, :], ALU.mult, ALU.add,
                            ["bigA", "lvec", ("cv", c)], [("cv", c)])
                        stt("dve", CV[:, c, :], PRET[:, c, 2:TB + 2], cw[:, 2, c:c + 1], CV[:, c, :], ALU.mult, ALU.add,
                            ["bigA", "lvec", ("cv", c)], [("cv", c)])
                        if c % 3 == 2:
                            yield
                    for c4 in range(3):
                        cs_ = slice(c4 * 4, c4 * 4 + 4)
                        ck = [("cv", c) for c in range(c4 * 4, c4 * 4 + 4)]
                        SG = PRET[:, cs_, 0:TB]
                        act(SG, CV[:, cs_, :], AF.Exp, ck + ["bigA"], ["bigA"], scale=-1.0)
                        act(SG, SG, AF.Ln, ["bigA"], ["bigA"], bias=1.0)
                        act(SG, SG, AF.Exp, ["bigA"], ["bigA"], scale=-1.0)
                        tt("dve", CV[:, cs_, :], CV[:, cs_, :], SG, ALU.mult, ck + ["bigA"], ck)
                        yield
                    for c in range(8):
                        act(STB[0][:, :TB], CV[:, c, :], AF.Square, [("cv", c)], ["stb0"])
                        mm(psb(7)[:, :TB], C("blk", True), STB[0][:, :TB], True, True, ["stb0", "cstb"], [pk(7)])
                        act(STG[1][:, :TB], psb(7)[:, :TB], AF.Ln, [pk(7)], ["stg1"], bias=EPS, scale=1.0)
                        act(STG[1][:, :TB], STG[1][:, :TB], AF.Exp, ["stg1"], ["stg1"], scale=-0.5)
                        i = nb()
                        stt("dve", STB[i][:, :TB], CV[:, c, :], 0.125 if c < 4 else 1.0, STG[1][:, :TB], ALU.mult, ALU.mult,
                            [("cv", c), "stg1"], [f"stb{i}"])
                        cp("act", CV[:, c, :], STB[i][:, :TB], [f"stb{i}"], [("cv", c)])
                        dstT = QDT if c < 4 else KDT
                        cc = c % 4
                        dma("pool", dstT[s.name][2 * cc:2 * cc + 2].rearrange("h d t -> (h d) t")[:, t0:t0 + TB], STB[i][:, :TB],
                            [f"stb{i}"], [("qkdt", s.name)])
                        yield
                    for t2 in range(TB // 128):
                        for grp, dstM in enumerate((QTM, KTM, VTM)):
                            for cc in range(4):
                                tr(psb(7)[:, cc * 128:(cc + 1) * 128], CV[:, grp * 4 + cc, t2 * 128:(t2 + 1) * 128], C("ident"),
                                   [("cv", grp * 4 + cc), "cst"], [pk(7)])
                            i = nb()
                            cp("dve", STB[i][:, :], psb(7)[:, :], [pk(7)], [f"stb{i}"])
                            dma("pool", dstM[s.name][t0 + t2 * 128:t0 + (t2 + 1) * 128, :], STB[i][:, :], [f"stb{i}"], [("tm", s.name)])
                            yield

        if stop_after == "stageB":
            for _ in genB():
                pass
            P.fence()
            return finish()

        for t2 in range(NCTX // 128):
            dma("sp", STG[0][:, 0:128].rearrange("p (g d) -> p g d", g=2),
                ck_in[l, :, t2 * 128:(t2 + 1) * 128, :].rearrange("g p d -> p g d"), [], ["stg0"])
            tr(psb(0)[:, 0:128], STG[0][:, 0:128], C("ident"), ["stg0", "cst"], [pk(0)])
            cp("dve", KT_all[:, t2 * 128:(t2 + 1) * 128], psb(0)[:, 0:128], [pk(0)], ["kt"])
            dma("sp", STG[1][:, 0:128].rearrange("p (g d) -> p g d", g=2),
                cv_in[l, :, t2 * 128:(t2 + 1) * 128, :].rearrange("g p d -> p g d"), [], ["stg1"])
            cp("dve", VA_all[:, t2, :, 0:64], STG[1][:, 0:128].rearrange("p (g d) -> p g d", g=2), ["stg1"], ["va"])
        QBz = [[WAR[:, (qp * 2 + g) * 512:(qp * 2 + g + 1) * 512] for g in range(2)] for qp in range(2)]
        VAp = WAR[:, 2048:2048 + NVT * 256].rearrange("p (t g d) -> p t g d", t=NVT, g=2)
        P.add("pool", lambda e: e.memset(WAR[:, 0:2048 + NVT * 256], 0.0), [], ["qbz", "vap"])
        cp("pool", VAp[:, :, :, 0:65], VA_all[:, :, :, :], ["va", "vap"], ["vap"])
        RS = WAR[:, 2048 + NVT * 256:2048 + NVT * 256 + 1024].bitcast(F32)
        items = []
        qcount = 0
        for s in seqs:
            ktiles = []
            if s.is_sample:
                ktiles += list(range(NCTX // 128))
            ktiles += [(s.key0 + s.nctx) // 128 + i for i in range(s.T // 128)]
            for qi in range(s.T // 128):
                for n_, kt in enumerate(ktiles):
                    items.append((s, qi, n_, kt, n_ == 0, n_ == len(ktiles) - 1, qcount % 2))
                qcount += 1
        LAG = 1

        def genAttn():
            for idx in range(len(items) + LAG):
                if idx < len(items):
                    s, qi, n_, kt, first, last, qp = items[idx]
                    q0 = qi * 128
                    if first:
                        for g2 in range(2):
                            dma("sp", QBz[qp][g2][64 * g2:64 * g2 + 64, :].rearrange("p (j t) -> p j t", j=4),
                                QA[s.name][4 * g2:4 * g2 + 4, :, q0:q0 + 128].rearrange("j d t -> d j t"),
                                [("qa", s.name), "qbz"], [("qb", qp)])
                    r_ = idx % 2
                    for g in range(2):
                        mm(PS[r_][:, g * 512:(g + 1) * 512], KT_all[:, kt * 128:(kt + 1) * 128],
                           QBz[qp][g], True, True, ["kt", ("qb", qp)],
                           [pk(2 * r_), pk(2 * r_ + 1)])
                    act(XBW[r_][:, :], PS[r_][:, :], AF.Exp, [pk(2 * r_), pk(2 * r_ + 1)], [("ptt", r_)], scale=0.125)
                if idx >= LAG:
                    s, qi, n_, kt, first, last, qp = items[idx - LAG]
                    q0 = qi * 128
                    r_ = (idx - LAG) % 2
                    for g in range(2):
                        ob = 4 + g
                        mm(psb(ob)[:, :], VAp[:, kt, g, :], XBW[r_][:, g * 512:(g + 1) * 512], first, last,
                           ["vap", ("ptt", r_)], [pk(ob)])
                    if last:
                        for g in range(2):
                            ob = 4 + g
                            cp("dve", RS[64:65, :], psb(ob)[64:65, :], [pk(ob)], ["rs"])
                            act(RS[64:65, :], RS[64:65, :], AF.Ln, ["rs"], ["rs"])
                            act(RS[64:65, :], RS[64:65, :], AF.Exp, ["rs"], ["rs"], scale=-1.0)
                            mm(psb(6)[0:64, :], C("ones")[64:65, 0:64], RS[64:65, :], True, True, ["rs", "cst"], [pk(6)])
                            cp("dve", STG[0][0:64, :], psb(ob)[0:64, :], [pk(ob)], ["stg0"])
                            tt("dve", OB[:, :], STG[0][0:64, :], psb(6)[0:64, :], ALU.mult, ["stg0", pk(6)], ["ob"])
                            dma("sp", OA[s.name][4 * g:4 * g + 4, :, q0:q0 + 128].rearrange("j d t -> d j t"),
                                OB[:, :].rearrange("p (j t) -> p j t", j=4), ["ob"], [("oa", s.name)])
                yield

        gB_, gAt_ = genB(), genAttn()
        doneB = doneA = False
        it_ = 0
        while not (doneB and doneA):
            if not doneB:
                try:
                    next(gB_)
                except StopIteration:
                    doneB = True
            for _ in range(3 + (1 if it_ % 4 == 3 else 0)):
                if not doneA:
                    try:
                        next(gAt_)
                    except StopIteration:
                        doneA = True
            it_ += 1
        P.fence()
        if stop_after == "attn":
            return finish()

        H8 = 8
        CUT = 0

        def v3(t):
            return t.rearrange("p (h j) -> p h j", h=H8)

        woff = [0]

        def wtake(n, f32=False):
            ap = WAR[:, woff[0]:woff[0] + n]
            woff[0] += n
            return ap.bitcast(F32) if f32 else ap

        DS = [dict(SM=SM, DG1=DG1, DG3=DG3, DE=DE, E1=E1, E2=E2, E3=E3, PA=PA, PT=PT, RT=RT, XB=XB, BEK=BEK, BV=BV), None]
        DS[1] = dict(E1=wtake(1024, True), E2=wtake(1024, True), E3=wtake(1024, True), DG1=wtake(1024, True),
                     DG3=wtake(1024, True), SM=wtake(320, True),
                     PA=[wtake(512), wtake(512)], PT=[wtake(512), wtake(512)], RT=[wtake(512), wtake(512)],
                     XB=[wtake(512) for _ in range(4)], BEK=wtake(512), BV=wtake(512), DE=wtake(512))
        def w64(n):
            ap = WAR[0:64, woff[0]:woff[0] + n]
            woff[0] += n
            return ap.rearrange("p (x i) -> p x i", x=16)

        NWT2 = [[NWT[d][:, :, :], w64(1024)] for d in range(2)]
        QDEC2 = [[QDEC[d][:, :, :], w64(1024)] for d in range(2)]
        U02 = [[U0[d], wtake(1024, True)] for d in range(2)]
        QKM2 = [[QKM[d], wtake(512)] for d in range(2)]
        KDEC2 = [[KDEC[d], wtake(512)] for d in range(2)]
        EGL22 = [[EGL2[d][:, :], wtake(32, True)] for d in range(2)]
        ab = [0]
        fbk = [0]
        ppr = [0]

        def abank():
            ab[0] = (ab[0] + 1) % 6
            return ab[0]

        def fbank():
            fbk[0] ^= 1
            return 6 + fbk[0]

        def ppair():
            ppr[0] = (ppr[0] + 1) % 3
            return ppr[0]

        idb = bcast(C("identst", True), 1, H8)
        ist = bcast(C("identst"), 1, H8)

        def msk(name):
            return bcast(C(name, True), 1, H8)

        def prep(s, d, m, par):
            NWTp, QDECp, U0p, QKMp, KDECp, EGL2p = NWT2[d][par], QDEC2[d][par], U02[d][par], QKM2[d][par], KDEC2[d][par], EGL22[d][par]
            kq = (d, par)
            T_ = DS[d]
            SMd, DG1d, DG3d, DEd = T_["SM"], T_["DG1"], T_["DG3"], T_["DE"]
            E1d, E2d, E3d = T_["E1"], T_["E2"], T_["E3"]
            PAd, PTd, RTd, XBd, BEKd, BVd = T_["PA"], T_["PT"], T_["RT"], T_["XB"], T_["BEK"], T_["BV"]

            def K(name, *x):
                return (name, d) + tuple(x)

            gti = s.tile0 + m
            sfx = "f" if d == 0 else "b"
            rows = slice(m * 128, (m + 1) * 128)
            dma("sp", KTC[d][:, :, :], KDT[s.name][:, :, rows].rearrange("h d t -> d h t"), [("qkdt", s.name)], [("ktc", d)])
            dma("sp", QTC[d][:, :, :], QDT[s.name][:, :, rows].rearrange("h d t -> d h t"), [("qkdt", s.name)], [("qtc", d)])
            dma("sp", KTMC[d][:, :], KTM[s.name][rows, :], [("tm", s.name)], [("ktmc", d)])
            dma("sp", QTMC[d][:, :], QTM[s.name][rows, :], [("tm", s.name)], [("qtmc", d)])
            dma("sp", VTMC[d][:, :], VTM[s.name][rows, :], [("tm", s.name)], [("vtmc", d)])
            la = LA[:, gti, 8 * d:8 * d + 8]
            lb = LB[:, gti, 8 * d:8 * d + 8]
            be_ = BETA[:, gti, 8 * d:8 * d + 8]
            bg = fbank()
            mm(psb(bg)[:, 0:8], C("tri_" + sfx), la, True, True, ["la", "cst"], [pk(bg)])
            mm(psb(bg)[:, 8:16], C("half0"), la, True, True, ["la", "cst"], [pk(bg)])
            mm(psb(bg)[:, 16:24], C("half1"), la, True, True, ["la", "cst"], [pk(bg)])
            cp("dve", SMd[:, 0:24], psb(bg)[:, 0:24], [pk(bg)], [K("sm")])
            yield
            tt("dve", SMd[:, 24:32], SMd[:, 0:8], lb, ALU.add, [K("sm"), "lb"], [K("sm_glb")])
            act(SMd[:, 32:40], SMd[:, 0:8], AF.Exp, [K("sm")], [K("sm_eg")])
            cp("dve", SMd[0:64, 40:48], SMd[0:64, 8:16], [K("sm")], [K("sm_glo")])
            cp("dve", SMd[64:128, 40:48], SMd[64:128, 16:24], [K("sm")], [K("sm_glo")])
            tt("dve", SMd[:, 48:56], SMd[:, 40:48], SMd[:, 0:8], ALU.subtract, [K("sm"), K("sm_glo")], [K("sm_ek")])
            act(SMd[:, 48:56], SMd[:, 48:56], AF.Exp, [K("sm_ek")], [K("sm_ek")])
            act(EGL2p, SMd[:, 8:24], AF.Exp, [K("sm")], [("egl2",) + kq])
            tt("dve", SMd[:, 56:64], be_, SMd[:, 32:40], ALU.mult, ["beta", K("sm_eg")], [K("sm_be")])
            tsc("dve", SMd[:, 64:72], SMd[:, 0:8], -1.0, None, ALU.mult, None, [K("sm")], [K("sm_ng")])
            tt("pool", v3(DG1d), ist, bcast(SMd[:, 24:32], 2, 64), ALU.mult, ["cst", K("sm_glb")], [K("dg1")])
            tt("pool", v3(DG3d), ist, bcast(SMd[:, 0:8], 2, 64), ALU.mult, ["cst", K("sm")], [K("dg3")])
            tt("dve", v3(DEd), ist, bcast(SMd[:, 32:40], 2, 64), ALU.mult, ["cst", K("sm_eg")], [K("de")])
            yield
            b1 = fbank()
            mm(psb(b1)[:, :], C("negblk"), DG3d, True, False, [K("dg3"), "cst"], [pk(b1)])
            mm(psb(b1)[:, :], C("ident"), bcast(SMd[:, 24:32], 2, 64), False, False, [K("sm_glb"), "cst"], [pk(b1)])
            mm(psb(b1)[:, :], C("ident", True), msk("m1_" + sfx), False, True, ["cstb"], [pk(b1)])
            act(E1d, psb(b1)[:, :], AF.Exp, [pk(b1)], [K("e1")])
            yield
            b2 = fbank()
            mm(psb(b2)[:, :], C("blk"), DG1d, True, False, [K("dg1"), "cst"], [pk(b2)])
            mm(psb(b2)[:, :], C("ident"), bcast(SMd[:, 64:72], 2, 64), False, False, [K("sm_ng"), "cst"], [pk(b2)])
            mm(psb(b2)[:, :], C("ident", True), msk("m2_" + sfx), False, True, ["cstb"], [pk(b2)])
            act(E2d, psb(b2)[:, :], AF.Exp, [pk(b2)], [K("e2")])
            yield
            b3 = fbank()
            mm(psb(b3)[:, :], C("blk"), DG3d, True, False, [K("dg3"), "cst"], [pk(b3)])
            mm(psb(b3)[:, :], C("ident"), bcast(SMd[:, 64:72], 2, 64), False, False, [K("sm_ng"), "cst"], [pk(b3)])
            mm(psb(b3)[:, :], C("ident", True), msk("m3_" + sfx), False, True, ["cstb"], [pk(b3)])
            act(E3d, psb(b3)[:, :], AF.Exp, [pk(b3)], [K("e3")])
            yield
            bkk, bqk = abank(), abank()
            for h in range(H8):
                for a in range(2):
                    ts_ = slice(64 * a, 64 * a + 64)
                    mm(psb(bkk)[ts_, h * 64:(h + 1) * 64], KTC[d][:, h, ts_], KTC[d][:, h, ts_], True, True,
                       [("ktc", d)], [pk(bkk)], tile_position=(0, 64 * a))
                    mm(psb(bqk)[ts_, h * 64:(h + 1) * 64], KTC[d][:, h, ts_], QTC[d][:, h, ts_], True, True,
                       [("ktc", d), ("qtc", d)], [pk(bqk)], tile_position=(0, 64 * a))
            A_, AT_ = PAd[0], PTd[0]
            kA, kAT = K("pa", 0), K("pt", 0)
            tt("dve", A_, psb(bkk)[:, :], E1d, ALU.mult, [pk(bkk), K("e1")], [kA])
            tt("dve", AT_, psb(bkk)[:, :], E2d, ALU.mult, [pk(bkk), K("e2")], [kAT])
            tt("dve", QKMp, psb(bqk)[:, :], E3d, ALU.mult, [pk(bqk), K("e3")], [("qkm",) + kq])
            yield

            def grp(L, R, lkey, rkey):
                bank = abank()
                for h in range(H8):
                    for a in range(2):
                        ts_ = slice(64 * a, 64 * a + 64)
                        hs = slice(h * 64, (h + 1) * 64)
                        mm(psb(bank)[ts_, hs], L[ts_, hs], R[ts_, hs], True, True, [lkey, rkey], [pk(bank)],
                           tile_position=(64 * a, 64 * a))
                return bank

            D_, DT_ = PAd[1], PTd[1]
            kD, kDT = K("pa", 1), K("pt", 1)
            X = [XBd[0], XBd[1], BEKd, BVd, XBd[2], XBd[3]]
            kX = [K("xb", 0), K("xb", 1), K("bek"), K("bv"), K("xb", 2), K("xb", 3)]
            kR = [K("rt", 0), K("rt", 1)]
            tt("pool", v3(D_), v3(A_), msk("mask8"), ALU.mult, [kA, "cstb"], [kD])
            tt("pool", v3(DT_), v3(AT_), msk("mask8"), ALU.mult, [kAT, "cstb"], [kDT])
            tt("dve", v3(X[2]), idb, v3(DT_), ALU.subtract, ["cstb", kDT], [kX[2]])
            yield
            g1 = grp(DT_, D_, kDT, kD)
            g2 = grp(D_, DT_, kD, kDT)
            cp("act", X[0], psb(g1)[:, :], [pk(g1)], [kX[0]])
            tt("dve", v3(RTd[1]), v3(X[0]), idb, ALU.add, [kX[0], "cstb"], [kR[1]])
            cp("dve", RTd[0], psb(g2)[:, :], [pk(g2)], [kR[0]])
            yield
            g3 = grp(RTd[0], X[0], kR[0], kX[0])
            tt("dve", v3(X[1]), v3(psb(g3)[:, :]), idb, ALU.add, [pk(g3), "cstb"], [kX[1]])
            g1 = grp(RTd[1], X[2], kR[1], kX[2])
            cp("act", X[3], psb(g1)[:, :], [pk(g1)], [kX[3]])
            yield
            g2 = grp(X[3], X[1], kX[3], kX[1])
            g3 = grp(X[1], X[3], kX[1], kX[3])
            cp("act", X[4], psb(g2)[:, :], [pk(g2)], [kX[4]])
            cp("dve", X[5], psb(g3)[:, :], [pk(g3)], [kX[5]])
            yield
            Tb, kT = [X[4], X[0]], [kX[4], kX[0]]
            Mb, kM = [X[5], X[1]], [kX[5], kX[1]]
            cur = 0
            for li, mname in enumerate(("moff8", "moff16", "moff32")):
                last = (li == 2)
                nxt = 1 - cur
                tt("pool", v3(D_), v3(A_), msk(mname), ALU.mult, [kA, "cstb"], [kD])
                if not last:
                    tt("pool", v3(DT_), v3(AT_), msk(mname), ALU.mult, [kAT, "cstb"], [kDT])
                g1 = grp(D_, Mb[cur], kD, kM[cur])
                cp("act", RTd[1], psb(g1)[:, :], [pk(g1)], [kR[1]])
                if not last:
                    g2 = grp(DT_, Tb[cur], kDT, kT[cur])
                    cp("dve", RTd[0], psb(g2)[:, :], [pk(g2)], [kR[0]])
                yield
                g3 = grp(Tb[cur], RTd[1], kT[cur], kR[1])
                tt("dve", Mb[nxt], Mb[cur], psb(g3)[:, :], ALU.subtract, [kM[cur], pk(g3)], [kM[nxt]])
                if not last:
                    g1 = grp(Mb[cur], RTd[0], kM[cur], kR[0])
                    tt("dve", Tb[nxt], Tb[cur], psb(g1)[:, :], ALU.subtract, [kT[cur], pk(g1)], [kT[nxt]])
                cur = nxt
                yield
            TTm = Mb[cur]
            tkey = kM[cur]
            tt("dve", v3(BEKd), v3(KTMC[d][:, :]), bcast(SMd[:, 56:64], 2, 64), ALU.mult, [("ktmc", d), K("sm_be")], [K("bek")])
            tt("pool", v3(KDECp), v3(KTMC[d][:, :]), bcast(SMd[:, 48:56], 2, 64), ALU.mult, [("ktmc", d), K("sm_ek")], [("kdec",) + kq])
            tt("pool", v3(BVd), v3(VTMC[d][:, :]), bcast(be_, 2, 64), ALU.mult, [("vtmc", d), "beta"], [K("bv")])
            yield
            pp_ = ppair()
            pkeys = [pk(2 * pp_), pk(2 * pp_ + 1)]
            bu0 = abank()
            while bu0 in (2 * pp_, 2 * pp_ + 1):
                bu0 = abank()
            for h in range(H8):
                for a in range(2):
                    ts_ = slice(64 * a, 64 * a + 64)
                    hs = slice(h * 64, (h + 1) * 64)
                    cs = slice((a * 8 + h) * 64, (a * 8 + h) * 64 + 64)
                    mm(PS[pp_][0:64, cs], BEKd[ts_, hs], TTm[ts_, hs], True, True, [K("bek"), tkey], pkeys,
                       tile_position=(64 * a, 0))
                    mm(psb(bu0)[ts_, hs], TTm[ts_, hs], BVd[ts_, hs], True, True, [tkey, K("bv")], [pk(bu0)],
                       tile_position=(64 * a, 64 * a))
            tsc("dve", NWTp, PS[pp_][0:64, :].rearrange("p (x i) -> p x i", x=16), -1.0, None, ALU.mult, None,
                pkeys, [("nwt",) + kq])
            cp("act", U0p, psb(bu0)[:, :], [pk(bu0)], [("u0",) + kq])
            yield
            pp_ = ppair()
            pkeys = [pk(2 * pp_), pk(2 * pp_ + 1)]
            for a in range(2):
                mm(PS[pp_][0:64, a * 512:(a + 1) * 512], C("half%d" % a, True)[:, 0:64], DEd, True, True,
                   [K("de"), "cstb"], pkeys)
            for a in range(2):
                tt("dve", QDECp[:, a * 8:(a + 1) * 8, :],
                   QTC[d][:, :, a * 64:(a + 1) * 64],
                   PS[pp_][0:64, a * 512:(a + 1) * 512].rearrange("p (h i) -> p h i", h=8), ALU.mult,
                   [("qtc", d)] + pkeys, [("qdec",) + kq])
            yield

        def steps(s, d, m, par, first_visit):
            NWTp, QDECp, U0p, QKMp, KDECp, EGL2p = NWT2[d][par], QDEC2[d][par], U02[d][par], QKM2[d][par], KDEC2[d][par], EGL22[d][par]
            kq = (d, par)
            SMd = DS[d]["SM"]

            def K(name, *x):
                return (name, d) + tuple(x)

            rows = slice(m * 128, (m + 1) * 128)
            for a in ((0, 1) if d == 0 else (1, 0)):
                ts_ = slice(64 * a, 64 * a + 64)
                pu = abank()
                for h in range(H8):
                    hs = slice(h * 64, (h + 1) * 64)
                    mm(psb(pu)[ts_, hs], NWTp[:, a * 8 + h, :], SBF[d][:, h, :], True, True,
                       [("nwt",) + kq, ("sbf", d)], [pk(pu)], tile_position=(0, 64 * a))
                tt("dve", UB[d][ts_, :], U0p[ts_, :], psb(pu)[ts_, :], ALU.add, [("u0",) + kq, pk(pu)], [("ub", d)])
                yield
                po, pob, pS_ = abank(), abank(), abank()
                for h in range(H8):
                    hs = slice(h * 64, (h + 1) * 64)
                    mm(psb(po)[ts_, hs], QDECp[:, a * 8 + h, :], SBF[d][:, h, :], True, True,
                       [("qdec",) + kq, ("sbf", d)], [pk(po)], tile_position=(0, 64 * a))
                    mm(psb(pob)[ts_, hs], QKMp[ts_, hs], UB[d][ts_, hs], True, True,
                       [("qkm",) + kq, ("ub", d)], [pk(pob)], tile_position=(64 * a, 64 * a))
                    mm(psb(pS_)[0:64, hs], KDECp[ts_, hs], UB[d][ts_, hs], True, True,
                       [("kdec",) + kq, ("ub", d)], [pk(pS_)], tile_position=(64 * a, 0))
                tt("dve", S32[d][:, :, :], S32[d][:, :, :], bcast(EGL2p[0:64, a * 8:a * 8 + 8], 2, 64), ALU.mult,
                   [("s32", d), ("egl2",) + kq], [("s32", d)])
                tt("dve", S32[d][:, :, :], S32[d][:, :, :], psb(pS_)[0:64, :].rearrange("p (h v) -> p h v", h=H8), ALU.add,
                   [("s32", d), pk(pS_)], [("s32", d)])
                cp("act", SBF[d][:, :, :], S32[d][:, :, :], [("s32", d)], [("sbf", d)])
                cp("act", OACC[d][ts_, :], psb(po)[ts_, :], [pk(po)], [("oacc", d)])
                tt("dve", OACC[d][ts_, :], OACC[d][ts_, :], psb(pob)[ts_, :], ALU.add, [("oacc", d), pk(pob)], [("oacc", d)])
                yield
            if first_visit:
                dma("sp", OPART[s.name][rows, :], OACC[d][:, :], [("oacc", d)], [("opart", s.name, m)])
            else:
                dma("sp", STG[d][:, :], OPART[s.name][rows, :], [("opart", s.name, m)], [f"stg{d}"])
                tt("dve", OACC[d][:, :], OACC[d][:, :], STG[d][:, :], ALU.add, [("oacc", d), f"stg{d}"], [("oacc", d)])
                tt("pool", STG[2 + d][:, :], OACC[d][:, :], OACC[d][:, :], ALU.mult, [("oacc", d)], [f"stg{2 + d}"])
                P.add("dve", lambda e: e.tensor_reduce(SMd[:, 80:88], v3(STG[2 + d][:, :]), AX.X, ALU.add),
                      [f"stg{2 + d}"], [K("sm_rn")])
                act(SMd[:, 80:88], SMd[:, 80:88], AF.Sqrt, [K("sm_rn")], [K("sm_rn")], bias=EPS, scale=1.0 / 64)
                recip(SMd[:, 80:88], SMd[:, 80:88], [K("sm_rn")], [K("sm_rn")])
                yield
                tt("dve", v3(OACC[d][:, :]), v3(OACC[d][:, :]), bcast(SMd[:, 80:88], 2, 64), ALU.mult,
                   [("oacc", d), K("sm_rn")], [("oacc", d)])
                tt("dve", v3(OACC[d][:, :]), v3(OACC[d][:, :]), bcast(dng[:], 1, H8), ALU.mult,
                   [("oacc", d), "lvec"], [("oacc", d)])
                dma("sp", STB[d][:, :], GS[s.name][rows, :], [("gs", s.name)], [f"stb{d}"])
                tt("dve", OACC[d][:, :], OACC[d][:, :], STB[d][:, :], ALU.mult, [("oacc", d), f"stb{d}"], [("oacc", d)])
                bt = fbank()
                for cc in range(4):
                    tr(psb(bt)[:, cc * 128:(cc + 1) * 128], OACC[d][:, cc * 128:(cc + 1) * 128], C("ident"),
                       [("oacc", d), "cst"], [pk(bt)])
                cp("act", STB[2 + d][:, :], psb(bt)[:, :], [pk(bt)], [f"stb{2 + d}"])
                dma("sp", OD[s.name][:, :, rows], STB[2 + d][:, :].rearrange("p (c t) -> p c t", c=4), [f"stb{2 + d}"],
                    [("od", s.name)])

        for s in seqs:
            NP_ = s.T // 128
            for d in range(2):
                if s.is_sample:
                    dma("sp", S32[d][:, :, :], st_in[l, d].rearrange("h k v -> k h v"), [], [("s32", d)])
                else:
                    P.add("pool", lambda e, d=d: e.memset(S32[d][:, :, :], 0.0), [], [("s32", d)])
                cp("act", SBF[d][:, :, :], S32[d][:, :, :], [("s32", d)], [("sbf", d)])
            visited = set()

            def mof(d, step):
                return step if d == 0 else NP_ - 1 - step

            def rr(gens):
                active = list(gens)
                while active:
                    for g_ in list(active):
                        try:
                            next(g_)
                        except StopIteration:
                            active.remove(g_)

            rr([prep(s, d, mof(d, 0), 0) for d in range(2)])
            for step in range(NP_):
                gens = []
                for d in range(2):
                    m = mof(d, step)
                    gens.append(steps(s, d, m, step % 2, m not in visited))
                for d in range(2):
                    visited.add(mof(d, step))
                if step + 1 < NP_:
                    for d in range(2):
                        gens.append(prep(s, d, mof(d, step + 1), (step + 1) % 2))
                rr(gens)
            if not s.is_sample:
                for d in range(2):
                    dma("sp", nst_out[s.idx, l, d].rearrange("h k v -> k h v"), S32[d][:, :, :], [("s32", d)], [("nst", s.idx)])
        P.fence()
        if stop_after == "scan":
            return finish()

        Wpa = WAR[:, 0:4096].rearrange("p (k n) -> p k n", k=4)
        Wpd = WAR[:, 4096:8192].rearrange("p (k n) -> p k n", k=4)
        Wo = WAR[:, 8192:16384].rearrange("p (k n) -> p k n", k=8)
        load_w(Wpa, w_pa[l], 4)
        load_w(Wpd, w_pd[l], 4)
        load_w(Wo, w_out[l], 8)
        def w3(off, n, f32, shape3):
            ap = WAR[:, off:off + n]
            if f32:
                ap = ap.bitcast(F32)
            return ap.rearrange("p (c t) -> p c t", c=shape3)

        DSET = [
            dict(HB=HB[:, :, :], GATB=GATB[:, :, :], XT=XT, X1T=X1T, S0=STG[0][:, :], S1=STG[1][:, :], R1=R1[:, :], RSTD=RSTD[:, :],
                 bpa=0, bpd=1, bop=(2, 3), bst=0, k="0"),
            dict(HB=w3(16384, 4096, False, 16), GATB=w3(20480, 4096, False, 16), XT=w3(24576, 4096, True, 8),
                 X1T=w3(28672, 4096, True, 8), S0=WAR[:, 32768:33792].bitcast(F32), S1=WAR[:, 33792:34816].bitcast(F32),
                 R1=WAR[:, 34816:35328].bitcast(F32), RSTD=WAR[:, 35328:35840].bitcast(F32),
                 bpa=4, bpd=5, bop=(6, 7), bst=4, k="1"),
        ]
        jobsD = [(s, blk) for s in seqs for blk in range(s.T // TB)]

        def blockD(j, slot):
            s, blk = jobsD[j]
            mj = s.mj
            t0 = blk * TB
            B_ = DSET[slot]
            kk_ = B_["k"]
            HBd, GATd, XTd, X1Td, S0, S1, R1d, RSTDd = B_["HB"], B_["GATB"], B_["XT"], B_["X1T"], B_["S0"], B_["S1"], B_["R1"], B_["RSTD"]
            OATd, ODTd, MGd = HBd[:, 0:4, :], HBd[:, 4:8, :], HBd[:, 8:16, :]
            khb, kht, kg, kx, kx1, ks0, ks1 = "dhb" + kk_, "dht" + kk_, "dg" + kk_, "dx" + kk_, "dx1" + kk_, "ds0" + kk_, "ds1" + kk_
            for c in range(4):
                dma("sp", OATd[:, c, :], OA[s.name][2 * c:2 * c + 2].rearrange("h d t -> (h d) t")[:, t0:t0 + TB],
                    [("oa", s.name)], [khb])
            dma("sp", ODTd, OD[s.name][:, :, t0:t0 + TB], [("od", s.name)], [khb])
            dma("sp", GATd, GATES[s.name][:, :, t0:t0 + TB], [("gates", s.name, blk)], [kg])
            dma("sp", XTd, XRES[s.name][:, :, t0:t0 + TB], [("xres", s.name, blk)], [kx])
            yield
            for n in range(8):
                ns = slice(n * 128, (n + 1) * 128)
                for k in range(4):
                    mm(psb(B_["bpa"])[:, :TB], Wpa[:, k, ns], OATd[:, k, :], k == 0, k == 3, ["war", khb], [pk(B_["bpa"])])
                for k in range(4):
                    mm(psb(B_["bpd"])[:, :TB], Wpd[:, k, ns], ODTd[:, k, :], k == 0, k == 3, ["war", khb], [pk(B_["bpd"])])
                tt("dve", S0[:, :TB], psb(B_["bpa"])[:, :TB], GATd[:, n, :], ALU.mult, [pk(B_["bpa"]), kg], [ks0])
                tt("dve", S1[:, :TB], psb(B_["bpd"])[:, :TB], GATd[:, 8 + n, :], ALU.mult, [pk(B_["bpd"]), kg], [ks1])
                tt("pool", MGd[:, n, :], S0[:, :TB], S1[:, :TB], ALU.add, [ks0, ks1], [kht])
                yield
            for n in range(8):
                ns = slice(n * 128, (n + 1) * 128)
                bk = B_["bop"][n % 2]
                for k in range(8):
                    mm(psb(bk)[:, :TB], Wo[:, k, ns], MGd[:, k, :], k == 0, k == 7, ["war", kht], [pk(bk)])
                stt("dve", X1Td[:, n, :], psb(bk)[:, :TB], MOD[:, 16 + n, mj:mj + 1], XTd[:, n, :], ALU.mult, ALU.add,
                    [pk(bk), kx] + MK, [kx1])
                yield
            dma("sp", XRES[s.name][:, :, t0:t0 + TB], X1Td, [kx1], [("xres", s.name, blk)])
            rms_stats(X1Td, TB, kx1, sq=HBd[:, 0:8, :], sqkey=khb, bank=B_["bst"], r1=R1d, rstd=RSTDd, rkey=kk_)
            yield
            tt("dve", XTd, X1Td, bcast(RSTDd, 1, 8), ALU.mult, [kx1, "rstd" + kk_], [kx])
            for c in range(8):
                if c % 2 == 0:
                    act(MGd[:, c, :], XTd[:, c, :], AF.Identity, [kx] + MK, [kht],
                        scale=A2[:, c, mj:mj + 1], bias=MOD[:, 24 + c, mj:mj + 1])
                else:
                    tsc("dve", MGd[:, c, :], XTd[:, c, :], A2[:, c, mj:mj + 1], MOD[:, 24 + c, mj:mj + 1], ALU.mult, ALU.add,
                        [kx] + MK, [kht])
            dma("sp", H2[s.name][:, :, t0:t0 + TB], MGd, [kht], [("h2", s.name, blk)])

        def two_way(make, n, stagger):
            gens = [None, None]
            nxt = 0
            started = 0
            while True:
                progressed = False
                for slot in range(2):
                    if gens[slot] is None and nxt < n and (slot == 0 or started >= stagger or nxt > 1):
                        gens[slot] = make(nxt, slot)
                        nxt += 1
                    if gens[slot] is not None:
                        try:
                            next(gens[slot])
                            progressed = True
                            if slot == 0:
                                started += 1
                        except StopIteration:
                            gens[slot] = None
                            progressed = True
                if not progressed and nxt >= n and gens[0] is None and gens[1] is None:
                    break

        two_way(blockD, len(jobsD), 9)
        P.fence()
        if stop_after == "stageD":
            return finish()

        W1h = WAR[:, 0:8 * 2048].rearrange("p (k n) -> p k n", k=8)
        W2h = WAR[:, 16384:16384 + 16 * 1024].rearrange("p (k n) -> p k n", k=16)
        for hf in range(2):
            load_w(W1h, w1[l][:, hf * 2048:(hf + 1) * 2048], 8)
            load_w(W2h, w2[l][hf * 2048:(hf + 1) * 2048, :], 16)
            jobs = [(s, blk) for s in seqs for blk in range(s.T // TB)]
            XTs = [(XT, "bigA"), (X1T, "bigB")]
            H2Ts = [(HB[:, 0:8, :], "hb"), (HB[:, 8:16, :], "ht")]

            def ef_loads(j):
                s, blk = jobs[j]
                t0 = blk * TB
                h2t, hkey = H2Ts[j % 2]
                xt, xkey = XTs[j % 2]
                dma("sp", h2t, H2[s.name][:, :, t0:t0 + TB], [("h2", s.name, blk)], [hkey])
                dma("sp", xt, XRES[s.name][:, :, t0:t0 + TB], [("xres", s.name, blk)], [xkey])

            ef_loads(0)
            for j, (s, blk) in enumerate(jobs):
                mj = s.mj
                t0 = blk * TB
                H2T, hkey = H2Ts[j % 2]
                XTj, xkey = XTs[j % 2]
                if j + 1 < len(jobs):
                    ef_loads(j + 1)
                for f in range(16):
                    bk = 1 + f % 3
                    for k in range(8):
                        mm(psb(bk)[:, :TB], W1h[:, k, f * 128:(f + 1) * 128], H2T[:, k, :], k == 0, k == 7, ["war", hkey], [pk(bk)])
                    i = nstg()
                    act(STG[i][:, :TB], psb(bk)[:, :TB], AF.Relu, [pk(bk), "lvec"], [f"stg{i}"],
                        bias=b1f[:, hf * 16 + f:hf * 16 + f + 1])
                    tt("dve" if f % 2 == 0 else "pool", GATB[:, f, :], STG[i][:, :TB], STG[i][:, :TB], ALU.mult,
                       [f"stg{i}"], ["gatb"])
                for n in range(8):
                    bk = 4 + n % 2
                    for f in range(16):
                        mm(psb(bk)[:, :TB], W2h[:, f, n * 128:(n + 1) * 128], GATB[:, f, :], f == 0, f == 15, ["war", "gatb"], [pk(bk)])
                    stt("dve", XTj[:, n, :], psb(bk)[:, :TB], MOD[:, 40 + n, mj:mj + 1], XTj[:, n, :], ALU.mult, ALU.add,
                        [pk(bk), xkey] + MK, [xkey])
                    if hf == 0:
                        tsc("dve", XTj[:, n, :], XTj[:, n, :], GB2[:, n, mj:mj + 1], None, ALU.add, None, [xkey] + MK, [xkey])
                if not (l == NLAYERS - 1 and hf == 1):
                    dma("pool", XRES[s.name][:, :, t0:t0 + TB], XTj, [xkey], [("xres", s.name, blk)])
                else:
                    rms_stats(XTj, TB, xkey, sq=GATB[:, 0:8, :], sqkey="gatb")
                    tt("dve", XTj, XTj, bcast(RSTD[:, :], 1, 8), ALU.mult, [xkey, "rstd"], [xkey])
                    tt("dve", XTj, XTj, bcast(fnf[:], 2, TB), ALU.mult, [xkey, "fnf"], [xkey])
                    dst = ys_out if s.is_sample else yp_out[s.idx * TP:(s.idx + 1) * TP, :]
                    for t2 in range(TB // 128):
                        for hh in range(2):
                            bk = 6 + hh
                            for c in range(4):
                                tr(psb(bk)[:, c * 128:(c + 1) * 128], XTj[:, hh * 4 + c, t2 * 128:(t2 + 1) * 128], C("ident"),
                                   [xkey, "cst"], [pk(bk)])
                            i = nstg()
                            cp("act" if hh == 0 else "dve", STG[i][:, :], psb(bk)[:, :], [pk(bk)], [f"stg{i}"])
                            dma("pool", dst[t0 + t2 * 128:t0 + (t2 + 1) * 128, hh * 512:(hh + 1) * 512], STG[i][:, :],
                                [f"stg{i}"], [("y", s.name)])
        P.fence()
        if stop_after == f"layer{l}":
            return finish()

    return finish()


def make_in_maps(inputs):
    cos, sin = _rope_tables()
    maps = []
    for core in range(8):
        b = core % 4
        m = {
            "xs": np.ascontiguousarray(inputs["x_sample"][b]),
            "xp": np.ascontiguousarray(inputs["x_prompt"][core * NPS:(core + 1) * NPS].reshape(NPS * TP, D)),
            "cvec": np.ascontiguousarray(np.stack([inputs["c"][b], inputs["c_ctx"]], 0)),
            "ck": np.ascontiguousarray(inputs["cache_k"][b]),
            "cv": np.ascontiguousarray(inputs["cache_v"][b]),
            "st": np.ascontiguousarray(inputs["state_delta"][b]),
            "a_log": np.ascontiguousarray(inputs["a_log"].reshape(DEPTH, 16)),
            "dt_bias": np.ascontiguousarray(inputs["dt_bias"].reshape(DEPTH, 16)),
            "cst": CST, "ropecos": cos, "ropesin": sin,
        }
        for k in ["w_mod", "b_mod", "norm1", "norm2", "w_in", "conv_w", "q_gain", "k_gain", "dn_gain",
                  "w_pa", "w_pd", "w_out", "w1", "b1", "w2", "b2", "final_norm"]:
            m[k] = np.ascontiguousarray(inputs[k])
        maps.append(m)
    return maps


_NC_CACHE = {}


def kernel(**inputs):
    inputs = {k: np.asarray(v) for k, v in inputs.items()}
    if "nc" not in _NC_CACHE:
        _NC_CACHE["nc"] = build_program()
    nc = _NC_CACHE["nc"]
    maps = make_in_maps(inputs)
    res = run_bass_kernel_spmd(nc, maps, core_ids=list(range(8)))
    r = res.results
    y_sample = np.stack([r[b]["ys"] for b in range(4)], 0)
    y_prompt = np.concatenate([r[c]["yp"].reshape(NPS, TP, D) for c in range(8)], 0)
    nk = np.concatenate([r[c]["nk"] for c in range(8)], 0)
    nv = np.concatenate([r[c]["nv"] for c in range(8)], 0)
    nst = np.concatenate([r[c]["nst"] for c in range(8)], 0)
    return (y_prompt.astype(np.float32), y_sample.astype(np.float32), nk.astype(np.float32),
            nv.astype(np.float32), nst.astype(np.float32))
```

```python
import numpy as np
from contextlib import ExitStack
import concourse.bass as bass
import concourse.mybir as mybir
from concourse.bass_utils import run_bass_kernel_spmd

F32 = mybir.dt.float32
BF16 = mybir.dt.bfloat16
AF = mybir.ActivationFunctionType
ALU = mybir.AluOpType
AX = mybir.AxisListType

D = 1024
DEPTH = 2
TS = 4096
TP = 256
NPS = 4
NCTX = 256
INW = 4896
DFF = 4096
EPS = 1e-6
NEG = -30000.0

C_QA, C_KA, C_VA, C_QD, C_KD, C_VD, C_GO, C_AI, C_BI, C_GA, C_GD = 0, 512, 640, 768, 1280, 1792, 2304, 2816, 2832, 2848, 3872


ENGS = ["pe", "act", "dve", "pool", "sp"]
N_DMA_SEMS = {"sp": 40, "pool": 16, "act": 8}
EPOCH = 30000


class Ev:
    __slots__ = ("dma", "eng", "idx", "sem", "val")

    def __init__(self, dma, eng, idx, sem=None, val=None):
        self.dma, self.eng, self.idx, self.sem, self.val = dma, eng, idx, sem, val


class Rec:
    __slots__ = ("eng", "fn", "waits", "signal", "idx", "dma", "sig_sem", "sig_val")

    def __init__(self, eng, fn):
        self.eng, self.fn = eng, fn
        self.waits = []
        self.signal = False
        self.dma = None


class Buf:
    __slots__ = ("w", "r")

    def __init__(self):
        self.w = None
        self.r = []


class Prog:
    def __init__(self, nc):
        self.nc = nc
        self.ops = {e: [] for e in ENGS}
        self.waited = {e: {p: -1 for p in ENGS} for e in ENGS}
        self.waited_dma = {e: {} for e in ENGS}
        self.bufs = {}
        self.dma_count = {q: 0 for q in N_DMA_SEMS}

    def buf(self, k):
        b = self.bufs.get(k)
        if b is None:
            b = self.bufs[k] = Buf()
        return b

    def add(self, eng, fn, reads=(), writes=(), dma=False):
        rec = Rec(eng, fn)
        rec.idx = len(self.ops[eng])
        deps = []
        for k in reads:
            b = self.buf(k)
            if b.w is not None:
                deps.append((b.w, True))
        for k in writes:
            b = self.buf(k)
            if b.w is not None:
                deps.append((b.w, False))
            for r in b.r:
                deps.append((r, False))
        if dma:
            d = self.dma_count[eng]
            n = N_DMA_SEMS[eng]
            si, val = d % n, 16 * (d // n + 1)
            if d >= n:
                deps.append((Ev(True, eng, None, si, val - 16), True))
            rec.dma = (si, val)
            ev = Ev(True, eng, rec.idx, si, val)
            self.dma_count[eng] += 1
        else:
            ev = Ev(False, eng, rec.idx)
        for dep, raw in deps:
            if dep.dma:
                key = (dep.eng, dep.sem)
                if self.waited_dma[eng].get(key, 0) >= dep.val:
                    continue
                self.waited_dma[eng][key] = dep.val
                rec.waits.append(dep)
            else:
                if dep.eng == eng and eng == "pe":
                    continue
                if self.waited[eng][dep.eng] >= dep.idx:
                    continue
                self.waited[eng][dep.eng] = dep.idx
                rec.waits.append(dep)
                self.ops[dep.eng][dep.idx].signal = True
        for k in reads:
            self.buf(k).r.append(ev)
        for k in writes:
            b = self.buf(k)
            b.w = ev
            b.r = []
        self.ops[eng].append(rec)
        return rec

    def fence(self):
        last = {e: len(self.ops[e]) - 1 for e in ["pe", "act", "dve", "pool"]}
        dma_evs = []
        for q, n in N_DMA_SEMS.items():
            d = self.dma_count[q]
            for i in range(min(n, d)):
                uses = (d - i + n - 1) // n
                dma_evs.append(Ev(True, q, None, i, 16 * uses))
        for e in ENGS:
            rec = Rec(e, lambda eng: eng.nop())
            rec.idx = len(self.ops[e])
            for p, li in last.items():
                if p == e or li < 0:
                    continue
                j = li
                while j >= 0 and self.ops[p][j].dma is not None:
                    j -= 1
                if j < 0 or self.waited[e][p] >= j:
                    continue
                self.waited[e][p] = j
                rec.waits.append(Ev(False, p, j))
                self.ops[p][j].signal = True
            for dep in dma_evs:
                key = (dep.eng, dep.sem)
                if self.waited_dma[e].get(key, 0) >= dep.val:
                    continue
                self.waited_dma[e][key] = dep.val
                rec.waits.append(dep)
            self.ops[e].append(rec)

    def emit(self, es):
        nc = self.nc
        comp_sems = {}
        for e in ["pe", "act", "dve", "pool"]:
            cnt = 0
            for rec in self.ops[e]:
                if rec.signal and rec.dma is None:
                    ep = cnt // EPOCH
                    if (e, ep) not in comp_sems:
                        comp_sems[(e, ep)] = es.enter_context(nc.semaphore(f"c_{e}_{ep}"))
                    rec.sig_sem = comp_sems[(e, ep)]
                    rec.sig_val = cnt % EPOCH + 1
                    cnt += 1
        dma_sems = {}
        for q, n in N_DMA_SEMS.items():
            for i in range(min(n, self.dma_count[q])):
                dma_sems[(q, i)] = es.enter_context(nc.semaphore(f"d_{q}_{i}"))
        final_waits = []
        for q, n in N_DMA_SEMS.items():
            d = self.dma_count[q]
            for i in range(min(n, d)):
                uses = (d - i + n - 1) // n
                final_waits.append((dma_sems[(q, i)], 16 * uses))
        block = es.enter_context(nc.Block())
        ops = self.ops

        def run(engname, eng):
            for rec in ops[engname]:
                for dep in rec.waits:
                    if dep.dma:
                        eng.wait_ge(dma_sems[(dep.eng, dep.sem)], dep.val)
                    else:
                        prod = ops[dep.eng][dep.idx]
                        eng.wait_ge(prod.sig_sem, prod.sig_val)
                ins = rec.fn(eng)
                if rec.dma is not None:
                    ins.then_inc(dma_sems[(engname, rec.dma[0])], 16)
                elif rec.signal:
                    ins.then_inc(rec.sig_sem, 1)
            if engname == "sp":
                for s, v in final_waits:
                    eng.wait_ge(s, v)

        @block.tensor
        def _(t):
            run("pe", t)

        @block.scalar
        def _(a):
            run("act", a)

        @block.vector
        def _(v):
            run("dve", v)

        @block.gpsimd
        def _(g):
            run("pool", g)

        @block.sync
        def _(s):
            run("sp", s)


CST_LAYOUT = {}


def _build_consts():
    p = np.arange(128)
    cols = []
    off = 0

    def put(name, arr):
        nonlocal off
        arr = np.asarray(arr, np.float32).reshape(128, -1)
        CST_LAYOUT[name] = (off, arr.shape[1])
        cols.append(arr)
        off += arr.shape[1]

    put("ident", np.eye(128))
    put("ones", np.ones((128, 128)))
    put("negones", -np.ones((128, 128)))
    half = p // 64
    put("blk", (half[:, None] == half[None, :]).astype(np.float32))
    put("negblk", -(half[:, None] == half[None, :]).astype(np.float32))
    put("identst", (p[:, None] % 64 == np.arange(64)[None, :]).astype(np.float32))
    same = half[:, None] == half[None, :]
    put("tri_f", (same & (p[:, None] <= p[None, :])).astype(np.float32))
    put("tri_b", (same & (p[:, None] >= p[None, :])).astype(np.float32))
    put("half0", np.repeat((p < 64).astype(np.float32)[:, None], 128, 1))
    put("half1", np.repeat((p >= 64).astype(np.float32)[:, None], 128, 1))
    il = p % 64
    j = np.arange(64)

    def m(keep):
        return np.where(keep, 0.0, NEG).astype(np.float32)

    put("m1_f", m(il[:, None] > j[None, :]))
    put("m1_b", m(il[:, None] < j[None, :]))
    put("m2_f", m(j[None, :] > il[:, None]))
    put("m2_b", m(j[None, :] < il[:, None]))
    put("m3_f", m(j[None, :] >= il[:, None]))
    put("m3_b", m(j[None, :] <= il[:, None]))
    put("mask8", ((il[:, None] // 8) == (j[None, :] // 8)).astype(np.float32))
    for sz in (8, 16, 32):
        put("moff%d" % sz, (((il[:, None] // (2 * sz)) == (j[None, :] // (2 * sz)))
                            & ((il[:, None] // sz) != (j[None, :] // sz))).astype(np.float32))
    R = np.zeros((128, 128), np.float32)
    for q in range(128):
        if q % 64 < 32:
            R[q, q + 32] = -1.0
        else:
            R[q, q - 32] = 1.0
    put("rot", R.T)
    return np.concatenate(cols, 1)


CST = _build_consts()
NCST = CST.shape[1]


def _rope_tables():
    t = np.arange(TS)
    row = (t // 64).astype(np.float32)
    col = (t % 64).astype(np.float32)
    inv = (10000.0 ** (-np.arange(16, dtype=np.float32) / 16)).astype(np.float32)
    ang = np.concatenate([row[:, None] * inv, col[:, None] * inv], -1).astype(np.float32)
    cos = np.cos(ang).astype(np.float32).T
    sin = np.sin(ang).astype(np.float32).T
    return np.tile(cos, (4, 1)).copy(), np.tile(sin, (4, 1)).copy()


class Seq:
    def __init__(self, name, T, is_sample, idx, key0, tile0):
        self.name, self.T, self.is_sample, self.idx = name, T, is_sample, idx
        self.key0 = key0
        self.tile0 = tile0
        self.nctx = NCTX if is_sample else 0
        self.mj = 0 if is_sample else 1


def bcast(ap, axis, n):
    shp = list(ap.shape)
    shp.insert(axis, n)
    return ap.unsqueeze(axis).broadcast_to(shp)


def build_program(debug_outs=(), stop_after=None):
    nc = bass.Bass("TRN2", target_bir_lowering=False)
    es = ExitStack()
    P = Prog(nc)
    dbg = set(debug_outs)

    def din(name, shape, dt=F32):
        return nc.dram_tensor(name, list(shape), dt, kind="ExternalInput").ap()

    def dout(name, shape, dt=F32):
        return nc.dram_tensor(name, list(shape), dt, kind="ExternalOutput").ap()

    def dscr(name, shape, dt=F32):
        kind = "ExternalOutput" if name in dbg else "Internal"
        return nc.dram_tensor(name, list(shape), dt, kind=kind).ap()

    xs_in = din("xs", [TS, D])
    xp_in = din("xp", [NPS * TP, D])
    cvec = din("cvec", [2, D])
    ck_in = din("ck", [DEPTH, 2, NCTX, 64])
    cv_in = din("cv", [DEPTH, 2, NCTX, 64])
    st_in = din("st", [DEPTH, 2, 8, 64, 64])
    w_mod = din("w_mod", [DEPTH, D, 6 * D])
    b_mod = din("b_mod", [DEPTH, 6 * D])
    norm1 = din("norm1", [DEPTH, D])
    norm2 = din("norm2", [DEPTH, D])
    w_in = din("w_in", [DEPTH, D, INW])
    conv_w = din("conv_w", [DEPTH, 3, 1536])
    q_gain = din("q_gain", [DEPTH, 64])
    k_gain = din("k_gain", [DEPTH, 64])
    a_log = din("a_log", [DEPTH, 16])
    dt_bias = din("dt_bias", [DEPTH, 16])
    dn_gain = din("dn_gain", [DEPTH, 64])
    w_pa = din("w_pa", [DEPTH, 512, D])
    w_pd = din("w_pd", [DEPTH, 512, D])
    w_out = din("w_out", [DEPTH, D, D])
    w1 = din("w1", [DEPTH, D, DFF])
    b1 = din("b1", [DEPTH, DFF])
    w2 = din("w2", [DEPTH, DFF, D])
    b2 = din("b2", [DEPTH, D])
    final_norm = din("final_norm", [D])
    cst_in = din("cst", [128, NCST])
    cos_in = din("ropecos", [128, TS])
    sin_in = din("ropesin", [128, TS])
    ys_out = dout("ys", [TS, D])
    yp_out = dout("yp", [NPS * TP, D])
    nk_out = dout("nk", [NPS, DEPTH, 2, TP, 64])
    nv_out = dout("nv", [NPS, DEPTH, 2, TP, 64])
    nst_out = dout("nst", [NPS, DEPTH, 2, 8, 64, 64])

    seqs = [Seq("s", TS, True, 0, 0, 0)]
    for i in range(NPS):
        seqs.append(Seq(f"p{i}", TP, False, i, NCTX + TS + i * TP, TS // 128 + i * (TP // 128)))
    NKEY = NCTX + TS + NPS * TP
    NTILE = TS // 128 + NPS * TP // 128
    NVT = NKEY // 128

    XRES, PRE, QA, GATES, GS, QDT, KDT, QTM, KTM, VTM, OPART, OA, OD, X1, H2, ACTS = ({} for _ in range(16))
    for s in seqs:
        T = s.T
        XRES[s.name] = dscr(f"xres_{s.name}", [128, 8, T])
        PRE[s.name] = dscr(f"pre_{s.name}", [128, 12, T + 2])
        QA[s.name] = dscr(f"qa_{s.name}", [8, 64, T], BF16)
        GATES[s.name] = dscr(f"gates_{s.name}", [128, 16, T], BF16)
        GS[s.name] = dscr(f"gs_{s.name}", [T, 512], BF16)
        QDT[s.name] = dscr(f"qdt_{s.name}", [8, 64, T], BF16)
        KDT[s.name] = dscr(f"kdt_{s.name}", [8, 64, T], BF16)
        QTM[s.name] = dscr(f"qtm_{s.name}", [T, 512], BF16)
        KTM[s.name] = dscr(f"ktm_{s.name}", [T, 512], BF16)
        VTM[s.name] = dscr(f"vtm_{s.name}", [T, 512], BF16)
        OPART[s.name] = dscr(f"opart_{s.name}", [T, 512])
        OA[s.name] = dscr(f"oa_{s.name}", [8, 64, T], BF16)
        OD[s.name] = dscr(f"od_{s.name}", [128, 4, T], BF16)
        H2[s.name] = dscr(f"h2_{s.name}", [128, 8, T], BF16)

    def sb(name, shape, dt=F32):
        return es.enter_context(nc.sbuf_tensor("sb_" + name, list(shape), dt))

    cst = sb("cst", [128, NCST])
    cstb = sb("cstb", [128, NCST], BF16)

    def C(name, bf=False):
        o, n = CST_LAYOUT[name]
        return (cstb if bf else cst)[:, o:o + n]

    WAR = sb("warena", [128, 40960], BF16)
    KT_all = sb("kt_all", [128, NKEY], BF16)
    VA_all = sb("va_all", [128, NVT, 2, 65], BF16)
    LA = sb("la", [128, NTILE, 16])
    LB = sb("lb", [128, NTILE, 16])
    BETA = sb("beta", [128, NTILE, 16])
    MOD = sb("mod", [128, 48, 2])
    A1 = sb("a1", [128, 8, 2])
    A2 = sb("a2", [128, 8, 2])
    GB2 = sb("gb2", [128, 8, 2])
    n1f = sb("n1f", [128, 8])
    n2f = sb("n2f", [128, 8])
    fnf = sb("fnf", [128, 8])
    b2f = sb("b2f", [128, 8])
    b1f = sb("b1f", [128, 32])
    bmf = sb("bmf", [128, 48])
    cfm = sb("cfm", [128, 8, 2])
    scb = sb("scb", [128, 8, 2], BF16)
    qg = sb("qg", [128, 1])
    kg = sb("kg", [128, 1])
    cw = sb("cw", [128, 3, 12])
    dtb = sb("dtb", [128, 16])
    negA = sb("negA", [128, 16])
    dng = sb("dng", [128, 64])
    TB = 256
    BIGA = sb("bigA", [128, 12 * 258])
    BIGB = sb("bigB", [128, 12 * 256])
    HB = sb("hb16", [128, 16, 256], BF16)
    GATB = sb("gatb", [128, 16, 256], BF16)
    R1 = sb("r1", [128, 256])
    RSTD = sb("rstd", [128, 256])
    STG = [sb(f"stg{i}", [128, 512]) for i in range(4)]
    STB = [sb(f"stb{i}", [128, 512], BF16) for i in range(4)]
    COSB = sb("cosb", [128, 256])
    SINB = sb("sinb", [128, 256])
    SMALL = sb("small", [128, 64])
    SM = sb("sm", [128, 160])
    VAB = sb("vab", [128, 160])
    KTC = [sb(f"ktc{i}", [64, 8, 128], BF16) for i in range(2)]
    QTC = [sb(f"qtc{i}", [64, 8, 128], BF16) for i in range(2)]
    KTMC = [sb(f"ktmc{i}", [128, 512], BF16) for i in range(2)]
    QTMC = [sb(f"qtmc{i}", [128, 512], BF16) for i in range(2)]
    VTMC = [sb(f"vtmc{i}", [128, 512], BF16) for i in range(2)]
    def v512(big, i, p0=0, p1=128):
        return big[p0:p1, i * 512:(i + 1) * 512]

    E1, E2, E3, DG1, DG3 = (v512(BIGA, i) for i in range(5))
    U0 = [v512(BIGA, 5), v512(BIGB, 0)]
    OACC = [v512(BIGB, 1), v512(BIGB, 2)]
    RS = v512(BIGB, 3)
    S32 = [v512(BIGB, 4 + i, 0, 64).rearrange("p (h v) -> p h v", h=8) for i in range(2)]
    HBf = HB[:, :, :].rearrange("p a b -> p (a b)")
    GBf = GATB[:, :, :].rearrange("p a b -> p (a b)")
    PA = [v512(HBf, 0), v512(HBf, 1)]
    PT = [v512(HBf, 2), v512(HBf, 3)]
    RT = [v512(HBf, 4), v512(HBf, 5)]
    QKM = [v512(HBf, 6), v512(HBf, 7)]
    BEK = v512(GBf, 0)
    KDEC = [v512(GBf, 1), v512(GBf, 2)]
    BV = v512(GBf, 3)
    UB = [v512(GBf, 4), v512(GBf, 5)]
    DE = v512(GBf, 6)
    OB = v512(GBf, 7, 0, 64)
    XBW = [sb(f"xbw{i}", [128, 1024], BF16) for i in range(2)]
    XB = [XBW[0][:, 0:512], XBW[0][:, 512:1024], XBW[1][:, 0:512], XBW[1][:, 512:1024]]
    NWT = [sb(f"nwt{i}", [64, 16, 64], BF16) for i in range(2)]
    QDEC = [sb(f"qdec{i}", [64, 16, 64], BF16) for i in range(2)]
    SBF = [sb(f"sbf{i}", [64, 8, 64], BF16) for i in range(2)]
    EGL2 = [sb(f"egl2{i}", [128, 16]) for i in range(2)]
    QB = STB[3][:, :].rearrange("p (j t) -> p j t", j=4)
    PS = [es.enter_context(nc.psum_tensor(f"ps{i}", [128, 1024], F32)) for i in range(4)]
    dbg_mod = dscr("dbg_mod", [128, 48, 2])
    dbg_la = dscr("dbg_la", [128, NTILE, 16])
    dbg_lb = dscr("dbg_lb", [128, NTILE, 16])
    dbg_beta = dscr("dbg_beta", [128, NTILE, 16])

    XT = BIGA[:, 0:8 * 256].rearrange("p (c t) -> p c t", c=8)
    X1T = BIGB[:, 0:8 * 256].rearrange("p (c t) -> p c t", c=8)
    PRET = BIGA[:, :].rearrange("p (c t) -> p c t", c=12)
    CV = BIGB[:, :].rearrange("p (c t) -> p c t", c=12)
    SQ = HB[:, 0:8, :]
    HT = HB[:, 8:16, :]
    XTM = [BIGA[:, i * 1024:(i + 1) * 1024] for i in range(2)]
    XFM = [BIGB[:, i * 1024:(i + 1) * 1024].rearrange("p (c t) -> p c t", c=8) for i in range(2)]

    def psb(i):
        return PS[i // 2][:, (i % 2) * 512:(i % 2) * 512 + 512]

    def pk(i):
        return ("ps", i)

    def dma(q, out, in_, reads, writes, **kw):
        P.add(q, lambda e: e.dma_start(out=out, in_=in_, **kw), reads, writes, dma=True)

    def mm(out, lhsT, rhs, start, stop, reads, writes, **kw):
        P.add("pe", lambda e: e.matmul(out, lhsT, rhs, start=start, stop=stop, **kw), reads, writes)

    def tr(out, in_, ident, reads, writes):
        P.add("pe", lambda e: e.transpose(out, in_, ident), reads, writes)

    def act(out, in_, func, reads, writes, **kw):
        P.add("act", lambda e: e.activation(out, in_, func, **kw), reads, writes)

    def tt(eng, out, in0, in1, op, reads, writes):
        P.add(eng, lambda e: e.tensor_tensor(out, in0, in1, op), reads, writes)

    def tsc(eng, out, in0, s1, s2, op0, op1, reads, writes):
        if op1 is None:
            P.add(eng, lambda e: e.tensor_scalar(out, in0, s1, None, op0), reads, writes)
        else:
            P.add(eng, lambda e: e.tensor_scalar(out, in0, s1, s2, op0, op1), reads, writes)

    def stt(eng, out, in0, scalar, in1, op0, op1, reads, writes):
        P.add(eng, lambda e: e.scalar_tensor_tensor(out, in0, scalar, in1, op0, op1), reads, writes)

    def cp(eng, out, in_, reads, writes):
        if eng == "act":
            P.add("act", lambda e: e.copy(out, in_), reads, writes)
        else:
            P.add(eng, lambda e: e.tensor_copy(out, in_), reads, writes)

    def recip(out, in_, reads, writes):
        P.add("dve", lambda e: e.reciprocal(out, in_), reads, writes)

    def load_w(dst3, src2, K, wkey="war"):
        for k in range(K):
            dma("pool", dst3[:, k, :], src2[k * 128:(k + 1) * 128, :], [], [wkey])

    def rms_stats(src3, nt, srckey, sq=None, sqkey="hb", bank=0, r1=None, rstd=None, rkey=""):
        if sq is None:
            sq = SQ
        if r1 is None:
            r1, rstd = R1[:, :], RSTD[:, :]
        act(sq[:, :, :nt], src3, AF.Square, [srckey], [sqkey])
        for c in range(8):
            mm(psb(bank)[:, :nt], C("ones", True), sq[:, c, :nt], c == 0, c == 7, [sqkey, "cstb"], [pk(bank)])
        act(r1[:, :nt], psb(bank)[:, :nt], AF.Sqrt, [pk(bank)], ["r1" + rkey], bias=EPS, scale=1.0 / D)
        recip(rstd[:, :nt], r1[:, :nt], ["r1" + rkey], ["rstd" + rkey])

    dma("sp", cst[:], cst_in, [], ["cst"])
    cp("dve", cstb[:], cst[:], ["cst"], ["cstb"])
    P.add("pool", lambda e: e.memset(VA_all[:], 1.0), [], ["va"])
    dma("sp", fnf[:], final_norm.rearrange("(k p) -> p k", p=128), [], ["fnf"], allow_slow_non_contiguous=True)
    for j in range(2):
        dma("sp", cfm[:, :, j], cvec[j].rearrange("(k p) -> p k", p=128), [], ["cfm"], allow_slow_non_contiguous=True)
    act(scb[:], cfm[:], AF.Silu, ["cfm"], ["scb"])
    P.add("pool", lambda e: e.memset(SMALL[:], 0.0), [], ["small"])
    for s in seqs:
        for col in (0, s.T + 1):
            dma("sp", PRE[s.name][:, :, col:col + 1], SMALL[:, 0:12].unsqueeze(2), ["small"], [("pre", s.name, "pad", col)],
                allow_slow_non_contiguous=True)

    it = 0
    for s in seqs:
        src = xs_in if s.is_sample else xp_in[s.idx * TP:(s.idx + 1) * TP, :]
        for ti in range(s.T // 128):
            b = it % 2
            dma("sp", XTM[b], src[ti * 128:(ti + 1) * 128, :], [], ["bigA"])
            for c in range(8):
                tr(PS[b][:, c * 128:(c + 1) * 128], XTM[b][:, c * 128:(c + 1) * 128], C("ident"),
                   ["bigA", "cst"], [pk(2 * b), pk(2 * b + 1)])
            cp("act" if it % 2 == 0 else "dve", XFM[b], PS[b][:].rearrange("p (c t) -> p c t", c=8),
               [pk(2 * b), pk(2 * b + 1)], ["bigB"])
            dma("pool", XRES[s.name][:, :, ti * 128:(ti + 1) * 128], XFM[b], ["bigB"], [("xres", s.name, ti // 2)])
            it += 1

    def finish():
        P.emit(es)
        es.close()
        return nc

    if stop_after == "stage0":
        return finish()

    NLAYERS = DEPTH
    for l in range(NLAYERS):
        for (t_, src) in ((n1f, norm1[l]), (n2f, norm2[l]), (b2f, b2[l])):
            dma("sp", t_[:], src.rearrange("(k p) -> p k", p=128), [], ["lvec"], allow_slow_non_contiguous=True)
        dma("sp", b1f[:], b1[l].rearrange("(k p) -> p k", p=128), [], ["lvec"], allow_slow_non_contiguous=True)
        dma("sp", bmf[:], b_mod[l].rearrange("(k p) -> p k", p=128), [], ["lvec"], allow_slow_non_contiguous=True)
        for hh in range(2):
            dma("sp", qg[64 * hh:64 * hh + 64, :], q_gain[l].rearrange("(d o) -> d o", o=1), [], ["lvec"], allow_slow_non_contiguous=True)
            dma("sp", kg[64 * hh:64 * hh + 64, :], k_gain[l].rearrange("(d o) -> d o", o=1), [], ["lvec"], allow_slow_non_contiguous=True)
        for j in range(3):
            dma("sp", cw[:, j, :], conv_w[l, j].rearrange("(c p) -> p c", p=128), [], ["lvec"], allow_slow_non_contiguous=True)
        dma("sp", dtb[:], dt_bias[l:l + 1, :].broadcast_to([128, 16]), [], ["lvec"], allow_slow_non_contiguous=True)
        dma("sp", negA[:], a_log[l:l + 1, :].broadcast_to([128, 16]), [], ["lvec"], allow_slow_non_contiguous=True)
        dma("sp", dng[:], dn_gain[l:l + 1, :].broadcast_to([128, 64]), [], ["lvec"], allow_slow_non_contiguous=True)
        act(negA[:], negA[:], AF.Exp, ["lvec"], ["lvec2"])
        tsc("dve", negA[:], negA[:], -1.0, None, ALU.mult, None, ["lvec2"], ["lvec2"])

        Wm = WAR[:, 0:8 * 3072].rearrange("p (k n) -> p k n", k=8)
        for hh in range(2):
            load_w(Wm, w_mod[l][:, hh * 3072:(hh + 1) * 3072], 8)
            for n in range(24):
                nn = hh * 24 + n
                for k in range(8):
                    mm(psb(0)[:, nn * 2:nn * 2 + 2], Wm[:, k, n * 128:(n + 1) * 128], scb[:, k, :], k == 0, k == 7,
                       ["war", "scb"], [pk(0)])
        tt("dve", MOD[:], psb(0)[:, 0:96].rearrange("p (n j) -> p n j", j=2), bcast(bmf[:], 2, 2), ALU.add,
           [pk(0), "lvec"], ["mod"])
        stt("dve", A1[:], MOD[:, 8:16, :], 1.0, bcast(n1f[:], 2, 2), ALU.add, ALU.mult, ["mod", "lvec"], ["mod2"])
        stt("dve", A2[:], MOD[:, 32:40, :], 1.0, bcast(n2f[:], 2, 2), ALU.add, ALU.mult, ["mod", "lvec"], ["mod2"])
        tt("dve", GB2[:], MOD[:, 40:48, :], bcast(b2f[:], 2, 2), ALU.mult, ["mod", "lvec"], ["mod2"])
        MK = ["mod", "mod2", "lvec", "lvec2"]
        if stop_after == "adaln":
            dma("sp", dbg_mod, MOD[:], ["mod"], ["dbgmod"])
            return finish()

        Win = WAR[:, 0:8 * INW].rearrange("p (k n) -> p k n", k=8)
        load_w(Win, w_in[l], 8)
        bank_rr = [0]

        def next_bank():
            bank_rr[0] = (bank_rr[0] % 4) + 1
            return bank_rr[0]

        stg_rr = [0]

        def nstg():
            stg_rr[0] = (stg_rr[0] + 1) % 4
            return stg_rr[0]

        jobsA = [(s, blk) for s in seqs for blk in range(s.T // TB)]
        XTsA = [(XT, "bigA"), (X1T, "bigB")]
        HTsA = [(HB[:, 8:16, :], "ht"), (GATB[:, 0:8, :], "gatb")]

        def blockA(j):
            s, blk = jobsA[j]
            mj = s.mj
            t0 = blk * TB
            XTj, xkey = XTsA[j % 2]
            HTj, hkey = HTsA[j % 2]
            dma("sp", XTj, XRES[s.name][:, :, t0:t0 + TB], [("xres", s.name, blk)], [xkey])
            if s.is_sample:
                dma("sp", COSB[:], cos_in[:, t0:t0 + TB], [], ["cosb"])
                dma("sp", SINB[:], sin_in[:, t0:t0 + TB], [], ["sinb"])
            rms_stats(XTj, TB, xkey)
            tt("dve", XTj, XTj, bcast(RSTD[:, :], 1, 8), ALU.mult, [xkey, "rstd"], [xkey])
            for c in range(8):
                if c % 2 == 0:
                    act(HTj[:, c, :], XTj[:, c, :], AF.Identity, [xkey] + MK, [hkey],
                        scale=A1[:, c, mj:mj + 1], bias=MOD[:, c, mj:mj + 1])
                else:
                    tsc("dve", HTj[:, c, :], XTj[:, c, :], A1[:, c, mj:mj + 1], MOD[:, c, mj:mj + 1], ALU.mult, ALU.add,
                        [xkey] + MK, [hkey])

            yield

            def fm_chunk(col0):
                bk = next_bank()
                for k in range(8):
                    mm(psb(bk)[:, :TB], Win[:, k, col0:col0 + 128], HTj[:, k, :], k == 0, k == 7, ["war", hkey], [pk(bk)])
                return bk

            def qk_epi(c, bk):
                is_k = (c == 4)
                gain = kg if is_k else qg
                cp("act", STG[0][:, :TB], psb(bk)[:, :TB], [pk(bk)], ["stg0"])
                act(STB[0][:, :TB], psb(bk)[:, :TB], AF.Square, [pk(bk)], ["stb0"])
                yield
                mm(psb(6)[:, :TB], C("blk", True), STB[0][:, :TB], True, True, ["stb0", "cstb"], [pk(6)])
                act(STG[1][:, :TB], psb(6)[:, :TB], AF.Sqrt, [pk(6)], ["stg1"], bias=EPS, scale=1.0 / 64)
                recip(STG[1][:, :TB], STG[1][:, :TB], ["stg1"], ["stg1"])
                stt("dve", STG[0][:, :TB], STG[0][:, :TB], gain[:, 0:1], STG[1][:, :TB], ALU.mult, ALU.mult,
                    ["stg0", "stg1", "lvec"], ["stg0"])
                yield
                kcol = s.key0 + s.nctx + t0
                dst = KT_all[:, kcol:kcol + TB] if is_k else STB[1][:, :TB]
                dkey = "kt" if is_k else "stb1"
                if s.is_sample:
                    mm(psb(7)[:, :TB], C("rot"), STG[0][:, :TB], True, True, ["stg0", "cst"], [pk(7)])
                    tt("dve", STG[2][:, :TB], STG[0][:, :TB], COSB[:], ALU.mult, ["stg0", "cosb"], ["stg2"])
                    yield
                    tt("dve", STG[3][:, :TB], psb(7)[:, :TB], SINB[:], ALU.mult, [pk(7), "sinb"], ["stg3"])
                    tt("dve", dst, STG[2][:, :TB], STG[3][:, :TB], ALU.add, ["stg2", "stg3"], [dkey])
                else:
                    cp("dve", dst, STG[0][:, :TB], ["stg0"], [dkey])
                if not is_k:
                    dma("pool", QA[s.name][2 * c:2 * c + 2].rearrange("h d t -> (h d) t")[:, t0:t0 + TB], STB[1][:, :TB],
                        ["stb1"], [("qa", s.name)])
                elif not s.is_sample:
                    for t2 in range(TB // 128):
                        tr(psb(7)[:, t2 * 128:(t2 + 1) * 128], STG[0][:, t2 * 128:(t2 + 1) * 128], C("ident"),
                           ["stg0", "cst"], [pk(7)])
                    cp("act", STG[2][:, :TB], psb(7)[:, :TB], [pk(7)], ["stg2"])
                    for t2 in range(TB // 128):
                        for g in range(2):
                            dma("pool", nk_out[s.idx, l, g, t0 + t2 * 128:t0 + (t2 + 1) * 128, :],
                                STG[2][:, t2 * 128 + g * 64:t2 * 128 + g * 64 + 64], ["stg2"], [("nk", s.idx)])
                yield

            hrr = [0]

            def nh():
                hrr[0] = (hrr[0] + 1) % 4
                return hrr[0]

            def filler():
                for c in range(12):
                    bk = fm_chunk(C_QD + c * 128)
                    i = nh()
                    cp("act" if c % 2 == 0 else "dve", STG[i][:, 256:512], psb(bk)[:, :TB], [pk(bk)], [f"stgh{i}"])
                    dma("pool", PRE[s.name][:, c, 1 + t0:1 + t0 + TB], STG[i][:, 256:512], [f"stgh{i}"], [("pre", s.name, blk)])
                    yield
                for c in range(16):
                    bk = fm_chunk(C_GA + c * 128)
                    i = nh()
                    act(STB[i][:, 256:512], psb(bk)[:, :TB], AF.Sigmoid, [pk(bk)], [f"stbh{i}"])
                    dma("pool", GATES[s.name][:, c, t0:t0 + TB], STB[i][:, 256:512], [f"stbh{i}"], [("gates", s.name, blk)])
                    yield

            fg = filler()
            for c in range(5):
                bk = fm_chunk(C_QA + c * 128)
                for _ in qk_epi(c, bk):
                    next(fg, None)
            yield
            for _ in fg:
                pass
            for t2 in range(TB // 128):
                tsl = slice(t2 * 128, (t2 + 1) * 128)
                gti = s.tile0 + (t0 // 128) + t2
                vt = (s.key0 + s.nctx + t0) // 128 + t2
                for k in range(8):
                    mm(psb(5)[:, 0:512], HTj[:, k, tsl], Win[:, k, C_GO:C_GO + 512], k == 0, k == 7, ["war", hkey], [pk(5)])
                for k in range(8):
                    mm(psb(6)[:, 0:128], HTj[:, k, tsl], Win[:, k, C_VA:C_VA + 128], k == 0, k == 7, ["war", hkey], [pk(6)])
                for k in range(8):
                    mm(psb(6)[:, 128:160], HTj[:, k, tsl], Win[:, k, C_AI:C_AI + 32], k == 0, k == 7, ["war", hkey], [pk(6)])
                i = nstg()
                act(STB[i][:, :], psb(5)[:, :], AF.Silu, [pk(5)], [f"stb{i}", f"stbh{i}"])
                dma("pool", GS[s.name][t0 + t2 * 128:t0 + (t2 + 1) * 128, :], STB[i][:, :], [f"stb{i}", f"stbh{i}"], [("gs", s.name)])
                cp("dve", VAB[:, :], psb(6)[:, 0:160], [pk(6)], ["vab"])
                cp("dve", VA_all[:, vt, :, 0:64], VAB[:, 0:128].rearrange("p (g d) -> p g d", g=2), ["vab"], ["va"])
                if not s.is_sample:
                    for g in range(2):
                        dma("pool", nv_out[s.idx, l, g, t0 + t2 * 128:t0 + (t2 + 1) * 128, :], VAB[:, g * 64:g * 64 + 64],
                            ["vab"], [("nv", s.idx)])
                tt("dve", SM[:, 0:16], VAB[:, 128:144], dtb[:], ALU.add, ["vab", "lvec"], ["sm"])
                act(SM[:, 16:32], SM[:, 0:16], AF.Exp, ["sm"], ["sm1"])
                act(SM[:, 32:48], SM[:, 16:32], AF.Ln, ["sm1"], ["sm2"], bias=1.0)
                tt("dve", LA[:, gti, :], SM[:, 32:48], negA[:], ALU.mult, ["sm2", "lvec2"], ["la"])
                act(BETA[:, gti, :], VAB[:, 144:160], AF.Sigmoid, ["vab"], ["beta"])
                act(LB[:, gti, :], BETA[:, gti, :], AF.Ln, ["beta"], ["lb"])

        gA = [blockA(j) for j in range(len(jobsA))]
        next(gA[0])
        for j in range(len(jobsA)):
            next(gA[j])
            if j + 1 < len(jobsA):
                next(gA[j + 1])
            for _ in gA[j]:
                pass
        if "dbg_la" in dbg:
            dma("sp", dbg_la, LA[:], ["la"], ["dbgla"])
            dma("sp", dbg_lb, LB[:], ["lb"], ["dbglb"])
            dma("sp", dbg_beta, BETA[:], ["beta"], ["dbgbeta"])
        P.fence()
        if stop_after == "stageA":
            return finish()

        brr = [0]

        def nb():
            brr[0] = brr[0] % 3 + 1
            return brr[0]

        def genB():
            for s in seqs:
                for blk in range(s.T // TB):
                    t0 = blk * TB
                    dma("sp", PRET, PRE[s.name][:, :, t0:t0 + TB + 2],
                        [("pre", s.name, b_) for b_ in range(max(0, blk - 1), min(s.T // TB, blk + 2))]
                        + [("pre", s.name, "pad", 0), ("pre", s.name, "pad", s.T + 1)], ["bigA"])
                    for c in range(12):
                        tsc("dve", CV[:, c, :], PRET[:, c, 0:TB], cw[:, 0, c:c + 1], None, ALU.mult, None, ["bigA", "lvec"], [("cv", c)])
                        stt("dve", CV[:, c, :], PRET[:, c, 1:TB + 1], cw[:, 1, c:c + 1], CV[:, c, :], ALU.mult, ALU.add,
                            ["bigA", "lvec", ("cv", c)], [("cv", c)])
                        stt("dve", CV[:, c, :], PRET[:, c, 2:TB + 2], cw[:, 2, c:c + 1], CV[:, c, :], ALU.mult, ALU.add,
                            ["bigA", "lvec", ("cv", c)], [("cv", c)])
                        if c % 3 == 2:
                            yield
                    for c4 in range(3):
                        cs_ = slice(c4 * 4, c4 * 4 + 4)
                        ck = [("cv", c) for c in range(c4 * 4, c4 * 4 + 4)]
                        SG = PRET[:, cs_, 0:TB]
                        act(SG, CV[:, cs_, :], AF.Exp, ck + ["bigA"], ["bigA"], scale=-1.0)
                        act(SG, SG, AF.Ln, ["bigA"], ["bigA"], bias=1.0)
                        act(SG, SG, AF.Exp, ["bigA"], ["bigA"], scale=-1.0)
                        tt("dve", CV[:, cs_, :], CV[:, cs_, :], SG, ALU.mult, ck + ["bigA"], ck)
                        yield
                    for c in range(8):
                        act(STB[0][:, :TB], CV[:, c, :], AF.Square, [("cv", c)], ["stb0"])
                        mm(psb(7)[:, :TB], C("blk", True), STB[0][:, :TB], True, True, ["stb0", "cstb"], [pk(7)])
                        act(STG[1][:, :TB], psb(7)[:, :TB], AF.Ln, [pk(7)], ["stg1"], bias=EPS, scale=1.0)
                        act(STG[1][:, :TB], STG[1][:, :TB], AF.Exp, ["stg1"], ["stg1"], scale=-0.5)
                        i = nb()
                        stt("dve", STB[i][:, :TB], CV[:, c, :], 0.125 if c < 4 else 1.0, STG[1][:, :TB], ALU.mult, ALU.mult,
                            [("cv", c), "stg1"], [f"stb{i}"])
                        cp("act", CV[:, c, :], STB[i][:, :TB], [f"stb{i}"], [("cv", c)])
                        dstT = QDT if c < 4 else KDT
                        cc = c % 4
                        dma("pool", dstT[s.name][2 * cc:2 * cc + 2].rearrange("h d t -> (h d) t")[:, t0:t0 + TB], STB[i][:, :TB],
                            [f"stb{i}"], [("qkdt", s.name)])
                        yield
                    for t2 in range(TB // 128):
                        for grp, dstM in enumerate((QTM, KTM, VTM)):
                            for cc in range(4):
                                tr(psb(7)[:, cc * 128:(cc + 1) * 128], CV[:, grp * 4 + cc, t2 * 128:(t2 + 1) * 128], C("ident"),
                                   [("cv", grp * 4 + cc), "cst"], [pk(7)])
                            i = nb()
                            cp("dve", STB[i][:, :], psb(7)[:, :], [pk(7)], [f"stb{i}"])
                            dma("pool", dstM[s.name][t0 + t2 * 128:t0 + (t2 + 1) * 128, :], STB[i][:, :], [f"stb{i}"], [("tm", s.name)])
                            yield

        if stop_after == "stageB":
            for _ in genB():
                pass
            P.fence()
            return finish()

        for t2 in range(NCTX // 128):
            dma("sp", STG[0][:, 0:128].rearrange("p (g d) -> p g d", g=2),
                ck_in[l, :, t2 * 128:(t2 + 1) * 128, :].rearrange("g p d -> p g d"), [], ["stg0"])
            tr(psb(0)[:, 0:128], STG[0][:, 0:128], C("ident"), ["stg0", "cst"], [pk(0)])
            cp("dve", KT_all[:, t2 * 128:(t2 + 1) * 128], psb(0)[:, 0:128], [pk(0)], ["kt"])
            dma("sp", STG[1][:, 0:128].rearrange("p (g d) -> p g d", g=2),
                cv_in[l, :, t2 * 128:(t2 + 1) * 128, :].rearrange("g p d -> p g d"), [], ["stg1"])
            cp("dve", VA_all[:, t2, :, 0:64], STG[1][:, 0:128].rearrange("p (g d) -> p g d", g=2), ["stg1"], ["va"])
        QBz = [[WAR[:, (qp * 2 + g) * 512:(qp * 2 + g + 1) * 512] for g in range(2)] for qp in range(2)]
        VAp = WAR[:, 2048:2048 + NVT * 256].rearrange("p (t g d) -> p t g d", t=NVT, g=2)
        P.add("pool", lambda e: e.memset(WAR[:, 0:2048 + NVT * 256], 0.0), [], ["qbz", "vap"])
        cp("pool", VAp[:, :, :, 0:65], VA_all[:, :, :, :], ["va", "vap"], ["vap"])
        RS = WAR[:, 2048 + NVT * 256:2048 + NVT * 256 + 1024].bitcast(F32)
        items = []
        qcount = 0
        for s in seqs:
            ktiles = []
            if s.is_sample:
                ktiles += list(range(NCTX // 128))
            ktiles += [(s.key0 + s.nctx) // 128 + i for i in range(s.T // 128)]
            for qi in range(s.T // 128):
                for n_, kt in enumerate(ktiles):
                    items.append((s, qi, n_, kt, n_ == 0, n_ == len(ktiles) - 1, qcount % 2))
                qcount += 1
        LAG = 1

        def genAttn():
            for idx in range(len(items) + LAG):
                if idx < len(items):
                    s, qi, n_, kt, first, last, qp = items[idx]
                    q0 = qi * 128
                    if first:
                        for g2 in range(2):
                            dma("sp", QBz[qp][g2][64 * g2:64 * g2 + 64, :].rearrange("p (j t) -> p j t", j=4),
                                QA[s.name][4 * g2:4 * g2 + 4, :, q0:q0 + 128].rearrange("j d t -> d j t"),
                                [("qa", s.name), "qbz"], [("qb", qp)])
                    r_ = idx % 2
                    for g in range(2):
                        mm(PS[r_][:, g * 512:(g + 1) * 512], KT_all[:, kt * 128:(kt + 1) * 128],
                           QBz[qp][g], True, True, ["kt", ("qb", qp)],
                           [pk(2 * r_), pk(2 * r_ + 1)])
                    act(XBW[r_][:, :], PS[r_][:, :], AF.Exp, [pk(2 * r_), pk(2 * r_ + 1)], [("ptt", r_)], scale=0.125)
                if idx >= LAG:
                    s, qi, n_, kt, first, last, qp = items[idx - LAG]
                    q0 = qi * 128
                    r_ = (idx - LAG) % 2
                    for g in range(2):
                        ob = 4 + g
                        mm(psb(ob)[:, :], VAp[:, kt, g, :], XBW[r_][:, g * 512:(g + 1) * 512], first, last,
                           ["vap", ("ptt", r_)], [pk(ob)])
                    if last:
                        for g in range(2):
                            ob = 4 + g
                            cp("dve", RS[64:65, :], psb(ob)[64:65, :], [pk(ob)], ["rs"])
                            act(RS[64:65, :], RS[64:65, :], AF.Ln, ["rs"], ["rs"])
                            act(RS[64:65, :], RS[64:65, :], AF.Exp, ["rs"], ["rs"], scale=-1.0)
                            mm(psb(6)[0:64, :], C("ones")[64:65, 0:64], RS[64:65, :], True, True, ["rs", "cst"], [pk(6)])
                            cp("dve", STG[0][0:64, :], psb(ob)[0:64, :], [pk(ob)], ["stg0"])
                            tt("dve", OB[:, :], STG[0][0:64, :], psb(6)[0:64, :], ALU.mult, ["stg0", pk(6)], ["ob"])
                            dma("sp", OA[s.name][4 * g:4 * g + 4, :, q0:q0 + 128].rearrange("j d t -> d j t"),
                                OB[:, :].rearrange("p (j t) -> p j t", j=4), ["ob"], [("oa", s.name)])
                yield

        gB_, gAt_ = genB(), genAttn()
        doneB = doneA = False
        it_ = 0
        while not (doneB and doneA):
            if not doneB:
                try:
                    next(gB_)
                except StopIteration:
                    doneB = True
            for _ in range(3 + (1 if it_ % 4 == 3 else 0)):
                if not doneA:
                    try:
                        next(gAt_)
                    except StopIteration:
                        doneA = True
            it_ += 1
        P.fence()
        if stop_after == "attn":
            return finish()

        H8 = 8
        CUT = 0

        def v3(t):
            return t.rearrange("p (h j) -> p h j", h=H8)

        woff = [0]

        def wtake(n, f32=False):
            ap = WAR[:, woff[0]:woff[0] + n]
            woff[0] += n
            return ap.bitcast(F32) if f32 else ap

        DS = [dict(SM=SM, DG1=DG1, DG3=DG3, DE=DE, E1=E1, E2=E2, E3=E3, PA=PA, PT=PT, RT=RT, XB=XB, BEK=BEK, BV=BV), None]
        DS[1] = dict(E1=wtake(1024, True), E2=wtake(1024, True), E3=wtake(1024, True), DG1=wtake(1024, True),
                     DG3=wtake(1024, True), SM=wtake(320, True),
                     PA=[wtake(512), wtake(512)], PT=[wtake(512), wtake(512)], RT=[wtake(512), wtake(512)],
                     XB=[wtake(512) for _ in range(4)], BEK=wtake(512), BV=wtake(512), DE=wtake(512))
        def w64(n):
            ap = WAR[0:64, woff[0]:woff[0] + n]
            woff[0] += n
            return ap.rearrange("p (x i) -> p x i", x=16)

        NWT2 = [[NWT[d][:, :, :], w64(1024)] for d in range(2)]
        QDEC2 = [[QDEC[d][:, :, :], w64(1024)] for d in range(2)]
        U02 = [[U0[d], wtake(1024, True)] for d in range(2)]
        QKM2 = [[QKM[d], wtake(512)] for d in range(2)]
        KDEC2 = [[KDEC[d], wtake(512)] for d in range(2)]
        EGL22 = [[EGL2[d][:, :], wtake(32, True)] for d in range(2)]
        ab = [0]
        fbk = [0]
        ppr = [0]

        def abank():
            ab[0] = (ab[0] + 1) % 6
            return ab[0]

        def fbank():
            fbk[0] ^= 1
            return 6 + fbk[0]

        def ppair():
            ppr[0] = (ppr[0] + 1) % 3
            return ppr[0]

        idb = bcast(C("identst", True), 1, H8)
        ist = bcast(C("identst"), 1, H8)

        def msk(name):
            return bcast(C(name, True), 1, H8)

        def prep(s, d, m, par):
            NWTp, QDECp, U0p, QKMp, KDECp, EGL2p = NWT2[d][par], QDEC2[d][par], U02[d][par], QKM2[d][par], KDEC2[d][par], EGL22[d][par]
            kq = (d, par)
            T_ = DS[d]
            SMd, DG1d, DG3d, DEd = T_["SM"], T_["DG1"], T_["DG3"], T_["DE"]
            E1d, E2d, E3d = T_["E1"], T_["E2"], T_["E3"]
            PAd, PTd, RTd, XBd, BEKd, BVd = T_["PA"], T_["PT"], T_["RT"], T_["XB"], T_["BEK"], T_["BV"]

            def K(name, *x):
                return (name, d) + tuple(x)

            gti = s.tile0 + m
            sfx = "f" if d == 0 else "b"
            rows = slice(m * 128, (m + 1) * 128)
            dma("sp", KTC[d][:, :, :], KDT[s.name][:, :, rows].rearrange("h d t -> d h t"), [("qkdt", s.name)], [("ktc", d)])
            dma("sp", QTC[d][:, :, :], QDT[s.name][:, :, rows].rearrange("h d t -> d h t"), [("qkdt", s.name)], [("qtc", d)])
            dma("sp", KTMC[d][:, :], KTM[s.name][rows, :], [("tm", s.name)], [("ktmc", d)])
            dma("sp", QTMC[d][:, :], QTM[s.name][rows, :], [("tm", s.name)], [("qtmc", d)])
            dma("sp", VTMC[d][:, :], VTM[s.name][rows, :], [("tm", s.name)], [("vtmc", d)])
            la = LA[:, gti, 8 * d:8 * d + 8]
            lb = LB[:, gti, 8 * d:8 * d + 8]
            be_ = BETA[:, gti, 8 * d:8 * d + 8]
            bg = fbank()
            mm(psb(bg)[:, 0:8], C("tri_" + sfx), la, True, True, ["la", "cst"], [pk(bg)])
            mm(psb(bg)[:, 8:16], C("half0"), la, True, True, ["la", "cst"], [pk(bg)])
            mm(psb(bg)[:, 16:24], C("half1"), la, True, True, ["la", "cst"], [pk(bg)])
            cp("dve", SMd[:, 0:24], psb(bg)[:, 0:24], [pk(bg)], [K("sm")])
            yield
            tt("dve", SMd[:, 24:32], SMd[:, 0:8], lb, ALU.add, [K("sm"), "lb"], [K("sm_glb")])
            act(SMd[:, 32:40], SMd[:, 0:8], AF.Exp, [K("sm")], [K("sm_eg")])
            cp("dve", SMd[0:64, 40:48], SMd[0:64, 8:16], [K("sm")], [K("sm_glo")])
            cp("dve", SMd[64:128, 40:48], SMd[64:128, 16:24], [K("sm")], [K("sm_glo")])
            tt("dve", SMd[:, 48:56], SMd[:, 40:48], SMd[:, 0:8], ALU.subtract, [K("sm"), K("sm_glo")], [K("sm_ek")])
            act(SMd[:, 48:56], SMd[:, 48:56], AF.Exp, [K("sm_ek")], [K("sm_ek")])
            act(EGL2p, SMd[:, 8:24], AF.Exp, [K("sm")], [("egl2",) + kq])
            tt("dve", SMd[:, 56:64], be_, SMd[:, 32:40], ALU.mult, ["beta", K("sm_eg")], [K("sm_be")])
            tsc("dve", SMd[:, 64:72], SMd[:, 0:8], -1.0, None, ALU.mult, None, [K("sm")], [K("sm_ng")])
            tt("pool", v3(DG1d), ist, bcast(SMd[:, 24:32], 2, 64), ALU.mult, ["cst", K("sm_glb")], [K("dg1")])
            tt("pool", v3(DG3d), ist, bcast(SMd[:, 0:8], 2, 64), ALU.mult, ["cst", K("sm")], [K("dg3")])
            tt("dve", v3(DEd), ist, bcast(SMd[:, 32:40], 2, 64), ALU.mult, ["cst", K("sm_eg")], [K("de")])
            yield
            b1 = fbank()
            mm(psb(b1)[:, :], C("negblk"), DG3d, True, False, [K("dg3"), "cst"], [pk(b1)])
            mm(psb(b1)[:, :], C("ident"), bcast(SMd[:, 24:32], 2, 64), False, False, [K("sm_glb"), "cst"], [pk(b1)])
            mm(psb(b1)[:, :], C("ident", True), msk("m1_" + sfx), False, True, ["cstb"], [pk(b1)])
            act(E1d, psb(b1)[:, :], AF.Exp, [pk(b1)], [K("e1")])
            yield
            b2 = fbank()
            mm(psb(b2)[:, :], C("blk"), DG1d, True, False, [K("dg1"), "cst"], [pk(b2)])
            mm(psb(b2)[:, :], C("ident"), bcast(SMd[:, 64:72], 2, 64), False, False, [K("sm_ng"), "cst"], [pk(b2)])
            mm(psb(b2)[:, :], C("ident", True), msk("m2_" + sfx), False, True, ["cstb"], [pk(b2)])
            act(E2d, psb(b2)[:, :], AF.Exp, [pk(b2)], [K("e2")])
            yield
            b3 = fbank()
            mm(psb(b3)[:, :], C("blk"), DG3d, True, False, [K("dg3"), "cst"], [pk(b3)])
            mm(psb(b3)[:, :], C("ident"), bcast(SMd[:, 64:72], 2, 64), False, False, [K("sm_ng"), "cst"], [pk(b3)])
            mm(psb(b3)[:, :], C("ident", True), msk("m3_" + sfx), False, True, ["cstb"], [pk(b3)])
            act(E3d, psb(b3)[:, :], AF.Exp, [pk(b3)], [K("e3")])
            yield
            bkk, bqk = abank(), abank()
            for h in range(H8):
                for a in range(2):
                    ts_ = slice(64 * a, 64 * a + 64)
                    mm(psb(bkk)[ts_, h * 64:(h + 1) * 64], KTC[d][:, h, ts_], KTC[d][:, h, ts_], True, True,
                       [("ktc", d)], [pk(bkk)], tile_position=(0, 64 * a))
                    mm(psb(bqk)[ts_, h * 64:(h + 1) * 64], KTC[d][:, h, ts_], QTC[d][:, h, ts_], True, True,
                       [("ktc", d), ("qtc", d)], [pk(bqk)], tile_position=(0, 64 * a))
            A_, AT_ = PAd[0], PTd[0]
            kA, kAT = K("pa", 0), K("pt", 0)
            tt("dve", A_, psb(bkk)[:, :], E1d, ALU.mult, [pk(bkk), K("e1")], [kA])
            tt("dve", AT_, psb(bkk)[:, :], E2d, ALU.mult, [pk(bkk), K("e2")], [kAT])
            tt("dve", QKMp, psb(bqk)[:, :], E3d, ALU.mult, [pk(bqk), K("e3")], [("qkm",) + kq])
            yield

            def grp(L, R, lkey, rkey):
                bank = abank()
                for h in range(H8):
                    for a in range(2):
                        ts_ = slice(64 * a, 64 * a + 64)
                        hs = slice(h * 64, (h + 1) * 64)
                        mm(psb(bank)[ts_, hs], L[ts_, hs], R[ts_, hs], True, True, [lkey, rkey], [pk(bank)],
                           tile_position=(64 * a, 64 * a))
                return bank

            D_, DT_ = PAd[1], PTd[1]
            kD, kDT = K("pa", 1), K("pt", 1)
            X = [XBd[0], XBd[1], BEKd, BVd, XBd[2], XBd[3]]
            kX = [K("xb", 0), K("xb", 1), K("bek"), K("bv"), K("xb", 2), K("xb", 3)]
            kR = [K("rt", 0), K("rt", 1)]
            tt("pool", v3(D_), v3(A_), msk("mask8"), ALU.mult, [kA, "cstb"], [kD])
            tt("pool", v3(DT_), v3(AT_), msk("mask8"), ALU.mult, [kAT, "cstb"], [kDT])
            tt("dve", v3(X[2]), idb, v3(DT_), ALU.subtract, ["cstb", kDT], [kX[2]])
            yield
            g1 = grp(DT_, D_, kDT, kD)
            g2 = grp(D_, DT_, kD, kDT)
            cp("act", X[0], psb(g1)[:, :], [pk(g1)], [kX[0]])
            tt("dve", v3(RTd[1]), v3(X[0]), idb, ALU.add, [kX[0], "cstb"], [kR[1]])
            cp("dve", RTd[0], psb(g2)[:, :], [pk(g2)], [kR[0]])
            yield
            g3 = grp(RTd[0], X[0], kR[0], kX[0])
            tt("dve", v3(X[1]), v3(psb(g3)[:, :]), idb, ALU.add, [pk(g3), "cstb"], [kX[1]])
            g1 = grp(RTd[1], X[2], kR[1], kX[2])
            cp("act", X[3], psb(g1)[:, :], [pk(g1)], [kX[3]])
            yield
            g2 = grp(X[3], X[1], kX[3], kX[1])
            g3 = grp(X[1], X[3], kX[1], kX[3])
            cp("act", X[4], psb(g2)[:, :], [pk(g2)], [kX[4]])
            cp("dve", X[5], psb(g3)[:, :], [pk(g3)], [kX[5]])
            yield
            Tb, kT = [X[4], X[0]], [kX[4], kX[0]]
            Mb, kM = [X[5], X[1]], [kX[5], kX[1]]
            cur = 0
            for li, mname in enumerate(("moff8", "moff16", "moff32")):
                last = (li == 2)
                nxt = 1 - cur
                tt("pool", v3(D_), v3(A_), msk(mname), ALU.mult, [kA, "cstb"], [kD])
                if not last:
                    tt("pool", v3(DT_), v3(AT_), msk(mname), ALU.mult, [kAT, "cstb"], [kDT])
                g1 = grp(D_, Mb[cur], kD, kM[cur])
                cp("act", RTd[1], psb(g1)[:, :], [pk(g1)], [kR[1]])
                if not last:
                    g2 = grp(DT_, Tb[cur], kDT, kT[cur])
                    cp("dve", RTd[0], psb(g2)[:, :], [pk(g2)], [kR[0]])
                yield
                g3 = grp(Tb[cur], RTd[1], kT[cur], kR[1])
                tt("dve", Mb[nxt], Mb[cur], psb(g3)[:, :], ALU.subtract, [kM[cur], pk(g3)], [kM[nxt]])
                if not last:
                    g1 = grp(Mb[cur], RTd[0], kM[cur], kR[0])
                    tt("dve", Tb[nxt], Tb[cur], psb(g1)[:, :], ALU.subtract, [kT[cur], pk(g1)], [kT[nxt]])
                cur = nxt
                yield
            TTm = Mb[cur]
            tkey = kM[cur]
            tt("dve", v3(BEKd), v3(KTMC[d][:, :]), bcast(SMd[:, 56:64], 2, 64), ALU.mult, [("ktmc", d), K("sm_be")], [K("bek")])
            tt("pool", v3(KDECp), v3(KTMC[d][:, :]), bcast(SMd[:, 48:56], 2, 64), ALU.mult, [("ktmc", d), K("sm_ek")], [("kdec",) + kq])
            tt("pool", v3(BVd), v3(VTMC[d][:, :]), bcast(be_, 2, 64), ALU.mult, [("vtmc", d), "beta"], [K("bv")])
            yield
            pp_ = ppair()
            pkeys = [pk(2 * pp_), pk(2 * pp_ + 1)]
            bu0 = abank()
            while bu0 in (2 * pp_, 2 * pp_ + 1):
                bu0 = abank()
            for h in range(H8):
                for a in range(2):
                    ts_ = slice(64 * a, 64 * a + 64)
                    hs = slice(h * 64, (h + 1) * 64)
                    cs = slice((a * 8 + h) * 64, (a * 8 + h) * 64 + 64)
                    mm(PS[pp_][0:64, cs], BEKd[ts_, hs], TTm[ts_, hs], True, True, [K("bek"), tkey], pkeys,
                       tile_position=(64 * a, 0))
                    mm(psb(bu0)[ts_, hs], TTm[ts_, hs], BVd[ts_, hs], True, True, [tkey, K("bv")], [pk(bu0)],
                       tile_position=(64 * a, 64 * a))
            tsc("dve", NWTp, PS[pp_][0:64, :].rearrange("p (x i) -> p x i", x=16), -1.0, None, ALU.mult, None,
                pkeys, [("nwt",) + kq])
            cp("act", U0p, psb(bu0)[:, :], [pk(bu0)], [("u0",) + kq])
            yield
            pp_ = ppair()
            pkeys = [pk(2 * pp_), pk(2 * pp_ + 1)]
            for a in range(2):
                mm(PS[pp_][0:64, a * 512:(a + 1) * 512], C("half%d" % a, True)[:, 0:64], DEd, True, True,
                   [K("de"), "cstb"], pkeys)
            for a in range(2):
                tt("dve", QDECp[:, a * 8:(a + 1) * 8, :],
                   QTC[d][:, :, a * 64:(a + 1) * 64],
                   PS[pp_][0:64, a * 512:(a + 1) * 512].rearrange("p (h i) -> p h i", h=8), ALU.mult,
                   [("qtc", d)] + pkeys, [("qdec",) + kq])
            yield

        def steps(s, d, m, par, first_visit):
            NWTp, QDECp, U0p, QKMp, KDECp, EGL2p = NWT2[d][par], QDEC2[d][par], U02[d][par], QKM2[d][par], KDEC2[d][par], EGL22[d][par]
            kq = (d, par)
            SMd = DS[d]["SM"]

            def K(name, *x):
                return (name, d) + tuple(x)

            rows = slice(m * 128, (m + 1) * 128)
            for a in ((0, 1) if d == 0 else (1, 0)):
                ts_ = slice(64 * a, 64 * a + 64)
                pu = abank()
                for h in range(H8):
                    hs = slice(h * 64, (h + 1) * 64)
                    mm(psb(pu)[ts_, hs], NWTp[:, a * 8 + h, :], SBF[d][:, h, :], True, True,
                       [("nwt",) + kq, ("sbf", d)], [pk(pu)], tile_position=(0, 64 * a))
                tt("dve", UB[d][ts_, :], U0p[ts_, :], psb(pu)[ts_, :], ALU.add, [("u0",) + kq, pk(pu)], [("ub", d)])
                yield
                po, pob, pS_ = abank(), abank(), abank()
                for h in range(H8):
                    hs = slice(h * 64, (h + 1) * 64)
                    mm(psb(po)[ts_, hs], QDECp[:, a * 8 + h, :], SBF[d][:, h, :], True, True,
                       [("qdec",) + kq, ("sbf", d)], [pk(po)], tile_position=(0, 64 * a))
                    mm(psb(pob)[ts_, hs], QKMp[ts_, hs], UB[d][ts_, hs], True, True,
                       [("qkm",) + kq, ("ub", d)], [pk(pob)], tile_position=(64 * a, 64 * a))
                    mm(psb(pS_)[0:64, hs], KDECp[ts_, hs], UB[d][ts_, hs], True, True,
                       [("kdec",) + kq, ("ub", d)], [pk(pS_)], tile_position=(64 * a, 0))
                tt("dve", S32[d][:, :, :], S32[d][:, :, :], bcast(EGL2p[0:64, a * 8:a * 8 + 8], 2, 64), ALU.mult,
                   [("s32", d), ("egl2",) + kq], [("s32", d)])
                tt("dve", S32[d][:, :, :], S32[d][:, :, :], psb(pS_)[0:64, :].rearrange("p (h v) -> p h v", h=H8), ALU.add,
                   [("s32", d), pk(pS_)], [("s32", d)])
                cp("act", SBF[d][:, :, :], S32[d][:, :, :], [("s32", d)], [("sbf", d)])
                cp("act", OACC[d][ts_, :], psb(po)[ts_, :], [pk(po)], [("oacc", d)])
                tt("dve", OACC[d][ts_, :], OACC[d][ts_, :], psb(pob)[ts_, :], ALU.add, [("oacc", d), pk(pob)], [("oacc", d)])
                yield
            if first_visit:
                dma("sp", OPART[s.name][rows, :], OACC[d][:, :], [("oacc", d)], [("opart", s.name, m)])
            else:
                dma("sp", STG[d][:, :], OPART[s.name][rows, :], [("opart", s.name, m)], [f"stg{d}"])
                tt("dve", OACC[d][:, :], OACC[d][:, :], STG[d][:, :], ALU.add, [("oacc", d), f"stg{d}"], [("oacc", d)])
                tt("pool", STG[2 + d][:, :], OACC[d][:, :], OACC[d][:, :], ALU.mult, [("oacc", d)], [f"stg{2 + d}"])
                P.add("dve", lambda e: e.tensor_reduce(SMd[:, 80:88], v3(STG[2 + d][:, :]), AX.X, ALU.add),
                      [f"stg{2 + d}"], [K("sm_rn")])
                act(SMd[:, 80:88], SMd[:, 80:88], AF.Sqrt, [K("sm_rn")], [K("sm_rn")], bias=EPS, scale=1.0 / 64)
                recip(SMd[:, 80:88], SMd[:, 80:88], [K("sm_rn")], [K("sm_rn")])
                yield
                tt("dve", v3(OACC[d][:, :]), v3(OACC[d][:, :]), bcast(SMd[:, 80:88], 2, 64), ALU.mult,
                   [("oacc", d), K("sm_rn")], [("oacc", d)])
                tt("dve", v3(OACC[d][:, :]), v3(OACC[d][:, :]), bcast(dng[:], 1, H8), ALU.mult,
                   [("oacc", d), "lvec"], [("oacc", d)])
                dma("sp", STB[d][:, :], GS[s.name][rows, :], [("gs", s.name)], [f"stb{d}"])
                tt("dve", OACC[d][:, :], OACC[d][:, :], STB[d][:, :], ALU.mult, [("oacc", d), f"stb{d}"], [("oacc", d)])
                bt = fbank()
                for cc in range(4):
                    tr(psb(bt)[:, cc * 128:(cc + 1) * 128], OACC[d][:, cc * 128:(cc + 1) * 128], C("ident"),
                       [("oacc", d), "cst"], [pk(bt)])
                cp("act", STB[2 + d][:, :], psb(bt)[:, :], [pk(bt)], [f"stb{2 + d}"])
                dma("sp", OD[s.name][:, :, rows], STB[2 + d][:, :].rearrange("p (c t) -> p c t", c=4), [f"stb{2 + d}"],
                    [("od", s.name)])

        for s in seqs:
            NP_ = s.T // 128
            for d in range(2):
                if s.is_sample:
                    dma("sp", S32[d][:, :, :], st_in[l, d].rearrange("h k v -> k h v"), [], [("s32", d)])
                else:
                    P.add("pool", lambda e, d=d: e.memset(S32[d][:, :, :], 0.0), [], [("s32", d)])
                cp("act", SBF[d][:, :, :], S32[d][:, :, :], [("s32", d)], [("sbf", d)])
            visited = set()

            def mof(d, step):
                return step if d == 0 else NP_ - 1 - step

            def rr(gens):
                active = list(gens)
                while active:
                    for g_ in list(active):
                        try:
                            next(g_)
                        except StopIteration:
                            active.remove(g_)

            rr([prep(s, d, mof(d, 0), 0) for d in range(2)])
            for step in range(NP_):
                gens = []
                for d in range(2):
                    m = mof(d, step)
                    gens.append(steps(s, d, m, step % 2, m not in visited))
                for d in range(2):
                    visited.add(mof(d, step))
                if step + 1 < NP_:
                    for d in range(2):
                        gens.append(prep(s, d, mof(d, step + 1), (step + 1) % 2))
                rr(gens)
            if not s.is_sample:
                for d in range(2):
                    dma("sp", nst_out[s.idx, l, d].rearrange("h k v -> k h v"), S32[d][:, :, :], [("s32", d)], [("nst", s.idx)])
        P.fence()
        if stop_after == "scan":
            return finish()

        Wpa = WAR[:, 0:4096].rearrange("p (k n) -> p k n", k=4)
        Wpd = WAR[:, 4096:8192].rearrange("p (k n) -> p k n", k=4)
        Wo = WAR[:, 8192:16384].rearrange("p (k n) -> p k n", k=8)
        load_w(Wpa, w_pa[l], 4)
        load_w(Wpd, w_pd[l], 4)
        load_w(Wo, w_out[l], 8)
        def w3(off, n, f32, shape3):
            ap = WAR[:, off:off + n]
            if f32:
                ap = ap.bitcast(F32)
            return ap.rearrange("p (c t) -> p c t", c=shape3)

        DSET = [
            dict(HB=HB[:, :, :], GATB=GATB[:, :, :], XT=XT, X1T=X1T, S0=STG[0][:, :], S1=STG[1][:, :], R1=R1[:, :], RSTD=RSTD[:, :],
                 bpa=0, bpd=1, bop=(2, 3), bst=0, k="0"),
            dict(HB=w3(16384, 4096, False, 16), GATB=w3(20480, 4096, False, 16), XT=w3(24576, 4096, True, 8),
                 X1T=w3(28672, 4096, True, 8), S0=WAR[:, 32768:33792].bitcast(F32), S1=WAR[:, 33792:34816].bitcast(F32),
                 R1=WAR[:, 34816:35328].bitcast(F32), RSTD=WAR[:, 35328:35840].bitcast(F32),
                 bpa=4, bpd=5, bop=(6, 7), bst=4, k="1"),
        ]
        jobsD = [(s, blk) for s in seqs for blk in range(s.T // TB)]

        def blockD(j, slot):
            s, blk = jobsD[j]
            mj = s.mj
            t0 = blk * TB
            B_ = DSET[slot]
            kk_ = B_["k"]
            HBd, GATd, XTd, X1Td, S0, S1, R1d, RSTDd = B_["HB"], B_["GATB"], B_["XT"], B_["X1T"], B_["S0"], B_["S1"], B_["R1"], B_["RSTD"]
            OATd, ODTd, MGd = HBd[:, 0:4, :], HBd[:, 4:8, :], HBd[:, 8:16, :]
            khb, kht, kg, kx, kx1, ks0, ks1 = "dhb" + kk_, "dht" + kk_, "dg" + kk_, "dx" + kk_, "dx1" + kk_, "ds0" + kk_, "ds1" + kk_
            for c in range(4):
                dma("sp", OATd[:, c, :], OA[s.name][2 * c:2 * c + 2].rearrange("h d t -> (h d) t")[:, t0:t0 + TB],
                    [("oa", s.name)], [khb])
            dma("sp", ODTd, OD[s.name][:, :, t0:t0 + TB], [("od", s.name)], [khb])
            dma("sp", GATd, GATES[s.name][:, :, t0:t0 + TB], [("gates", s.name, blk)], [kg])
            dma("sp", XTd, XRES[s.name][:, :, t0:t0 + TB], [("xres", s.name, blk)], [kx])
            yield
            for n in range(8):
                ns = slice(n * 128, (n + 1) * 128)
                for k in range(4):
                    mm(psb(B_["bpa"])[:, :TB], Wpa[:, k, ns], OATd[:, k, :], k == 0, k == 3, ["war", khb], [pk(B_["bpa"])])
                for k in range(4):
                    mm(psb(B_["bpd"])[:, :TB], Wpd[:, k, ns], ODTd[:, k, :], k == 0, k == 3, ["war", khb], [pk(B_["bpd"])])
                tt("dve", S0[:, :TB], psb(B_["bpa"])[:, :TB], GATd[:, n, :], ALU.mult, [pk(B_["bpa"]), kg], [ks0])
                tt("dve", S1[:, :TB], psb(B_["bpd"])[:, :TB], GATd[:, 8 + n, :], ALU.mult, [pk(B_["bpd"]), kg], [ks1])
                tt("pool", MGd[:, n, :], S0[:, :TB], S1[:, :TB], ALU.add, [ks0, ks1], [kht])
                yield
            for n in range(8):
                ns = slice(n * 128, (n + 1) * 128)
                bk = B_["bop"][n % 2]
                for k in range(8):
                    mm(psb(bk)[:, :TB], Wo[:, k, ns], MGd[:, k, :], k == 0, k == 7, ["war", kht], [pk(bk)])
                stt("dve", X1Td[:, n, :], psb(bk)[:, :TB], MOD[:, 16 + n, mj:mj + 1], XTd[:, n, :], ALU.mult, ALU.add,
                    [pk(bk), kx] + MK, [kx1])
                yield
            dma("sp", XRES[s.name][:, :, t0:t0 + TB], X1Td, [kx1], [("xres", s.name, blk)])
            rms_stats(X1Td, TB, kx1, sq=HBd[:, 0:8, :], sqkey=khb, bank=B_["bst"], r1=R1d, rstd=RSTDd, rkey=kk_)
            yield
            tt("dve", XTd, X1Td, bcast(RSTDd, 1, 8), ALU.mult, [kx1, "rstd" + kk_], [kx])
            for c in range(8):
                if c % 2 == 0:
                    act(MGd[:, c, :], XTd[:, c, :], AF.Identity, [kx] + MK, [kht],
                        scale=A2[:, c, mj:mj + 1], bias=MOD[:, 24 + c, mj:mj + 1])
                else:
                    tsc("dve", MGd[:, c, :], XTd[:, c, :], A2[:, c, mj:mj + 1], MOD[:, 24 + c, mj:mj + 1], ALU.mult, ALU.add,
                        [kx] + MK, [kht])
            dma("sp", H2[s.name][:, :, t0:t0 + TB], MGd, [kht], [("h2", s.name, blk)])

        def two_way(make, n, stagger):
            gens = [None, None]
            nxt = 0
            started = 0
            while True:
                progressed = False
                for slot in range(2):
                    if gens[slot] is None and nxt < n and (slot == 0 or started >= stagger or nxt > 1):
                        gens[slot] = make(nxt, slot)
                        nxt += 1
                    if gens[slot] is not None:
                        try:
                            next(gens[slot])
                            progressed = True
                            if slot == 0:
                                started += 1
                        except StopIteration:
                            gens[slot] = None
                            progressed = True
                if not progressed and nxt >= n and gens[0] is None and gens[1] is None:
                    break

        two_way(blockD, len(jobsD), 9)
        P.fence()
        if stop_after == "stageD":
            return finish()

        W1h = WAR[:, 0:8 * 2048].rearrange("p (k n) -> p k n", k=8)
        W2h = WAR[:, 16384:16384 + 16 * 1024].rearrange("p (k n) -> p k n", k=16)
        for hf in range(2):
            load_w(W1h, w1[l][:, hf * 2048:(hf + 1) * 2048], 8)
            load_w(W2h, w2[l][hf * 2048:(hf + 1) * 2048, :], 16)
            jobs = [(s, blk) for s in seqs for blk in range(s.T // TB)]
            XTs = [(XT, "bigA"), (X1T, "bigB")]
            H2Ts = [(HB[:, 0:8, :], "hb"), (HB[:, 8:16, :], "ht")]

            def ef_loads(j):
                s, blk = jobs[j]
                t0 = blk * TB
                h2t, hkey = H2Ts[j % 2]
                xt, xkey = XTs[j % 2]
                dma("sp", h2t, H2[s.name][:, :, t0:t0 + TB], [("h2", s.name, blk)], [hkey])
                dma("sp", xt, XRES[s.name][:, :, t0:t0 + TB], [("xres", s.name, blk)], [xkey])

            ef_loads(0)
            for j, (s, blk) in enumerate(jobs):
                mj = s.mj
                t0 = blk * TB
                H2T, hkey = H2Ts[j % 2]
                XTj, xkey = XTs[j % 2]
                if j + 1 < len(jobs):
                    ef_loads(j + 1)
                for f in range(16):
                    bk = 1 + f % 3
                    for k in range(8):
                        mm(psb(bk)[:, :TB], W1h[:, k, f * 128:(f + 1) * 128], H2T[:, k, :], k == 0, k == 7, ["war", hkey], [pk(bk)])
                    i = nstg()
                    act(STG[i][:, :TB], psb(bk)[:, :TB], AF.Relu, [pk(bk), "lvec"], [f"stg{i}"],
                        bias=b1f[:, hf * 16 + f:hf * 16 + f + 1])
                    tt("dve" if f % 2 == 0 else "pool", GATB[:, f, :], STG[i][:, :TB], STG[i][:, :TB], ALU.mult,
                       [f"stg{i}"], ["gatb"])
                for n in range(8):
                    bk = 4 + n % 2
                    for f in range(16):
                        mm(psb(bk)[:, :TB], W2h[:, f, n * 128:(n + 1) * 128], GATB[:, f, :], f == 0, f == 15, ["war", "gatb"], [pk(bk)])
                    stt("dve", XTj[:, n, :], psb(bk)[:, :TB], MOD[:, 40 + n, mj:mj + 1], XTj[:, n, :], ALU.mult, ALU.add,
                        [pk(bk), xkey] + MK, [xkey])
                    if hf == 0:
                        tsc("dve", XTj[:, n, :], XTj[:, n, :], GB2[:, n, mj:mj + 1], None, ALU.add, None, [xkey] + MK, [xkey])
                if not (l == NLAYERS - 1 and hf == 1):
                    dma("pool", XRES[s.name][:, :, t0:t0 + TB], XTj, [xkey], [("xres", s.name, blk)])
                else:
                    rms_stats(XTj, TB, xkey, sq=GATB[:, 0:8, :], sqkey="gatb")
                    tt("dve", XTj, XTj, bcast(RSTD[:, :], 1, 8), ALU.mult, [xkey, "rstd"], [xkey])
                    tt("dve", XTj, XTj, bcast(fnf[:], 2, TB), ALU.mult, [xkey, "fnf"], [xkey])
                    dst = ys_out if s.is_sample else yp_out[s.idx * TP:(s.idx + 1) * TP, :]
                    for t2 in range(TB // 128):
                        for hh in range(2):
                            bk = 6 + hh
                            for c in range(4):
                                tr(psb(bk)[:, c * 128:(c + 1) * 128], XTj[:, hh * 4 + c, t2 * 128:(t2 + 1) * 128], C("ident"),
                                   [xkey, "cst"], [pk(bk)])
                            i = nstg()
                            cp("act" if hh == 0 else "dve", STG[i][:, :], psb(bk)[:, :], [pk(bk)], [f"stg{i}"])
                            dma("pool", dst[t0 + t2 * 128:t0 + (t2 + 1) * 128, hh * 512:(hh + 1) * 512], STG[i][:, :],
                                [f"stg{i}"], [("y", s.name)])
        P.fence()
        if stop_after == f"layer{l}":
            return finish()

    return finish()


def make_in_maps(inputs):
    cos, sin = _rope_tables()
    maps = []
    for core in range(8):
        b = core % 4
        m = {
            "xs": np.ascontiguousarray(inputs["x_sample"][b]),
            "xp": np.ascontiguousarray(inputs["x_prompt"][core * NPS:(core + 1) * NPS].reshape(NPS * TP, D)),
            "cvec": np.ascontiguousarray(np.stack([inputs["c"][b], inputs["c_ctx"]], 0)),
            "ck": np.ascontiguousarray(inputs["cache_k"][b]),
            "cv": np.ascontiguousarray(inputs["cache_v"][b]),
            "st": np.ascontiguousarray(inputs["state_delta"][b]),
            "a_log": np.ascontiguousarray(inputs["a_log"].reshape(DEPTH, 16)),
            "dt_bias": np.ascontiguousarray(inputs["dt_bias"].reshape(DEPTH, 16)),
            "cst": CST, "ropecos": cos, "ropesin": sin,
        }
        for k in ["w_mod", "b_mod", "norm1", "norm2", "w_in", "conv_w", "q_gain", "k_gain", "dn_gain",
                  "w_pa", "w_pd", "w_out", "w1", "b1", "w2", "b2", "final_norm"]:
            m[k] = np.ascontiguousarray(inputs[k])
        maps.append(m)
    return maps


_NC_CACHE = {}


def kernel(**inputs):
    inputs = {k: np.asarray(v) for k, v in inputs.items()}
    if "nc" not in _NC_CACHE:
        _NC_CACHE["nc"] = build_program()
    nc = _NC_CACHE["nc"]
    maps = make_in_maps(inputs)
    res = run_bass_kernel_spmd(nc, maps, core_ids=list(range(8)))
    r = res.results
    y_sample = np.stack([r[b]["ys"] for b in range(4)], 0)
    y_prompt = np.concatenate([r[c]["yp"].reshape(NPS, TP, D) for c in range(8)], 0)
    nk = np.concatenate([r[c]["nk"] for c in range(8)], 0)
    nv = np.concatenate([r[c]["nv"] for c in range(8)], 0)
    nst = np.concatenate([r[c]["nst"] for c in range(8)], 0)
    return (y_prompt.astype(np.float32), y_sample.astype(np.float32), nk.astype(np.float32),
            nv.astype(np.float32), nst.astype(np.float32))
```

```python
import numpy as np
from contextlib import ExitStack
import concourse.bass as bass
import concourse.mybir as mybir
from concourse.bass_utils import run_bass_kernel_spmd

F32 = mybir.dt.float32
BF16 = mybir.dt.bfloat16
AF = mybir.ActivationFunctionType
ALU = mybir.AluOpType
AX = mybir.AxisListType

D = 1024
DEPTH = 2
TS = 4096
TP = 256
NPS = 4
NCTX = 256
INW = 4896
DFF = 4096
EPS = 1e-6
NEG = -30000.0

C_QA, C_KA, C_VA, C_QD, C_KD, C_VD, C_GO, C_AI, C_BI, C_GA, C_GD = 0, 512, 640, 768, 1280, 1792, 2304, 2816, 2832, 2848, 3872


ENGS = ["pe", "act", "dve", "pool", "sp"]
N_DMA_SEMS = {"sp": 40, "pool": 16, "act": 8}
EPOCH = 30000


class Ev:
    __slots__ = ("dma", "eng", "idx", "sem", "val")

    def __init__(self, dma, eng, idx, sem=None, val=None):
        self.dma, self.eng, self.idx, self.sem, self.val = dma, eng, idx, sem, val


class Rec:
    __slots__ = ("eng", "fn", "waits", "signal", "idx", "dma", "sig_sem", "sig_val")

    def __init__(self, eng, fn):
        self.eng, self.fn = eng, fn
        self.waits = []
        self.signal = False
        self.dma = None


class Buf:
    __slots__ = ("w", "r")

    def __init__(self):
        self.w = None
        self.r = []


class Prog:
    def __init__(self, nc):
        self.nc = nc
        self.ops = {e: [] for e in ENGS}
        self.waited = {e: {p: -1 for p in ENGS} for e in ENGS}
        self.waited_dma = {e: {} for e in ENGS}
        self.bufs = {}
        self.dma_count = {q: 0 for q in N_DMA_SEMS}

    def buf(self, k):
        b = self.bufs.get(k)
        if b is None:
            b = self.bufs[k] = Buf()
        return b

    def add(self, eng, fn, reads=(), writes=(), dma=False):
        rec = Rec(eng, fn)
        rec.idx = len(self.ops[eng])
        deps = []
        for k in reads:
            b = self.buf(k)
            if b.w is not None:
                deps.append((b.w, True))
        for k in writes:
            b = self.buf(k)
            if b.w is not None:
                deps.append((b.w, False))
            for r in b.r:
                deps.append((r, False))
        if dma:
            d = self.dma_count[eng]
            n = N_DMA_SEMS[eng]
            si, val = d % n, 16 * (d // n + 1)
            if d >= n:
                deps.append((Ev(True, eng, None, si, val - 16), True))
            rec.dma = (si, val)
            ev = Ev(True, eng, rec.idx, si, val)
            self.dma_count[eng] += 1
        else:
            ev = Ev(False, eng, rec.idx)
        for dep, raw in deps:
            if dep.dma:
                key = (dep.eng, dep.sem)
                if self.waited_dma[eng].get(key, 0) >= dep.val:
                    continue
                self.waited_dma[eng][key] = dep.val
                rec.waits.append(dep)
            else:
                if dep.eng == eng and eng == "pe":
                    continue
                if self.waited[eng][dep.eng] >= dep.idx:
                    continue
                self.waited[eng][dep.eng] = dep.idx
                rec.waits.append(dep)
                self.ops[dep.eng][dep.idx].signal = True
        for k in reads:
            self.buf(k).r.append(ev)
        for k in writes:
            b = self.buf(k)
            b.w = ev
            b.r = []
        self.ops[eng].append(rec)
        return rec

    def fence(self):
        last = {e: len(self.ops[e]) - 1 for e in ["pe", "act", "dve", "pool"]}
        dma_evs = []
        for q, n in N_DMA_SEMS.items():
            d = self.dma_count[q]
            for i in range(min(n, d)):
                uses = (d - i + n - 1) // n
                dma_evs.append(Ev(True, q, None, i, 16 * uses))
        for e in ENGS:
            rec = Rec(e, lambda eng: eng.nop())
            rec.idx = len(self.ops[e])
            for p, li in last.items():
                if p == e or li < 0:
                    continue
                j = li
                while j >= 0 and self.ops[p][j].dma is not None:
                    j -= 1
                if j < 0 or self.waited[e][p] >= j:
                    continue
                self.waited[e][p] = j
                rec.waits.append(Ev(False, p, j))
                self.ops[p][j].signal = True
            for dep in dma_evs:
                key = (dep.eng, dep.sem)
                if self.waited_dma[e].get(key, 0) >= dep.val:
                    continue
                self.waited_dma[e][key] = dep.val
                rec.waits.append(dep)
            self.ops[e].append(rec)

    def emit(self, es):
        nc = self.nc
        comp_sems = {}
        for e in ["pe", "act", "dve", "pool"]:
            cnt = 0
            for rec in self.ops[e]:
                if rec.signal and rec.dma is None:
                    ep = cnt // EPOCH
                    if (e, ep) not in comp_sems:
                        comp_sems[(e, ep)] = es.enter_context(nc.semaphore(f"c_{e}_{ep}"))
                    rec.sig_sem = comp_sems[(e, ep)]
                    rec.sig_val = cnt % EPOCH + 1
                    cnt += 1
        dma_sems = {}
        for q, n in N_DMA_SEMS.items():
            for i in range(min(n, self.dma_count[q])):
                dma_sems[(q, i)] = es.enter_context(nc.semaphore(f"d_{q}_{i}"))
        final_waits = []
        for q, n in N_DMA_SEMS.items():
            d = self.dma_count[q]
            for i in range(min(n, d)):
                uses = (d - i + n - 1) // n
                final_waits.append((dma_sems[(q, i)], 16 * uses))
        block = es.enter_context(nc.Block())
        ops = self.ops

        def run(engname, eng):
            for rec in ops[engname]:
                for dep in rec.waits:
                    if dep.dma:
                        eng.wait_ge(dma_sems[(dep.eng, dep.sem)], dep.val)
                    else:
                        prod = ops[dep.eng][dep.idx]
                        eng.wait_ge(prod.sig_sem, prod.sig_val)
                ins = rec.fn(eng)
                if rec.dma is not None:
                    ins.then_inc(dma_sems[(engname, rec.dma[0])], 16)
                elif rec.signal:
                    ins.then_inc(rec.sig_sem, 1)
            if engname == "sp":
                for s, v in final_waits:
                    eng.wait_ge(s, v)

        @block.tensor
        def _(t):
            run("pe", t)

        @block.scalar
        def _(a):
            run("act", a)

        @block.vector
        def _(v):
            run("dve", v)

        @block.gpsimd
        def _(g):
            run("pool", g)

        @block.sync
        def _(s):
            run("sp", s)


CST_LAYOUT = {}


def _build_consts():
    p = np.arange(128)
    cols = []
    off = 0

    def put(name, arr):
        nonlocal off
        arr = np.asarray(arr, np.float32).reshape(128, -1)
        CST_LAYOUT[name] = (off, arr.shape[1])
        cols.append(arr)
        off += arr.shape[1]

    put("ident", np.eye(128))
    put("ones", np.ones((128, 128)))
    put("negones", -np.ones((128, 128)))
    half = p // 64
    put("blk", (half[:, None] == half[None, :]).astype(np.float32))
    put("negblk", -(half[:, None] == half[None, :]).astype(np.float32))
    put("identst", (p[:, None] % 64 == np.arange(64)[None, :]).astype(np.float32))
    same = half[:, None] == half[None, :]
    put("tri_f", (same & (p[:, None] <= p[None, :])).astype(np.float32))
    put("tri_b", (same & (p[:, None] >= p[None, :])).astype(np.float32))
    put("half0", np.repeat((p < 64).astype(np.float32)[:, None], 128, 1))
    put("half1", np.repeat((p >= 64).astype(np.float32)[:, None], 128, 1))
    il = p % 64
    j = np.arange(64)

    def m(keep):
        return np.where(keep, 0.0, NEG).astype(np.float32)

    put("m1_f", m(il[:, None] > j[None, :]))
    put("m1_b", m(il[:, None] < j[None, :]))
    put("m2_f", m(j[None, :] > il[:, None]))
    put("m2_b", m(j[None, :] < il[:, None]))
    put("m3_f", m(j[None, :] >= il[:, None]))
    put("m3_b", m(j[None, :] <= il[:, None]))
    put("mask8", ((il[:, None] // 8) == (j[None, :] // 8)).astype(np.float32))
    for sz in (8, 16, 32):
        put("moff%d" % sz, (((il[:, None] // (2 * sz)) == (j[None, :] // (2 * sz)))
                            & ((il[:, None] // sz) != (j[None, :] // sz))).astype(np.float32))
    R = np.zeros((128, 128), np.float32)
    for q in range(128):
        if q % 64 < 32:
            R[q, q + 32] = -1.0
        else:
            R[q, q - 32] = 1.0
    put("rot", R.T)
    return np.concatenate(cols, 1)


CST = _build_consts()
NCST = CST.shape[1]


def _rope_tables():
    t = np.arange(TS)
    row = (t // 64).astype(np.float32)
    col = (t % 64).astype(np.float32)
    inv = (10000.0 ** (-np.arange(16, dtype=np.float32) / 16)).astype(np.float32)
    ang = np.concatenate([row[:, None] * inv, col[:, None] * inv], -1).astype(np.float32)
    cos = np.cos(ang).astype(np.float32).T
    sin = np.sin(ang).astype(np.float32).T
    return np.tile(cos, (4, 1)).copy(), np.tile(sin, (4, 1)).copy()


class Seq:
    def __init__(self, name, T, is_sample, idx, key0, tile0):
        self.name, self.T, self.is_sample, self.idx = name, T, is_sample, idx
        self.key0 = key0
        self.tile0 = tile0
        self.nctx = NCTX if is_sample else 0
        self.mj = 0 if is_sample else 1


def bcast(ap, axis, n):
    shp = list(ap.shape)
    shp.insert(axis, n)
    return ap.unsqueeze(axis).broadcast_to(shp)


def build_program(debug_outs=(), stop_after=None):
    nc = bass.Bass("TRN2", target_bir_lowering=False)
    es = ExitStack()
    P = Prog(nc)
    dbg = set(debug_outs)

    def din(name, shape, dt=F32):
        return nc.dram_tensor(name, list(shape), dt, kind="ExternalInput").ap()

    def dout(name, shape, dt=F32):
        return nc.dram_tensor(name, list(shape), dt, kind="ExternalOutput").ap()

    def dscr(name, shape, dt=F32):
        kind = "ExternalOutput" if name in dbg else "Internal"
        return nc.dram_tensor(name, list(shape), dt, kind=kind).ap()

    xs_in = din("xs", [TS, D])
    xp_in = din("xp", [NPS * TP, D])
    cvec = din("cvec", [2, D])
    ck_in = din("ck", [DEPTH, 2, NCTX, 64])
    cv_in = din("cv", [DEPTH, 2, NCTX, 64])
    st_in = din("st", [DEPTH, 2, 8, 64, 64])
    w_mod = din("w_mod", [DEPTH, D, 6 * D])
    b_mod = din("b_mod", [DEPTH, 6 * D])
    norm1 = din("norm1", [DEPTH, D])
    norm2 = din("norm2", [DEPTH, D])
    w_in = din("w_in", [DEPTH, D, INW])
    conv_w = din("conv_w", [DEPTH, 3, 1536])
    q_gain = din("q_gain", [DEPTH, 64])
    k_gain = din("k_gain", [DEPTH, 64])
    a_log = din("a_log", [DEPTH, 16])
    dt_bias = din("dt_bias", [DEPTH, 16])
    dn_gain = din("dn_gain", [DEPTH, 64])
    w_pa = din("w_pa", [DEPTH, 512, D])
    w_pd = din("w_pd", [DEPTH, 512, D])
    w_out = din("w_out", [DEPTH, D, D])
    w1 = din("w1", [DEPTH, D, DFF])
    b1 = din("b1", [DEPTH, DFF])
    w2 = din("w2", [DEPTH, DFF, D])
    b2 = din("b2", [DEPTH, D])
    final_norm = din("final_norm", [D])
    cst_in = din("cst", [128, NCST])
    cos_in = din("ropecos", [128, TS])
    sin_in = din("ropesin", [128, TS])
    ys_out = dout("ys", [TS, D])
    yp_out = dout("yp", [NPS * TP, D])
    nk_out = dout("nk", [NPS, DEPTH, 2, TP, 64])
    nv_out = dout("nv", [NPS, DEPTH, 2, TP, 64])
    nst_out = dout("nst", [NPS, DEPTH, 2, 8, 64, 64])

    seqs = [Seq("s", TS, True, 0, 0, 0)]
    for i in range(NPS):
        seqs.append(Seq(f"p{i}", TP, False, i, NCTX + TS + i * TP, TS // 128 + i * (TP // 128)))
    NKEY = NCTX + TS + NPS * TP
    NTILE = TS // 128 + NPS * TP // 128
    NVT = NKEY // 128

    XRES, PRE, QA, GATES, GS, QDT, KDT, QTM, KTM, VTM, OPART, OA, OD, X1, H2, ACTS = ({} for _ in range(16))
    for s in seqs:
        T = s.T
        XRES[s.name] = dscr(f"xres_{s.name}", [128, 8, T])
        PRE[s.name] = dscr(f"pre_{s.name}", [128, 12, T + 2])
        QA[s.name] = dscr(f"qa_{s.name}", [8, 64, T], BF16)
        GATES[s.name] = dscr(f"gates_{s.name}", [128, 16, T], BF16)
        GS[s.name] = dscr(f"gs_{s.name}", [T, 512], BF16)
        QDT[s.name] = dscr(f"qdt_{s.name}", [8, 64, T], BF16)
        KDT[s.name] = dscr(f"kdt_{s.name}", [8, 64, T], BF16)
        QTM[s.name] = dscr(f"qtm_{s.name}", [T, 512], BF16)
        KTM[s.name] = dscr(f"ktm_{s.name}", [T, 512], BF16)
        VTM[s.name] = dscr(f"vtm_{s.name}", [T, 512], BF16)
        OPART[s.name] = dscr(f"opart_{s.name}", [T, 512])
        OA[s.name] = dscr(f"oa_{s.name}", [8, 64, T], BF16)
        OD[s.name] = dscr(f"od_{s.name}", [128, 4, T], BF16)
        H2[s.name] = dscr(f"h2_{s.name}", [128, 8, T], BF16)

    def sb(name, shape, dt=F32):
        return es.enter_context(nc.sbuf_tensor("sb_" + name, list(shape), dt))

    cst = sb("cst", [128, NCST])
    cstb = sb("cstb", [128, NCST], BF16)

    def C(name, bf=False):
        o, n = CST_LAYOUT[name]
        return (cstb if bf else cst)[:, o:o + n]

    WAR = sb("warena", [128, 40960], BF16)
    KT_all = sb("kt_all", [128, NKEY], BF16)
    VA_all = sb("va_all", [128, NVT, 2, 65], BF16)
    LA = sb("la", [128, NTILE, 16])
    LB = sb("lb", [128, NTILE, 16])
    BETA = sb("beta", [128, NTILE, 16])
    MOD = sb("mod", [128, 48, 2])
    A1 = sb("a1", [128, 8, 2])
    A2 = sb("a2", [128, 8, 2])
    GB2 = sb("gb2", [128, 8, 2])
    n1f = sb("n1f", [128, 8])
    n2f = sb("n2f", [128, 8])
    fnf = sb("fnf", [128, 8])
    b2f = sb("b2f", [128, 8])
    b1f = sb("b1f", [128, 32])
    bmf = sb("bmf", [128, 48])
    cfm = sb("cfm", [128, 8, 2])
    scb = sb("scb", [128, 8, 2], BF16)
    qg = sb("qg", [128, 1])
    kg = sb("kg", [128, 1])
    cw = sb("cw", [128, 3, 12])
    dtb = sb("dtb", [128, 16])
    negA = sb("negA", [128, 16])
    dng = sb("dng", [128, 64])
    TB = 256
    BIGA = sb("bigA", [128, 12 * 258])
    BIGB = sb("bigB", [128, 12 * 256])
    HB = sb("hb16", [128, 16, 256], BF16)
    GATB = sb("gatb", [128, 16, 256], BF16)
    R1 = sb("r1", [128, 256])
    RSTD = sb("rstd", [128, 256])
    STG = [sb(f"stg{i}", [128, 512]) for i in range(4)]
    STB = [sb(f"stb{i}", [128, 512], BF16) for i in range(4)]
    COSB = sb("cosb", [128, 256])
    SINB = sb("sinb", [128, 256])
    SMALL = sb("small", [128, 64])
    SM = sb("sm", [128, 160])
    VAB = sb("vab", [128, 160])
    KTC = [sb(f"ktc{i}", [64, 8, 128], BF16) for i in range(2)]
    QTC = [sb(f"qtc{i}", [64, 8, 128], BF16) for i in range(2)]
    KTMC = [sb(f"ktmc{i}", [128, 512], BF16) for i in range(2)]
    QTMC = [sb(f"qtmc{i}", [128, 512], BF16) for i in range(2)]
    VTMC = [sb(f"vtmc{i}", [128, 512], BF16) for i in range(2)]
    def v512(big, i, p0=0, p1=128):
        return big[p0:p1, i * 512:(i + 1) * 512]

    E1, E2, E3, DG1, DG3 = (v512(BIGA, i) for i in range(5))
    U0 = [v512(BIGA, 5), v512(BIGB, 0)]
    OACC = [v512(BIGB, 1), v512(BIGB, 2)]
    RS = v512(BIGB, 3)
    S32 = [v512(BIGB, 4 + i, 0, 64).rearrange("p (h v) -> p h v", h=8) for i in range(2)]
    HBf = HB[:, :, :].rearrange("p a b -> p (a b)")
    GBf = GATB[:, :, :].rearrange("p a b -> p (a b)")
    PA = [v512(HBf, 0), v512(HBf, 1)]
    PT = [v512(HBf, 2), v512(HBf, 3)]
    RT = [v512(HBf, 4), v512(HBf, 5)]
    QKM = [v512(HBf, 6), v512(HBf, 7)]
    BEK = v512(GBf, 0)
    KDEC = [v512(GBf, 1), v512(GBf, 2)]
    BV = v512(GBf, 3)
    UB = [v512(GBf, 4), v512(GBf, 5)]
    DE = v512(GBf, 6)
    OB = v512(GBf, 7, 0, 64)
    XBW = [sb(f"xbw{i}", [128, 1024], BF16) for i in range(2)]
    XB = [XBW[0][:, 0:512], XBW[0][:, 512:1024], XBW[1][:, 0:512], XBW[1][:, 512:1024]]
    NWT = [sb(f"nwt{i}", [64, 16, 64], BF16) for i in range(2)]
    QDEC = [sb(f"qdec{i}", [64, 16, 64], BF16) for i in range(2)]
    SBF = [sb(f"sbf{i}", [64, 8, 64], BF16) for i in range(2)]
    EGL2 = [sb(f"egl2{i}", [128, 16]) for i in range(2)]
    QB = STB[3][:, :].rearrange("p (j t) -> p j t", j=4)
    PS = [es.enter_context(nc.psum_tensor(f"ps{i}", [128, 1024], F32)) for i in range(4)]
    dbg_mod = dscr("dbg_mod", [128, 48, 2])
    dbg_la = dscr("dbg_la", [128, NTILE, 16])
    dbg_lb = dscr("dbg_lb", [128, NTILE, 16])
    dbg_beta = dscr("dbg_beta", [128, NTILE, 16])

    XT = BIGA[:, 0:8 * 256].rearrange("p (c t) -> p c t", c=8)
    X1T = BIGB[:, 0:8 * 256].rearrange("p (c t) -> p c t", c=8)
    PRET = BIGA[:, :].rearrange("p (c t) -> p c t", c=12)
    CV = BIGB[:, :].rearrange("p (c t) -> p c t", c=12)
    SQ = HB[:, 0:8, :]
    HT = HB[:, 8:16, :]
    XTM = [BIGA[:, i * 1024:(i + 1) * 1024] for i in range(2)]
    XFM = [BIGB[:, i * 1024:(i + 1) * 1024].rearrange("p (c t) -> p c t", c=8) for i in range(2)]

    def psb(i):
        return PS[i // 2][:, (i % 2) * 512:(i % 2) * 512 + 512]

    def pk(i):
        return ("ps", i)

    def dma(q, out, in_, reads, writes, **kw):
        P.add(q, lambda e: e.dma_start(out=out, in_=in_, **kw), reads, writes, dma=True)

    def mm(out, lhsT, rhs, start, stop, reads, writes, **kw):
        P.add("pe", lambda e: e.matmul(out, lhsT, rhs, start=start, stop=stop, **kw), reads, writes)

    def tr(out, in_, ident, reads, writes):
        P.add("pe", lambda e: e.transpose(out, in_, ident), reads, writes)

    def act(out, in_, func, reads, writes, **kw):
        P.add("act", lambda e: e.activation(out, in_, func, **kw), reads, writes)

    def tt(eng, out, in0, in1, op, reads, writes):
        P.add(eng, lambda e: e.tensor_tensor(out, in0, in1, op), reads, writes)

    def tsc(eng, out, in0, s1, s2, op0, op1, reads, writes):
        if op1 is None:
            P.add(eng, lambda e: e.tensor_scalar(out, in0, s1, None, op0), reads, writes)
        else:
            P.add(eng, lambda e: e.tensor_scalar(out, in0, s1, s2, op0, op1), reads, writes)

    def stt(eng, out, in0, scalar, in1, op0, op1, reads, writes):
        P.add(eng, lambda e: e.scalar_tensor_tensor(out, in0, scalar, in1, op0, op1), reads, writes)

    def cp(eng, out, in_, reads, writes):
        if eng == "act":
            P.add("act", lambda e: e.copy(out, in_), reads, writes)
        else:
            P.add(eng, lambda e: e.tensor_copy(out, in_), reads, writes)

    def recip(out, in_, reads, writes):
        P.add("dve", lambda e: e.reciprocal(out, in_), reads, writes)

    def load_w(dst3, src2, K, wkey="war"):
        for k in range(K):
            dma("pool", dst3[:, k, :], src2[k * 128:(k + 1) * 128, :], [], [wkey])

    def rms_stats(src3, nt, srckey, sq=None, sqkey="hb", bank=0, r1=None, rstd=None, rkey=""):
        if sq is None:
            sq = SQ
        if r1 is None:
            r1, rstd = R1[:, :], RSTD[:, :]
        act(sq[:, :, :nt], src3, AF.Square, [srckey], [sqkey])
        for c in range(8):
            mm(psb(bank)[:, :nt], C("ones", True), sq[:, c, :nt], c == 0, c == 7, [sqkey, "cstb"], [pk(bank)])
        act(r1[:, :nt], psb(bank)[:, :nt], AF.Ln, [pk(bank)], ["r1" + rkey], bias=EPS, scale=1.0 / D)
        act(rstd[:, :nt], r1[:, :nt], AF.Exp, ["r1" + rkey], ["rstd" + rkey], scale=-0.5)

    dma("sp", cst[:], cst_in, [], ["cst"])
    cp("dve", cstb[:], cst[:], ["cst"], ["cstb"])
    P.add("pool", lambda e: e.memset(VA_all[:], 1.0), [], ["va"])
    dma("sp", fnf[:], final_norm.rearrange("(k p) -> p k", p=128), [], ["fnf"], allow_slow_non_contiguous=True)
    for j in range(2):
        dma("sp", cfm[:, :, j], cvec[j].rearrange("(k p) -> p k", p=128), [], ["cfm"], allow_slow_non_contiguous=True)
    act(scb[:], cfm[:], AF.Silu, ["cfm"], ["scb"])
    P.add("pool", lambda e: e.memset(SMALL[:], 0.0), [], ["small"])
    for s in seqs:
        for col in (0, s.T + 1):
            dma("sp", PRE[s.name][:, :, col:col + 1], SMALL[:, 0:12].unsqueeze(2), ["small"], [("pre", s.name, "pad", col)],
                allow_slow_non_contiguous=True)

    it = 0
    for s in seqs:
        src = xs_in if s.is_sample else xp_in[s.idx * TP:(s.idx + 1) * TP, :]
        for ti in range(s.T // 128):
            b = it % 2
            dma("sp", XTM[b], src[ti * 128:(ti + 1) * 128, :], [], ["bigA"])
            for c in range(8):
                tr(PS[b][:, c * 128:(c + 1) * 128], XTM[b][:, c * 128:(c + 1) * 128], C("ident"),
                   ["bigA", "cst"], [pk(2 * b), pk(2 * b + 1)])
            cp("act" if it % 2 == 0 else "dve", XFM[b], PS[b][:].rearrange("p (c t) -> p c t", c=8),
               [pk(2 * b), pk(2 * b + 1)], ["bigB"])
            dma("pool", XRES[s.name][:, :, ti * 128:(ti + 1) * 128], XFM[b], ["bigB"], [("xres", s.name, ti // 2)])
            it += 1

    def finish():
        P.emit(es)
        es.close()
        return nc

    if stop_after == "stage0":
        return finish()

    NLAYERS = DEPTH
    for l in range(NLAYERS):
        for (t_, src) in ((n1f, norm1[l]), (n2f, norm2[l]), (b2f, b2[l])):
            dma("sp", t_[:], src.rearrange("(k p) -> p k", p=128), [], ["lvec"], allow_slow_non_contiguous=True)
        dma("sp", b1f[:], b1[l].rearrange("(k p) -> p k", p=128), [], ["lvec"], allow_slow_non_contiguous=True)
        dma("sp", bmf[:], b_mod[l].rearrange("(k p) -> p k", p=128), [], ["lvec"], allow_slow_non_contiguous=True)
        for hh in range(2):
            dma("sp", qg[64 * hh:64 * hh + 64, :], q_gain[l].rearrange("(d o) -> d o", o=1), [], ["lvec"], allow_slow_non_contiguous=True)
            dma("sp", kg[64 * hh:64 * hh + 64, :], k_gain[l].rearrange("(d o) -> d o", o=1), [], ["lvec"], allow_slow_non_contiguous=True)
        for j in range(3):
            dma("sp", cw[:, j, :], conv_w[l, j].rearrange("(c p) -> p c", p=128), [], ["lvec"], allow_slow_non_contiguous=True)
        dma("sp", dtb[:], dt_bias[l:l + 1, :].broadcast_to([128, 16]), [], ["lvec"], allow_slow_non_contiguous=True)
        dma("sp", negA[:], a_log[l:l + 1, :].broadcast_to([128, 16]), [], ["lvec"], allow_slow_non_contiguous=True)
        dma("sp", dng[:], dn_gain[l:l + 1, :].broadcast_to([128, 64]), [], ["lvec"], allow_slow_non_contiguous=True)
        act(negA[:], negA[:], AF.Exp, ["lvec"], ["lvec2"])
        tsc("dve", negA[:], negA[:], -1.0, None, ALU.mult, None, ["lvec2"], ["lvec2"])

        Wm = WAR[:, 0:8 * 3072].rearrange("p (k n) -> p k n", k=8)
        for hh in range(2):
            load_w(Wm, w_mod[l][:, hh * 3072:(hh + 1) * 3072], 8)
            for n in range(24):
                nn = hh * 24 + n
                for k in range(8):
                    mm(psb(0)[:, nn * 2:nn * 2 + 2], Wm[:, k, n * 128:(n + 1) * 128], scb[:, k, :], k == 0, k == 7,
                       ["war", "scb"], [pk(0)])
        tt("dve", MOD[:], psb(0)[:, 0:96].rearrange("p (n j) -> p n j", j=2), bcast(bmf[:], 2, 2), ALU.add,
           [pk(0), "lvec"], ["mod"])
        stt("dve", A1[:], MOD[:, 8:16, :], 1.0, bcast(n1f[:], 2, 2), ALU.add, ALU.mult, ["mod", "lvec"], ["mod2"])
        stt("dve", A2[:], MOD[:, 32:40, :], 1.0, bcast(n2f[:], 2, 2), ALU.add, ALU.mult, ["mod", "lvec"], ["mod2"])
        tt("dve", GB2[:], MOD[:, 40:48, :], bcast(b2f[:], 2, 2), ALU.mult, ["mod", "lvec"], ["mod2"])
        MK = ["mod", "mod2", "lvec", "lvec2"]
        if stop_after == "adaln":
            dma("sp", dbg_mod, MOD[:], ["mod"], ["dbgmod"])
            return finish()

        Win = WAR[:, 0:8 * INW].rearrange("p (k n) -> p k n", k=8)
        load_w(Win, w_in[l], 8)
        bank_rr = [0]

        def next_bank():
            bank_rr[0] = (bank_rr[0] % 4) + 1
            return bank_rr[0]

        stg_rr = [0]

        def nstg():
            stg_rr[0] = (stg_rr[0] + 1) % 4
            return stg_rr[0]

        jobsA = [(s, blk) for s in seqs for blk in range(s.T // TB)]
        XTsA = [(XT, "bigA"), (X1T, "bigB")]
        HTsA = [(HB[:, 8:16, :], "ht"), (GATB[:, 0:8, :], "gatb")]

        def blockA(j):
            s, blk = jobsA[j]
            mj = s.mj
            t0 = blk * TB
            XTj, xkey = XTsA[j % 2]
            HTj, hkey = HTsA[j % 2]
            dma("sp", XTj, XRES[s.name][:, :, t0:t0 + TB], [("xres", s.name, blk)], [xkey])
            if s.is_sample:
                dma("sp", COSB[:], cos_in[:, t0:t0 + TB], [], ["cosb"])
                dma("sp", SINB[:], sin_in[:, t0:t0 + TB], [], ["sinb"])
            rms_stats(XTj, TB, xkey)
            tt("dve", XTj, XTj, bcast(RSTD[:, :], 1, 8), ALU.mult, [xkey, "rstd"], [xkey])
            for c in range(8):
                if c % 2 == 0:
                    act(HTj[:, c, :], XTj[:, c, :], AF.Identity, [xkey] + MK, [hkey],
                        scale=A1[:, c, mj:mj + 1], bias=MOD[:, c, mj:mj + 1])
                else:
                    tsc("dve", HTj[:, c, :], XTj[:, c, :], A1[:, c, mj:mj + 1], MOD[:, c, mj:mj + 1], ALU.mult, ALU.add,
                        [xkey] + MK, [hkey])

            yield

            def fm_chunk(col0):
                bk = next_bank()
                for k in range(8):
                    mm(psb(bk)[:, :TB], Win[:, k, col0:col0 + 128], HTj[:, k, :], k == 0, k == 7, ["war", hkey], [pk(bk)])
                return bk

            def qk_epi(c, bk):
                is_k = (c == 4)
                gain = kg if is_k else qg
                cp("act", STG[0][:, :TB], psb(bk)[:, :TB], [pk(bk)], ["stg0"])
                act(STB[0][:, :TB], psb(bk)[:, :TB], AF.Square, [pk(bk)], ["stb0"])
                yield
                mm(psb(6)[:, :TB], C("blk", True), STB[0][:, :TB], True, True, ["stb0", "cstb"], [pk(6)])
                act(STG[1][:, :TB], psb(6)[:, :TB], AF.Ln, [pk(6)], ["stg1"], bias=EPS, scale=1.0 / 64)
                act(STG[1][:, :TB], STG[1][:, :TB], AF.Exp, ["stg1"], ["stg1"], scale=-0.5)
                stt("dve", STG[0][:, :TB], STG[0][:, :TB], gain[:, 0:1], STG[1][:, :TB], ALU.mult, ALU.mult,
                    ["stg0", "stg1", "lvec"], ["stg0"])
                yield
                kcol = s.key0 + s.nctx + t0
                dst = KT_all[:, kcol:kcol + TB] if is_k else STB[1][:, :TB]
                dkey = "kt" if is_k else "stb1"
                if s.is_sample:
                    mm(psb(7)[:, :TB], C("rot"), STG[0][:, :TB], True, True, ["stg0", "cst"], [pk(7)])
                    tt("dve", STG[2][:, :TB], STG[0][:, :TB], COSB[:], ALU.mult, ["stg0", "cosb"], ["stg2"])
                    yield
                    tt("dve", STG[3][:, :TB], psb(7)[:, :TB], SINB[:], ALU.mult, [pk(7), "sinb"], ["stg3"])
                    tt("dve", dst, STG[2][:, :TB], STG[3][:, :TB], ALU.add, ["stg2", "stg3"], [dkey])
                else:
                    cp("dve", dst, STG[0][:, :TB], ["stg0"], [dkey])
                if not is_k:
                    dma("pool", QA[s.name][2 * c:2 * c + 2].rearrange("h d t -> (h d) t")[:, t0:t0 + TB], STB[1][:, :TB],
                        ["stb1"], [("qa", s.name)])
                elif not s.is_sample:
                    for t2 in range(TB // 128):
                        tr(psb(7)[:, t2 * 128:(t2 + 1) * 128], STG[0][:, t2 * 128:(t2 + 1) * 128], C("ident"),
                           ["stg0", "cst"], [pk(7)])
                    cp("act", STG[2][:, :TB], psb(7)[:, :TB], [pk(7)], ["stg2"])
                    for t2 in range(TB // 128):
                        for g in range(2):
                            dma("pool", nk_out[s.idx, l, g, t0 + t2 * 128:t0 + (t2 + 1) * 128, :],
                                STG[2][:, t2 * 128 + g * 64:t2 * 128 + g * 64 + 64], ["stg2"], [("nk", s.idx)])
                yield

            hrr = [0]

            def nh():
                hrr[0] = (hrr[0] + 1) % 4
                return hrr[0]

            def filler():
                for c in range(12):
                    bk = fm_chunk(C_QD + c * 128)
                    i = nh()
                    cp("act" if c % 2 == 0 else "dve", STG[i][:, 256:512], psb(bk)[:, :TB], [pk(bk)], [f"stgh{i}"])
                    dma("pool", PRE[s.name][:, c, 1 + t0:1 + t0 + TB], STG[i][:, 256:512], [f"stgh{i}"], [("pre", s.name, blk)])
                    yield
                for c in range(16):
                    bk = fm_chunk(C_GA + c * 128)
                    i = nh()
                    act(STB[i][:, 256:512], psb(bk)[:, :TB], AF.Sigmoid, [pk(bk)], [f"stbh{i}"])
                    dma("pool", GATES[s.name][:, c, t0:t0 + TB], STB[i][:, 256:512], [f"stbh{i}"], [("gates", s.name, blk)])
                    yield

            fg = filler()
            for c in range(5):
                bk = fm_chunk(C_QA + c * 128)
                for _ in qk_epi(c, bk):
                    next(fg, None)
            yield
            for _ in fg:
                pass
            for t2 in range(TB // 128):
                tsl = slice(t2 * 128, (t2 + 1) * 128)
                gti = s.tile0 + (t0 // 128) + t2
                vt = (s.key0 + s.nctx + t0) // 128 + t2
                for k in range(8):
                    mm(psb(5)[:, 0:512], HTj[:, k, tsl], Win[:, k, C_GO:C_GO + 512], k == 0, k == 7, ["war", hkey], [pk(5)])
                for k in range(8):
                    mm(psb(6)[:, 0:128], HTj[:, k, tsl], Win[:, k, C_VA:C_VA + 128], k == 0, k == 7, ["war", hkey], [pk(6)])
                for k in range(8):
                    mm(psb(6)[:, 128:160], HTj[:, k, tsl], Win[:, k, C_AI:C_AI + 32], k == 0, k == 7, ["war", hkey], [pk(6)])
                i = nstg()
                act(STB[i][:, :], psb(5)[:, :], AF.Silu, [pk(5)], [f"stb{i}", f"stbh{i}"])
                dma("pool", GS[s.name][t0 + t2 * 128:t0 + (t2 + 1) * 128, :], STB[i][:, :], [f"stb{i}", f"stbh{i}"], [("gs", s.name)])
                cp("dve", VAB[:, :], psb(6)[:, 0:160], [pk(6)], ["vab"])
                cp("dve", VA_all[:, vt, :, 0:64], VAB[:, 0:128].rearrange("p (g d) -> p g d", g=2), ["vab"], ["va"])
                if not s.is_sample:
                    for g in range(2):
                        dma("pool", nv_out[s.idx, l, g, t0 + t2 * 128:t0 + (t2 + 1) * 128, :], VAB[:, g * 64:g * 64 + 64],
                            ["vab"], [("nv", s.idx)])
                tt("dve", SM[:, 0:16], VAB[:, 128:144], dtb[:], ALU.add, ["vab", "lvec"], ["sm"])
                act(SM[:, 16:32], SM[:, 0:16], AF.Exp, ["sm"], ["sm1"])
                act(SM[:, 32:48], SM[:, 16:32], AF.Ln, ["sm1"], ["sm2"], bias=1.0)
                tt("dve", LA[:, gti, :], SM[:, 32:48], negA[:], ALU.mult, ["sm2", "lvec2"], ["la"])
                act(BETA[:, gti, :], VAB[:, 144:160], AF.Sigmoid, ["vab"], ["beta"])
                act(LB[:, gti, :], BETA[:, gti, :], AF.Ln, ["beta"], ["lb"])

        gA = [blockA(j) for j in range(len(jobsA))]
        next(gA[0])
        for j in range(len(jobsA)):
            next(gA[j])
            if j + 1 < len(jobsA):
                next(gA[j + 1])
            for _ in gA[j]:
                pass
        if "dbg_la" in dbg:
            dma("sp", dbg_la, LA[:], ["la"], ["dbgla"])
            dma("sp", dbg_lb, LB[:], ["lb"], ["dbglb"])
            dma("sp", dbg_beta, BETA[:], ["beta"], ["dbgbeta"])
        P.fence()
        if stop_after == "stageA":
            return finish()

        brr = [0]

        def nb():
            brr[0] = brr[0] % 3 + 1
            return brr[0]

        def genB():
            for s in seqs:
                for blk in range(s.T // TB):
                    t0 = blk * TB
                    dma("sp", PRET, PRE[s.name][:, :, t0:t0 + TB + 2],
                        [("pre", s.name, b_) for b_ in range(max(0, blk - 1), min(s.T // TB, blk + 2))]
                        + [("pre", s.name, "pad", 0), ("pre", s.name, "pad", s.T + 1)], ["bigA"])
                    for c in range(12):
                        tsc("dve", CV[:, c, :], PRET[:, c, 0:TB], cw[:, 0, c:c + 1], None, ALU.mult, None, ["bigA", "lvec"], [("cv", c)])
                        stt("dve", CV[:, c, :], PRET[:, c, 1:TB + 1], cw[:, 1, c:c + 1], CV[:, c, :], ALU.mult, ALU.add,
                            ["bigA", "lvec", ("cv", c)], [("cv", c)])
                        stt("dve", CV[:, c, :], PRET[:, c, 2:TB + 2], cw[:, 2, c:c + 1], CV[:, c, :], ALU.mult, ALU.add,
                            ["bigA", "lvec", ("cv", c)], [("cv", c)])
                        if c % 3 == 2:
                            yield
                    for c4 in range(3):
                        cs_ = slice(c4 * 4, c4 * 4 + 4)
                        ck = [("cv", c) for c in range(c4 * 4, c4 * 4 + 4)]
                        SG = PRET[:, cs_, 0:TB]
                        act(SG, CV[:, cs_, :], AF.Exp, ck + ["bigA"], ["bigA"], scale=-1.0)
                        act(SG, SG, AF.Ln, ["bigA"], ["bigA"], bias=1.0)
                        act(SG, SG, AF.Exp, ["bigA"], ["bigA"], scale=-1.0)
                        tt("dve", CV[:, cs_, :], CV[:, cs_, :], SG, ALU.mult, ck + ["bigA"], ck)
                        yield
                    for c in range(8):
                        act(STB[0][:, :TB], CV[:, c, :], AF.Square, [("cv", c)], ["stb0"])
                        mm(psb(7)[:, :TB], C("blk", True), STB[0][:, :TB], True, True, ["stb0", "cstb"], [pk(7)])
                        act(STG[1][:, :TB], psb(7)[:, :TB], AF.Ln, [pk(7)], ["stg1"], bias=EPS, scale=1.0)
                        act(STG[1][:, :TB], STG[1][:, :TB], AF.Exp, ["stg1"], ["stg1"], scale=-0.5)
                        i = nb()
                        stt("dve", STB[i][:, :TB], CV[:, c, :], 0.125 if c < 4 else 1.0, STG[1][:, :TB], ALU.mult, ALU.mult,
                            [("cv", c), "stg1"], [f"stb{i}"])
                        cp("act", CV[:, c, :], STB[i][:, :TB], [f"stb{i}"], [("cv", c)])
                        dstT = QDT if c < 4 else KDT
                        cc = c % 4
                        dma("pool", dstT[s.name][2 * cc:2 * cc + 2].rearrange("h d t -> (h d) t")[:, t0:t0 + TB], STB[i][:, :TB],
                            [f"stb{i}"], [("qkdt", s.name)])
                        yield
                    for t2 in range(TB // 128):
                        for grp, dstM in enumerate((QTM, KTM, VTM)):
                            for cc in range(4):
                                tr(psb(7)[:, cc * 128:(cc + 1) * 128], CV[:, grp * 4 + cc, t2 * 128:(t2 + 1) * 128], C("ident"),
                                   [("cv", grp * 4 + cc), "cst"], [pk(7)])
                            i = nb()
                            cp("dve", STB[i][:, :], psb(7)[:, :], [pk(7)], [f"stb{i}"])
                            dma("pool", dstM[s.name][t0 + t2 * 128:t0 + (t2 + 1) * 128, :], STB[i][:, :], [f"stb{i}"], [("tm", s.name)])
                            yield

        if stop_after == "stageB":
            for _ in genB():
                pass
            P.fence()
            return finish()

        for t2 in range(NCTX // 128):
            dma("sp", STG[0][:, 0:128].rearrange("p (g d) -> p g d", g=2),
                ck_in[l, :, t2 * 128:(t2 + 1) * 128, :].rearrange("g p d -> p g d"), [], ["stg0"])
            tr(psb(0)[:, 0:128], STG[0][:, 0:128], C("ident"), ["stg0", "cst"], [pk(0)])
            cp("dve", KT_all[:, t2 * 128:(t2 + 1) * 128], psb(0)[:, 0:128], [pk(0)], ["kt"])
            dma("sp", STG[1][:, 0:128].rearrange("p (g d) -> p g d", g=2),
                cv_in[l, :, t2 * 128:(t2 + 1) * 128, :].rearrange("g p d -> p g d"), [], ["stg1"])
            cp("dve", VA_all[:, t2, :, 0:64], STG[1][:, 0:128].rearrange("p (g d) -> p g d", g=2), ["stg1"], ["va"])
        QBz = [[WAR[:, (qp * 2 + g) * 512:(qp * 2 + g + 1) * 512] for g in range(2)] for qp in range(2)]
        VAp = WAR[:, 2048:2048 + NVT * 256].rearrange("p (t g d) -> p t g d", t=NVT, g=2)
        P.add("pool", lambda e: e.memset(WAR[:, 0:2048 + NVT * 256], 0.0), [], ["qbz", "vap"])
        cp("pool", VAp[:, :, :, 0:65], VA_all[:, :, :, :], ["va", "vap"], ["vap"])
        RS = WAR[:, 2048 + NVT * 256:2048 + NVT * 256 + 1024].bitcast(F32)
        items = []
        qcount = 0
        for s in seqs:
            ktiles = []
            if s.is_sample:
                ktiles += list(range(NCTX // 128))
            ktiles += [(s.key0 + s.nctx) // 128 + i for i in range(s.T // 128)]
            for qi in range(s.T // 128):
                for n_, kt in enumerate(ktiles):
                    items.append((s, qi, n_, kt, n_ == 0, n_ == len(ktiles) - 1, qcount % 2))
                qcount += 1
        LAG = 1

        def genAttn():
            for idx in range(len(items) + LAG):
                if idx < len(items):
                    s, qi, n_, kt, first, last, qp = items[idx]
                    q0 = qi * 128
                    if first:
                        for g2 in range(2):
                            dma("sp", QBz[qp][g2][64 * g2:64 * g2 + 64, :].rearrange("p (j t) -> p j t", j=4),
                                QA[s.name][4 * g2:4 * g2 + 4, :, q0:q0 + 128].rearrange("j d t -> d j t"),
                                [("qa", s.name), "qbz"], [("qb", qp)])
                    r_ = idx % 2
                    for g in range(2):
                        mm(PS[r_][:, g * 512:(g + 1) * 512], KT_all[:, kt * 128:(kt + 1) * 128],
                           QBz[qp][g], True, True, ["kt", ("qb", qp)],
                           [pk(2 * r_), pk(2 * r_ + 1)])
                    act(XBW[r_][:, :], PS[r_][:, :], AF.Exp, [pk(2 * r_), pk(2 * r_ + 1)], [("ptt", r_)], scale=0.125)
                if idx >= LAG:
                    s, qi, n_, kt, first, last, qp = items[idx - LAG]
                    q0 = qi * 128
                    r_ = (idx - LAG) % 2
                    for g in range(2):
                        ob = 4 + g
                        mm(psb(ob)[:, :], VAp[:, kt, g, :], XBW[r_][:, g * 512:(g + 1) * 512], first, last,
                           ["vap", ("ptt", r_)], [pk(ob)])
                    if last:
                        for g in range(2):
                            ob = 4 + g
                            cp("dve", RS[64:65, :], psb(ob)[64:65, :], [pk(ob)], ["rs"])
                            act(RS[64:65, :], RS[64:65, :], AF.Ln, ["rs"], ["rs"])
                            act(RS[64:65, :], RS[64:65, :], AF.Exp, ["rs"], ["rs"], scale=-1.0)
                            mm(psb(6)[0:64, :], C("ones")[64:65, 0:64], RS[64:65, :], True, True, ["rs", "cst"], [pk(6)])
                            cp("dve", STG[0][0:64, :], psb(ob)[0:64, :], [pk(ob)], ["stg0"])
                            tt("dve", OB[:, :], STG[0][0:64, :], psb(6)[0:64, :], ALU.mult, ["stg0", pk(6)], ["ob"])
                            dma("sp", OA[s.name][4 * g:4 * g + 4, :, q0:q0 + 128].rearrange("j d t -> d j t"),
                                OB[:, :].rearrange("p (j t) -> p j t", j=4), ["ob"], [("oa", s.name)])
                yield

        gB_, gAt_ = genB(), genAttn()
        doneB = doneA = False
        it_ = 0
        while not (doneB and doneA):
            if not doneB:
                try:
                    next(gB_)
                except StopIteration:
                    doneB = True
            for _ in range(3 + (1 if it_ % 4 == 3 else 0)):
                if not doneA:
                    try:
                        next(gAt_)
                    except StopIteration:
                        doneA = True
            it_ += 1
        P.fence()
        if stop_after == "attn":
            return finish()

        H8 = 8
        CUT = 0

        def v3(t):
            return t.rearrange("p (h j) -> p h j", h=H8)

        woff = [0]

        def wtake(n, f32=False):
            ap = WAR[:, woff[0]:woff[0] + n]
            woff[0] += n
            return ap.bitcast(F32) if f32 else ap

        DS = [dict(SM=SM, DG1=DG1, DG3=DG3, DE=DE, E1=E1, E2=E2, E3=E3, PA=PA, PT=PT, RT=RT, XB=XB, BEK=BEK, BV=BV), None]
        DS[1] = dict(E1=wtake(1024, True), E2=wtake(1024, True), E3=wtake(1024, True), DG1=wtake(1024, True),
                     DG3=wtake(1024, True), SM=wtake(320, True),
                     PA=[wtake(512), wtake(512)], PT=[wtake(512), wtake(512)], RT=[wtake(512), wtake(512)],
                     XB=[wtake(512) for _ in range(4)], BEK=wtake(512), BV=wtake(512), DE=wtake(512))
        def w64(n):
            ap = WAR[0:64, woff[0]:woff[0] + n]
            woff[0] += n
            return ap.rearrange("p (x i) -> p x i", x=16)

        NWT2 = [[NWT[d][:, :, :], w64(1024)] for d in range(2)]
        QDEC2 = [[QDEC[d][:, :, :], w64(1024)] for d in range(2)]
        U02 = [[U0[d], wtake(1024, True)] for d in range(2)]
        QKM2 = [[QKM[d], wtake(512)] for d in range(2)]
        KDEC2 = [[KDEC[d], wtake(512)] for d in range(2)]
        EGL22 = [[EGL2[d][:, :], wtake(32, True)] for d in range(2)]
        ab = [0]
        fbk = [0]
        ppr = [0]

        def abank():
            ab[0] = (ab[0] + 1) % 6
            return ab[0]

        def fbank():
            fbk[0] ^= 1
            return 6 + fbk[0]

        def ppair():
            ppr[0] = (ppr[0] + 1) % 3
            return ppr[0]

        idb = bcast(C("identst", True), 1, H8)
        ist = bcast(C("identst"), 1, H8)

        def msk(name):
            return bcast(C(name, True), 1, H8)

        def prep(s, d, m, par):
            NWTp, QDECp, U0p, QKMp, KDECp, EGL2p = NWT2[d][par], QDEC2[d][par], U02[d][par], QKM2[d][par], KDEC2[d][par], EGL22[d][par]
            kq = (d, par)
            T_ = DS[d]
            SMd, DG1d, DG3d, DEd = T_["SM"], T_["DG1"], T_["DG3"], T_["DE"]
            E1d, E2d, E3d = T_["E1"], T_["E2"], T_["E3"]
            PAd, PTd, RTd, XBd, BEKd, BVd = T_["PA"], T_["PT"], T_["RT"], T_["XB"], T_["BEK"], T_["BV"]

            def K(name, *x):
                return (name, d) + tuple(x)

            gti = s.tile0 + m
            sfx = "f" if d == 0 else "b"
            rows = slice(m * 128, (m + 1) * 128)
            dma("sp", KTC[d][:, :, :], KDT[s.name][:, :, rows].rearrange("h d t -> d h t"), [("qkdt", s.name)], [("ktc", d)])
            dma("sp", QTC[d][:, :, :], QDT[s.name][:, :, rows].rearrange("h d t -> d h t"), [("qkdt", s.name)], [("qtc", d)])
            dma("sp", KTMC[d][:, :], KTM[s.name][rows, :], [("tm", s.name)], [("ktmc", d)])
            dma("sp", QTMC[d][:, :], QTM[s.name][rows, :], [("tm", s.name)], [("qtmc", d)])
            dma("sp", VTMC[d][:, :], VTM[s.name][rows, :], [("tm", s.name)], [("vtmc", d)])
            la = LA[:, gti, 8 * d:8 * d + 8]
            lb = LB[:, gti, 8 * d:8 * d + 8]
            be_ = BETA[:, gti, 8 * d:8 * d + 8]
            bg = fbank()
            mm(psb(bg)[:, 0:8], C("tri_" + sfx), la, True, True, ["la", "cst"], [pk(bg)])
            mm(psb(bg)[:, 8:16], C("half0"), la, True, True, ["la", "cst"], [pk(bg)])
            mm(psb(bg)[:, 16:24], C("half1"), la, True, True, ["la", "cst"], [pk(bg)])
            cp("dve", SMd[:, 0:24], psb(bg)[:, 0:24], [pk(bg)], [K("sm")])
            yield
            tt("dve", SMd[:, 24:32], SMd[:, 0:8], lb, ALU.add, [K("sm"), "lb"], [K("sm_glb")])
            act(SMd[:, 32:40], SMd[:, 0:8], AF.Exp, [K("sm")], [K("sm_eg")])
            cp("dve", SMd[0:64, 40:48], SMd[0:64, 8:16], [K("sm")], [K("sm_glo")])
            cp("dve", SMd[64:128, 40:48], SMd[64:128, 16:24], [K("sm")], [K("sm_glo")])
            tt("dve", SMd[:, 48:56], SMd[:, 40:48], SMd[:, 0:8], ALU.subtract, [K("sm"), K("sm_glo")], [K("sm_ek")])
            act(SMd[:, 48:56], SMd[:, 48:56], AF.Exp, [K("sm_ek")], [K("sm_ek")])
            act(EGL2p, SMd[:, 8:24], AF.Exp, [K("sm")], [("egl2",) + kq])
            tt("dve", SMd[:, 56:64], be_, SMd[:, 32:40], ALU.mult, ["beta", K("sm_eg")], [K("sm_be")])
            tsc("dve", SMd[:, 64:72], SMd[:, 0:8], -1.0, None, ALU.mult, None, [K("sm")], [K("sm_ng")])
            tt("pool", v3(DG1d), ist, bcast(SMd[:, 24:32], 2, 64), ALU.mult, ["cst", K("sm_glb")], [K("dg1")])
            tt("pool", v3(DG3d), ist, bcast(SMd[:, 0:8], 2, 64), ALU.mult, ["cst", K("sm")], [K("dg3")])
            tt("dve", v3(DEd), ist, bcast(SMd[:, 32:40], 2, 64), ALU.mult, ["cst", K("sm_eg")], [K("de")])
            yield
            b1 = fbank()
            mm(psb(b1)[:, :], C("negblk"), DG3d, True, False, [K("dg3"), "cst"], [pk(b1)])
            mm(psb(b1)[:, :], C("ident"), bcast(SMd[:, 24:32], 2, 64), False, False, [K("sm_glb"), "cst"], [pk(b1)])
            mm(psb(b1)[:, :], C("ident", True), msk("m1_" + sfx), False, True, ["cstb"], [pk(b1)])
            act(E1d, psb(b1)[:, :], AF.Exp, [pk(b1)], [K("e1")])
            yield
            b2 = fbank()
            mm(psb(b2)[:, :], C("blk"), DG1d, True, False, [K("dg1"), "cst"], [pk(b2)])
            mm(psb(b2)[:, :], C("ident"), bcast(SMd[:, 64:72], 2, 64), False, False, [K("sm_ng"), "cst"], [pk(b2)])
            mm(psb(b2)[:, :], C("ident", True), msk("m2_" + sfx), False, True, ["cstb"], [pk(b2)])
            act(E2d, psb(b2)[:, :], AF.Exp, [pk(b2)], [K("e2")])
            yield
            b3 = fbank()
            mm(psb(b3)[:, :], C("blk"), DG3d, True, False, [K("dg3"), "cst"], [pk(b3)])
            mm(psb(b3)[:, :], C("ident"), bcast(SMd[:, 64:72], 2, 64), False, False, [K("sm_ng"), "cst"], [pk(b3)])
            mm(psb(b3)[:, :], C("ident", True), msk("m3_" + sfx), False, True, ["cstb"], [pk(b3)])
            act(E3d, psb(b3)[:, :], AF.Exp, [pk(b3)], [K("e3")])
            yield
            bkk, bqk = abank(), abank()
            for h in range(H8):
                for a in range(2):
                    ts_ = slice(64 * a, 64 * a + 64)
                    mm(psb(bkk)[ts_, h * 64:(h + 1) * 64], KTC[d][:, h, ts_], KTC[d][:, h, ts_], True, True,
                       [("ktc", d)], [pk(bkk)], tile_position=(0, 64 * a))
                    mm(psb(bqk)[ts_, h * 64:(h + 1) * 64], KTC[d][:, h, ts_], QTC[d][:, h, ts_], True, True,
                       [("ktc", d), ("qtc", d)], [pk(bqk)], tile_position=(0, 64 * a))
            A_, AT_ = PAd[0], PTd[0]
            kA, kAT = K("pa", 0), K("pt", 0)
            tt("dve", A_, psb(bkk)[:, :], E1d, ALU.mult, [pk(bkk), K("e1")], [kA])
            tt("dve", AT_, psb(bkk)[:, :], E2d, ALU.mult, [pk(bkk), K("e2")], [kAT])
            tt("dve", QKMp, psb(bqk)[:, :], E3d, ALU.mult, [pk(bqk), K("e3")], [("qkm",) + kq])
            yield

            def grp(L, R, lkey, rkey):
                bank = abank()
                for h in range(H8):
                    for a in range(2):
                        ts_ = slice(64 * a, 64 * a + 64)
                        hs = slice(h * 64, (h + 1) * 64)
                        mm(psb(bank)[ts_, hs], L[ts_, hs], R[ts_, hs], True, True, [lkey, rkey], [pk(bank)],
                           tile_position=(64 * a, 64 * a))
                return bank

            D_, DT_ = PAd[1], PTd[1]
            kD, kDT = K("pa", 1), K("pt", 1)
            X = [XBd[0], XBd[1], BEKd, BVd, XBd[2], XBd[3]]
            kX = [K("xb", 0), K("xb", 1), K("bek"), K("bv"), K("xb", 2), K("xb", 3)]
            kR = [K("rt", 0), K("rt", 1)]
            tt("pool", v3(D_), v3(A_), msk("mask8"), ALU.mult, [kA, "cstb"], [kD])
            tt("pool", v3(DT_), v3(AT_), msk("mask8"), ALU.mult, [kAT, "cstb"], [kDT])
            tt("dve", v3(X[2]), idb, v3(DT_), ALU.subtract, ["cstb", kDT], [kX[2]])
            yield
            g1 = grp(DT_, D_, kDT, kD)
            g2 = grp(D_, DT_, kD, kDT)
            cp("act", X[0], psb(g1)[:, :], [pk(g1)], [kX[0]])
            tt("dve", v3(RTd[1]), v3(X[0]), idb, ALU.add, [kX[0], "cstb"], [kR[1]])
            cp("dve", RTd[0], psb(g2)[:, :], [pk(g2)], [kR[0]])
            yield
            g3 = grp(RTd[0], X[0], kR[0], kX[0])
            tt("dve", v3(X[1]), v3(psb(g3)[:, :]), idb, ALU.add, [pk(g3), "cstb"], [kX[1]])
            g1 = grp(RTd[1], X[2], kR[1], kX[2])
            cp("act", X[3], psb(g1)[:, :], [pk(g1)], [kX[3]])
            yield
            g2 = grp(X[3], X[1], kX[3], kX[1])
            g3 = grp(X[1], X[3], kX[1], kX[3])
            cp("act", X[4], psb(g2)[:, :], [pk(g2)], [kX[4]])
            cp("dve", X[5], psb(g3)[:, :], [pk(g3)], [kX[5]])
            yield
            Tb, kT = [X[4], X[0]], [kX[4], kX[0]]
            Mb, kM = [X[5], X[1]], [kX[5], kX[1]]
            cur = 0
            for li, mname in enumerate(("moff8", "moff16", "moff32")):
                last = (li == 2)
                nxt = 1 - cur
                tt("pool", v3(D_), v3(A_), msk(mname), ALU.mult, [kA, "cstb"], [kD])
                if not last:
                    tt("pool", v3(DT_), v3(AT_), msk(mname), ALU.mult, [kAT, "cstb"], [kDT])
                g1 = grp(D_, Mb[cur], kD, kM[cur])
                cp("act", RTd[1], psb(g1)[:, :], [pk(g1)], [kR[1]])
                if not last:
                    g2 = grp(DT_, Tb[cur], kDT, kT[cur])
                    cp("dve", RTd[0], psb(g2)[:, :], [pk(g2)], [kR[0]])
                yield
                g3 = grp(Tb[cur], RTd[1], kT[cur], kR[1])
                tt("dve", Mb[nxt], Mb[cur], psb(g3)[:, :], ALU.subtract, [kM[cur], pk(g3)], [kM[nxt]])
                if not last:
                    g1 = grp(Mb[cur], RTd[0], kM[cur], kR[0])
                    tt("dve", Tb[nxt], Tb[cur], psb(g1)[:, :], ALU.subtract, [kT[cur], pk(g1)], [kT[nxt]])
                cur = nxt
                yield
            TTm = Mb[cur]
            tkey = kM[cur]
            tt("dve", v3(BEKd), v3(KTMC[d][:, :]), bcast(SMd[:, 56:64], 2, 64), ALU.mult, [("ktmc", d), K("sm_be")], [K("bek")])
            tt("pool", v3(KDECp), v3(KTMC[d][:, :]), bcast(SMd[:, 48:56], 2, 64), ALU.mult, [("ktmc", d), K("sm_ek")], [("kdec",) + kq])
            tt("pool", v3(BVd), v3(VTMC[d][:, :]), bcast(be_, 2, 64), ALU.mult, [("vtmc", d), "beta"], [K("bv")])
            yield
            pp_ = ppair()
            pkeys = [pk(2 * pp_), pk(2 * pp_ + 1)]
            bu0 = abank()
            while bu0 in (2 * pp_, 2 * pp_ + 1):
                bu0 = abank()
            for h in range(H8):
                for a in range(2):
                    ts_ = slice(64 * a, 64 * a + 64)
                    hs = slice(h * 64, (h + 1) * 64)
                    cs = slice((a * 8 + h) * 64, (a * 8 + h) * 64 + 64)
                    mm(PS[pp_][0:64, cs], BEKd[ts_, hs], TTm[ts_, hs], True, True, [K("bek"), tkey], pkeys,
                       tile_position=(64 * a, 0))
                    mm(psb(bu0)[ts_, hs], TTm[ts_, hs], BVd[ts_, hs], True, True, [tkey, K("bv")], [pk(bu0)],
                       tile_position=(64 * a, 64 * a))
            tsc("dve", NWTp, PS[pp_][0:64, :].rearrange("p (x i) -> p x i", x=16), -1.0, None, ALU.mult, None,
                pkeys, [("nwt",) + kq])
            cp("act", U0p, psb(bu0)[:, :], [pk(bu0)], [("u0",) + kq])
            yield
            pp_ = ppair()
            pkeys = [pk(2 * pp_), pk(2 * pp_ + 1)]
            for a in range(2):
                mm(PS[pp_][0:64, a * 512:(a + 1) * 512], C("half%d" % a, True)[:, 0:64], DEd, True, True,
                   [K("de"), "cstb"], pkeys)
            for a in range(2):
                tt("dve", QDECp[:, a * 8:(a + 1) * 8, :],
                   QTC[d][:, :, a * 64:(a + 1) * 64],
                   PS[pp_][0:64, a * 512:(a + 1) * 512].rearrange("p (h i) -> p h i", h=8), ALU.mult,
                   [("qtc", d)] + pkeys, [("qdec",) + kq])
            yield

        def steps(s, d, m, par, first_visit):
            NWTp, QDECp, U0p, QKMp, KDECp, EGL2p = NWT2[d][par], QDEC2[d][par], U02[d][par], QKM2[d][par], KDEC2[d][par], EGL22[d][par]
            kq = (d, par)
            SMd = DS[d]["SM"]

            def K(name, *x):
                return (name, d) + tuple(x)

            rows = slice(m * 128, (m + 1) * 128)
            for a in ((0, 1) if d == 0 else (1, 0)):
                ts_ = slice(64 * a, 64 * a + 64)
                pu = abank()
                for h in range(H8):
                    hs = slice(h * 64, (h + 1) * 64)
                    mm(psb(pu)[ts_, hs], NWTp[:, a * 8 + h, :], SBF[d][:, h, :], True, True,
                       [("nwt",) + kq, ("sbf", d)], [pk(pu)], tile_position=(0, 64 * a))
                tt("dve", UB[d][ts_, :], U0p[ts_, :], psb(pu)[ts_, :], ALU.add, [("u0",) + kq, pk(pu)], [("ub", d)])
                yield
                po, pob, pS_ = abank(), abank(), abank()
                for h in range(H8):
                    hs = slice(h * 64, (h + 1) * 64)
                    mm(psb(po)[ts_, hs], QDECp[:, a * 8 + h, :], SBF[d][:, h, :], True, True,
                       [("qdec",) + kq, ("sbf", d)], [pk(po)], tile_position=(0, 64 * a))
                    mm(psb(pob)[ts_, hs], QKMp[ts_, hs], UB[d][ts_, hs], True, True,
                       [("qkm",) + kq, ("ub", d)], [pk(pob)], tile_position=(64 * a, 64 * a))
                    mm(psb(pS_)[0:64, hs], KDECp[ts_, hs], UB[d][ts_, hs], True, True,
                       [("kdec",) + kq, ("ub", d)], [pk(pS_)], tile_position=(64 * a, 0))
                tt("dve", S32[d][:, :, :], S32[d][:, :, :], bcast(EGL2p[0:64, a * 8:a * 8 + 8], 2, 64), ALU.mult,
                   [("s32", d), ("egl2",) + kq], [("s32", d)])
                tt("dve", S32[d][:, :, :], S32[d][:, :, :], psb(pS_)[0:64, :].rearrange("p (h v) -> p h v", h=H8), ALU.add,
                   [("s32", d), pk(pS_)], [("s32", d)])
                cp("act", SBF[d][:, :, :], S32[d][:, :, :], [("s32", d)], [("sbf", d)])
                cp("act", OACC[d][ts_, :], psb(po)[ts_, :], [pk(po)], [("oacc", d)])
                tt("dve", OACC[d][ts_, :], OACC[d][ts_, :], psb(pob)[ts_, :], ALU.add, [("oacc", d), pk(pob)], [("oacc", d)])
                yield
            if first_visit:
                dma("sp", OPART[s.name][rows, :], OACC[d][:, :], [("oacc", d)], [("opart", s.name, m)])
            else:
                dma("sp", STG[d][:, :], OPART[s.name][rows, :], [("opart", s.name, m)], [f"stg{d}"])
                tt("dve", OACC[d][:, :], OACC[d][:, :], STG[d][:, :], ALU.add, [("oacc", d), f"stg{d}"], [("oacc", d)])
                tt("pool", STG[2 + d][:, :], OACC[d][:, :], OACC[d][:, :], ALU.mult, [("oacc", d)], [f"stg{2 + d}"])
                P.add("dve", lambda e: e.tensor_reduce(SMd[:, 80:88], v3(STG[2 + d][:, :]), AX.X, ALU.add),
                      [f"stg{2 + d}"], [K("sm_rn")])
                act(SMd[:, 80:88], SMd[:, 80:88], AF.Ln, [K("sm_rn")], [K("sm_rn")], bias=EPS, scale=1.0 / 64)
                act(SMd[:, 80:88], SMd[:, 80:88], AF.Exp, [K("sm_rn")], [K("sm_rn")], scale=-0.5)
                yield
                tt("dve", v3(OACC[d][:, :]), v3(OACC[d][:, :]), bcast(SMd[:, 80:88], 2, 64), ALU.mult,
                   [("oacc", d), K("sm_rn")], [("oacc", d)])
                tt("dve", v3(OACC[d][:, :]), v3(OACC[d][:, :]), bcast(dng[:], 1, H8), ALU.mult,
                   [("oacc", d), "lvec"], [("oacc", d)])
                dma("sp", STB[d][:, :], GS[s.name][rows, :], [("gs", s.name)], [f"stb{d}"])
                tt("dve", OACC[d][:, :], OACC[d][:, :], STB[d][:, :], ALU.mult, [("oacc", d), f"stb{d}"], [("oacc", d)])
                bt = fbank()
                for cc in range(4):
                    tr(psb(bt)[:, cc * 128:(cc + 1) * 128], OACC[d][:, cc * 128:(cc + 1) * 128], C("ident"),
                       [("oacc", d), "cst"], [pk(bt)])
                cp("act", STB[2 + d][:, :], psb(bt)[:, :], [pk(bt)], [f"stb{2 + d}"])
                dma("sp", OD[s.name][:, :, rows], STB[2 + d][:, :].rearrange("p (c t) -> p c t", c=4), [f"stb{2 + d}"],
                    [("od", s.name)])

        for s in seqs:
            NP_ = s.T // 128
            for d in range(2):
                if s.is_sample:
                    dma("sp", S32[d][:, :, :], st_in[l, d].rearrange("h k v -> k h v"), [], [("s32", d)])
                else:
                    P.add("pool", lambda e, d=d: e.memset(S32[d][:, :, :], 0.0), [], [("s32", d)])
                cp("act", SBF[d][:, :, :], S32[d][:, :, :], [("s32", d)], [("sbf", d)])
            visited = set()

            def mof(d, step):
                return step if d == 0 else NP_ - 1 - step

            def rr(gens):
                active = list(gens)
                while active:
                    for g_ in list(active):
                        try:
                            next(g_)
                        except StopIteration:
                            active.remove(g_)

            rr([prep(s, d, mof(d, 0), 0) for d in range(2)])
            for step in range(NP_):
                gens = []
                for d in range(2):
                    m = mof(d, step)
                    gens.append(steps(s, d, m, step % 2, m not in visited))
                for d in range(2):
                    visited.add(mof(d, step))
                if step + 1 < NP_:
                    for d in range(2):
                        gens.append(prep(s, d, mof(d, step + 1), (step + 1) % 2))
                rr(gens)
            if not s.is_sample:
                for d in range(2):
                    dma("sp", nst_out[s.idx, l, d].rearrange("h k v -> k h v"), S32[d][:, :, :], [("s32", d)], [("nst", s.idx)])
        P.fence()
        if stop_after == "scan":
            return finish()

        Wpa = WAR[:, 0:4096].rearrange("p (k n) -> p k n", k=4)
        Wpd = WAR[:, 4096:8192].rearrange("p (k n) -> p k n", k=4)
        Wo = WAR[:, 8192:16384].rearrange("p (k n) -> p k n", k=8)
        load_w(Wpa, w_pa[l], 4)
        load_w(Wpd, w_pd[l], 4)
        load_w(Wo, w_out[l], 8)
        def w3(off, n, f32, shape3):
            ap = WAR[:, off:off + n]
            if f32:
                ap = ap.bitcast(F32)
            return ap.rearrange("p (c t) -> p c t", c=shape3)

        DSET = [
            dict(HB=HB[:, :, :], GATB=GATB[:, :, :], XT=XT, X1T=X1T, S0=STG[0][:, :], S1=STG[1][:, :], R1=R1[:, :], RSTD=RSTD[:, :],
                 bpa=0, bpd=1, bop=(2, 3), bst=0, k="0"),
            dict(HB=w3(16384, 4096, False, 16), GATB=w3(20480, 4096, False, 16), XT=w3(24576, 4096, True, 8),
                 X1T=w3(28672, 4096, True, 8), S0=WAR[:, 32768:33792].bitcast(F32), S1=WAR[:, 33792:34816].bitcast(F32),
                 R1=WAR[:, 34816:35328].bitcast(F32), RSTD=WAR[:, 35328:35840].bitcast(F32),
                 bpa=4, bpd=5, bop=(6, 7), bst=4, k="1"),
        ]
        jobsD = [(s, blk) for s in seqs for blk in range(s.T // TB)]

        def blockD(j, slot):
            s, blk = jobsD[j]
            mj = s.mj
            t0 = blk * TB
            B_ = DSET[slot]
            kk_ = B_["k"]
            HBd, GATd, XTd, X1Td, S0, S1, R1d, RSTDd = B_["HB"], B_["GATB"], B_["XT"], B_["X1T"], B_["S0"], B_["S1"], B_["R1"], B_["RSTD"]
            OATd, ODTd, MGd = HBd[:, 0:4, :], HBd[:, 4:8, :], HBd[:, 8:16, :]
            khb, kht, kg, kx, kx1, ks0, ks1 = "dhb" + kk_, "dht" + kk_, "dg" + kk_, "dx" + kk_, "dx1" + kk_, "ds0" + kk_, "ds1" + kk_
            for c in range(4):
                dma("sp", OATd[:, c, :], OA[s.name][2 * c:2 * c + 2].rearrange("h d t -> (h d) t")[:, t0:t0 + TB],
                    [("oa", s.name)], [khb])
            dma("sp", ODTd, OD[s.name][:, :, t0:t0 + TB], [("od", s.name)], [khb])
            dma("sp", GATd, GATES[s.name][:, :, t0:t0 + TB], [("gates", s.name, blk)], [kg])
            dma("sp", XTd, XRES[s.name][:, :, t0:t0 + TB], [("xres", s.name, blk)], [kx])
            yield
            for n in range(8):
                ns = slice(n * 128, (n + 1) * 128)
                for k in range(4):
                    mm(psb(B_["bpa"])[:, :TB], Wpa[:, k, ns], OATd[:, k, :], k == 0, k == 3, ["war", khb], [pk(B_["bpa"])])
                for k in range(4):
                    mm(psb(B_["bpd"])[:, :TB], Wpd[:, k, ns], ODTd[:, k, :], k == 0, k == 3, ["war", khb], [pk(B_["bpd"])])
                tt("dve", S0[:, :TB], psb(B_["bpa"])[:, :TB], GATd[:, n, :], ALU.mult, [pk(B_["bpa"]), kg], [ks0])
                tt("dve", S1[:, :TB], psb(B_["bpd"])[:, :TB], GATd[:, 8 + n, :], ALU.mult, [pk(B_["bpd"]), kg], [ks1])
                tt("pool", MGd[:, n, :], S0[:, :TB], S1[:, :TB], ALU.add, [ks0, ks1], [kht])
                yield
            for n in range(8):
                ns = slice(n * 128, (n + 1) * 128)
                bk = B_["bop"][n % 2]
                for k in range(8):
                    mm(psb(bk)[:, :TB], Wo[:, k, ns], MGd[:, k, :], k == 0, k == 7, ["war", kht], [pk(bk)])
                stt("dve", X1Td[:, n, :], psb(bk)[:, :TB], MOD[:, 16 + n, mj:mj + 1], XTd[:, n, :], ALU.mult, ALU.add,
                    [pk(bk), kx] + MK, [kx1])
                yield
            dma("sp", XRES[s.name][:, :, t0:t0 + TB], X1Td, [kx1], [("xres", s.name, blk)])
            rms_stats(X1Td, TB, kx1, sq=HBd[:, 0:8, :], sqkey=khb, bank=B_["bst"], r1=R1d, rstd=RSTDd, rkey=kk_)
            yield
            tt("dve", XTd, X1Td, bcast(RSTDd, 1, 8), ALU.mult, [kx1, "rstd" + kk_], [kx])
            for c in range(8):
                if c % 2 == 0:
                    act(MGd[:, c, :], XTd[:, c, :], AF.Identity, [kx] + MK, [kht],
                        scale=A2[:, c, mj:mj + 1], bias=MOD[:, 24 + c, mj:mj + 1])
                else:
                    tsc("dve", MGd[:, c, :], XTd[:, c, :], A2[:, c, mj:mj + 1], MOD[:, 24 + c, mj:mj + 1], ALU.mult, ALU.add,
                        [kx] + MK, [kht])
            dma("sp", H2[s.name][:, :, t0:t0 + TB], MGd, [kht], [("h2", s.name, blk)])

        def two_way(make, n, stagger):
            gens = [None, None]
            nxt = 0
            started = 0
            while True:
                progressed = False
                for slot in range(2):
                    if gens[slot] is None and nxt < n and (slot == 0 or started >= stagger or nxt > 1):
                        gens[slot] = make(nxt, slot)
                        nxt += 1
                    if gens[slot] is not None:
                        try:
                            next(gens[slot])
                            progressed = True
                            if slot == 0:
                                started += 1
                        except StopIteration:
                            gens[slot] = None
                            progressed = True
                if not progressed and nxt >= n and gens[0] is None and gens[1] is None:
                    break

        two_way(blockD, len(jobsD), 9)
        P.fence()
        if stop_after == "stageD":
            return finish()

        W1h = WAR[:, 0:8 * 2048].rearrange("p (k n) -> p k n", k=8)
        W2h = WAR[:, 16384:16384 + 16 * 1024].rearrange("p (k n) -> p k n", k=16)
        for hf in range(2):
            load_w(W1h, w1[l][:, hf * 2048:(hf + 1) * 2048], 8)
            load_w(W2h, w2[l][hf * 2048:(hf + 1) * 2048, :], 16)
            jobs = [(s, blk) for s in seqs for blk in range(s.T // TB)]
            XTs = [(XT, "bigA"), (X1T, "bigB")]
            H2Ts = [(HB[:, 0:8, :], "hb"), (HB[:, 8:16, :], "ht")]

            def ef_loads(j):
                s, blk = jobs[j]
                t0 = blk * TB
                h2t, hkey = H2Ts[j % 2]
                xt, xkey = XTs[j % 2]
                dma("sp", h2t, H2[s.name][:, :, t0:t0 + TB], [("h2", s.name, blk)], [hkey])
                dma("sp", xt, XRES[s.name][:, :, t0:t0 + TB], [("xres", s.name, blk)], [xkey])

            ef_loads(0)
            for j, (s, blk) in enumerate(jobs):
                mj = s.mj
                t0 = blk * TB
                H2T, hkey = H2Ts[j % 2]
                XTj, xkey = XTs[j % 2]
                if j + 1 < len(jobs):
                    ef_loads(j + 1)
                for f in range(16):
                    bk = 1 + f % 3
                    for k in range(8):
                        mm(psb(bk)[:, :TB], W1h[:, k, f * 128:(f + 1) * 128], H2T[:, k, :], k == 0, k == 7, ["war", hkey], [pk(bk)])
                    i = nstg()
                    act(STG[i][:, :TB], psb(bk)[:, :TB], AF.Relu, [pk(bk), "lvec"], [f"stg{i}"],
                        bias=b1f[:, hf * 16 + f:hf * 16 + f + 1])
                    tt("dve" if f % 2 == 0 else "pool", GATB[:, f, :], STG[i][:, :TB], STG[i][:, :TB], ALU.mult,
                       [f"stg{i}"], ["gatb"])
                for n in range(8):
                    bk = 4 + n % 2
                    for f in range(16):
                        mm(psb(bk)[:, :TB], W2h[:, f, n * 128:(n + 1) * 128], GATB[:, f, :], f == 0, f == 15, ["war", "gatb"], [pk(bk)])
                    stt("dve", XTj[:, n, :], psb(bk)[:, :TB], MOD[:, 40 + n, mj:mj + 1], XTj[:, n, :], ALU.mult, ALU.add,
                        [pk(bk), xkey] + MK, [xkey])
                    if hf == 0:
                        tsc("dve", XTj[:, n, :], XTj[:, n, :], GB2[:, n, mj:mj + 1], None, ALU.add, None, [xkey] + MK, [xkey])
                if not (l == NLAYERS - 1 and hf == 1):
                    dma("pool", XRES[s.name][:, :, t0:t0 + TB], XTj, [xkey], [("xres", s.name, blk)])
                else:
                    rms_stats(XTj, TB, xkey, sq=GATB[:, 0:8, :], sqkey="gatb")
                    tt("dve", XTj, XTj, bcast(RSTD[:, :], 1, 8), ALU.mult, [xkey, "rstd"], [xkey])
                    tt("dve", XTj, XTj, bcast(fnf[:], 2, TB), ALU.mult, [xkey, "fnf"], [xkey])
                    dst = ys_out if s.is_sample else yp_out[s.idx * TP:(s.idx + 1) * TP, :]
                    for t2 in range(TB // 128):
                        for hh in range(2):
                            bk = 6 + hh
                            for c in range(4):
                                tr(psb(bk)[:, c * 128:(c + 1) * 128], XTj[:, hh * 4 + c, t2 * 128:(t2 + 1) * 128], C("ident"),
                                   [xkey, "cst"], [pk(bk)])
                            i = nstg()
                            cp("act" if hh == 0 else "dve", STG[i][:, :], psb(bk)[:, :], [pk(bk)], [f"stg{i}"])
                            dma("pool", dst[t0 + t2 * 128:t0 + (t2 + 1) * 128, hh * 512:(hh + 1) * 512], STG[i][:, :],
                                [f"stg{i}"], [("y", s.name)])
        P.fence()
        if stop_after == f"layer{l}":
            return finish()

    return finish()


def make_in_maps(inputs):
    cos, sin = _rope_tables()
    maps = []
    for core in range(8):
        b = core % 4
        m = {
            "xs": np.ascontiguousarray(inputs["x_sample"][b]),
            "xp": np.ascontiguousarray(inputs["x_prompt"][core * NPS:(core + 1) * NPS].reshape(NPS * TP, D)),
            "cvec": np.ascontiguousarray(np.stack([inputs["c"][b], inputs["c_ctx"]], 0)),
            "ck": np.ascontiguousarray(inputs["cache_k"][b]),
            "cv": np.ascontiguousarray(inputs["cache_v"][b]),
            "st": np.ascontiguousarray(inputs["state_delta"][b]),
            "a_log": np.ascontiguousarray(inputs["a_log"].reshape(DEPTH, 16)),
            "dt_bias": np.ascontiguousarray(inputs["dt_bias"].reshape(DEPTH, 16)),
            "cst": CST, "ropecos": cos, "ropesin": sin,
        }
        for k in ["w_mod", "b_mod", "norm1", "norm2", "w_in", "conv_w", "q_gain", "k_gain", "dn_gain",
                  "w_pa", "w_pd", "w_out", "w1", "b1", "w2", "b2", "final_norm"]:
            m[k] = np.ascontiguousarray(inputs[k])
        maps.append(m)
    return maps


_NC_CACHE = {}


def kernel(**inputs):
    inputs = {k: np.asarray(v) for k, v in inputs.items()}
    if "nc" not in _NC_CACHE:
        _NC_CACHE["nc"] = build_program()
    nc = _NC_CACHE["nc"]
    maps = make_in_maps(inputs)
    res = run_bass_kernel_spmd(nc, maps, core_ids=list(range(8)))
    r = res.results
    y_sample = np.stack([r[b]["ys"] for b in range(4)], 0)
    y_prompt = np.concatenate([r[c]["yp"].reshape(NPS, TP, D) for c in range(8)], 0)
    nk = np.concatenate([r[c]["nk"] for c in range(8)], 0)
    nv = np.concatenate([r[c]["nv"] for c in range(8)], 0)
    nst = np.concatenate([r[c]["nst"] for c in range(8)], 0)
    return (y_prompt.astype(np.float32), y_sample.astype(np.float32), nk.astype(np.float32),
            nv.astype(np.float32), nst.astype(np.float32))
```

```python
import numpy as np
from contextlib import ExitStack
import concourse.bass as bass
import concourse.mybir as mybir
from concourse.bass_utils import run_bass_kernel_spmd

F32 = mybir.dt.float32
BF16 = mybir.dt.bfloat16
AF = mybir.ActivationFunctionType
ALU = mybir.AluOpType
AX = mybir.AxisListType

D = 1024
DEPTH = 2
TS = 4096
TP = 256
NPS = 4
NCTX = 256
INW = 4896
DFF = 4096
EPS = 1e-6
NEG = -30000.0

C_QA, C_KA, C_VA, C_QD, C_KD, C_VD, C_GO, C_AI, C_BI, C_GA, C_GD = 0, 512, 640, 768, 1280, 1792, 2304, 2816, 2832, 2848, 3872


ENGS = ["pe", "act", "dve", "pool", "sp"]
N_DMA_SEMS = {"sp": 40, "pool": 16, "act": 8}
EPOCH = 30000


class Ev:
    __slots__ = ("dma", "eng", "idx", "sem", "val")

    def __init__(self, dma, eng, idx, sem=None, val=None):
        self.dma, self.eng, self.idx, self.sem, self.val = dma, eng, idx, sem, val


class Rec:
    __slots__ = ("eng", "fn", "waits", "signal", "idx", "dma", "sig_sem", "sig_val")

    def __init__(self, eng, fn):
        self.eng, self.fn = eng, fn
        self.waits = []
        self.signal = False
        self.dma = None


class Buf:
    __slots__ = ("w", "r")

    def __init__(self):
        self.w = None
        self.r = []


class Prog:
    def __init__(self, nc):
        self.nc = nc
        self.ops = {e: [] for e in ENGS}
        self.waited = {e: {p: -1 for p in ENGS} for e in ENGS}
        self.waited_dma = {e: {} for e in ENGS}
        self.bufs = {}
        self.dma_count = {q: 0 for q in N_DMA_SEMS}

    def buf(self, k):
        b = self.bufs.get(k)
        if b is None:
            b = self.bufs[k] = Buf()
        return b

    def add(self, eng, fn, reads=(), writes=(), dma=False):
        rec = Rec(eng, fn)
        rec.idx = len(self.ops[eng])
        deps = []
        for k in reads:
            b = self.buf(k)
            if b.w is not None:
                deps.append((b.w, True))
        for k in writes:
            b = self.buf(k)
            if b.w is not None:
                deps.append((b.w, False))
            for r in b.r:
                deps.append((r, False))
        if dma:
            d = self.dma_count[eng]
            n = N_DMA_SEMS[eng]
            si, val = d % n, 16 * (d // n + 1)
            if d >= n:
                deps.append((Ev(True, eng, None, si, val - 16), True))
            rec.dma = (si, val)
            ev = Ev(True, eng, rec.idx, si, val)
            self.dma_count[eng] += 1
        else:
            ev = Ev(False, eng, rec.idx)
        for dep, raw in deps:
            if dep.dma:
                key = (dep.eng, dep.sem)
                if self.waited_dma[eng].get(key, 0) >= dep.val:
                    continue
                self.waited_dma[eng][key] = dep.val
                rec.waits.append(dep)
            else:
                if dep.eng == eng and eng == "pe":
                    continue
                if self.waited[eng][dep.eng] >= dep.idx:
                    continue
                self.waited[eng][dep.eng] = dep.idx
                rec.waits.append(dep)
                self.ops[dep.eng][dep.idx].signal = True
        for k in reads:
            self.buf(k).r.append(ev)
        for k in writes:
            b = self.buf(k)
            b.w = ev
            b.r = []
        self.ops[eng].append(rec)
        return rec

    def fence(self):
        last = {e: len(self.ops[e]) - 1 for e in ["pe", "act", "dve", "pool"]}
        dma_evs = []
        for q, n in N_DMA_SEMS.items():
            d = self.dma_count[q]
            for i in range(min(n, d)):
                uses = (d - i + n - 1) // n
                dma_evs.append(Ev(True, q, None, i, 16 * uses))
        for e in ENGS:
            rec = Rec(e, lambda eng: eng.nop())
            rec.idx = len(self.ops[e])
            for p, li in last.items():
                if p == e or li < 0:
                    continue
                j = li
                while j >= 0 and self.ops[p][j].dma is not None:
                    j -= 1
                if j < 0 or self.waited[e][p] >= j:
                    continue
                self.waited[e][p] = j
                rec.waits.append(Ev(False, p, j))
                self.ops[p][j].signal = True
            for dep in dma_evs:
                key = (dep.eng, dep.sem)
                if self.waited_dma[e].get(key, 0) >= dep.val:
                    continue
                self.waited_dma[e][key] = dep.val
                rec.waits.append(dep)
            self.ops[e].append(rec)

    def emit(self, es):
        nc = self.nc
        comp_sems = {}
        for e in ["pe", "act", "dve", "pool"]:
            cnt = 0
            for rec in self.ops[e]:
                if rec.signal and rec.dma is None:
                    ep = cnt // EPOCH
                    if (e, ep) not in comp_sems:
                        comp_sems[(e, ep)] = es.enter_context(nc.semaphore(f"c_{e}_{ep}"))
                    rec.sig_sem = comp_sems[(e, ep)]
                    rec.sig_val = cnt % EPOCH + 1
                    cnt += 1
        dma_sems = {}
        for q, n in N_DMA_SEMS.items():
            for i in range(min(n, self.dma_count[q])):
                dma_sems[(q, i)] = es.enter_context(nc.semaphore(f"d_{q}_{i}"))
        final_waits = []
        for q, n in N_DMA_SEMS.items():
            d = self.dma_count[q]
            for i in range(min(n, d)):
                uses = (d - i + n - 1) // n
                final_waits.append((dma_sems[(q, i)], 16 * uses))
        block = es.enter_context(nc.Block())
        ops = self.ops

        def run(engname, eng):
            for rec in ops[engname]:
                for dep in rec.waits:
                    if dep.dma:
                        eng.wait_ge(dma_sems[(dep.eng, dep.sem)], dep.val)
                    else:
                        prod = ops[dep.eng][dep.idx]
                        eng.wait_ge(prod.sig_sem, prod.sig_val)
                ins = rec.fn(eng)
                if rec.dma is not None:
                    ins.then_inc(dma_sems[(engname, rec.dma[0])], 16)
                elif rec.signal:
                    ins.then_inc(rec.sig_sem, 1)
            if engname == "sp":
                for s, v in final_waits:
                    eng.wait_ge(s, v)

        @block.tensor
        def _(t):
            run("pe", t)

        @block.scalar
        def _(a):
            run("act", a)

        @block.vector
        def _(v):
            run("dve", v)

        @block.gpsimd
        def _(g):
            run("pool", g)

        @block.sync
        def _(s):
            run("sp", s)


CST_LAYOUT = {}


def _build_consts():
    p = np.arange(128)
    cols = []
    off = 0

    def put(name, arr):
        nonlocal off
        arr = np.asarray(arr, np.float32).reshape(128, -1)
        CST_LAYOUT[name] = (off, arr.shape[1])
        cols.append(arr)
        off += arr.shape[1]

    put("ident", np.eye(128))
    put("ones", np.ones((128, 128)))
    put("negones", -np.ones((128, 128)))
    half = p // 64
    put("blk", (half[:, None] == half[None, :]).astype(np.float32))
    put("negblk", -(half[:, None] == half[None, :]).astype(np.float32))
    put("identst", (p[:, None] % 64 == np.arange(64)[None, :]).astype(np.float32))
    same = half[:, None] == half[None, :]
    put("tri_f", (same & (p[:, None] <= p[None, :])).astype(np.float32))
    put("tri_b", (same & (p[:, None] >= p[None, :])).astype(np.float32))
    put("half0", np.repeat((p < 64).astype(np.float32)[:, None], 128, 1))
    put("half1", np.repeat((p >= 64).astype(np.float32)[:, None], 128, 1))
    il = p % 64
    j = np.arange(64)

    def m(keep):
        return np.where(keep, 0.0, NEG).astype(np.float32)

    put("m1_f", m(il[:, None] > j[None, :]))
    put("m1_b", m(il[:, None] < j[None, :]))
    put("m2_f", m(j[None, :] > il[:, None]))
    put("m2_b", m(j[None, :] < il[:, None]))
    put("m3_f", m(j[None, :] >= il[:, None]))
    put("m3_b", m(j[None, :] <= il[:, None]))
    put("mask8", ((il[:, None] // 8) == (j[None, :] // 8)).astype(np.float32))
    for sz in (8, 16, 32):
        put("moff%d" % sz, (((il[:, None] // (2 * sz)) == (j[None, :] // (2 * sz)))
                            & ((il[:, None] // sz) != (j[None, :] // sz))).astype(np.float32))
    R = np.zeros((128, 128), np.float32)
    for q in range(128):
        if q % 64 < 32:
            R[q, q + 32] = -1.0
        else:
            R[q, q - 32] = 1.0
    put("rot", R.T)
    return np.concatenate(cols, 1)


CST = _build_consts()
NCST = CST.shape[1]


def _rope_tables():
    t = np.arange(TS)
    row = (t // 64).astype(np.float32)
    col = (t % 64).astype(np.float32)
    inv = (10000.0 ** (-np.arange(16, dtype=np.float32) / 16)).astype(np.float32)
    ang = np.concatenate([row[:, None] * inv, col[:, None] * inv], -1).astype(np.float32)
    cos = np.cos(ang).astype(np.float32).T
    sin = np.sin(ang).astype(np.float32).T
    return np.tile(cos, (4, 1)).copy(), np.tile(sin, (4, 1)).copy()


class Seq:
    def __init__(self, name, T, is_sample, idx, key0, tile0):
        self.name, self.T, self.is_sample, self.idx = name, T, is_sample, idx
        self.key0 = key0
        self.tile0 = tile0
        self.nctx = NCTX if is_sample else 0
        self.mj = 0 if is_sample else 1


def bcast(ap, axis, n):
    shp = list(ap.shape)
    shp.insert(axis, n)
    return ap.unsqueeze(axis).broadcast_to(shp)


def build_program(debug_outs=(), stop_after=None):
    nc = bass.Bass("TRN2", target_bir_lowering=False)
    es = ExitStack()
    P = Prog(nc)
    dbg = set(debug_outs)

    def din(name, shape, dt=F32):
        return nc.dram_tensor(name, list(shape), dt, kind="ExternalInput").ap()

    def dout(name, shape, dt=F32):
        return nc.dram_tensor(name, list(shape), dt, kind="ExternalOutput").ap()

    def dscr(name, shape, dt=F32):
        kind = "ExternalOutput" if name in dbg else "Internal"
        return nc.dram_tensor(name, list(shape), dt, kind=kind).ap()

    xs_in = din("xs", [TS, D])
    xp_in = din("xp", [NPS * TP, D])
    cvec = din("cvec", [2, D])
    ck_in = din("ck", [DEPTH, 2, NCTX, 64])
    cv_in = din("cv", [DEPTH, 2, NCTX, 64])
    st_in = din("st", [DEPTH, 2, 8, 64, 64])
    w_mod = din("w_mod", [DEPTH, D, 6 * D])
    b_mod = din("b_mod", [DEPTH, 6 * D])
    norm1 = din("norm1", [DEPTH, D])
    norm2 = din("norm2", [DEPTH, D])
    w_in = din("w_in", [DEPTH, D, INW])
    conv_w = din("conv_w", [DEPTH, 3, 1536])
    q_gain = din("q_gain", [DEPTH, 64])
    k_gain = din("k_gain", [DEPTH, 64])
    a_log = din("a_log", [DEPTH, 16])
    dt_bias = din("dt_bias", [DEPTH, 16])
    dn_gain = din("dn_gain", [DEPTH, 64])
    w_pa = din("w_pa", [DEPTH, 512, D])
    w_pd = din("w_pd", [DEPTH, 512, D])
    w_out = din("w_out", [DEPTH, D, D])
    w1 = din("w1", [DEPTH, D, DFF])
    b1 = din("b1", [DEPTH, DFF])
    w2 = din("w2", [DEPTH, DFF, D])
    b2 = din("b2", [DEPTH, D])
    final_norm = din("final_norm", [D])
    cst_in = din("cst", [128, NCST])
    cos_in = din("ropecos", [128, TS])
    sin_in = din("ropesin", [128, TS])
    ys_out = dout("ys", [TS, D])
    yp_out = dout("yp", [NPS * TP, D])
    nk_out = dout("nk", [NPS, DEPTH, 2, TP, 64])
    nv_out = dout("nv", [NPS, DEPTH, 2, TP, 64])
    nst_out = dout("nst", [NPS, DEPTH, 2, 8, 64, 64])

    seqs = [Seq("s", TS, True, 0, 0, 0)]
    for i in range(NPS):
        seqs.append(Seq(f"p{i}", TP, False, i, NCTX + TS + i * TP, TS // 128 + i * (TP // 128)))
    NKEY = NCTX + TS + NPS * TP
    NTILE = TS // 128 + NPS * TP // 128
    NVT = NKEY // 128

    XRES, PRE, QA, GATES, GS, QDT, KDT, QTM, KTM, VTM, OPART, OA, OD, X1, H2, ACTS = ({} for _ in range(16))
    for s in seqs:
        T = s.T
        XRES[s.name] = dscr(f"xres_{s.name}", [128, 8, T])
        PRE[s.name] = dscr(f"pre_{s.name}", [128, 12, T + 2])
        QA[s.name] = dscr(f"qa_{s.name}", [8, 64, T], BF16)
        GATES[s.name] = dscr(f"gates_{s.name}", [128, 16, T], BF16)
        GS[s.name] = dscr(f"gs_{s.name}", [T, 512], BF16)
        QDT[s.name] = dscr(f"qdt_{s.name}", [8, 64, T], BF16)
        KDT[s.name] = dscr(f"kdt_{s.name}", [8, 64, T], BF16)
        QTM[s.name] = dscr(f"qtm_{s.name}", [T, 512], BF16)
        KTM[s.name] = dscr(f"ktm_{s.name}", [T, 512], BF16)
        VTM[s.name] = dscr(f"vtm_{s.name}", [T, 512], BF16)
        OPART[s.name] = dscr(f"opart_{s.name}", [T, 512])
        OA[s.name] = dscr(f"oa_{s.name}", [8, 64, T], BF16)
        OD[s.name] = dscr(f"od_{s.name}", [128, 4, T], BF16)
        H2[s.name] = dscr(f"h2_{s.name}", [128, 8, T], BF16)

    def sb(name, shape, dt=F32):
        return es.enter_context(nc.sbuf_tensor("sb_" + name, list(shape), dt))

    cst = sb("cst", [128, NCST])
    cstb = sb("cstb", [128, NCST], BF16)

    def C(name, bf=False):
        o, n = CST_LAYOUT[name]
        return (cstb if bf else cst)[:, o:o + n]

    WAR = sb("warena", [128, 40960], BF16)
    KT_all = sb("kt_all", [128, NKEY], BF16)
    VA_all = sb("va_all", [128, NVT, 2, 65], BF16)
    LA = sb("la", [128, NTILE, 16])
    LB = sb("lb", [128, NTILE, 16])
    BETA = sb("beta", [128, NTILE, 16])
    MOD = sb("mod", [128, 48, 2])
    A1 = sb("a1", [128, 8, 2])
    A2 = sb("a2", [128, 8, 2])
    GB2 = sb("gb2", [128, 8, 2])
    n1f = sb("n1f", [128, 8])
    n2f = sb("n2f", [128, 8])
    fnf = sb("fnf", [128, 8])
    b2f = sb("b2f", [128, 8])
    b1f = sb("b1f", [128, 32])
    bmf = sb("bmf", [128, 48])
    cfm = sb("cfm", [128, 8, 2])
    scb = sb("scb", [128, 8, 2], BF16)
    qg = sb("qg", [128, 1])
    kg = sb("kg", [128, 1])
    cw = sb("cw", [128, 3, 12])
    dtb = sb("dtb", [128, 16])
    negA = sb("negA", [128, 16])
    dng = sb("dng", [128, 64])
    TB = 256
    BIGA = sb("bigA", [128, 12 * 258])
    BIGB = sb("bigB", [128, 12 * 256])
    HB = sb("hb16", [128, 16, 256], BF16)
    GATB = sb("gatb", [128, 16, 256], BF16)
    R1 = sb("r1", [128, 256])
    RSTD = sb("rstd", [128, 256])
    STG = [sb(f"stg{i}", [128, 512]) for i in range(4)]
    STB = [sb(f"stb{i}", [128, 512], BF16) for i in range(4)]
    COSB = sb("cosb", [128, 256])
    SINB = sb("sinb", [128, 256])
    SMALL = sb("small", [128, 64])
    SM = sb("sm", [128, 160])
    VAB = sb("vab", [128, 160])
    KTC = [sb(f"ktc{i}", [64, 8, 128], BF16) for i in range(2)]
    QTC = [sb(f"qtc{i}", [64, 8, 128], BF16) for i in range(2)]
    KTMC = [sb(f"ktmc{i}", [128, 512], BF16) for i in range(2)]
    QTMC = [sb(f"qtmc{i}", [128, 512], BF16) for i in range(2)]
    VTMC = [sb(f"vtmc{i}", [128, 512], BF16) for i in range(2)]
    def v512(big, i, p0=0, p1=128):
        return big[p0:p1, i * 512:(i + 1) * 512]

    E1, E2, E3, DG1, DG3 = (v512(BIGA, i) for i in range(5))
    U0 = [v512(BIGA, 5), v512(BIGB, 0)]
    OACC = [v512(BIGB, 1), v512(BIGB, 2)]
    RS = v512(BIGB, 3)
    S32 = [v512(BIGB, 4 + i, 0, 64).rearrange("p (h v) -> p h v", h=8) for i in range(2)]
    HBf = HB[:, :, :].rearrange("p a b -> p (a b)")
    GBf = GATB[:, :, :].rearrange("p a b -> p (a b)")
    PA = [v512(HBf, 0), v512(HBf, 1)]
    PT = [v512(HBf, 2), v512(HBf, 3)]
    RT = [v512(HBf, 4), v512(HBf, 5)]
    QKM = [v512(HBf, 6), v512(HBf, 7)]
    BEK = v512(GBf, 0)
    KDEC = [v512(GBf, 1), v512(GBf, 2)]
    BV = v512(GBf, 3)
    UB = [v512(GBf, 4), v512(GBf, 5)]
    DE = v512(GBf, 6)
    OB = v512(GBf, 7, 0, 64)
    XBW = [sb(f"xbw{i}", [128, 1024], BF16) for i in range(2)]
    XB = [XBW[0][:, 0:512], XBW[0][:, 512:1024], XBW[1][:, 0:512], XBW[1][:, 512:1024]]
    NWT = [sb(f"nwt{i}", [64, 16, 64], BF16) for i in range(2)]
    QDEC = [sb(f"qdec{i}", [64, 16, 64], BF16) for i in range(2)]
    SBF = [sb(f"sbf{i}", [64, 8, 64], BF16) for i in range(2)]
    EGL2 = [sb(f"egl2{i}", [128, 16]) for i in range(2)]
    QB = STB[3][:, :].rearrange("p (j t) -> p j t", j=4)
    PS = [es.enter_context(nc.psum_tensor(f"ps{i}", [128, 1024], F32)) for i in range(4)]
    dbg_mod = dscr("dbg_mod", [128, 48, 2])
    dbg_la = dscr("dbg_la", [128, NTILE, 16])
    dbg_lb = dscr("dbg_lb", [128, NTILE, 16])
    dbg_beta = dscr("dbg_beta", [128, NTILE, 16])

    XT = BIGA[:, 0:8 * 256].rearrange("p (c t) -> p c t", c=8)
    X1T = BIGB[:, 0:8 * 256].rearrange("p (c t) -> p c t", c=8)
    PRET = BIGA[:, :].rearrange("p (c t) -> p c t", c=12)
    CV = BIGB[:, :].rearrange("p (c t) -> p c t", c=12)
    SQ = HB[:, 0:8, :]
    HT = HB[:, 8:16, :]
    XTM = [BIGA[:, i * 1024:(i + 1) * 1024] for i in range(2)]
    XFM = [BIGB[:, i * 1024:(i + 1) * 1024].rearrange("p (c t) -> p c t", c=8) for i in range(2)]

    def psb(i):
        return PS[i // 2][:, (i % 2) * 512:(i % 2) * 512 + 512]

    def pk(i):
        return ("ps", i)

    def dma(q, out, in_, reads, writes, **kw):
        P.add(q, lambda e: e.dma_start(out=out, in_=in_, **kw), reads, writes, dma=True)

    def mm(out, lhsT, rhs, start, stop, reads, writes, **kw):
        P.add("pe", lambda e: e.matmul(out, lhsT, rhs, start=start, stop=stop, **kw), reads, writes)

    def tr(out, in_, ident, reads, writes):
        P.add("pe", lambda e: e.transpose(out, in_, ident), reads, writes)

    def act(out, in_, func, reads, writes, **kw):
        P.add("act", lambda e: e.activation(out, in_, func, **kw), reads, writes)

    def tt(eng, out, in0, in1, op, reads, writes):
        P.add(eng, lambda e: e.tensor_tensor(out, in0, in1, op), reads, writes)

    def tsc(eng, out, in0, s1, s2, op0, op1, reads, writes):
        if op1 is None:
            P.add(eng, lambda e: e.tensor_scalar(out, in0, s1, None, op0), reads, writes)
        else:
            P.add(eng, lambda e: e.tensor_scalar(out, in0, s1, s2, op0, op1), reads, writes)

    def stt(eng, out, in0, scalar, in1, op0, op1, reads, writes):
        P.add(eng, lambda e: e.scalar_tensor_tensor(out, in0, scalar, in1, op0, op1), reads, writes)

    def cp(eng, out, in_, reads, writes):
        if eng == "act":
            P.add("act", lambda e: e.copy(out, in_), reads, writes)
        else:
            P.add(eng, lambda e: e.tensor_copy(out, in_), reads, writes)

    def recip(out, in_, reads, writes):
        P.add("dve", lambda e: e.reciprocal(out, in_), reads, writes)

    def load_w(dst3, src2, K, wkey="war"):
        for k in range(K):
            dma("pool", dst3[:, k, :], src2[k * 128:(k + 1) * 128, :], [], [wkey])

    def rms_stats(src3, nt, srckey, sq=None, sqkey="hb", bank=0, r1=None, rstd=None, rkey=""):
        if sq is None:
            sq = SQ
        if r1 is None:
            r1, rstd = R1[:, :], RSTD[:, :]
        act(sq[:, :, :nt], src3, AF.Square, [srckey], [sqkey])
        for c in range(8):
            mm(psb(bank)[:, :nt], C("ones", True), sq[:, c, :nt], c == 0, c == 7, [sqkey, "cstb"], [pk(bank)])
        act(r1[:, :nt], psb(bank)[:, :nt], AF.Ln, [pk(bank)], ["r1" + rkey], bias=EPS, scale=1.0 / D)
        act(rstd[:, :nt], r1[:, :nt], AF.Exp, ["r1" + rkey], ["rstd" + rkey], scale=-0.5)

    dma("sp", cst[:], cst_in, [], ["cst"])
    cp("dve", cstb[:], cst[:], ["cst"], ["cstb"])
    P.add("pool", lambda e: e.memset(VA_all[:], 1.0), [], ["va"])
    dma("sp", fnf[:], final_norm.rearrange("(k p) -> p k", p=128), [], ["fnf"], allow_slow_non_contiguous=True)
    for j in range(2):
        dma("sp", cfm[:, :, j], cvec[j].rearrange("(k p) -> p k", p=128), [], ["cfm"], allow_slow_non_contiguous=True)
    act(scb[:], cfm[:], AF.Silu, ["cfm"], ["scb"])
    P.add("pool", lambda e: e.memset(SMALL[:], 0.0), [], ["small"])
    for s in seqs:
        for col in (0, s.T + 1):
            dma("sp", PRE[s.name][:, :, col:col + 1], SMALL[:, 0:12].unsqueeze(2), ["small"], [("pre", s.name, "pad", col)],
                allow_slow_non_contiguous=True)

    it = 0
    for s in seqs:
        src = xs_in if s.is_sample else xp_in[s.idx * TP:(s.idx + 1) * TP, :]
        for ti in range(s.T // 128):
            b = it % 2
            dma("sp", XTM[b], src[ti * 128:(ti + 1) * 128, :], [], ["bigA"])
            for c in range(8):
                tr(PS[b][:, c * 128:(c + 1) * 128], XTM[b][:, c * 128:(c + 1) * 128], C("ident"),
                   ["bigA", "cst"], [pk(2 * b), pk(2 * b + 1)])
            cp("act" if it % 2 == 0 else "dve", XFM[b], PS[b][:].rearrange("p (c t) -> p c t", c=8),
               [pk(2 * b), pk(2 * b + 1)], ["bigB"])
            dma("pool", XRES[s.name][:, :, ti * 128:(ti + 1) * 128], XFM[b], ["bigB"], [("xres", s.name, ti // 2)])
            it += 1

    def finish():
        P.emit(es)
        es.close()
        return nc

    if stop_after == "stage0":
        return finish()

    NLAYERS = DEPTH
    for l in range(NLAYERS):
        for (t_, src) in ((n1f, norm1[l]), (n2f, norm2[l]), (b2f, b2[l])):
            dma("sp", t_[:], src.rearrange("(k p) -> p k", p=128), [], ["lvec"], allow_slow_non_contiguous=True)
        dma("sp", b1f[:], b1[l].rearrange("(k p) -> p k", p=128), [], ["lvec"], allow_slow_non_contiguous=True)
        dma("sp", bmf[:], b_mod[l].rearrange("(k p) -> p k", p=128), [], ["lvec"], allow_slow_non_contiguous=True)
        for hh in range(2):
            dma("sp", qg[64 * hh:64 * hh + 64, :], q_gain[l].rearrange("(d o) -> d o", o=1), [], ["lvec"], allow_slow_non_contiguous=True)
            dma("sp", kg[64 * hh:64 * hh + 64, :], k_gain[l].rearrange("(d o) -> d o", o=1), [], ["lvec"], allow_slow_non_contiguous=True)
        for j in range(3):
            dma("sp", cw[:, j, :], conv_w[l, j].rearrange("(c p) -> p c", p=128), [], ["lvec"], allow_slow_non_contiguous=True)
        dma("sp", dtb[:], dt_bias[l:l + 1, :].broadcast_to([128, 16]), [], ["lvec"], allow_slow_non_contiguous=True)
        dma("sp", negA[:], a_log[l:l + 1, :].broadcast_to([128, 16]), [], ["lvec"], allow_slow_non_contiguous=True)
        dma("sp", dng[:], dn_gain[l:l + 1, :].broadcast_to([128, 64]), [], ["lvec"], allow_slow_non_contiguous=True)
        act(negA[:], negA[:], AF.Exp, ["lvec"], ["lvec2"])
        tsc("dve", negA[:], negA[:], -1.0, None, ALU.mult, None, ["lvec2"], ["lvec2"])

        Wm = WAR[:, 0:8 * 3072].rearrange("p (k n) -> p k n", k=8)
        for hh in range(2):
            load_w(Wm, w_mod[l][:, hh * 3072:(hh + 1) * 3072], 8)
            for n in range(24):
                nn = hh * 24 + n
                for k in range(8):
                    mm(psb(0)[:, nn * 2:nn * 2 + 2], Wm[:, k, n * 128:(n + 1) * 128], scb[:, k, :], k == 0, k == 7,
                       ["war", "scb"], [pk(0)])
        tt("dve", MOD[:], psb(0)[:, 0:96].rearrange("p (n j) -> p n j", j=2), bcast(bmf[:], 2, 2), ALU.add,
           [pk(0), "lvec"], ["mod"])
        stt("dve", A1[:], MOD[:, 8:16, :], 1.0, bcast(n1f[:], 2, 2), ALU.add, ALU.mult, ["mod", "lvec"], ["mod2"])
        stt("dve", A2[:], MOD[:, 32:40, :], 1.0, bcast(n2f[:], 2, 2), ALU.add, ALU.mult, ["mod", "lvec"], ["mod2"])
        tt("dve", GB2[:], MOD[:, 40:48, :], bcast(b2f[:], 2, 2), ALU.mult, ["mod", "lvec"], ["mod2"])
        MK = ["mod", "mod2", "lvec", "lvec2"]
        if stop_after == "adaln":
            dma("sp", dbg_mod, MOD[:], ["mod"], ["dbgmod"])
            return finish()

        Win = WAR[:, 0:8 * INW].rearrange("p (k n) -> p k n", k=8)
        load_w(Win, w_in[l], 8)
        bank_rr = [0]

        def next_bank():
            bank_rr[0] = (bank_rr[0] % 4) + 1
            return bank_rr[0]

        stg_rr = [0]

        def nstg():
            stg_rr[0] = (stg_rr[0] + 1) % 4
            return stg_rr[0]

        jobsA = [(s, blk) for s in seqs for blk in range(s.T // TB)]
        XTsA = [(XT, "bigA"), (X1T, "bigB")]
        HTsA = [(HB[:, 8:16, :], "ht"), (GATB[:, 0:8, :], "gatb")]

        def blockA(j):
            s, blk = jobsA[j]
            mj = s.mj
            t0 = blk * TB
            XTj, xkey = XTsA[j % 2]
            HTj, hkey = HTsA[j % 2]
            dma("sp", XTj, XRES[s.name][:, :, t0:t0 + TB], [("xres", s.name, blk)], [xkey])
            if s.is_sample:
                dma("sp", COSB[:], cos_in[:, t0:t0 + TB], [], ["cosb"])
                dma("sp", SINB[:], sin_in[:, t0:t0 + TB], [], ["sinb"])
            rms_stats(XTj, TB, xkey)
            tt("dve", XTj, XTj, bcast(RSTD[:, :], 1, 8), ALU.mult, [xkey, "rstd"], [xkey])
            for c in range(8):
                if c % 2 == 0:
                    act(HTj[:, c, :], XTj[:, c, :], AF.Identity, [xkey] + MK, [hkey],
                        scale=A1[:, c, mj:mj + 1], bias=MOD[:, c, mj:mj + 1])
                else:
                    tsc("dve", HTj[:, c, :], XTj[:, c, :], A1[:, c, mj:mj + 1], MOD[:, c, mj:mj + 1], ALU.mult, ALU.add,
                        [xkey] + MK, [hkey])

            yield

            def fm_chunk(col0):
                bk = next_bank()
                for k in range(8):
                    mm(psb(bk)[:, :TB], Win[:, k, col0:col0 + 128], HTj[:, k, :], k == 0, k == 7, ["war", hkey], [pk(bk)])
                return bk

            def qk_epi(c, bk):
                is_k = (c == 4)
                gain = kg if is_k else qg
                cp("act", STG[0][:, :TB], psb(bk)[:, :TB], [pk(bk)], ["stg0"])
                act(STB[0][:, :TB], psb(bk)[:, :TB], AF.Square, [pk(bk)], ["stb0"])
                yield
                mm(psb(6)[:, :TB], C("blk", True), STB[0][:, :TB], True, True, ["stb0", "cstb"], [pk(6)])
                act(STG[1][:, :TB], psb(6)[:, :TB], AF.Ln, [pk(6)], ["stg1"], bias=EPS, scale=1.0 / 64)
                act(STG[1][:, :TB], STG[1][:, :TB], AF.Exp, ["stg1"], ["stg1"], scale=-0.5)
                stt("dve", STG[0][:, :TB], STG[0][:, :TB], gain[:, 0:1], STG[1][:, :TB], ALU.mult, ALU.mult,
                    ["stg0", "stg1", "lvec"], ["stg0"])
                yield
                kcol = s.key0 + s.nctx + t0
                dst = KT_all[:, kcol:kcol + TB] if is_k else STB[1][:, :TB]
                dkey = "kt" if is_k else "stb1"
                if s.is_sample:
                    mm(psb(7)[:, :TB], C("rot"), STG[0][:, :TB], True, True, ["stg0", "cst"], [pk(7)])
                    tt("dve", STG[2][:, :TB], STG[0][:, :TB], COSB[:], ALU.mult, ["stg0", "cosb"], ["stg2"])
                    yield
                    tt("dve", STG[3][:, :TB], psb(7)[:, :TB], SINB[:], ALU.mult, [pk(7), "sinb"], ["stg3"])
                    tt("dve", dst, STG[2][:, :TB], STG[3][:, :TB], ALU.add, ["stg2", "stg3"], [dkey])
                else:
                    cp("dve", dst, STG[0][:, :TB], ["stg0"], [dkey])
                if not is_k:
                    dma("pool", QA[s.name][2 * c:2 * c + 2].rearrange("h d t -> (h d) t")[:, t0:t0 + TB], STB[1][:, :TB],
                        ["stb1"], [("qa", s.name)])
                elif not s.is_sample:
                    for t2 in range(TB // 128):
                        tr(psb(7)[:, t2 * 128:(t2 + 1) * 128], STG[0][:, t2 * 128:(t2 + 1) * 128], C("ident"),
                           ["stg0", "cst"], [pk(7)])
                    cp("act", STG[2][:, :TB], psb(7)[:, :TB], [pk(7)], ["stg2"])
                    for t2 in range(TB // 128):
                        for g in range(2):
                            dma("pool", nk_out[s.idx, l, g, t0 + t2 * 128:t0 + (t2 + 1) * 128, :],
                                STG[2][:, t2 * 128 + g * 64:t2 * 128 + g * 64 + 64], ["stg2"], [("nk", s.idx)])
                yield

            hrr = [0]

            def nh():
                hrr[0] = (hrr[0] + 1) % 4
                return hrr[0]

            def filler():
                for c in range(12):
                    bk = fm_chunk(C_QD + c * 128)
                    i = nh()
                    cp("act" if c % 2 == 0 else "dve", STG[i][:, 256:512], psb(bk)[:, :TB], [pk(bk)], [f"stgh{i}"])
                    dma("pool", PRE[s.name][:, c, 1 + t0:1 + t0 + TB], STG[i][:, 256:512], [f"stgh{i}"], [("pre", s.name, blk)])
                    yield
                for c in range(16):
                    bk = fm_chunk(C_GA + c * 128)
                    i = nh()
                    act(STG[i][:, 256:512], psb(bk)[:, :TB], AF.Exp, [pk(bk)], [f"stgh{i}"], scale=-1.0)
                    act(STG[i][:, 256:512], STG[i][:, 256:512], AF.Ln, [f"stgh{i}"], [f"stgh{i}"], bias=1.0)
                    act(STB[i][:, 256:512], STG[i][:, 256:512], AF.Exp, [f"stgh{i}"], [f"stbh{i}"], scale=-1.0)
                    dma("pool", GATES[s.name][:, c, t0:t0 + TB], STB[i][:, 256:512], [f"stbh{i}"], [("gates", s.name, blk)])
                    yield

            fg = filler()
            for c in range(5):
                bk = fm_chunk(C_QA + c * 128)
                for _ in qk_epi(c, bk):
                    next(fg, None)
            yield
            for _ in fg:
                pass
            for t2 in range(TB // 128):
                tsl = slice(t2 * 128, (t2 + 1) * 128)
                gti = s.tile0 + (t0 // 128) + t2
                vt = (s.key0 + s.nctx + t0) // 128 + t2
                for k in range(8):
                    mm(psb(5)[:, 0:512], HTj[:, k, tsl], Win[:, k, C_GO:C_GO + 512], k == 0, k == 7, ["war", hkey], [pk(5)])
                for k in range(8):
                    mm(psb(6)[:, 0:128], HTj[:, k, tsl], Win[:, k, C_VA:C_VA + 128], k == 0, k == 7, ["war", hkey], [pk(6)])
                for k in range(8):
                    mm(psb(6)[:, 128:160], HTj[:, k, tsl], Win[:, k, C_AI:C_AI + 32], k == 0, k == 7, ["war", hkey], [pk(6)])
                i = nstg()
                act(STB[i][:, :], psb(5)[:, :], AF.Silu, [pk(5)], [f"stb{i}", f"stbh{i}"])
                dma("pool", GS[s.name][t0 + t2 * 128:t0 + (t2 + 1) * 128, :], STB[i][:, :], [f"stb{i}", f"stbh{i}"], [("gs", s.name)])
                cp("dve", VAB[:, :], psb(6)[:, 0:160], [pk(6)], ["vab"])
                cp("dve", VA_all[:, vt, :, 0:64], VAB[:, 0:128].rearrange("p (g d) -> p g d", g=2), ["vab"], ["va"])
                if not s.is_sample:
                    for g in range(2):
                        dma("pool", nv_out[s.idx, l, g, t0 + t2 * 128:t0 + (t2 + 1) * 128, :], VAB[:, g * 64:g * 64 + 64],
                            ["vab"], [("nv", s.idx)])
                tt("dve", SM[:, 0:16], VAB[:, 128:144], dtb[:], ALU.add, ["vab", "lvec"], ["sm"])
                act(SM[:, 16:32], SM[:, 0:16], AF.Exp, ["sm"], ["sm1"])
                act(SM[:, 32:48], SM[:, 16:32], AF.Ln, ["sm1"], ["sm2"], bias=1.0)
                tt("dve", LA[:, gti, :], SM[:, 32:48], negA[:], ALU.mult, ["sm2", "lvec2"], ["la"])
                act(BETA[:, gti, :], VAB[:, 144:160], AF.Sigmoid, ["vab"], ["beta"])
                act(LB[:, gti, :], BETA[:, gti, :], AF.Ln, ["beta"], ["lb"])

        gA = [blockA(j) for j in range(len(jobsA))]
        next(gA[0])
        for j in range(len(jobsA)):
            next(gA[j])
            if j + 1 < len(jobsA):
                next(gA[j + 1])
            for _ in gA[j]:
                pass
        if "dbg_la" in dbg:
            dma("sp", dbg_la, LA[:], ["la"], ["dbgla"])
            dma("sp", dbg_lb, LB[:], ["lb"], ["dbglb"])
            dma("sp", dbg_beta, BETA[:], ["beta"], ["dbgbeta"])
        P.fence()
        if stop_after == "stageA":
            return finish()

        brr = [0]

        def nb():
            brr[0] = brr[0] % 3 + 1
            return brr[0]

        def genB():
            for s in seqs:
                for blk in range(s.T // TB):
                    t0 = blk * TB
                    dma("sp", PRET, PRE[s.name][:, :, t0:t0 + TB + 2],
                        [("pre", s.name, b_) for b_ in range(max(0, blk - 1), min(s.T // TB, blk + 2))]
                        + [("pre", s.name, "pad", 0), ("pre", s.name, "pad", s.T + 1)], ["bigA"])
                    for c in range(12):
                        tsc("dve", CV[:, c, :], PRET[:, c, 0:TB], cw[:, 0, c:c + 1], None, ALU.mult, None, ["bigA", "lvec"], [("cv", c)])
                        stt("dve", CV[:, c, :], PRET[:, c, 1:TB + 1], cw[:, 1, c:c + 1], CV[:, c, :], ALU.mult, ALU.add,
                            ["bigA", "lvec", ("cv", c)], [("cv", c)])
                        stt("dve", CV[:, c, :], PRET[:, c, 2:TB + 2], cw[:, 2, c:c + 1], CV[:, c, :], ALU.mult, ALU.add,
                            ["bigA", "lvec", ("cv", c)], [("cv", c)])
                        if c % 3 == 2:
                            yield
                    for c4 in range(3):
                        cs_ = slice(c4 * 4, c4 * 4 + 4)
                        ck = [("cv", c) for c in range(c4 * 4, c4 * 4 + 4)]
                        SG = PRET[:, cs_, 0:TB]
                        act(SG, CV[:, cs_, :], AF.Exp, ck + ["bigA"], ["bigA"], scale=-1.0)
                        act(SG, SG, AF.Ln, ["bigA"], ["bigA"], bias=1.0)
                        act(SG, SG, AF.Exp, ["bigA"], ["bigA"], scale=-1.0)
                        tt("dve", CV[:, cs_, :], CV[:, cs_, :], SG, ALU.mult, ck + ["bigA"], ck)
                        yield
                    for c in range(8):
                        act(STB[0][:, :TB], CV[:, c, :], AF.Square, [("cv", c)], ["stb0"])
                        mm(psb(7)[:, :TB], C("blk", True), STB[0][:, :TB], True, True, ["stb0", "cstb"], [pk(7)])
                        act(STG[1][:, :TB], psb(7)[:, :TB], AF.Ln, [pk(7)], ["stg1"], bias=EPS, scale=1.0)
                        act(STG[1][:, :TB], STG[1][:, :TB], AF.Exp, ["stg1"], ["stg1"], scale=-0.5)
                        i = nb()
                        stt("dve", STB[i][:, :TB], CV[:, c, :], 0.125 if c < 4 else 1.0, STG[1][:, :TB], ALU.mult, ALU.mult,
                            [("cv", c), "stg1"], [f"stb{i}"])
                        cp("act", CV[:, c, :], STB[i][:, :TB], [f"stb{i}"], [("cv", c)])
                        dstT = QDT if c < 4 else KDT
                        cc = c % 4
                        dma("pool", dstT[s.name][2 * cc:2 * cc + 2].rearrange("h d t -> (h d) t")[:, t0:t0 + TB], STB[i][:, :TB],
                            [f"stb{i}"], [("qkdt", s.name)])
                        yield
                    for t2 in range(TB // 128):
                        for grp, dstM in enumerate((QTM, KTM, VTM)):
                            for cc in range(4):
                                tr(psb(7)[:, cc * 128:(cc + 1) * 128], CV[:, grp * 4 + cc, t2 * 128:(t2 + 1) * 128], C("ident"),
                                   [("cv", grp * 4 + cc), "cst"], [pk(7)])
                            i = nb()
                            cp("dve", STB[i][:, :], psb(7)[:, :], [pk(7)], [f"stb{i}"])
                            dma("pool", dstM[s.name][t0 + t2 * 128:t0 + (t2 + 1) * 128, :], STB[i][:, :], [f"stb{i}"], [("tm", s.name)])
                            yield

        if stop_after == "stageB":
            for _ in genB():
                pass
            P.fence()
            return finish()

        for t2 in range(NCTX // 128):
            dma("sp", STG[0][:, 0:128].rearrange("p (g d) -> p g d", g=2),
                ck_in[l, :, t2 * 128:(t2 + 1) * 128, :].rearrange("g p d -> p g d"), [], ["stg0"])
            tr(psb(0)[:, 0:128], STG[0][:, 0:128], C("ident"), ["stg0", "cst"], [pk(0)])
            cp("dve", KT_all[:, t2 * 128:(t2 + 1) * 128], psb(0)[:, 0:128], [pk(0)], ["kt"])
            dma("sp", STG[1][:, 0:128].rearrange("p (g d) -> p g d", g=2),
                cv_in[l, :, t2 * 128:(t2 + 1) * 128, :].rearrange("g p d -> p g d"), [], ["stg1"])
            cp("dve", VA_all[:, t2, :, 0:64], STG[1][:, 0:128].rearrange("p (g d) -> p g d", g=2), ["stg1"], ["va"])
        QBz = [[WAR[:, (qp * 2 + g) * 512:(qp * 2 + g + 1) * 512] for g in range(2)] for qp in range(2)]
        VAp = WAR[:, 2048:2048 + NVT * 256].rearrange("p (t g d) -> p t g d", t=NVT, g=2)
        P.add("pool", lambda e: e.memset(WAR[:, 0:2048 + NVT * 256], 0.0), [], ["qbz", "vap"])
        cp("pool", VAp[:, :, :, 0:65], VA_all[:, :, :, :], ["va", "vap"], ["vap"])
        RS = WAR[:, 2048 + NVT * 256:2048 + NVT * 256 + 1024].bitcast(F32)
        items = []
        qcount = 0
        for s in seqs:
            ktiles = []
            if s.is_sample:
                ktiles += list(range(NCTX // 128))
            ktiles += [(s.key0 + s.nctx) // 128 + i for i in range(s.T // 128)]
            for qi in range(s.T // 128):
                for n_, kt in enumerate(ktiles):
                    items.append((s, qi, n_, kt, n_ == 0, n_ == len(ktiles) - 1, qcount % 2))
                qcount += 1
        LAG = 1

        def genAttn():
            for idx in range(len(items) + LAG):
                if idx < len(items):
                    s, qi, n_, kt, first, last, qp = items[idx]
                    q0 = qi * 128
                    if first:
                        for g2 in range(2):
                            dma("sp", QBz[qp][g2][64 * g2:64 * g2 + 64, :].rearrange("p (j t) -> p j t", j=4),
                                QA[s.name][4 * g2:4 * g2 + 4, :, q0:q0 + 128].rearrange("j d t -> d j t"),
                                [("qa", s.name), "qbz"], [("qb", qp)])
                    r_ = idx % 2
                    for g in range(2):
                        mm(PS[r_][:, g * 512:(g + 1) * 512], KT_all[:, kt * 128:(kt + 1) * 128],
                           QBz[qp][g], True, True, ["kt", ("qb", qp)],
                           [pk(2 * r_), pk(2 * r_ + 1)])
                    act(XBW[r_][:, :], PS[r_][:, :], AF.Exp, [pk(2 * r_), pk(2 * r_ + 1)], [("ptt", r_)], scale=0.125)
                if idx >= LAG:
                    s, qi, n_, kt, first, last, qp = items[idx - LAG]
                    q0 = qi * 128
                    r_ = (idx - LAG) % 2
                    for g in range(2):
                        ob = 4 + g
                        mm(psb(ob)[:, :], VAp[:, kt, g, :], XBW[r_][:, g * 512:(g + 1) * 512], first, last,
                           ["vap", ("ptt", r_)], [pk(ob)])
                    if last:
                        for g in range(2):
                            ob = 4 + g
                            cp("dve", RS[64:65, :], psb(ob)[64:65, :], [pk(ob)], ["rs"])
                            act(RS[64:65, :], RS[64:65, :], AF.Ln, ["rs"], ["rs"])
                            act(RS[64:65, :], RS[64:65, :], AF.Exp, ["rs"], ["rs"], scale=-1.0)
                            mm(psb(6)[0:64, :], C("ones")[64:65, 0:64], RS[64:65, :], True, True, ["rs", "cst"], [pk(6)])
                            cp("dve", STG[0][0:64, :], psb(ob)[0:64, :], [pk(ob)], ["stg0"])
                            tt("dve", OB[:, :], STG[0][0:64, :], psb(6)[0:64, :], ALU.mult, ["stg0", pk(6)], ["ob"])
                            dma("sp", OA[s.name][4 * g:4 * g + 4, :, q0:q0 + 128].rearrange("j d t -> d j t"),
                                OB[:, :].rearrange("p (j t) -> p j t", j=4), ["ob"], [("oa", s.name)])
                yield

        gB_, gAt_ = genB(), genAttn()
        doneB = doneA = False
        it_ = 0
        while not (doneB and doneA):
            if not doneB:
                try:
                    next(gB_)
                except StopIteration:
                    doneB = True
            for _ in range(3 + (1 if it_ % 4 == 3 else 0)):
                if not doneA:
                    try:
                        next(gAt_)
                    except StopIteration:
                        doneA = True
            it_ += 1
        P.fence()
        if stop_after == "attn":
            return finish()

        H8 = 8
        CUT = 0

        def v3(t):
            return t.rearrange("p (h j) -> p h j", h=H8)

        woff = [0]

        def wtake(n, f32=False):
            ap = WAR[:, woff[0]:woff[0] + n]
            woff[0] += n
            return ap.bitcast(F32) if f32 else ap

        DS = [dict(SM=SM, DG1=DG1, DG3=DG3, DE=DE, E1=E1, E2=E2, E3=E3, PA=PA, PT=PT, RT=RT, XB=XB, BEK=BEK, BV=BV), None]
        DS[1] = dict(E1=wtake(1024, True), E2=wtake(1024, True), E3=wtake(1024, True), DG1=wtake(1024, True),
                     DG3=wtake(1024, True), SM=wtake(320, True),
                     PA=[wtake(512), wtake(512)], PT=[wtake(512), wtake(512)], RT=[wtake(512), wtake(512)],
                     XB=[wtake(512) for _ in range(4)], BEK=wtake(512), BV=wtake(512), DE=wtake(512))
        def w64(n):
            ap = WAR[0:64, woff[0]:woff[0] + n]
            woff[0] += n
            return ap.rearrange("p (x i) -> p x i", x=16)

        NWT2 = [[NWT[d][:, :, :], w64(1024)] for d in range(2)]
        QDEC2 = [[QDEC[d][:, :, :], w64(1024)] for d in range(2)]
        U02 = [[U0[d], wtake(1024, True)] for d in range(2)]
        QKM2 = [[QKM[d], wtake(512)] for d in range(2)]
        KDEC2 = [[KDEC[d], wtake(512)] for d in range(2)]
        EGL22 = [[EGL2[d][:, :], wtake(32, True)] for d in range(2)]
        ab = [0]
        fbk = [0]
        ppr = [0]

        def abank():
            ab[0] = (ab[0] + 1) % 6
            return ab[0]

        def fbank():
            fbk[0] ^= 1
            return 6 + fbk[0]

        def ppair():
            ppr[0] = (ppr[0] + 1) % 3
            return ppr[0]

        idb = bcast(C("identst", True), 1, H8)
        ist = bcast(C("identst"), 1, H8)

        def msk(name):
            return bcast(C(name, True), 1, H8)

        def prep(s, d, m, par):
            NWTp, QDECp, U0p, QKMp, KDECp, EGL2p = NWT2[d][par], QDEC2[d][par], U02[d][par], QKM2[d][par], KDEC2[d][par], EGL22[d][par]
            kq = (d, par)
            T_ = DS[d]
            SMd, DG1d, DG3d, DEd = T_["SM"], T_["DG1"], T_["DG3"], T_["DE"]
            E1d, E2d, E3d = T_["E1"], T_["E2"], T_["E3"]
            PAd, PTd, RTd, XBd, BEKd, BVd = T_["PA"], T_["PT"], T_["RT"], T_["XB"], T_["BEK"], T_["BV"]

            def K(name, *x):
                return (name, d) + tuple(x)

            gti = s.tile0 + m
            sfx = "f" if d == 0 else "b"
            rows = slice(m * 128, (m + 1) * 128)
            dma("sp", KTC[d][:, :, :], KDT[s.name][:, :, rows].rearrange("h d t -> d h t"), [("qkdt", s.name)], [("ktc", d)])
            dma("sp", QTC[d][:, :, :], QDT[s.name][:, :, rows].rearrange("h d t -> d h t"), [("qkdt", s.name)], [("qtc", d)])
            dma("sp", KTMC[d][:, :], KTM[s.name][rows, :], [("tm", s.name)], [("ktmc", d)])
            dma("sp", QTMC[d][:, :], QTM[s.name][rows, :], [("tm", s.name)], [("qtmc", d)])
            dma("sp", VTMC[d][:, :], VTM[s.name][rows, :], [("tm", s.name)], [("vtmc", d)])
            la = LA[:, gti, 8 * d:8 * d + 8]
            lb = LB[:, gti, 8 * d:8 * d + 8]
            be_ = BETA[:, gti, 8 * d:8 * d + 8]
            bg = fbank()
            mm(psb(bg)[:, 0:8], C("tri_" + sfx), la, True, True, ["la", "cst"], [pk(bg)])
            mm(psb(bg)[:, 8:16], C("half0"), la, True, True, ["la", "cst"], [pk(bg)])
            mm(psb(bg)[:, 16:24], C("half1"), la, True, True, ["la", "cst"], [pk(bg)])
            cp("dve", SMd[:, 0:24], psb(bg)[:, 0:24], [pk(bg)], [K("sm")])
            yield
            tt("dve", SMd[:, 24:32], SMd[:, 0:8], lb, ALU.add, [K("sm"), "lb"], [K("sm_glb")])
            act(SMd[:, 32:40], SMd[:, 0:8], AF.Exp, [K("sm")], [K("sm_eg")])
            cp("dve", SMd[0:64, 40:48], SMd[0:64, 8:16], [K("sm")], [K("sm_glo")])
            cp("dve", SMd[64:128, 40:48], SMd[64:128, 16:24], [K("sm")], [K("sm_glo")])
            tt("dve", SMd[:, 48:56], SMd[:, 40:48], SMd[:, 0:8], ALU.subtract, [K("sm"), K("sm_glo")], [K("sm_ek")])
            act(SMd[:, 48:56], SMd[:, 48:56], AF.Exp, [K("sm_ek")], [K("sm_ek")])
            act(EGL2p, SMd[:, 8:24], AF.Exp, [K("sm")], [("egl2",) + kq])
            tt("dve", SMd[:, 56:64], be_, SMd[:, 32:40], ALU.mult, ["beta", K("sm_eg")], [K("sm_be")])
            tsc("dve", SMd[:, 64:72], SMd[:, 0:8], -1.0, None, ALU.mult, None, [K("sm")], [K("sm_ng")])
            tt("pool", v3(DG1d), ist, bcast(SMd[:, 24:32], 2, 64), ALU.mult, ["cst", K("sm_glb")], [K("dg1")])
            tt("pool", v3(DG3d), ist, bcast(SMd[:, 0:8], 2, 64), ALU.mult, ["cst", K("sm")], [K("dg3")])
            tt("dve", v3(DEd), ist, bcast(SMd[:, 32:40], 2, 64), ALU.mult, ["cst", K("sm_eg")], [K("de")])
            yield
            b1 = fbank()
            mm(psb(b1)[:, :], C("negblk"), DG3d, True, False, [K("dg3"), "cst"], [pk(b1)])
            mm(psb(b1)[:, :], C("ident"), bcast(SMd[:, 24:32], 2, 64), False, False, [K("sm_glb"), "cst"], [pk(b1)])
            mm(psb(b1)[:, :], C("ident", True), msk("m1_" + sfx), False, True, ["cstb"], [pk(b1)])
            act(E1d, psb(b1)[:, :], AF.Exp, [pk(b1)], [K("e1")])
            yield
            b2 = fbank()
            mm(psb(b2)[:, :], C("blk"), DG1d, True, False, [K("dg1"), "cst"], [pk(b2)])
            mm(psb(b2)[:, :], C("ident"), bcast(SMd[:, 64:72], 2, 64), False, False, [K("sm_ng"), "cst"], [pk(b2)])
            mm(psb(b2)[:, :], C("ident", True), msk("m2_" + sfx), False, True, ["cstb"], [pk(b2)])
            act(E2d, psb(b2)[:, :], AF.Exp, [pk(b2)], [K("e2")])
            yield
            b3 = fbank()
            mm(psb(b3)[:, :], C("blk"), DG3d, True, False, [K("dg3"), "cst"], [pk(b3)])
            mm(psb(b3)[:, :], C("ident"), bcast(SMd[:, 64:72], 2, 64), False, False, [K("sm_ng"), "cst"], [pk(b3)])
            mm(psb(b3)[:, :], C("ident", True), msk("m3_" + sfx), False, True, ["cstb"], [pk(b3)])
            act(E3d, psb(b3)[:, :], AF.Exp, [pk(b3)], [K("e3")])
            yield
            bkk, bqk = abank(), abank()
            for h in range(H8):
                for a in range(2):
                    ts_ = slice(64 * a, 64 * a + 64)
                    mm(psb(bkk)[ts_, h * 64:(h + 1) * 64], KTC[d][:, h, ts_], KTC[d][:, h, ts_], True, True,
                       [("ktc", d)], [pk(bkk)], tile_position=(0, 64 * a))
                    mm(psb(bqk)[ts_, h * 64:(h + 1) * 64], KTC[d][:, h, ts_], QTC[d][:, h, ts_], True, True,
                       [("ktc", d), ("qtc", d)], [pk(bqk)], tile_position=(0, 64 * a))
            A_, AT_ = PAd[0], PTd[0]
            kA, kAT = K("pa", 0), K("pt", 0)
            tt("dve", A_, psb(bkk)[:, :], E1d, ALU.mult, [pk(bkk), K("e1")], [kA])
            tt("dve", AT_, psb(bkk)[:, :], E2d, ALU.mult, [pk(bkk), K("e2")], [kAT])
            tt("dve", QKMp, psb(bqk)[:, :], E3d, ALU.mult, [pk(bqk), K("e3")], [("qkm",) + kq])
            yield

            def grp(L, R, lkey, rkey):
                bank = abank()
                for h in range(H8):
                    for a in range(2):
                        ts_ = slice(64 * a, 64 * a + 64)
                        hs = slice(h * 64, (h + 1) * 64)
                        mm(psb(bank)[ts_, hs], L[ts_, hs], R[ts_, hs], True, True, [lkey, rkey], [pk(bank)],
                           tile_position=(64 * a, 64 * a))
                return bank

            D_, DT_ = PAd[1], PTd[1]
            kD, kDT = K("pa", 1), K("pt", 1)
            X = [XBd[0], XBd[1], BEKd, BVd, XBd[2], XBd[3]]
            kX = [K("xb", 0), K("xb", 1), K("bek"), K("bv"), K("xb", 2), K("xb", 3)]
            kR = [K("rt", 0), K("rt", 1)]
            tt("pool", v3(D_), v3(A_), msk("mask8"), ALU.mult, [kA, "cstb"], [kD])
            tt("pool", v3(DT_), v3(AT_), msk("mask8"), ALU.mult, [kAT, "cstb"], [kDT])
            tt("dve", v3(X[2]), idb, v3(DT_), ALU.subtract, ["cstb", kDT], [kX[2]])
            yield
            g1 = grp(DT_, D_, kDT, kD)
            g2 = grp(D_, DT_, kD, kDT)
            cp("act", X[0], psb(g1)[:, :], [pk(g1)], [kX[0]])
            tt("dve", v3(RTd[1]), v3(X[0]), idb, ALU.add, [kX[0], "cstb"], [kR[1]])
            cp("dve", RTd[0], psb(g2)[:, :], [pk(g2)], [kR[0]])
            yield
            g3 = grp(RTd[0], X[0], kR[0], kX[0])
            tt("dve", v3(X[1]), v3(psb(g3)[:, :]), idb, ALU.add, [pk(g3), "cstb"], [kX[1]])
            g1 = grp(RTd[1], X[2], kR[1], kX[2])
            cp("act", X[3], psb(g1)[:, :], [pk(g1)], [kX[3]])
            yield
            g2 = grp(X[3], X[1], kX[3], kX[1])
            g3 = grp(X[1], X[3], kX[1], kX[3])
            cp("act", X[4], psb(g2)[:, :], [pk(g2)], [kX[4]])
            cp("dve", X[5], psb(g3)[:, :], [pk(g3)], [kX[5]])
            yield
            Tb, kT = [X[4], X[0]], [kX[4], kX[0]]
            Mb, kM = [X[5], X[1]], [kX[5], kX[1]]
            cur = 0
            for li, mname in enumerate(("moff8", "moff16", "moff32")):
                last = (li == 2)
                nxt = 1 - cur
                tt("pool", v3(D_), v3(A_), msk(mname), ALU.mult, [kA, "cstb"], [kD])
                if not last:
                    tt("pool", v3(DT_), v3(AT_), msk(mname), ALU.mult, [kAT, "cstb"], [kDT])
                g1 = grp(D_, Mb[cur], kD, kM[cur])
                cp("act", RTd[1], psb(g1)[:, :], [pk(g1)], [kR[1]])
                if not last:
                    g2 = grp(DT_, Tb[cur], kDT, kT[cur])
                    cp("dve", RTd[0], psb(g2)[:, :], [pk(g2)], [kR[0]])
                yield
                g3 = grp(Tb[cur], RTd[1], kT[cur], kR[1])
                tt("dve", Mb[nxt], Mb[cur], psb(g3)[:, :], ALU.subtract, [kM[cur], pk(g3)], [kM[nxt]])
                if not last:
                    g1 = grp(Mb[cur], RTd[0], kM[cur], kR[0])
                    tt("dve", Tb[nxt], Tb[cur], psb(g1)[:, :], ALU.subtract, [kT[cur], pk(g1)], [kT[nxt]])
                cur = nxt
                yield
            TTm = Mb[cur]
            tkey = kM[cur]
            tt("dve", v3(BEKd), v3(KTMC[d][:, :]), bcast(SMd[:, 56:64], 2, 64), ALU.mult, [("ktmc", d), K("sm_be")], [K("bek")])
            tt("pool", v3(KDECp), v3(KTMC[d][:, :]), bcast(SMd[:, 48:56], 2, 64), ALU.mult, [("ktmc", d), K("sm_ek")], [("kdec",) + kq])
            tt("pool", v3(BVd), v3(VTMC[d][:, :]), bcast(be_, 2, 64), ALU.mult, [("vtmc", d), "beta"], [K("bv")])
            yield
            pp_ = ppair()
            pkeys = [pk(2 * pp_), pk(2 * pp_ + 1)]
            bu0 = abank()
            while bu0 in (2 * pp_, 2 * pp_ + 1):
                bu0 = abank()
            for h in range(H8):
                for a in range(2):
                    ts_ = slice(64 * a, 64 * a + 64)
                    hs = slice(h * 64, (h + 1) * 64)
                    cs = slice((a * 8 + h) * 64, (a * 8 + h) * 64 + 64)
                    mm(PS[pp_][0:64, cs], BEKd[ts_, hs], TTm[ts_, hs], True, True, [K("bek"), tkey], pkeys,
                       tile_position=(64 * a, 0))
                    mm(psb(bu0)[ts_, hs], TTm[ts_, hs], BVd[ts_, hs], True, True, [tkey, K("bv")], [pk(bu0)],
                       tile_position=(64 * a, 64 * a))
            tsc("dve", NWTp, PS[pp_][0:64, :].rearrange("p (x i) -> p x i", x=16), -1.0, None, ALU.mult, None,
                pkeys, [("nwt",) + kq])
            cp("act", U0p, psb(bu0)[:, :], [pk(bu0)], [("u0",) + kq])
            yield
            pp_ = ppair()
            pkeys = [pk(2 * pp_), pk(2 * pp_ + 1)]
            for a in range(2):
                mm(PS[pp_][0:64, a * 512:(a + 1) * 512], C("half%d" % a, True)[:, 0:64], DEd, True, True,
                   [K("de"), "cstb"], pkeys)
            for a in range(2):
                tt("dve", QDECp[:, a * 8:(a + 1) * 8, :],
                   QTC[d][:, :, a * 64:(a + 1) * 64],
                   PS[pp_][0:64, a * 512:(a + 1) * 512].rearrange("p (h i) -> p h i", h=8), ALU.mult,
                   [("qtc", d)] + pkeys, [("qdec",) + kq])
            yield

        def steps(s, d, m, par, first_visit):
            NWTp, QDECp, U0p, QKMp, KDECp, EGL2p = NWT2[d][par], QDEC2[d][par], U02[d][par], QKM2[d][par], KDEC2[d][par], EGL22[d][par]
            kq = (d, par)
            SMd = DS[d]["SM"]

            def K(name, *x):
                return (name, d) + tuple(x)

            rows = slice(m * 128, (m + 1) * 128)
            for a in ((0, 1) if d == 0 else (1, 0)):
                ts_ = slice(64 * a, 64 * a + 64)
                pu = abank()
                for h in range(H8):
                    hs = slice(h * 64, (h + 1) * 64)
                    mm(psb(pu)[ts_, hs], NWTp[:, a * 8 + h, :], SBF[d][:, h, :], True, True,
                       [("nwt",) + kq, ("sbf", d)], [pk(pu)], tile_position=(0, 64 * a))
                tt("dve", UB[d][ts_, :], U0p[ts_, :], psb(pu)[ts_, :], ALU.add, [("u0",) + kq, pk(pu)], [("ub", d)])
                yield
                po, pob, pS_ = abank(), abank(), abank()
                for h in range(H8):
                    hs = slice(h * 64, (h + 1) * 64)
                    mm(psb(po)[ts_, hs], QDECp[:, a * 8 + h, :], SBF[d][:, h, :], True, True,
                       [("qdec",) + kq, ("sbf", d)], [pk(po)], tile_position=(0, 64 * a))
                    mm(psb(pob)[ts_, hs], QKMp[ts_, hs], UB[d][ts_, hs], True, True,
                       [("qkm",) + kq, ("ub", d)], [pk(pob)], tile_position=(64 * a, 64 * a))
                    mm(psb(pS_)[0:64, hs], KDECp[ts_, hs], UB[d][ts_, hs], True, True,
                       [("kdec",) + kq, ("ub", d)], [pk(pS_)], tile_position=(64 * a, 0))
                tt("dve", S32[d][:, :, :], S32[d][:, :, :], bcast(EGL2p[0:64, a * 8:a * 8 + 8], 2, 64), ALU.mult,
                   [("s32", d), ("egl2",) + kq], [("s32", d)])
                tt("dve", S32[d][:, :, :], S32[d][:, :, :], psb(pS_)[0:64, :].rearrange("p (h v) -> p h v", h=H8), ALU.add,
                   [("s32", d), pk(pS_)], [("s32", d)])
                cp("act", SBF[d][:, :, :], S32[d][:, :, :], [("s32", d)], [("sbf", d)])
                cp("act", OACC[d][ts_, :], psb(po)[ts_, :], [pk(po)], [("oacc", d)])
                tt("dve", OACC[d][ts_, :], OACC[d][ts_, :], psb(pob)[ts_, :], ALU.add, [("oacc", d), pk(pob)], [("oacc", d)])
                yield
            if first_visit:
                dma("sp", OPART[s.name][rows, :], OACC[d][:, :], [("oacc", d)], [("opart", s.name, m)])
            else:
                dma("sp", STG[d][:, :], OPART[s.name][rows, :], [("opart", s.name, m)], [f"stg{d}"])
                tt("dve", OACC[d][:, :], OACC[d][:, :], STG[d][:, :], ALU.add, [("oacc", d), f"stg{d}"], [("oacc", d)])
                tt("pool", STG[2 + d][:, :], OACC[d][:, :], OACC[d][:, :], ALU.mult, [("oacc", d)], [f"stg{2 + d}"])
                P.add("dve", lambda e: e.tensor_reduce(SMd[:, 80:88], v3(STG[2 + d][:, :]), AX.X, ALU.add),
                      [f"stg{2 + d}"], [K("sm_rn")])
                act(SMd[:, 80:88], SMd[:, 80:88], AF.Ln, [K("sm_rn")], [K("sm_rn")], bias=EPS, scale=1.0 / 64)
                act(SMd[:, 80:88], SMd[:, 80:88], AF.Exp, [K("sm_rn")], [K("sm_rn")], scale=-0.5)
                yield
                tt("dve", v3(OACC[d][:, :]), v3(OACC[d][:, :]), bcast(SMd[:, 80:88], 2, 64), ALU.mult,
                   [("oacc", d), K("sm_rn")], [("oacc", d)])
                tt("dve", v3(OACC[d][:, :]), v3(OACC[d][:, :]), bcast(dng[:], 1, H8), ALU.mult,
                   [("oacc", d), "lvec"], [("oacc", d)])
                dma("sp", STB[d][:, :], GS[s.name][rows, :], [("gs", s.name)], [f"stb{d}"])
                tt("dve", OACC[d][:, :], OACC[d][:, :], STB[d][:, :], ALU.mult, [("oacc", d), f"stb{d}"], [("oacc", d)])
                bt = fbank()
                for cc in range(4):
                    tr(psb(bt)[:, cc * 128:(cc + 1) * 128], OACC[d][:, cc * 128:(cc + 1) * 128], C("ident"),
                       [("oacc", d), "cst"], [pk(bt)])
                cp("act", STB[2 + d][:, :], psb(bt)[:, :], [pk(bt)], [f"stb{2 + d}"])
                dma("sp", OD[s.name][:, :, rows], STB[2 + d][:, :].rearrange("p (c t) -> p c t", c=4), [f"stb{2 + d}"],
                    [("od", s.name)])

        for s in seqs:
            NP_ = s.T // 128
            for d in range(2):
                if s.is_sample:
                    dma("sp", S32[d][:, :, :], st_in[l, d].rearrange("h k v -> k h v"), [], [("s32", d)])
                else:
                    P.add("pool", lambda e, d=d: e.memset(S32[d][:, :, :], 0.0), [], [("s32", d)])
                cp("act", SBF[d][:, :, :], S32[d][:, :, :], [("s32", d)], [("sbf", d)])
            visited = set()

            def mof(d, step):
                return step if d == 0 else NP_ - 1 - step

            def rr(gens):
                active = list(gens)
                while active:
                    for g_ in list(active):
                        try:
                            next(g_)
                        except StopIteration:
                            active.remove(g_)

            rr([prep(s, d, mof(d, 0), 0) for d in range(2)])
            for step in range(NP_):
                gens = []
                for d in range(2):
                    m = mof(d, step)
                    gens.append(steps(s, d, m, step % 2, m not in visited))
                for d in range(2):
                    visited.add(mof(d, step))
                if step + 1 < NP_:
                    for d in range(2):
                        gens.append(prep(s, d, mof(d, step + 1), (step + 1) % 2))
                rr(gens)
            if not s.is_sample:
                for d in range(2):
                    dma("sp", nst_out[s.idx, l, d].rearrange("h k v -> k h v"), S32[d][:, :, :], [("s32", d)], [("nst", s.idx)])
        P.fence()
        if stop_after == "scan":
            return finish()

        Wpa = WAR[:, 0:4096].rearrange("p (k n) -> p k n", k=4)
        Wpd = WAR[:, 4096:8192].rearrange("p (k n) -> p k n", k=4)
        Wo = WAR[:, 8192:16384].rearrange("p (k n) -> p k n", k=8)
        load_w(Wpa, w_pa[l], 4)
        load_w(Wpd, w_pd[l], 4)
        load_w(Wo, w_out[l], 8)
        def w3(off, n, f32, shape3):
            ap = WAR[:, off:off + n]
            if f32:
                ap = ap.bitcast(F32)
            return ap.rearrange("p (c t) -> p c t", c=shape3)

        DSET = [
            dict(HB=HB[:, :, :], GATB=GATB[:, :, :], XT=XT, X1T=X1T, S0=STG[0][:, :], S1=STG[1][:, :], R1=R1[:, :], RSTD=RSTD[:, :],
                 bpa=0, bpd=1, bop=(2, 3), bst=0, k="0"),
            dict(HB=w3(16384, 4096, False, 16), GATB=w3(20480, 4096, False, 16), XT=w3(24576, 4096, True, 8),
                 X1T=w3(28672, 4096, True, 8), S0=WAR[:, 32768:33792].bitcast(F32), S1=WAR[:, 33792:34816].bitcast(F32),
                 R1=WAR[:, 34816:35328].bitcast(F32), RSTD=WAR[:, 35328:35840].bitcast(F32),
                 bpa=4, bpd=5, bop=(6, 7), bst=4, k="1"),
        ]
        jobsD = [(s, blk) for s in seqs for blk in range(s.T // TB)]

        def blockD(j, slot):
            s, blk = jobsD[j]
            mj = s.mj
            t0 = blk * TB
            B_ = DSET[slot]
            kk_ = B_["k"]
            HBd, GATd, XTd, X1Td, S0, S1, R1d, RSTDd = B_["HB"], B_["GATB"], B_["XT"], B_["X1T"], B_["S0"], B_["S1"], B_["R1"], B_["RSTD"]
            OATd, ODTd, MGd = HBd[:, 0:4, :], HBd[:, 4:8, :], HBd[:, 8:16, :]
            khb, kht, kg, kx, kx1, ks0, ks1 = "dhb" + kk_, "dht" + kk_, "dg" + kk_, "dx" + kk_, "dx1" + kk_, "ds0" + kk_, "ds1" + kk_
            for c in range(4):
                dma("sp", OATd[:, c, :], OA[s.name][2 * c:2 * c + 2].rearrange("h d t -> (h d) t")[:, t0:t0 + TB],
                    [("oa", s.name)], [khb])
            dma("sp", ODTd, OD[s.name][:, :, t0:t0 + TB], [("od", s.name)], [khb])
            dma("sp", GATd, GATES[s.name][:, :, t0:t0 + TB], [("gates", s.name, blk)], [kg])
            dma("sp", XTd, XRES[s.name][:, :, t0:t0 + TB], [("xres", s.name, blk)], [kx])
            yield
            for n in range(8):
                ns = slice(n * 128, (n + 1) * 128)
                for k in range(4):
                    mm(psb(B_["bpa"])[:, :TB], Wpa[:, k, ns], OATd[:, k, :], k == 0, k == 3, ["war", khb], [pk(B_["bpa"])])
                for k in range(4):
                    mm(psb(B_["bpd"])[:, :TB], Wpd[:, k, ns], ODTd[:, k, :], k == 0, k == 3, ["war", khb], [pk(B_["bpd"])])
                tt("dve", S0[:, :TB], psb(B_["bpa"])[:, :TB], GATd[:, n, :], ALU.mult, [pk(B_["bpa"]), kg], [ks0])
                tt("dve", S1[:, :TB], psb(B_["bpd"])[:, :TB], GATd[:, 8 + n, :], ALU.mult, [pk(B_["bpd"]), kg], [ks1])
                tt("pool", MGd[:, n, :], S0[:, :TB], S1[:, :TB], ALU.add, [ks0, ks1], [kht])
                yield
            for n in range(8):
                ns = slice(n * 128, (n + 1) * 128)
                bk = B_["bop"][n % 2]
                for k in range(8):
                    mm(psb(bk)[:, :TB], Wo[:, k, ns], MGd[:, k, :], k == 0, k == 7, ["war", kht], [pk(bk)])
                stt("dve", X1Td[:, n, :], psb(bk)[:, :TB], MOD[:, 16 + n, mj:mj + 1], XTd[:, n, :], ALU.mult, ALU.add,
                    [pk(bk), kx] + MK, [kx1])
                yield
            dma("sp", XRES[s.name][:, :, t0:t0 + TB], X1Td, [kx1], [("xres", s.name, blk)])
            rms_stats(X1Td, TB, kx1, sq=HBd[:, 0:8, :], sqkey=khb, bank=B_["bst"], r1=R1d, rstd=RSTDd, rkey=kk_)
            yield
            tt("dve", XTd, X1Td, bcast(RSTDd, 1, 8), ALU.mult, [kx1, "rstd" + kk_], [kx])
            for c in range(8):
                if c % 2 == 0:
                    act(MGd[:, c, :], XTd[:, c, :], AF.Identity, [kx] + MK, [kht],
                        scale=A2[:, c, mj:mj + 1], bias=MOD[:, 24 + c, mj:mj + 1])
                else:
                    tsc("dve", MGd[:, c, :], XTd[:, c, :], A2[:, c, mj:mj + 1], MOD[:, 24 + c, mj:mj + 1], ALU.mult, ALU.add,
                        [kx] + MK, [kht])
            dma("sp", H2[s.name][:, :, t0:t0 + TB], MGd, [kht], [("h2", s.name, blk)])

        def two_way(make, n, stagger):
            gens = [None, None]
            nxt = 0
            started = 0
            while True:
                progressed = False
                for slot in range(2):
                    if gens[slot] is None and nxt < n and (slot == 0 or started >= stagger or nxt > 1):
                        gens[slot] = make(nxt, slot)
                        nxt += 1
                    if gens[slot] is not None:
                        try:
                            next(gens[slot])
                            progressed = True
                            if slot == 0:
                                started += 1
                        except StopIteration:
                            gens[slot] = None
                            progressed = True
                if not progressed and nxt >= n and gens[0] is None and gens[1] is None:
                    break

        two_way(blockD, len(jobsD), 9)
        P.fence()
        if stop_after == "stageD":
            return finish()

        W1h = WAR[:, 0:8 * 2048].rearrange("p (k n) -> p k n", k=8)
        W2h = WAR[:, 16384:16384 + 16 * 1024].rearrange("p (k n) -> p k n", k=16)
        for hf in range(2):
            load_w(W1h, w1[l][:, hf * 2048:(hf + 1) * 2048], 8)
            load_w(W2h, w2[l][hf * 2048:(hf + 1) * 2048, :], 16)
            jobs = [(s, blk) for s in seqs for blk in range(s.T // TB)]
            XTs = [(XT, "bigA"), (X1T, "bigB")]
            H2Ts = [(HB[:, 0:8, :], "hb"), (HB[:, 8:16, :], "ht")]

            def ef_loads(j):
                s, blk = jobs[j]
                t0 = blk * TB
                h2t, hkey = H2Ts[j % 2]
                xt, xkey = XTs[j % 2]
                dma("sp", h2t, H2[s.name][:, :, t0:t0 + TB], [("h2", s.name, blk)], [hkey])
                dma("sp", xt, XRES[s.name][:, :, t0:t0 + TB], [("xres", s.name, blk)], [xkey])

            ef_loads(0)
            for j, (s, blk) in enumerate(jobs):
                mj = s.mj
                t0 = blk * TB
                H2T, hkey = H2Ts[j % 2]
                XTj, xkey = XTs[j % 2]
                if j + 1 < len(jobs):
                    ef_loads(j + 1)
                for f in range(16):
                    bk = 1 + f % 3
                    for k in range(8):
                        mm(psb(bk)[:, :TB], W1h[:, k, f * 128:(f + 1) * 128], H2T[:, k, :], k == 0, k == 7, ["war", hkey], [pk(bk)])
                    i = nstg()
                    act(STG[i][:, :TB], psb(bk)[:, :TB], AF.Relu, [pk(bk), "lvec"], [f"stg{i}"],
                        bias=b1f[:, hf * 16 + f:hf * 16 + f + 1])
                    tt("dve" if f % 2 == 0 else "pool", GATB[:, f, :], STG[i][:, :TB], STG[i][:, :TB], ALU.mult,
                       [f"stg{i}"], ["gatb"])
                for n in range(8):
                    bk = 4 + n % 2
                    for f in range(16):
                        mm(psb(bk)[:, :TB], W2h[:, f, n * 128:(n + 1) * 128], GATB[:, f, :], f == 0, f == 15, ["war", "gatb"], [pk(bk)])
                    stt("dve", XTj[:, n, :], psb(bk)[:, :TB], MOD[:, 40 + n, mj:mj + 1], XTj[:, n, :], ALU.mult, ALU.add,
                        [pk(bk), xkey] + MK, [xkey])
                    if hf == 0:
                        tsc("dve", XTj[:, n, :], XTj[:, n, :], GB2[:, n, mj:mj + 1], None, ALU.add, None, [xkey] + MK, [xkey])
                if not (l == NLAYERS - 1 and hf == 1):
                    dma("pool", XRES[s.name][:, :, t0:t0 + TB], XTj, [xkey], [("xres", s.name, blk)])
                else:
                    rms_stats(XTj, TB, xkey, sq=GATB[:, 0:8, :], sqkey="gatb")
                    tt("dve", XTj, XTj, bcast(RSTD[:, :], 1, 8), ALU.mult, [xkey, "rstd"], [xkey])
                    tt("dve", XTj, XTj, bcast(fnf[:], 2, TB), ALU.mult, [xkey, "fnf"], [xkey])
                    dst = ys_out if s.is_sample else yp_out[s.idx * TP:(s.idx + 1) * TP, :]
                    for t2 in range(TB // 128):
                        for hh in range(2):
                            bk = 6 + hh
                            for c in range(4):
                                tr(psb(bk)[:, c * 128:(c + 1) * 128], XTj[:, hh * 4 + c, t2 * 128:(t2 + 1) * 128], C("ident"),
                                   [xkey, "cst"], [pk(bk)])
                            i = nstg()
                            cp("act" if hh == 0 else "dve", STG[i][:, :], psb(bk)[:, :], [pk(bk)], [f"stg{i}"])
                            dma("pool", dst[t0 + t2 * 128:t0 + (t2 + 1) * 128, hh * 512:(hh + 1) * 512], STG[i][:, :],
                                [f"stg{i}"], [("y", s.name)])
        P.fence()
        if stop_after == f"layer{l}":
            return finish()

    return finish()


def make_in_maps(inputs):
    cos, sin = _rope_tables()
    maps = []
    for core in range(8):
        b = core % 4
        m = {
            "xs": np.ascontiguousarray(inputs["x_sample"][b]),
            "xp": np.ascontiguousarray(inputs["x_prompt"][core * NPS:(core + 1) * NPS].reshape(NPS * TP, D)),
            "cvec": np.ascontiguousarray(np.stack([inputs["c"][b], inputs["c_ctx"]], 0)),
            "ck": np.ascontiguousarray(inputs["cache_k"][b]),
            "cv": np.ascontiguousarray(inputs["cache_v"][b]),
            "st": np.ascontiguousarray(inputs["state_delta"][b]),
            "a_log": np.ascontiguousarray(inputs["a_log"].reshape(DEPTH, 16)),
            "dt_bias": np.ascontiguousarray(inputs["dt_bias"].reshape(DEPTH, 16)),
            "cst": CST, "ropecos": cos, "ropesin": sin,
        }
        for k in ["w_mod", "b_mod", "norm1", "norm2", "w_in", "conv_w", "q_gain", "k_gain", "dn_gain",
                  "w_pa", "w_pd", "w_out", "w1", "b1", "w2", "b2", "final_norm"]:
            m[k] = np.ascontiguousarray(inputs[k])
        maps.append(m)
    return maps


_NC_CACHE = {}


def kernel(**inputs):
    inputs = {k: np.asarray(v) for k, v in inputs.items()}
    if "nc" not in _NC_CACHE:
        _NC_CACHE["nc"] = build_program()
    nc = _NC_CACHE["nc"]
    maps = make_in_maps(inputs)
    res = run_bass_kernel_spmd(nc, maps, core_ids=list(range(8)))
    r = res.results
    y_sample = np.stack([r[b]["ys"] for b in range(4)], 0)
    y_prompt = np.concatenate([r[c]["yp"].reshape(NPS, TP, D) for c in range(8)], 0)
    nk = np.concatenate([r[c]["nk"] for c in range(8)], 0)
    nv = np.concatenate([r[c]["nv"] for c in range(8)], 0)
    nst = np.concatenate([r[c]["nst"] for c in range(8)], 0)
    return (y_prompt.astype(np.float32), y_sample.astype(np.float32), nk.astype(np.float32),
            nv.astype(np.float32), nst.astype(np.float32))
```

```python
import numpy as np
from contextlib import ExitStack
import concourse.bass as bass
import concourse.mybir as mybir
from concourse.bass_utils import run_bass_kernel_spmd

F32 = mybir.dt.float32
BF16 = mybir.dt.bfloat16
AF = mybir.ActivationFunctionType
ALU = mybir.AluOpType
AX = mybir.AxisListType

D = 1024
DEPTH = 2
TS = 4096
TP = 256
NPS = 4
NCTX = 256
INW = 4896
DFF = 4096
EPS = 1e-6
NEG = -30000.0

C_QA, C_KA, C_VA, C_QD, C_KD, C_VD, C_GO, C_AI, C_BI, C_GA, C_GD = 0, 512, 640, 768, 1280, 1792, 2304, 2816, 2832, 2848, 3872


ENGS = ["pe", "act", "dve", "pool", "sp"]
N_DMA_SEMS = {"sp": 40, "pool": 16, "act": 8}
EPOCH = 30000


class Ev:
    __slots__ = ("dma", "eng", "idx", "sem", "val")

    def __init__(self, dma, eng, idx, sem=None, val=None):
        self.dma, self.eng, self.idx, self.sem, self.val = dma, eng, idx, sem, val


class Rec:
    __slots__ = ("eng", "fn", "waits", "signal", "idx", "dma", "sig_sem", "sig_val")

    def __init__(self, eng, fn):
        self.eng, self.fn = eng, fn
        self.waits = []
        self.signal = False
        self.dma = None


class Buf:
    __slots__ = ("w", "r")

    def __init__(self):
        self.w = None
        self.r = []


class Prog:
    def __init__(self, nc):
        self.nc = nc
        self.ops = {e: [] for e in ENGS}
        self.waited = {e: {p: -1 for p in ENGS} for e in ENGS}
        self.waited_dma = {e: {} for e in ENGS}
        self.bufs = {}
        self.dma_count = {q: 0 for q in N_DMA_SEMS}

    def buf(self, k):
        b = self.bufs.get(k)
        if b is None:
            b = self.bufs[k] = Buf()
        return b

    def add(self, eng, fn, reads=(), writes=(), dma=False):
        rec = Rec(eng, fn)
        rec.idx = len(self.ops[eng])
        deps = []
        for k in reads:
            b = self.buf(k)
            if b.w is not None:
                deps.append((b.w, True))
        for k in writes:
            b = self.buf(k)
            if b.w is not None:
                deps.append((b.w, False))
            for r in b.r:
                deps.append((r, False))
        if dma:
            d = self.dma_count[eng]
            n = N_DMA_SEMS[eng]
            si, val = d % n, 16 * (d // n + 1)
            if d >= n:
                deps.append((Ev(True, eng, None, si, val - 16), True))
            rec.dma = (si, val)
            ev = Ev(True, eng, rec.idx, si, val)
            self.dma_count[eng] += 1
        else:
            ev = Ev(False, eng, rec.idx)
        for dep, raw in deps:
            if dep.dma:
                key = (dep.eng, dep.sem)
                if self.waited_dma[eng].get(key, 0) >= dep.val:
                    continue
                self.waited_dma[eng][key] = dep.val
                rec.waits.append(dep)
            else:
                if dep.eng == eng and eng == "pe":
                    continue
                if self.waited[eng][dep.eng] >= dep.idx:
                    continue
                self.waited[eng][dep.eng] = dep.idx
                rec.waits.append(dep)
                self.ops[dep.eng][dep.idx].signal = True
        for k in reads:
            self.buf(k).r.append(ev)
        for k in writes:
            b = self.buf(k)
            b.w = ev
            b.r = []
        self.ops[eng].append(rec)
        return rec

    def fence(self):
        last = {e: len(self.ops[e]) - 1 for e in ["pe", "act", "dve", "pool"]}
        dma_evs = []
        for q, n in N_DMA_SEMS.items():
            d = self.dma_count[q]
            for i in range(min(n, d)):
                uses = (d - i + n - 1) // n
                dma_evs.append(Ev(True, q, None, i, 16 * uses))
        for e in ENGS:
            rec = Rec(e, lambda eng: eng.nop())
            rec.idx = len(self.ops[e])
            for p, li in last.items():
                if p == e or li < 0:
                    continue
                j = li
                while j >= 0 and self.ops[p][j].dma is not None:
                    j -= 1
                if j < 0 or self.waited[e][p] >= j:
                    continue
                self.waited[e][p] = j
                rec.waits.append(Ev(False, p, j))
                self.ops[p][j].signal = True
            for dep in dma_evs:
                key = (dep.eng, dep.sem)
                if self.waited_dma[e].get(key, 0) >= dep.val:
                    continue
                self.waited_dma[e][key] = dep.val
                rec.waits.append(dep)
            self.ops[e].append(rec)

    def emit(self, es):
        nc = self.nc
        comp_sems = {}
        for e in ["pe", "act", "dve", "pool"]:
            cnt = 0
            for rec in self.ops[e]:
                if rec.signal and rec.dma is None:
                    ep = cnt // EPOCH
                    if (e, ep) not in comp_sems:
                        comp_sems[(e, ep)] = es.enter_context(nc.semaphore(f"c_{e}_{ep}"))
                    rec.sig_sem = comp_sems[(e, ep)]
                    rec.sig_val = cnt % EPOCH + 1
                    cnt += 1
        dma_sems = {}
        for q, n in N_DMA_SEMS.items():
            for i in range(min(n, self.dma_count[q])):
                dma_sems[(q, i)] = es.enter_context(nc.semaphore(f"d_{q}_{i}"))
        final_waits = []
        for q, n in N_DMA_SEMS.items():
            d = self.dma_count[q]
            for i in range(min(n, d)):
                uses = (d - i + n - 1) // n
                final_waits.append((dma_sems[(q, i)], 16 * uses))
        block = es.enter_context(nc.Block())
        ops = self.ops

        def run(engname, eng):
            for rec in ops[engname]:
                for dep in rec.waits:
                    if dep.dma:
                        eng.wait_ge(dma_sems[(dep.eng, dep.sem)], dep.val)
                    else:
                        prod = ops[dep.eng][dep.idx]
                        eng.wait_ge(prod.sig_sem, prod.sig_val)
                ins = rec.fn(eng)
                if rec.dma is not None:
                    ins.then_inc(dma_sems[(engname, rec.dma[0])], 16)
                elif rec.signal:
                    ins.then_inc(rec.sig_sem, 1)
            if engname == "sp":
                for s, v in final_waits:
                    eng.wait_ge(s, v)

        @block.tensor
        def _(t):
            run("pe", t)

        @block.scalar
        def _(a):
            run("act", a)

        @block.vector
        def _(v):
            run("dve", v)

        @block.gpsimd
        def _(g):
            run("pool", g)

        @block.sync
        def _(s):
            run("sp", s)


CST_LAYOUT = {}


def _build_consts():
    p = np.arange(128)
    cols = []
    off = 0

    def put(name, arr):
        nonlocal off
        arr = np.asarray(arr, np.float32).reshape(128, -1)
        CST_LAYOUT[name] = (off, arr.shape[1])
        cols.append(arr)
        off += arr.shape[1]

    put("ident", np.eye(128))
    put("ones", np.ones((128, 128)))
    put("negones", -np.ones((128, 128)))
    half = p // 64
    put("blk", (half[:, None] == half[None, :]).astype(np.float32))
    put("negblk", -(half[:, None] == half[None, :]).astype(np.float32))
    put("identst", (p[:, None] % 64 == np.arange(64)[None, :]).astype(np.float32))
    same = half[:, None] == half[None, :]
    put("tri_f", (same & (p[:, None] <= p[None, :])).astype(np.float32))
    put("tri_b", (same & (p[:, None] >= p[None, :])).astype(np.float32))
    put("half0", np.repeat((p < 64).astype(np.float32)[:, None], 128, 1))
    put("half1", np.repeat((p >= 64).astype(np.float32)[:, None], 128, 1))
    il = p % 64
    j = np.arange(64)

    def m(keep):
        return np.where(keep, 0.0, NEG).astype(np.float32)

    put("m1_f", m(il[:, None] > j[None, :]))
    put("m1_b", m(il[:, None] < j[None, :]))
    put("m2_f", m(j[None, :] > il[:, None]))
    put("m2_b", m(j[None, :] < il[:, None]))
    put("m3_f", m(j[None, :] >= il[:, None]))
    put("m3_b", m(j[None, :] <= il[:, None]))
    put("mask8", ((il[:, None] // 8) == (j[None, :] // 8)).astype(np.float32))
    for sz in (8, 16, 32):
        put("moff%d" % sz, (((il[:, None] // (2 * sz)) == (j[None, :] // (2 * sz)))
                            & ((il[:, None] // sz) != (j[None, :] // sz))).astype(np.float32))
    R = np.zeros((128, 128), np.float32)
    for q in range(128):
        if q % 64 < 32:
            R[q, q + 32] = -1.0
        else:
            R[q, q - 32] = 1.0
    put("rot", R.T)
    return np.concatenate(cols, 1)


CST = _build_consts()
NCST = CST.shape[1]


def _rope_tables():
    t = np.arange(TS)
    row = (t // 64).astype(np.float32)
    col = (t % 64).astype(np.float32)
    inv = (10000.0 ** (-np.arange(16, dtype=np.float32) / 16)).astype(np.float32)
    ang = np.concatenate([row[:, None] * inv, col[:, None] * inv], -1).astype(np.float32)
    cos = np.cos(ang).astype(np.float32).T
    sin = np.sin(ang).astype(np.float32).T
    return np.tile(cos, (4, 1)).copy(), np.tile(sin, (4, 1)).copy()


class Seq:
    def __init__(self, name, T, is_sample, idx, key0, tile0):
        self.name, self.T, self.is_sample, self.idx = name, T, is_sample, idx
        self.key0 = key0
        self.tile0 = tile0
        self.nctx = NCTX if is_sample else 0
        self.mj = 0 if is_sample else 1


def bcast(ap, axis, n):
    shp = list(ap.shape)
    shp.insert(axis, n)
    return ap.unsqueeze(axis).broadcast_to(shp)


def build_program(debug_outs=(), stop_after=None):
    nc = bass.Bass("TRN2", target_bir_lowering=False)
    es = ExitStack()
    P = Prog(nc)
    dbg = set(debug_outs)

    def din(name, shape, dt=F32):
        return nc.dram_tensor(name, list(shape), dt, kind="ExternalInput").ap()

    def dout(name, shape, dt=F32):
        return nc.dram_tensor(name, list(shape), dt, kind="ExternalOutput").ap()

    def dscr(name, shape, dt=F32):
        kind = "ExternalOutput" if name in dbg else "Internal"
        return nc.dram_tensor(name, list(shape), dt, kind=kind).ap()

    xs_in = din("xs", [TS, D])
    xp_in = din("xp", [NPS * TP, D])
    cvec = din("cvec", [2, D])
    ck_in = din("ck", [DEPTH, 2, NCTX, 64])
    cv_in = din("cv", [DEPTH, 2, NCTX, 64])
    st_in = din("st", [DEPTH, 2, 8, 64, 64])
    w_mod = din("w_mod", [DEPTH, D, 6 * D])
    b_mod = din("b_mod", [DEPTH, 6 * D])
    norm1 = din("norm1", [DEPTH, D])
    norm2 = din("norm2", [DEPTH, D])
    w_in = din("w_in", [DEPTH, D, INW])
    conv_w = din("conv_w", [DEPTH, 3, 1536])
    q_gain = din("q_gain", [DEPTH, 64])
    k_gain = din("k_gain", [DEPTH, 64])
    a_log = din("a_log", [DEPTH, 16])
    dt_bias = din("dt_bias", [DEPTH, 16])
    dn_gain = din("dn_gain", [DEPTH, 64])
    w_pa = din("w_pa", [DEPTH, 512, D])
    w_pd = din("w_pd", [DEPTH, 512, D])
    w_out = din("w_out", [DEPTH, D, D])
    w1 = din("w1", [DEPTH, D, DFF])
    b1 = din("b1", [DEPTH, DFF])
    w2 = din("w2", [DEPTH, DFF, D])
    b2 = din("b2", [DEPTH, D])
    final_norm = din("final_norm", [D])
    cst_in = din("cst", [128, NCST])
    cos_in = din("ropecos", [128, TS])
    sin_in = din("ropesin", [128, TS])
    ys_out = dout("ys", [TS, D])
    yp_out = dout("yp", [NPS * TP, D])
    nk_out = dout("nk", [NPS, DEPTH, 2, TP, 64])
    nv_out = dout("nv", [NPS, DEPTH, 2, TP, 64])
    nst_out = dout("nst", [NPS, DEPTH, 2, 8, 64, 64])

    seqs = [Seq("s", TS, True, 0, 0, 0)]
    for i in range(NPS):
        seqs.append(Seq(f"p{i}", TP, False, i, NCTX + TS + i * TP, TS // 128 + i * (TP // 128)))
    NKEY = NCTX + TS + NPS * TP
    NTILE = TS // 128 + NPS * TP // 128
    NVT = NKEY // 128

    XRES, PRE, QA, GATES, GS, QDT, KDT, QTM, KTM, VTM, OPART, OA, OD, X1, H2, ACTS = ({} for _ in range(16))
    for s in seqs:
        T = s.T
        XRES[s.name] = dscr(f"xres_{s.name}", [128, 8, T])
        PRE[s.name] = dscr(f"pre_{s.name}", [128, 12, T + 2])
        QA[s.name] = dscr(f"qa_{s.name}", [8, 64, T], BF16)
        GATES[s.name] = dscr(f"gates_{s.name}", [128, 16, T], BF16)
        GS[s.name] = dscr(f"gs_{s.name}", [T, 512], BF16)
        QDT[s.name] = dscr(f"qdt_{s.name}", [8, 64, T], BF16)
        KDT[s.name] = dscr(f"kdt_{s.name}", [8, 64, T], BF16)
        QTM[s.name] = dscr(f"qtm_{s.name}", [T, 512], BF16)
        KTM[s.name] = dscr(f"ktm_{s.name}", [T, 512], BF16)
        VTM[s.name] = dscr(f"vtm_{s.name}", [T, 512], BF16)
        OPART[s.name] = dscr(f"opart_{s.name}", [T, 512])
        OA[s.name] = dscr(f"oa_{s.name}", [8, 64, T], BF16)
        OD[s.name] = dscr(f"od_{s.name}", [128, 4, T], BF16)
        H2[s.name] = dscr(f"h2_{s.name}", [128, 8, T], BF16)

    def sb(name, shape, dt=F32):
        return es.enter_context(nc.sbuf_tensor("sb_" + name, list(shape), dt))

    cst = sb("cst", [128, NCST])
    cstb = sb("cstb", [128, NCST], BF16)

    def C(name, bf=False):
        o, n = CST_LAYOUT[name]
        return (cstb if bf else cst)[:, o:o + n]

    WAR = sb("warena", [128, 40960], BF16)
    KT_all = sb("kt_all", [128, NKEY], BF16)
    VA_all = sb("va_all", [128, NVT, 2, 65], BF16)
    LA = sb("la", [128, NTILE, 16])
    LB = sb("lb", [128, NTILE, 16])
    BETA = sb("beta", [128, NTILE, 16])
    MOD = sb("mod", [128, 48, 2])
    A1 = sb("a1", [128, 8, 2])
    A2 = sb("a2", [128, 8, 2])
    GB2 = sb("gb2", [128, 8, 2])
    n1f = sb("n1f", [128, 8])
    n2f = sb("n2f", [128, 8])
    fnf = sb("fnf", [128, 8])
    b2f = sb("b2f", [128, 8])
    b1f = sb("b1f", [128, 32])
    bmf = sb("bmf", [128, 48])
    cfm = sb("cfm", [128, 8, 2])
    scb = sb("scb", [128, 8, 2], BF16)
    qg = sb("qg", [128, 1])
    kg = sb("kg", [128, 1])
    cw = sb("cw", [128, 3, 12])
    dtb = sb("dtb", [128, 16])
    negA = sb("negA", [128, 16])
    dng = sb("dng", [128, 64])
    TB = 256
    BIGA = sb("bigA", [128, 12 * 258])
    BIGB = sb("bigB", [128, 12 * 256])
    HB = sb("hb16", [128, 16, 256], BF16)
    GATB = sb("gatb", [128, 16, 256], BF16)
    R1 = sb("r1", [128, 256])
    RSTD = sb("rstd", [128, 256])
    STG = [sb(f"stg{i}", [128, 512]) for i in range(4)]
    STB = [sb(f"stb{i}", [128, 512], BF16) for i in range(4)]
    COSB = sb("cosb", [128, 256])
    SINB = sb("sinb", [128, 256])
    SMALL = sb("small", [128, 64])
    SM = sb("sm", [128, 160])
    VAB = sb("vab", [128, 160])
    KTC = [sb(f"ktc{i}", [64, 8, 128], BF16) for i in range(2)]
    QTC = [sb(f"qtc{i}", [64, 8, 128], BF16) for i in range(2)]
    KTMC = [sb(f"ktmc{i}", [128, 512], BF16) for i in range(2)]
    QTMC = [sb(f"qtmc{i}", [128, 512], BF16) for i in range(2)]
    VTMC = [sb(f"vtmc{i}", [128, 512], BF16) for i in range(2)]
    def v512(big, i, p0=0, p1=128):
        return big[p0:p1, i * 512:(i + 1) * 512]

    E1, E2, E3, DG1, DG3 = (v512(BIGA, i) for i in range(5))
    U0 = [v512(BIGA, 5), v512(BIGB, 0)]
    OACC = [v512(BIGB, 1), v512(BIGB, 2)]
    RS = v512(BIGB, 3)
    S32 = [v512(BIGB, 4 + i, 0, 64).rearrange("p (h v) -> p h v", h=8) for i in range(2)]
    HBf = HB[:, :, :].rearrange("p a b -> p (a b)")
    GBf = GATB[:, :, :].rearrange("p a b -> p (a b)")
    PA = [v512(HBf, 0), v512(HBf, 1)]
    PT = [v512(HBf, 2), v512(HBf, 3)]
    RT = [v512(HBf, 4), v512(HBf, 5)]
    QKM = [v512(HBf, 6), v512(HBf, 7)]
    BEK = v512(GBf, 0)
    KDEC = [v512(GBf, 1), v512(GBf, 2)]
    BV = v512(GBf, 3)
    UB = [v512(GBf, 4), v512(GBf, 5)]
    DE = v512(GBf, 6)
    OB = v512(GBf, 7, 0, 64)
    XBW = [sb(f"xbw{i}", [128, 1024], BF16) for i in range(2)]
    XB = [XBW[0][:, 0:512], XBW[0][:, 512:1024], XBW[1][:, 0:512], XBW[1][:, 512:1024]]
    NWT = [sb(f"nwt{i}", [64, 16, 64], BF16) for i in range(2)]
    QDEC = [sb(f"qdec{i}", [64, 16, 64], BF16) for i in range(2)]
    SBF = [sb(f"sbf{i}", [64, 8, 64], BF16) for i in range(2)]
    EGL2 = [sb(f"egl2{i}", [128, 16]) for i in range(2)]
    QB = STB[3][:, :].rearrange("p (j t) -> p j t", j=4)
    PS = [es.enter_context(nc.psum_tensor(f"ps{i}", [128, 1024], F32)) for i in range(4)]
    dbg_mod = dscr("dbg_mod", [128, 48, 2])
    dbg_la = dscr("dbg_la", [128, NTILE, 16])
    dbg_lb = dscr("dbg_lb", [128, NTILE, 16])
    dbg_beta = dscr("dbg_beta", [128, NTILE, 16])

    XT = BIGA[:, 0:8 * 256].rearrange("p (c t) -> p c t", c=8)
    X1T = BIGB[:, 0:8 * 256].rearrange("p (c t) -> p c t", c=8)
    PRET = BIGA[:, :].rearrange("p (c t) -> p c t", c=12)
    CV = BIGB[:, :].rearrange("p (c t) -> p c t", c=12)
    SQ = HB[:, 0:8, :]
    HT = HB[:, 8:16, :]
    XTM = [BIGA[:, i * 1024:(i + 1) * 1024] for i in range(2)]
    XFM = [BIGB[:, i * 1024:(i + 1) * 1024].rearrange("p (c t) -> p c t", c=8) for i in range(2)]

    def psb(i):
        return PS[i // 2][:, (i % 2) * 512:(i % 2) * 512 + 512]

    def pk(i):
        return ("ps", i)

    def dma(q, out, in_, reads, writes, **kw):
        P.add(q, lambda e: e.dma_start(out=out, in_=in_, **kw), reads, writes, dma=True)

    def mm(out, lhsT, rhs, start, stop, reads, writes, **kw):
        P.add("pe", lambda e: e.matmul(out, lhsT, rhs, start=start, stop=stop, **kw), reads, writes)

    def tr(out, in_, ident, reads, writes):
        P.add("pe", lambda e: e.transpose(out, in_, ident), reads, writes)

    def act(out, in_, func, reads, writes, **kw):
        P.add("act", lambda e: e.activation(out, in_, func, **kw), reads, writes)

    def tt(eng, out, in0, in1, op, reads, writes):
        P.add(eng, lambda e: e.tensor_tensor(out, in0, in1, op), reads, writes)

    def tsc(eng, out, in0, s1, s2, op0, op1, reads, writes):
        if op1 is None:
            P.add(eng, lambda e: e.tensor_scalar(out, in0, s1, None, op0), reads, writes)
        else:
            P.add(eng, lambda e: e.tensor_scalar(out, in0, s1, s2, op0, op1), reads, writes)

    def stt(eng, out, in0, scalar, in1, op0, op1, reads, writes):
        P.add(eng, lambda e: e.scalar_tensor_tensor(out, in0, scalar, in1, op0, op1), reads, writes)

    def cp(eng, out, in_, reads, writes):
        if eng == "act":
            P.add("act", lambda e: e.copy(out, in_), reads, writes)
        else:
            P.add(eng, lambda e: e.tensor_copy(out, in_), reads, writes)

    def recip(out, in_, reads, writes):
        P.add("dve", lambda e: e.reciprocal(out, in_), reads, writes)

    def load_w(dst3, src2, K, wkey="war"):
        for k in range(K):
            dma("pool", dst3[:, k, :], src2[k * 128:(k + 1) * 128, :], [], [wkey])

    def rms_stats(src3, nt, srckey, sq=None, sqkey="hb", bank=0, r1=None, rstd=None, rkey=""):
        if sq is None:
            sq = SQ
        if r1 is None:
            r1, rstd = R1[:, :], RSTD[:, :]
        act(sq[:, :, :nt], src3, AF.Square, [srckey], [sqkey])
        for c in range(8):
            mm(psb(bank)[:, :nt], C("ones", True), sq[:, c, :nt], c == 0, c == 7, [sqkey, "cstb"], [pk(bank)])
        act(r1[:, :nt], psb(bank)[:, :nt], AF.Ln, [pk(bank)], ["r1" + rkey], bias=EPS, scale=1.0 / D)
        act(rstd[:, :nt], r1[:, :nt], AF.Exp, ["r1" + rkey], ["rstd" + rkey], scale=-0.5)

    dma("sp", cst[:], cst_in, [], ["cst"])
    cp("dve", cstb[:], cst[:], ["cst"], ["cstb"])
    P.add("pool", lambda e: e.memset(VA_all[:], 1.0), [], ["va"])
    dma("sp", fnf[:], final_norm.rearrange("(k p) -> p k", p=128), [], ["fnf"], allow_slow_non_contiguous=True)
    for j in range(2):
        dma("sp", cfm[:, :, j], cvec[j].rearrange("(k p) -> p k", p=128), [], ["cfm"], allow_slow_non_contiguous=True)
    act(scb[:], cfm[:], AF.Silu, ["cfm"], ["scb"])
    P.add("pool", lambda e: e.memset(SMALL[:], 0.0), [], ["small"])
    for s in seqs:
        for col in (0, s.T + 1):
            dma("sp", PRE[s.name][:, :, col:col + 1], SMALL[:, 0:12].unsqueeze(2), ["small"], [("pre", s.name, "pad", col)],
                allow_slow_non_contiguous=True)

    it = 0
    for s in seqs:
        src = xs_in if s.is_sample else xp_in[s.idx * TP:(s.idx + 1) * TP, :]
        for ti in range(s.T // 128):
            b = it % 2
            dma("sp", XTM[b], src[ti * 128:(ti + 1) * 128, :], [], ["bigA"])
            for c in range(8):
                tr(PS[b][:, c * 128:(c + 1) * 128], XTM[b][:, c * 128:(c + 1) * 128], C("ident"),
                   ["bigA", "cst"], [pk(2 * b), pk(2 * b + 1)])
            cp("act" if it % 2 == 0 else "dve", XFM[b], PS[b][:].rearrange("p (c t) -> p c t", c=8),
               [pk(2 * b), pk(2 * b + 1)], ["bigB"])
            dma("pool", XRES[s.name][:, :, ti * 128:(ti + 1) * 128], XFM[b], ["bigB"], [("xres", s.name, ti // 2)])
            it += 1

    def finish():
        P.emit(es)
        es.close()
        return nc

    if stop_after == "stage0":
        return finish()

    NLAYERS = DEPTH
    for l in range(NLAYERS):
        for (t_, src) in ((n1f, norm1[l]), (n2f, norm2[l]), (b2f, b2[l])):
            dma("sp", t_[:], src.rearrange("(k p) -> p k", p=128), [], ["lvec"], allow_slow_non_contiguous=True)
        dma("sp", b1f[:], b1[l].rearrange("(k p) -> p k", p=128), [], ["lvec"], allow_slow_non_contiguous=True)
        dma("sp", bmf[:], b_mod[l].rearrange("(k p) -> p k", p=128), [], ["lvec"], allow_slow_non_contiguous=True)
        for hh in range(2):
            dma("sp", qg[64 * hh:64 * hh + 64, :], q_gain[l].rearrange("(d o) -> d o", o=1), [], ["lvec"], allow_slow_non_contiguous=True)
            dma("sp", kg[64 * hh:64 * hh + 64, :], k_gain[l].rearrange("(d o) -> d o", o=1), [], ["lvec"], allow_slow_non_contiguous=True)
        for j in range(3):
            dma("sp", cw[:, j, :], conv_w[l, j].rearrange("(c p) -> p c", p=128), [], ["lvec"], allow_slow_non_contiguous=True)
        dma("sp", dtb[:], dt_bias[l:l + 1, :].broadcast_to([128, 16]), [], ["lvec"], allow_slow_non_contiguous=True)
        dma("sp", negA[:], a_log[l:l + 1, :].broadcast_to([128, 16]), [], ["lvec"], allow_slow_non_contiguous=True)
        dma("sp", dng[:], dn_gain[l:l + 1, :].broadcast_to([128, 64]), [], ["lvec"], allow_slow_non_contiguous=True)
        act(negA[:], negA[:], AF.Exp, ["lvec"], ["lvec2"])
        tsc("dve", negA[:], negA[:], -1.0, None, ALU.mult, None, ["lvec2"], ["lvec2"])

        Wm = WAR[:, 0:8 * 3072].rearrange("p (k n) -> p k n", k=8)
        for hh in range(2):
            load_w(Wm, w_mod[l][:, hh * 3072:(hh + 1) * 3072], 8)
            for n in range(24):
                nn = hh * 24 + n
                for k in range(8):
                    mm(psb(0)[:, nn * 2:nn * 2 + 2], Wm[:, k, n * 128:(n + 1) * 128], scb[:, k, :], k == 0, k == 7,
                       ["war", "scb"], [pk(0)])
        tt("dve", MOD[:], psb(0)[:, 0:96].rearrange("p (n j) -> p n j", j=2), bcast(bmf[:], 2, 2), ALU.add,
           [pk(0), "lvec"], ["mod"])
        stt("dve", A1[:], MOD[:, 8:16, :], 1.0, bcast(n1f[:], 2, 2), ALU.add, ALU.mult, ["mod", "lvec"], ["mod2"])
        stt("dve", A2[:], MOD[:, 32:40, :], 1.0, bcast(n2f[:], 2, 2), ALU.add, ALU.mult, ["mod", "lvec"], ["mod2"])
        tt("dve", GB2[:], MOD[:, 40:48, :], bcast(b2f[:], 2, 2), ALU.mult, ["mod", "lvec"], ["mod2"])
        MK = ["mod", "mod2", "lvec", "lvec2"]
        if stop_after == "adaln":
            dma("sp", dbg_mod, MOD[:], ["mod"], ["dbgmod"])
            return finish()

        Win = WAR[:, 0:8 * INW].rearrange("p (k n) -> p k n", k=8)
        load_w(Win, w_in[l], 8)
        bank_rr = [0]

        def next_bank():
            bank_rr[0] = (bank_rr[0] % 4) + 1
            return bank_rr[0]

        stg_rr = [0]

        def nstg():
            stg_rr[0] = (stg_rr[0] + 1) % 4
            return stg_rr[0]

        jobsA = [(s, blk) for s in seqs for blk in range(s.T // TB)]
        XTsA = [(XT, "bigA"), (X1T, "bigB")]
        HTsA = [(HB[:, 8:16, :], "ht"), (GATB[:, 0:8, :], "gatb")]

        def blockA(j):
            s, blk = jobsA[j]
            mj = s.mj
            t0 = blk * TB
            XTj, xkey = XTsA[j % 2]
            HTj, hkey = HTsA[j % 2]
            dma("sp", XTj, XRES[s.name][:, :, t0:t0 + TB], [("xres", s.name, blk)], [xkey])
            if s.is_sample:
                dma("sp", COSB[:], cos_in[:, t0:t0 + TB], [], ["cosb"])
                dma("sp", SINB[:], sin_in[:, t0:t0 + TB], [], ["sinb"])
            rms_stats(XTj, TB, xkey)
            tt("dve", XTj, XTj, bcast(RSTD[:, :], 1, 8), ALU.mult, [xkey, "rstd"], [xkey])
            for c in range(8):
                if c % 2 == 0:
                    act(HTj[:, c, :], XTj[:, c, :], AF.Identity, [xkey] + MK, [hkey],
                        scale=A1[:, c, mj:mj + 1], bias=MOD[:, c, mj:mj + 1])
                else:
                    tsc("dve", HTj[:, c, :], XTj[:, c, :], A1[:, c, mj:mj + 1], MOD[:, c, mj:mj + 1], ALU.mult, ALU.add,
                        [xkey] + MK, [hkey])

            yield

            def fm_chunk(col0):
                bk = next_bank()
                for k in range(8):
                    mm(psb(bk)[:, :TB], Win[:, k, col0:col0 + 128], HTj[:, k, :], k == 0, k == 7, ["war", hkey], [pk(bk)])
                return bk

            def qk_epi(c, bk):
                is_k = (c == 4)
                gain = kg if is_k else qg
                cp("act", STG[0][:, :TB], psb(bk)[:, :TB], [pk(bk)], ["stg0"])
                act(STB[0][:, :TB], psb(bk)[:, :TB], AF.Square, [pk(bk)], ["stb0"])
                yield
                mm(psb(6)[:, :TB], C("blk", True), STB[0][:, :TB], True, True, ["stb0", "cstb"], [pk(6)])
                act(STG[1][:, :TB], psb(6)[:, :TB], AF.Ln, [pk(6)], ["stg1"], bias=EPS, scale=1.0 / 64)
                act(STG[1][:, :TB], STG[1][:, :TB], AF.Exp, ["stg1"], ["stg1"], scale=-0.5)
                stt("dve", STG[0][:, :TB], STG[0][:, :TB], gain[:, 0:1], STG[1][:, :TB], ALU.mult, ALU.mult,
                    ["stg0", "stg1", "lvec"], ["stg0"])
                yield
                kcol = s.key0 + s.nctx + t0
                dst = KT_all[:, kcol:kcol + TB] if is_k else STB[1][:, :TB]
                dkey = "kt" if is_k else "stb1"
                if s.is_sample:
                    mm(psb(7)[:, :TB], C("rot"), STG[0][:, :TB], True, True, ["stg0", "cst"], [pk(7)])
                    tt("dve", STG[2][:, :TB], STG[0][:, :TB], COSB[:], ALU.mult, ["stg0", "cosb"], ["stg2"])
                    yield
                    tt("dve", STG[3][:, :TB], psb(7)[:, :TB], SINB[:], ALU.mult, [pk(7), "sinb"], ["stg3"])
                    tt("dve", dst, STG[2][:, :TB], STG[3][:, :TB], ALU.add, ["stg2", "stg3"], [dkey])
                else:
                    cp("dve", dst, STG[0][:, :TB], ["stg0"], [dkey])
                if not is_k:
                    dma("pool", QA[s.name][2 * c:2 * c + 2].rearrange("h d t -> (h d) t")[:, t0:t0 + TB], STB[1][:, :TB],
                        ["stb1"], [("qa", s.name)])
                elif not s.is_sample:
                    for t2 in range(TB // 128):
                        tr(psb(7)[:, t2 * 128:(t2 + 1) * 128], STG[0][:, t2 * 128:(t2 + 1) * 128], C("ident"),
                           ["stg0", "cst"], [pk(7)])
                    cp("act", STG[2][:, :TB], psb(7)[:, :TB], [pk(7)], ["stg2"])
                    for t2 in range(TB // 128):
                        for g in range(2):
                            dma("pool", nk_out[s.idx, l, g, t0 + t2 * 128:t0 + (t2 + 1) * 128, :],
                                STG[2][:, t2 * 128 + g * 64:t2 * 128 + g * 64 + 64], ["stg2"], [("nk", s.idx)])
                yield

            hrr = [0]

            def nh():
                hrr[0] = (hrr[0] + 1) % 4
                return hrr[0]

            def filler():
                for c in range(12):
                    bk = fm_chunk(C_QD + c * 128)
                    i = nh()
                    cp("act" if c % 2 == 0 else "dve", STG[i][:, 256:512], psb(bk)[:, :TB], [pk(bk)], [f"stgh{i}"])
                    dma("pool", PRE[s.name][:, c, 1 + t0:1 + t0 + TB], STG[i][:, 256:512], [f"stgh{i}"], [("pre", s.name, blk)])
                    yield
                for c in range(16):
                    bk = fm_chunk(C_GA + c * 128)
                    i = nh()
                    act(STB[i][:, 256:512], psb(bk)[:, :TB], AF.Sigmoid, [pk(bk)], [f"stbh{i}"])
                    dma("pool", GATES[s.name][:, c, t0:t0 + TB], STB[i][:, 256:512], [f"stbh{i}"], [("gates", s.name, blk)])
                    yield

            fg = filler()
            for c in range(5):
                bk = fm_chunk(C_QA + c * 128)
                for _ in qk_epi(c, bk):
                    next(fg, None)
            yield
            for _ in fg:
                pass
            for t2 in range(TB // 128):
                tsl = slice(t2 * 128, (t2 + 1) * 128)
                gti = s.tile0 + (t0 // 128) + t2
                vt = (s.key0 + s.nctx + t0) // 128 + t2
                for k in range(8):
                    mm(psb(5)[:, 0:512], HTj[:, k, tsl], Win[:, k, C_GO:C_GO + 512], k == 0, k == 7, ["war", hkey], [pk(5)])
                for k in range(8):
                    mm(psb(6)[:, 0:128], HTj[:, k, tsl], Win[:, k, C_VA:C_VA + 128], k == 0, k == 7, ["war", hkey], [pk(6)])
                for k in range(8):
                    mm(psb(6)[:, 128:160], HTj[:, k, tsl], Win[:, k, C_AI:C_AI + 32], k == 0, k == 7, ["war", hkey], [pk(6)])
                i = nstg()
                act(STB[i][:, :], psb(5)[:, :], AF.Silu, [pk(5)], [f"stb{i}", f"stbh{i}"])
                dma("pool", GS[s.name][t0 + t2 * 128:t0 + (t2 + 1) * 128, :], STB[i][:, :], [f"stb{i}", f"stbh{i}"], [("gs", s.name)])
                cp("dve", VAB[:, :], psb(6)[:, 0:160], [pk(6)], ["vab"])
                cp("dve", VA_all[:, vt, :, 0:64], VAB[:, 0:128].rearrange("p (g d) -> p g d", g=2), ["vab"], ["va"])
                if not s.is_sample:
                    for g in range(2):
                        dma("pool", nv_out[s.idx, l, g, t0 + t2 * 128:t0 + (t2 + 1) * 128, :], VAB[:, g * 64:g * 64 + 64],
                            ["vab"], [("nv", s.idx)])
                tt("dve", SM[:, 0:16], VAB[:, 128:144], dtb[:], ALU.add, ["vab", "lvec"], ["sm"])
                act(SM[:, 16:32], SM[:, 0:16], AF.Exp, ["sm"], ["sm1"])
                act(SM[:, 32:48], SM[:, 16:32], AF.Ln, ["sm1"], ["sm2"], bias=1.0)
                tt("dve", LA[:, gti, :], SM[:, 32:48], negA[:], ALU.mult, ["sm2", "lvec2"], ["la"])
                act(BETA[:, gti, :], VAB[:, 144:160], AF.Sigmoid, ["vab"], ["beta"])
                act(LB[:, gti, :], BETA[:, gti, :], AF.Ln, ["beta"], ["lb"])

        gA = [blockA(j) for j in range(len(jobsA))]
        next(gA[0])
        for j in range(len(jobsA)):
            next(gA[j])
            if j + 1 < len(jobsA):
                next(gA[j + 1])
            for _ in gA[j]:
                pass
        if "dbg_la" in dbg:
            dma("sp", dbg_la, LA[:], ["la"], ["dbgla"])
            dma("sp", dbg_lb, LB[:], ["lb"], ["dbglb"])
            dma("sp", dbg_beta, BETA[:], ["beta"], ["dbgbeta"])
        P.fence()
        if stop_after == "stageA":
            return finish()

        brr = [0]

        def nb():
            brr[0] = brr[0] % 3 + 1
            return brr[0]

        def genB():
            for s in seqs:
                for blk in range(s.T // TB):
                    t0 = blk * TB
                    dma("sp", PRET, PRE[s.name][:, :, t0:t0 + TB + 2],
                        [("pre", s.name, b_) for b_ in range(max(0, blk - 1), min(s.T // TB, blk + 2))]
                        + [("pre", s.name, "pad", 0), ("pre", s.name, "pad", s.T + 1)], ["bigA"])
                    for c in range(12):
                        tsc("dve", CV[:, c, :], PRET[:, c, 0:TB], cw[:, 0, c:c + 1], None, ALU.mult, None, ["bigA", "lvec"], [("cv", c)])
                        stt("dve", CV[:, c, :], PRET[:, c, 1:TB + 1], cw[:, 1, c:c + 1], CV[:, c, :], ALU.mult, ALU.add,
                            ["bigA", "lvec", ("cv", c)], [("cv", c)])
                        stt("dve", CV[:, c, :], PRET[:, c, 2:TB + 2], cw[:, 2, c:c + 1], CV[:, c, :], ALU.mult, ALU.add,
                            ["bigA", "lvec", ("cv", c)], [("cv", c)])
                        if c % 3 == 2:
                            yield
                    for c4 in range(3):
                        cs_ = slice(c4 * 4, c4 * 4 + 4)
                        ck = [("cv", c) for c in range(c4 * 4, c4 * 4 + 4)]
                        SG = PRET[:, cs_, 0:TB]
                        act(SG, CV[:, cs_, :], AF.Exp, ck + ["bigA"], ["bigA"], scale=-1.0)
                        act(SG, SG, AF.Ln, ["bigA"], ["bigA"], bias=1.0)
                        act(SG, SG, AF.Exp, ["bigA"], ["bigA"], scale=-1.0)
                        tt("dve", CV[:, cs_, :], CV[:, cs_, :], SG, ALU.mult, ck + ["bigA"], ck)
                        yield
                    for c in range(8):
                        act(STB[0][:, :TB], CV[:, c, :], AF.Square, [("cv", c)], ["stb0"])
                        mm(psb(7)[:, :TB], C("blk", True), STB[0][:, :TB], True, True, ["stb0", "cstb"], [pk(7)])
                        act(STG[1][:, :TB], psb(7)[:, :TB], AF.Ln, [pk(7)], ["stg1"], bias=EPS, scale=1.0)
                        act(STG[1][:, :TB], STG[1][:, :TB], AF.Exp, ["stg1"], ["stg1"], scale=-0.5)
                        i = nb()
                        stt("dve", STB[i][:, :TB], CV[:, c, :], 0.125 if c < 4 else 1.0, STG[1][:, :TB], ALU.mult, ALU.mult,
                            [("cv", c), "stg1"], [f"stb{i}"])
                        cp("act", CV[:, c, :], STB[i][:, :TB], [f"stb{i}"], [("cv", c)])
                        dstT = QDT if c < 4 else KDT
                        cc = c % 4
                        dma("pool", dstT[s.name][2 * cc:2 * cc + 2].rearrange("h d t -> (h d) t")[:, t0:t0 + TB], STB[i][:, :TB],
                            [f"stb{i}"], [("qkdt", s.name)])
                        yield
                    for t2 in range(TB // 128):
                        for grp, dstM in enumerate((QTM, KTM, VTM)):
                            for cc in range(4):
                                tr(psb(7)[:, cc * 128:(cc + 1) * 128], CV[:, grp * 4 + cc, t2 * 128:(t2 + 1) * 128], C("ident"),
                                   [("cv", grp * 4 + cc), "cst"], [pk(7)])
                            i = nb()
                            cp("dve", STB[i][:, :], psb(7)[:, :], [pk(7)], [f"stb{i}"])
                            dma("pool", dstM[s.name][t0 + t2 * 128:t0 + (t2 + 1) * 128, :], STB[i][:, :], [f"stb{i}"], [("tm", s.name)])
                            yield

        if stop_after == "stageB":
            for _ in genB():
                pass
            P.fence()
            return finish()

        for t2 in range(NCTX // 128):
            dma("sp", STG[0][:, 0:128].rearrange("p (g d) -> p g d", g=2),
                ck_in[l, :, t2 * 128:(t2 + 1) * 128, :].rearrange("g p d -> p g d"), [], ["stg0"])
            tr(psb(0)[:, 0:128], STG[0][:, 0:128], C("ident"), ["stg0", "cst"], [pk(0)])
            cp("dve", KT_all[:, t2 * 128:(t2 + 1) * 128], psb(0)[:, 0:128], [pk(0)], ["kt"])
            dma("sp", STG[1][:, 0:128].rearrange("p (g d) -> p g d", g=2),
                cv_in[l, :, t2 * 128:(t2 + 1) * 128, :].rearrange("g p d -> p g d"), [], ["stg1"])
            cp("dve", VA_all[:, t2, :, 0:64], STG[1][:, 0:128].rearrange("p (g d) -> p g d", g=2), ["stg1"], ["va"])
        QBz = [[WAR[:, (qp * 2 + g) * 512:(qp * 2 + g + 1) * 512] for g in range(2)] for qp in range(2)]
        VAp = WAR[:, 2048:2048 + NVT * 256].rearrange("p (t g d) -> p t g d", t=NVT, g=2)
        P.add("pool", lambda e: e.memset(WAR[:, 0:2048 + NVT * 256], 0.0), [], ["qbz", "vap"])
        cp("pool", VAp[:, :, :, 0:65], VA_all[:, :, :, :], ["va", "vap"], ["vap"])
        RS = WAR[:, 2048 + NVT * 256:2048 + NVT * 256 + 1024].bitcast(F32)
        items = []
        qcount = 0
        for s in seqs:
            ktiles = []
            if s.is_sample:
                ktiles += list(range(NCTX // 128))
            ktiles += [(s.key0 + s.nctx) // 128 + i for i in range(s.T // 128)]
            for qi in range(s.T // 128):
                for n_, kt in enumerate(ktiles):
                    items.append((s, qi, n_, kt, n_ == 0, n_ == len(ktiles) - 1, qcount % 2))
                qcount += 1
        LAG = 1

        def genAttn():
            for idx in range(len(items) + LAG):
                if idx < len(items):
                    s, qi, n_, kt, first, last, qp = items[idx]
                    q0 = qi * 128
                    if first:
                        for g2 in range(2):
                            dma("sp", QBz[qp][g2][64 * g2:64 * g2 + 64, :].rearrange("p (j t) -> p j t", j=4),
                                QA[s.name][4 * g2:4 * g2 + 4, :, q0:q0 + 128].rearrange("j d t -> d j t"),
                                [("qa", s.name), "qbz"], [("qb", qp)])
                    r_ = idx % 2
                    for g in range(2):
                        mm(PS[r_][:, g * 512:(g + 1) * 512], KT_all[:, kt * 128:(kt + 1) * 128],
                           QBz[qp][g], True, True, ["kt", ("qb", qp)],
                           [pk(2 * r_), pk(2 * r_ + 1)])
                    act(XBW[r_][:, :], PS[r_][:, :], AF.Exp, [pk(2 * r_), pk(2 * r_ + 1)], [("ptt", r_)], scale=0.125)
                if idx >= LAG:
                    s, qi, n_, kt, first, last, qp = items[idx - LAG]
                    q0 = qi * 128
                    r_ = (idx - LAG) % 2
                    for g in range(2):
                        ob = 4 + g
                        mm(psb(ob)[:, :], VAp[:, kt, g, :], XBW[r_][:, g * 512:(g + 1) * 512], first, last,
                           ["vap", ("ptt", r_)], [pk(ob)])
                    if last:
                        for g in range(2):
                            ob = 4 + g
                            cp("dve", RS[64:65, :], psb(ob)[64:65, :], [pk(ob)], ["rs"])
                            act(RS[64:65, :], RS[64:65, :], AF.Ln, ["rs"], ["rs"])
                            act(RS[64:65, :], RS[64:65, :], AF.Exp, ["rs"], ["rs"], scale=-1.0)
                            mm(psb(6)[0:64, :], C("ones")[64:65, 0:64], RS[64:65, :], True, True, ["rs", "cst"], [pk(6)])
                            cp("dve", STG[0][0:64, :], psb(ob)[0:64, :], [pk(ob)], ["stg0"])
                            tt("dve", OB[:, :], STG[0][0:64, :], psb(6)[0:64, :], ALU.mult, ["stg0", pk(6)], ["ob"])
                            dma("sp", OA[s.name][4 * g:4 * g + 4, :, q0:q0 + 128].rearrange("j d t -> d j t"),
                                OB[:, :].rearrange("p (j t) -> p j t", j=4), ["ob"], [("oa", s.name)])
                yield

        gB_, gAt_ = genB(), genAttn()
        doneB = doneA = False
        it_ = 0
        while not (doneB and doneA):
            if not doneB:
                try:
                    next(gB_)
                except StopIteration:
                    doneB = True
            for _ in range(3 + (1 if it_ % 4 == 3 else 0)):
                if not doneA:
                    try:
                        next(gAt_)
                    except StopIteration:
                        doneA = True
            it_ += 1
        P.fence()
        if stop_after == "attn":
            return finish()

        H8 = 8
        CUT = 0

        def v3(t):
            return t.rearrange("p (h j) -> p h j", h=H8)

        woff = [0]

        def wtake(n, f32=False):
            ap = WAR[:, woff[0]:woff[0] + n]
            woff[0] += n
            return ap.bitcast(F32) if f32 else ap

        DS = [dict(SM=SM, DG1=DG1, DG3=DG3, DE=DE, E1=E1, E2=E2, E3=E3, PA=PA, PT=PT, RT=RT, XB=XB, BEK=BEK, BV=BV), None]
        DS[1] = dict(E1=wtake(1024, True), E2=wtake(1024, True), E3=wtake(1024, True), DG1=wtake(1024, True),
                     DG3=wtake(1024, True), SM=wtake(320, True),
                     PA=[wtake(512), wtake(512)], PT=[wtake(512), wtake(512)], RT=[wtake(512), wtake(512)],
                     XB=[wtake(512) for _ in range(4)], BEK=wtake(512), BV=wtake(512), DE=wtake(512))
        def w64(n):
            ap = WAR[0:64, woff[0]:woff[0] + n]
            woff[0] += n
            return ap.rearrange("p (x i) -> p x i", x=16)

        NWT2 = [[NWT[d][:, :, :], w64(1024)] for d in range(2)]
        QDEC2 = [[QDEC[d][:, :, :], w64(1024)] for d in range(2)]
        U02 = [[U0[d], wtake(1024, True)] for d in range(2)]
        QKM2 = [[QKM[d], wtake(512)] for d in range(2)]
        KDEC2 = [[KDEC[d], wtake(512)] for d in range(2)]
        EGL22 = [[EGL2[d][:, :], wtake(32, True)] for d in range(2)]
        ab = [0]
        fbk = [0]
        ppr = [0]

        def abank():
            ab[0] = (ab[0] + 1) % 6
            return ab[0]

        def fbank():
            fbk[0] ^= 1
            return 6 + fbk[0]

        def ppair():
            ppr[0] = (ppr[0] + 1) % 3
            return ppr[0]

        idb = bcast(C("identst", True), 1, H8)
        ist = bcast(C("identst"), 1, H8)

        def msk(name):
            return bcast(C(name, True), 1, H8)

        def prep(s, d, m, par):
            NWTp, QDECp, U0p, QKMp, KDECp, EGL2p = NWT2[d][par], QDEC2[d][par], U02[d][par], QKM2[d][par], KDEC2[d][par], EGL22[d][par]
            kq = (d, par)
            T_ = DS[d]
            SMd, DG1d, DG3d, DEd = T_["SM"], T_["DG1"], T_["DG3"], T_["DE"]
            E1d, E2d, E3d = T_["E1"], T_["E2"], T_["E3"]
            PAd, PTd, RTd, XBd, BEKd, BVd = T_["PA"], T_["PT"], T_["RT"], T_["XB"], T_["BEK"], T_["BV"]

            def K(name, *x):
                return (name, d) + tuple(x)

            gti = s.tile0 + m
            sfx = "f" if d == 0 else "b"
            rows = slice(m * 128, (m + 1) * 128)
            dma("sp", KTC[d][:, :, :], KDT[s.name][:, :, rows].rearrange("h d t -> d h t"), [("qkdt", s.name)], [("ktc", d)])
            dma("sp", QTC[d][:, :, :], QDT[s.name][:, :, rows].rearrange("h d t -> d h t"), [("qkdt", s.name)], [("qtc", d)])
            dma("sp", KTMC[d][:, :], KTM[s.name][rows, :], [("tm", s.name)], [("ktmc", d)])
            dma("sp", QTMC[d][:, :], QTM[s.name][rows, :], [("tm", s.name)], [("qtmc", d)])
            dma("sp", VTMC[d][:, :], VTM[s.name][rows, :], [("tm", s.name)], [("vtmc", d)])
            la = LA[:, gti, 8 * d:8 * d + 8]
            lb = LB[:, gti, 8 * d:8 * d + 8]
            be_ = BETA[:, gti, 8 * d:8 * d + 8]
            bg = fbank()
            mm(psb(bg)[:, 0:8], C("tri_" + sfx), la, True, True, ["la", "cst"], [pk(bg)])
            mm(psb(bg)[:, 8:16], C("half0"), la, True, True, ["la", "cst"], [pk(bg)])
            mm(psb(bg)[:, 16:24], C("half1"), la, True, True, ["la", "cst"], [pk(bg)])
            cp("dve", SMd[:, 0:24], psb(bg)[:, 0:24], [pk(bg)], [K("sm")])
            yield
            tt("dve", SMd[:, 24:32], SMd[:, 0:8], lb, ALU.add, [K("sm"), "lb"], [K("sm_glb")])
            act(SMd[:, 32:40], SMd[:, 0:8], AF.Exp, [K("sm")], [K("sm_eg")])
            cp("dve", SMd[0:64, 40:48], SMd[0:64, 8:16], [K("sm")], [K("sm_glo")])
            cp("dve", SMd[64:128, 40:48], SMd[64:128, 16:24], [K("sm")], [K("sm_glo")])
            tt("dve", SMd[:, 48:56], SMd[:, 40:48], SMd[:, 0:8], ALU.subtract, [K("sm"), K("sm_glo")], [K("sm_ek")])
            act(SMd[:, 48:56], SMd[:, 48:56], AF.Exp, [K("sm_ek")], [K("sm_ek")])
            act(EGL2p, SMd[:, 8:24], AF.Exp, [K("sm")], [("egl2",) + kq])
            tt("dve", SMd[:, 56:64], be_, SMd[:, 32:40], ALU.mult, ["beta", K("sm_eg")], [K("sm_be")])
            tsc("dve", SMd[:, 64:72], SMd[:, 0:8], -1.0, None, ALU.mult, None, [K("sm")], [K("sm_ng")])
            tt("pool", v3(DG1d), ist, bcast(SMd[:, 24:32], 2, 64), ALU.mult, ["cst", K("sm_glb")], [K("dg1")])
            tt("pool", v3(DG3d), ist, bcast(SMd[:, 0:8], 2, 64), ALU.mult, ["cst", K("sm")], [K("dg3")])
            tt("dve", v3(DEd), ist, bcast(SMd[:, 32:40], 2, 64), ALU.mult, ["cst", K("sm_eg")], [K("de")])
            yield
            b1 = fbank()
            mm(psb(b1)[:, :], C("negblk"), DG3d, True, False, [K("dg3"), "cst"], [pk(b1)])
            mm(psb(b1)[:, :], C("ident"), bcast(SMd[:, 24:32], 2, 64), False, False, [K("sm_glb"), "cst"], [pk(b1)])
            mm(psb(b1)[:, :], C("ident", True), msk("m1_" + sfx), False, True, ["cstb"], [pk(b1)])
            act(E1d, psb(b1)[:, :], AF.Exp, [pk(b1)], [K("e1")])
            yield
            b2 = fbank()
            mm(psb(b2)[:, :], C("blk"), DG1d, True, False, [K("dg1"), "cst"], [pk(b2)])
            mm(psb(b2)[:, :], C("ident"), bcast(SMd[:, 64:72], 2, 64), False, False, [K("sm_ng"), "cst"], [pk(b2)])
            mm(psb(b2)[:, :], C("ident", True), msk("m2_" + sfx), False, True, ["cstb"], [pk(b2)])
            act(E2d, psb(b2)[:, :], AF.Exp, [pk(b2)], [K("e2")])
            yield
            b3 = fbank()
            mm(psb(b3)[:, :], C("blk"), DG3d, True, False, [K("dg3"), "cst"], [pk(b3)])
            mm(psb(b3)[:, :], C("ident"), bcast(SMd[:, 64:72], 2, 64), False, False, [K("sm_ng"), "cst"], [pk(b3)])
            mm(psb(b3)[:, :], C("ident", True), msk("m3_" + sfx), False, True, ["cstb"], [pk(b3)])
            act(E3d, psb(b3)[:, :], AF.Exp, [pk(b3)], [K("e3")])
            yield
            bkk, bqk = abank(), abank()
            for h in range(H8):
                for a in range(2):
                    ts_ = slice(64 * a, 64 * a + 64)
                    mm(psb(bkk)[ts_, h * 64:(h + 1) * 64], KTC[d][:, h, ts_], KTC[d][:, h, ts_], True, True,
                       [("ktc", d)], [pk(bkk)], tile_position=(0, 64 * a))
                    mm(psb(bqk)[ts_, h * 64:(h + 1) * 64], KTC[d][:, h, ts_], QTC[d][:, h, ts_], True, True,
                       [("ktc", d), ("qtc", d)], [pk(bqk)], tile_position=(0, 64 * a))
            A_, AT_ = PAd[0], PTd[0]
            kA, kAT = K("pa", 0), K("pt", 0)
            tt("dve", A_, psb(bkk)[:, :], E1d, ALU.mult, [pk(bkk), K("e1")], [kA])
            tt("dve", AT_, psb(bkk)[:, :], E2d, ALU.mult, [pk(bkk), K("e2")], [kAT])
            tt("dve", QKMp, psb(bqk)[:, :], E3d, ALU.mult, [pk(bqk), K("e3")], [("qkm",) + kq])
            yield

            def grp(L, R, lkey, rkey):
                bank = abank()
                for h in range(H8):
                    for a in range(2):
                        ts_ = slice(64 * a, 64 * a + 64)
                        hs = slice(h * 64, (h + 1) * 64)
                        mm(psb(bank)[ts_, hs], L[ts_, hs], R[ts_, hs], True, True, [lkey, rkey], [pk(bank)],
                           tile_position=(64 * a, 64 * a))
                return bank

            D_, DT_ = PAd[1], PTd[1]
            kD, kDT = K("pa", 1), K("pt", 1)
            X = [XBd[0], XBd[1], BEKd, BVd, XBd[2], XBd[3]]
            kX = [K("xb", 0), K("xb", 1), K("bek"), K("bv"), K("xb", 2), K("xb", 3)]
            kR = [K("rt", 0), K("rt", 1)]
            tt("pool", v3(D_), v3(A_), msk("mask8"), ALU.mult, [kA, "cstb"], [kD])
            tt("pool", v3(DT_), v3(AT_), msk("mask8"), ALU.mult, [kAT, "cstb"], [kDT])
            tt("dve", v3(X[2]), idb, v3(DT_), ALU.subtract, ["cstb", kDT], [kX[2]])
            yield
            g1 = grp(DT_, D_, kDT, kD)
            g2 = grp(D_, DT_, kD, kDT)
            cp("act", X[0], psb(g1)[:, :], [pk(g1)], [kX[0]])
            tt("dve", v3(RTd[1]), v3(X[0]), idb, ALU.add, [kX[0], "cstb"], [kR[1]])
            cp("dve", RTd[0], psb(g2)[:, :], [pk(g2)], [kR[0]])
            yield
            g3 = grp(RTd[0], X[0], kR[0], kX[0])
            tt("dve", v3(X[1]), v3(psb(g3)[:, :]), idb, ALU.add, [pk(g3), "cstb"], [kX[1]])
            g1 = grp(RTd[1], X[2], kR[1], kX[2])
            cp("act", X[3], psb(g1)[:, :], [pk(g1)], [kX[3]])
            yield
            g2 = grp(X[3], X[1], kX[3], kX[1])
            g3 = grp(X[1], X[3], kX[1], kX[3])
            cp("act", X[4], psb(g2)[:, :], [pk(g2)], [kX[4]])
            cp("dve", X[5], psb(g3)[:, :], [pk(g3)], [kX[5]])
            yield
            Tb, kT = [X[4], X[0]], [kX[4], kX[0]]
            Mb, kM = [X[5], X[1]], [kX[5], kX[1]]
            cur = 0
            for li, mname in enumerate(("moff8", "moff16", "moff32")):
                last = (li == 2)
                nxt = 1 - cur
                tt("pool", v3(D_), v3(A_), msk(mname), ALU.mult, [kA, "cstb"], [kD])
                if not last:
                    tt("pool", v3(DT_), v3(AT_), msk(mname), ALU.mult, [kAT, "cstb"], [kDT])
                g1 = grp(D_, Mb[cur], kD, kM[cur])
                cp("act", RTd[1], psb(g1)[:, :], [pk(g1)], [kR[1]])
                if not last:
                    g2 = grp(DT_, Tb[cur], kDT, kT[cur])
                    cp("dve", RTd[0], psb(g2)[:, :], [pk(g2)], [kR[0]])
                yield
                g3 = grp(Tb[cur], RTd[1], kT[cur], kR[1])
                tt("dve", Mb[nxt], Mb[cur], psb(g3)[:, :], ALU.subtract, [kM[cur], pk(g3)], [kM[nxt]])
                if not last:
                    g1 = grp(Mb[cur], RTd[0], kM[cur], kR[0])
                    tt("dve", Tb[nxt], Tb[cur], psb(g1)[:, :], ALU.subtract, [kT[cur], pk(g1)], [kT[nxt]])
                cur = nxt
                yield
            TTm = Mb[cur]
            tkey = kM[cur]
            tt("dve", v3(BEKd), v3(KTMC[d][:, :]), bcast(SMd[:, 56:64], 2, 64), ALU.mult, [("ktmc", d), K("sm_be")], [K("bek")])
            tt("pool", v3(KDECp), v3(KTMC[d][:, :]), bcast(SMd[:, 48:56], 2, 64), ALU.mult, [("ktmc", d), K("sm_ek")], [("kdec",) + kq])
            tt("pool", v3(BVd), v3(VTMC[d][:, :]), bcast(be_, 2, 64), ALU.mult, [("vtmc", d), "beta"], [K("bv")])
            yield
            pp_ = ppair()
            pkeys = [pk(2 * pp_), pk(2 * pp_ + 1)]
            bu0 = abank()
            while bu0 in (2 * pp_, 2 * pp_ + 1):
                bu0 = abank()
            for h in range(H8):
                for a in range(2):
                    ts_ = slice(64 * a, 64 * a + 64)
                    hs = slice(h * 64, (h + 1) * 64)
                    cs = slice((a * 8 + h) * 64, (a * 8 + h) * 64 + 64)
                    mm(PS[pp_][0:64, cs], BEKd[ts_, hs], TTm[ts_, hs], True, True, [K("bek"), tkey], pkeys,
                       tile_position=(64 * a, 0))
                    mm(psb(bu0)[ts_, hs], TTm[ts_, hs], BVd[ts_, hs], True, True, [tkey, K("bv")], [pk(bu0)],
                       tile_position=(64 * a, 64 * a))
            tsc("dve", NWTp, PS[pp_][0:64, :].rearrange("p (x i) -> p x i", x=16), -1.0, None, ALU.mult, None,
                pkeys, [("nwt",) + kq])
            cp("act", U0p, psb(bu0)[:, :], [pk(bu0)], [("u0",) + kq])
            yield
            pp_ = ppair()
            pkeys = [pk(2 * pp_), pk(2 * pp_ + 1)]
            for a in range(2):
                mm(PS[pp_][0:64, a * 512:(a + 1) * 512], C("half%d" % a, True)[:, 0:64], DEd, True, True,
                   [K("de"), "cstb"], pkeys)
            for a in range(2):
                tt("dve", QDECp[:, a * 8:(a + 1) * 8, :],
                   QTC[d][:, :, a * 64:(a + 1) * 64],
                   PS[pp_][0:64, a * 512:(a + 1) * 512].rearrange("p (h i) -> p h i", h=8), ALU.mult,
                   [("qtc", d)] + pkeys, [("qdec",) + kq])
            yield

        def steps(s, d, m, par, first_visit):
            NWTp, QDECp, U0p, QKMp, KDECp, EGL2p = NWT2[d][par], QDEC2[d][par], U02[d][par], QKM2[d][par], KDEC2[d][par], EGL22[d][par]
            kq = (d, par)
            SMd = DS[d]["SM"]

            def K(name, *x):
                return (name, d) + tuple(x)

            rows = slice(m * 128, (m + 1) * 128)
            for a in ((0, 1) if d == 0 else (1, 0)):
                ts_ = slice(64 * a, 64 * a + 64)
                pu = abank()
                for h in range(H8):
                    hs = slice(h * 64, (h + 1) * 64)
                    mm(psb(pu)[ts_, hs], NWTp[:, a * 8 + h, :], SBF[d][:, h, :], True, True,
                       [("nwt",) + kq, ("sbf", d)], [pk(pu)], tile_position=(0, 64 * a))
                tt("dve", UB[d][ts_, :], U0p[ts_, :], psb(pu)[ts_, :], ALU.add, [("u0",) + kq, pk(pu)], [("ub", d)])
                yield
                po, pob, pS_ = abank(), abank(), abank()
                for h in range(H8):
                    hs = slice(h * 64, (h + 1) * 64)
                    mm(psb(po)[ts_, hs], QDECp[:, a * 8 + h, :], SBF[d][:, h, :], True, True,
                       [("qdec",) + kq, ("sbf", d)], [pk(po)], tile_position=(0, 64 * a))
                    mm(psb(pob)[ts_, hs], QKMp[ts_, hs], UB[d][ts_, hs], True, True,
                       [("qkm",) + kq, ("ub", d)], [pk(pob)], tile_position=(64 * a, 64 * a))
                    mm(psb(pS_)[0:64, hs], KDECp[ts_, hs], UB[d][ts_, hs], True, True,
                       [("kdec",) + kq, ("ub", d)], [pk(pS_)], tile_position=(64 * a, 0))
                tt("dve", S32[d][:, :, :], S32[d][:, :, :], bcast(EGL2p[0:64, a * 8:a * 8 + 8], 2, 64), ALU.mult,
                   [("s32", d), ("egl2",) + kq], [("s32", d)])
                tt("dve", S32[d][:, :, :], S32[d][:, :, :], psb(pS_)[0:64, :].rearrange("p (h v) -> p h v", h=H8), ALU.add,
                   [("s32", d), pk(pS_)], [("s32", d)])
                cp("act", SBF[d][:, :, :], S32[d][:, :, :], [("s32", d)], [("sbf", d)])
                cp("act", OACC[d][ts_, :], psb(po)[ts_, :], [pk(po)], [("oacc", d)])
                tt("dve", OACC[d][ts_, :], OACC[d][ts_, :], psb(pob)[ts_, :], ALU.add, [("oacc", d), pk(pob)], [("oacc", d)])
                yield
            if first_visit:
                dma("sp", OPART[s.name][rows, :], OACC[d][:, :], [("oacc", d)], [("opart", s.name, m)])
            else:
                dma("sp", STG[d][:, :], OPART[s.name][rows, :], [("opart", s.name, m)], [f"stg{d}"])
                tt("dve", OACC[d][:, :], OACC[d][:, :], STG[d][:, :], ALU.add, [("oacc", d), f"stg{d}"], [("oacc", d)])
                tt("pool", STG[2 + d][:, :], OACC[d][:, :], OACC[d][:, :], ALU.mult, [("oacc", d)], [f"stg{2 + d}"])
                P.add("dve", lambda e: e.tensor_reduce(SMd[:, 80:88], v3(STG[2 + d][:, :]), AX.X, ALU.add),
                      [f"stg{2 + d}"], [K("sm_rn")])
                act(SMd[:, 80:88], SMd[:, 80:88], AF.Ln, [K("sm_rn")], [K("sm_rn")], bias=EPS, scale=1.0 / 64)
                act(SMd[:, 80:88], SMd[:, 80:88], AF.Exp, [K("sm_rn")], [K("sm_rn")], scale=-0.5)
                yield
                tt("dve", v3(OACC[d][:, :]), v3(OACC[d][:, :]), bcast(SMd[:, 80:88], 2, 64), ALU.mult,
                   [("oacc", d), K("sm_rn")], [("oacc", d)])
                tt("dve", v3(OACC[d][:, :]), v3(OACC[d][:, :]), bcast(dng[:], 1, H8), ALU.mult,
                   [("oacc", d), "lvec"], [("oacc", d)])
                dma("sp", STB[d][:, :], GS[s.name][rows, :], [("gs", s.name)], [f"stb{d}"])
                tt("dve", OACC[d][:, :], OACC[d][:, :], STB[d][:, :], ALU.mult, [("oacc", d), f"stb{d}"], [("oacc", d)])
                bt = fbank()
                for cc in range(4):
                    tr(psb(bt)[:, cc * 128:(cc + 1) * 128], OACC[d][:, cc * 128:(cc + 1) * 128], C("ident"),
                       [("oacc", d), "cst"], [pk(bt)])
                cp("act", STB[2 + d][:, :], psb(bt)[:, :], [pk(bt)], [f"stb{2 + d}"])
                dma("sp", OD[s.name][:, :, rows], STB[2 + d][:, :].rearrange("p (c t) -> p c t", c=4), [f"stb{2 + d}"],
                    [("od", s.name)])

        for s in seqs:
            NP_ = s.T // 128
            for d in range(2):
                if s.is_sample:
                    dma("sp", S32[d][:, :, :], st_in[l, d].rearrange("h k v -> k h v"), [], [("s32", d)])
                else:
                    P.add("pool", lambda e, d=d: e.memset(S32[d][:, :, :], 0.0), [], [("s32", d)])
                cp("act", SBF[d][:, :, :], S32[d][:, :, :], [("s32", d)], [("sbf", d)])
            visited = set()

            def mof(d, step):
                return step if d == 0 else NP_ - 1 - step

            def rr(gens):
                active = list(gens)
                while active:
                    for g_ in list(active):
                        try:
                            next(g_)
                        except StopIteration:
                            active.remove(g_)

            rr([prep(s, d, mof(d, 0), 0) for d in range(2)])
            for step in range(NP_):
                gens = []
                for d in range(2):
                    m = mof(d, step)
                    gens.append(steps(s, d, m, step % 2, m not in visited))
                for d in range(2):
                    visited.add(mof(d, step))
                if step + 1 < NP_:
                    for d in range(2):
                        gens.append(prep(s, d, mof(d, step + 1), (step + 1) % 2))
                rr(gens)
            if not s.is_sample:
                for d in range(2):
                    dma("sp", nst_out[s.idx, l, d].rearrange("h k v -> k h v"), S32[d][:, :, :], [("s32", d)], [("nst", s.idx)])
        P.fence()
        if stop_after == "scan":
            return finish()

        Wpa = WAR[:, 0:4096].rearrange("p (k n) -> p k n", k=4)
        Wpd = WAR[:, 4096:8192].rearrange("p (k n) -> p k n", k=4)
        Wo = WAR[:, 8192:16384].rearrange("p (k n) -> p k n", k=8)
        load_w(Wpa, w_pa[l], 4)
        load_w(Wpd, w_pd[l], 4)
        load_w(Wo, w_out[l], 8)
        def w3(off, n, f32, shape3):
            ap = WAR[:, off:off + n]
            if f32:
                ap = ap.bitcast(F32)
            return ap.rearrange("p (c t) -> p c t", c=shape3)

        DSET = [
            dict(HB=HB[:, :, :], GATB=GATB[:, :, :], XT=XT, X1T=X1T, S0=STG[0][:, :], S1=STG[1][:, :], R1=R1[:, :], RSTD=RSTD[:, :],
                 bpa=0, bpd=1, bop=(2, 3), bst=0, k="0"),
            dict(HB=w3(16384, 4096, False, 16), GATB=w3(20480, 4096, False, 16), XT=w3(24576, 4096, True, 8),
                 X1T=w3(28672, 4096, True, 8), S0=WAR[:, 32768:33792].bitcast(F32), S1=WAR[:, 33792:34816].bitcast(F32),
                 R1=WAR[:, 34816:35328].bitcast(F32), RSTD=WAR[:, 35328:35840].bitcast(F32),
                 bpa=4, bpd=5, bop=(6, 7), bst=4, k="1"),
        ]
        jobsD = [(s, blk) for s in seqs for blk in range(s.T // TB)]

        def blockD(j, slot):
            s, blk = jobsD[j]
            mj = s.mj
            t0 = blk * TB
            B_ = DSET[slot]
            kk_ = B_["k"]
            HBd, GATd, XTd, X1Td, S0, S1, R1d, RSTDd = B_["HB"], B_["GATB"], B_["XT"], B_["X1T"], B_["S0"], B_["S1"], B_["R1"], B_["RSTD"]
            OATd, ODTd, MGd = HBd[:, 0:4, :], HBd[:, 4:8, :], HBd[:, 8:16, :]
            khb, kht, kg, kx, kx1, ks0, ks1 = "dhb" + kk_, "dht" + kk_, "dg" + kk_, "dx" + kk_, "dx1" + kk_, "ds0" + kk_, "ds1" + kk_
            for c in range(4):
                dma("sp", OATd[:, c, :], OA[s.name][2 * c:2 * c + 2].rearrange("h d t -> (h d) t")[:, t0:t0 + TB],
                    [("oa", s.name)], [khb])
            dma("sp", ODTd, OD[s.name][:, :, t0:t0 + TB], [("od", s.name)], [khb])
            dma("sp", GATd, GATES[s.name][:, :, t0:t0 + TB], [("gates", s.name, blk)], [kg])
            dma("sp", XTd, XRES[s.name][:, :, t0:t0 + TB], [("xres", s.name, blk)], [kx])
            yield
            for n in range(8):
                ns = slice(n * 128, (n + 1) * 128)
                for k in range(4):
                    mm(psb(B_["bpa"])[:, :TB], Wpa[:, k, ns], OATd[:, k, :], k == 0, k == 3, ["war", khb], [pk(B_["bpa"])])
                for k in range(4):
                    mm(psb(B_["bpd"])[:, :TB], Wpd[:, k, ns], ODTd[:, k, :], k == 0, k == 3, ["war", khb], [pk(B_["bpd"])])
                tt("dve", S0[:, :TB], psb(B_["bpa"])[:, :TB], GATd[:, n, :], ALU.mult, [pk(B_["bpa"]), kg], [ks0])
                tt("dve", S1[:, :TB], psb(B_["bpd"])[:, :TB], GATd[:, 8 + n, :], ALU.mult, [pk(B_["bpd"]), kg], [ks1])
                tt("pool", MGd[:, n, :], S0[:, :TB], S1[:, :TB], ALU.add, [ks0, ks1], [kht])
                yield
            for n in range(8):
                ns = slice(n * 128, (n + 1) * 128)
                bk = B_["bop"][n % 2]
                for k in range(8):
                    mm(psb(bk)[:, :TB], Wo[:, k, ns], MGd[:, k, :], k == 0, k == 7, ["war", kht], [pk(bk)])
                stt("dve", X1Td[:, n, :], psb(bk)[:, :TB], MOD[:, 16 + n, mj:mj + 1], XTd[:, n, :], ALU.mult, ALU.add,
                    [pk(bk), kx] + MK, [kx1])
                yield
            dma("sp", XRES[s.name][:, :, t0:t0 + TB], X1Td, [kx1], [("xres", s.name, blk)])
            rms_stats(X1Td, TB, kx1, sq=HBd[:, 0:8, :], sqkey=khb, bank=B_["bst"], r1=R1d, rstd=RSTDd, rkey=kk_)
            yield
            tt("dve", XTd, X1Td, bcast(RSTDd, 1, 8), ALU.mult, [kx1, "rstd" + kk_], [kx])
            for c in range(8):
                if c % 2 == 0:
                    act(MGd[:, c, :], XTd[:, c, :], AF.Identity, [kx] + MK, [kht],
                        scale=A2[:, c, mj:mj + 1], bias=MOD[:, 24 + c, mj:mj + 1])
                else:
                    tsc("dve", MGd[:, c, :], XTd[:, c, :], A2[:, c, mj:mj + 1], MOD[:, 24 + c, mj:mj + 1], ALU.mult, ALU.add,
                        [kx] + MK, [kht])
            dma("sp", H2[s.name][:, :, t0:t0 + TB], MGd, [kht], [("h2", s.name, blk)])

        def two_way(make, n, stagger):
            gens = [None, None]
            nxt = 0
            started = 0
            while True:
                progressed = False
                for slot in range(2):
                    if gens[slot] is None and nxt < n and (slot == 0 or started >= stagger or nxt > 1):
                        gens[slot] = make(nxt, slot)
                        nxt += 1
                    if gens[slot] is not None:
                        try:
                            next(gens[slot])
                            progressed = True
                            if slot == 0:
                                started += 1
                        except StopIteration:
                            gens[slot] = None
                            progressed = True
                if not progressed and nxt >= n and gens[0] is None and gens[1] is None:
                    break

        two_way(blockD, len(jobsD), 9)
        P.fence()
        if stop_after == "stageD":
            return finish()

        W1h = WAR[:, 0:8 * 2048].rearrange("p (k n) -> p k n", k=8)
        W2h = WAR[:, 16384:16384 + 16 * 1024].rearrange("p (k n) -> p k n", k=16)
        for hf in range(2):
            load_w(W1h, w1[l][:, hf * 2048:(hf + 1) * 2048], 8)
            load_w(W2h, w2[l][hf * 2048:(hf + 1) * 2048, :], 16)
            jobs = [(s, blk) for s in seqs for blk in range(s.T // TB)]
            XTs = [(XT, "bigA"), (X1T, "bigB")]
            H2Ts = [(HB[:, 0:8, :], "hb"), (HB[:, 8:16, :], "ht")]

            def ef_loads(j):
                s, blk = jobs[j]
                t0 = blk * TB
                h2t, hkey = H2Ts[j % 2]
                xt, xkey = XTs[j % 2]
                dma("sp", h2t, H2[s.name][:, :, t0:t0 + TB], [("h2", s.name, blk)], [hkey])
                dma("sp", xt, XRES[s.name][:, :, t0:t0 + TB], [("xres", s.name, blk)], [xkey])

            ef_loads(0)
            for j, (s, blk) in enumerate(jobs):
                mj = s.mj
                t0 = blk * TB
                H2T, hkey = H2Ts[j % 2]
                XTj, xkey = XTs[j % 2]
                if j + 1 < len(jobs):
                    ef_loads(j + 1)
                for f in range(16):
                    bk = 1 + f % 3
                    for k in range(8):
                        mm(psb(bk)[:, :TB], W1h[:, k, f * 128:(f + 1) * 128], H2T[:, k, :], k == 0, k == 7, ["war", hkey], [pk(bk)])
                    i = nstg()
                    act(STG[i][:, :TB], psb(bk)[:, :TB], AF.Relu, [pk(bk), "lvec"], [f"stg{i}"],
                        bias=b1f[:, hf * 16 + f:hf * 16 + f + 1])
                    tt("dve", GATB[:, f, :], STG[i][:, :TB], STG[i][:, :TB], ALU.mult,
                       [f"stg{i}"], ["gatb"])
                for n in range(8):
                    bk = 4 + n % 2
                    for f in range(16):
                        mm(psb(bk)[:, :TB], W2h[:, f, n * 128:(n + 1) * 128], GATB[:, f, :], f == 0, f == 15, ["war", "gatb"], [pk(bk)])
                    stt("dve", XTj[:, n, :], psb(bk)[:, :TB], MOD[:, 40 + n, mj:mj + 1], XTj[:, n, :], ALU.mult, ALU.add,
                        [pk(bk), xkey] + MK, [xkey])
                    if hf == 0:
                        tsc("dve", XTj[:, n, :], XTj[:, n, :], GB2[:, n, mj:mj + 1], None, ALU.add, None, [xkey] + MK, [xkey])
                if not (l == NLAYERS - 1 and hf == 1):
                    dma("pool", XRES[s.name][:, :, t0:t0 + TB], XTj, [xkey], [("xres", s.name, blk)])
                else:
                    rms_stats(XTj, TB, xkey, sq=GATB[:, 0:8, :], sqkey="gatb")
                    tt("dve", XTj, XTj, bcast(RSTD[:, :], 1, 8), ALU.mult, [xkey, "rstd"], [xkey])
                    tt("dve", XTj, XTj, bcast(fnf[:], 2, TB), ALU.mult, [xkey, "fnf"], [xkey])
                    dst = ys_out if s.is_sample else yp_out[s.idx * TP:(s.idx + 1) * TP, :]
                    for t2 in range(TB // 128):
                        for hh in range(2):
                            bk = 6 + hh
                            for c in range(4):
                                tr(psb(bk)[:, c * 128:(c + 1) * 128], XTj[:, hh * 4 + c, t2 * 128:(t2 + 1) * 128], C("ident"),
                                   [xkey, "cst"], [pk(bk)])
                            i = nstg()
                            cp("act" if hh == 0 else "dve", STG[i][:, :], psb(bk)[:, :], [pk(bk)], [f"stg{i}"])
                            dma("pool", dst[t0 + t2 * 128:t0 + (t2 + 1) * 128, hh * 512:(hh + 1) * 512], STG[i][:, :],
                                [f"stg{i}"], [("y", s.name)])
        P.fence()
        if stop_after == f"layer{l}":
            return finish()

    return finish()


def make_in_maps(inputs):
    cos, sin = _rope_tables()
    maps = []
    for core in range(8):
        b = core % 4
        m = {
            "xs": np.ascontiguousarray(inputs["x_sample"][b]),
            "xp": np.ascontiguousarray(inputs["x_prompt"][core * NPS:(core + 1) * NPS].reshape(NPS * TP, D)),
            "cvec": np.ascontiguousarray(np.stack([inputs["c"][b], inputs["c_ctx"]], 0)),
            "ck": np.ascontiguousarray(inputs["cache_k"][b]),
            "cv": np.ascontiguousarray(inputs["cache_v"][b]),
            "st": np.ascontiguousarray(inputs["state_delta"][b]),
            "a_log": np.ascontiguousarray(inputs["a_log"].reshape(DEPTH, 16)),
            "dt_bias": np.ascontiguousarray(inputs["dt_bias"].reshape(DEPTH, 16)),
            "cst": CST, "ropecos": cos, "ropesin": sin,
        }
        for k in ["w_mod", "b_mod", "norm1", "norm2", "w_in", "conv_w", "q_gain", "k_gain", "dn_gain",
                  "w_pa", "w_pd", "w_out", "w1", "b1", "w2", "b2", "final_norm"]:
            m[k] = np.ascontiguousarray(inputs[k])
        maps.append(m)
    return maps


_NC_CACHE = {}


def kernel(**inputs):
    inputs = {k: np.asarray(v) for k, v in inputs.items()}
    if "nc" not in _NC_CACHE:
        _NC_CACHE["nc"] = build_program()
    nc = _NC_CACHE["nc"]
    maps = make_in_maps(inputs)
    res = run_bass_kernel_spmd(nc, maps, core_ids=list(range(8)))
    r = res.results
    y_sample = np.stack([r[b]["ys"] for b in range(4)], 0)
    y_prompt = np.concatenate([r[c]["yp"].reshape(NPS, TP, D) for c in range(8)], 0)
    nk = np.concatenate([r[c]["nk"] for c in range(8)], 0)
    nv = np.concatenate([r[c]["nv"] for c in range(8)], 0)
    nst = np.concatenate([r[c]["nst"] for c in range(8)], 0)
    return (y_prompt.astype(np.float32), y_sample.astype(np.float32), nk.astype(np.float32),
            nv.astype(np.float32), nst.astype(np.float32))
```
